# Optimizing a Trainium2 kernel written in Bass

```python
import math
import jax, jax.numpy as jnp
from jax import lax
import numpy as np

D_MODEL = 1024
BATCH = 16
SEQ = 256
DEPTH = 4
DEC_BATCH = 4
DEC_SEQ = 1024
PAST_LEN = 512

GRID_W = 64
N_EVEN = (DEPTH + 1) // 2
N_ODD = DEPTH // 2
SSD_HEADS = 16
SSD_HD = 64
SSD_INNER = SSD_HEADS * SSD_HD
SSD_GROUPS = 4
HEADS_PER_GROUP = SSD_HEADS // SSD_GROUPS
D_STATE = 128
CONV_W = 5
CONV_CH = SSD_INNER + 2 * SSD_GROUPS * D_STATE
CHUNK = 128
ATT_HEADS = 8
ATT_HD = 64
ATT_VD = 2 * ATT_HD
ATT_INNER = ATT_HEADS * ATT_VD
Q_BLOCK = 128
ROPE_THETA = 10000.0
ROPE_PAIRS = ATT_HD // 4
FOUR_GROUPS = 4
FFN_HIDDEN = -(-8 * D_MODEL // (3 * 256)) * 256
IN_AB = SSD_INNER + CONV_CH + 2 * SSD_HEADS + 3 * ATT_INNER
OUT_AB = SSD_INNER + ATT_INNER
EPS = 1e-6

kernel_name = "hybrid_ssd_diffattn_fnet_diffusion_step"


def rmsnorm(x, g):
    xf = x.astype(jnp.float32)
    y = xf * lax.rsqrt(jnp.mean(xf * xf, axis=-1, keepdims=True) + EPS)
    return (y * g.astype(jnp.float32)).astype(x.dtype)


def modulate(x, g, shift, scale):
    return rmsnorm(x, g) * (1 + scale) + shift


def depthwise_conv(x, w, b):
    y = lax.conv_general_dilated(
        x, w.astype(x.dtype)[:, None, :], window_strides=(1,),
        padding=[(CONV_W // 2, CONV_W // 2)],
        dimension_numbers=("NWC", "WIO", "NWC"),
        feature_group_count=x.shape[-1])
    return y + b


def ssd_scan(x, dt, a, bm, cm, h0):
    bsz, L = x.shape[:2]
    q = math.gcd(L, CHUNK)
    nc = L // q
    G, R, P, N = SSD_GROUPS, HEADS_PER_GROUP, SSD_HD, D_STATE
    xf = x.astype(jnp.float32).reshape(bsz, nc, q, G, R, P)
    dtc = dt.reshape(bsz, nc, q, G, R)
    bc = bm.astype(jnp.float32).reshape(bsz, nc, q, G, N)
    cc = cm.astype(jnp.float32).reshape(bsz, nc, q, G, N)
    acs = jnp.cumsum(dtc * a, axis=2)
    seg = acs[:, :, :, None] - acs[:, :, None, :]
    lower = jnp.tril(jnp.ones((q, q), dtype=bool))[None, None, :, :, None, None]
    decay = jnp.exp(jnp.where(lower, seg, -jnp.inf))
    cb = jnp.einsum("bcign,bcjgn->bcijg", cc, bc)
    w_ij = cb[..., None] * decay * dtc[:, :, None]
    y_in = jnp.einsum("bcijgr,bcjgrp->bcigrp", w_ij, xf)
    to_end = jnp.exp(acs[:, :, -1:] - acs) * dtc
    st = jnp.einsum("bcjgrp,bcjgn->bcgrpn", to_end[..., None] * xf, bc)
    chunk_decay = jnp.exp(acs[:, :, -1])

    def step(h, inp):
        dec, s = inp
        return dec[..., None, None] * h + s, h

    h_last, h_prev = lax.scan(step, h0.astype(jnp.float32),
                              (jnp.moveaxis(chunk_decay, 1, 0), jnp.moveaxis(st, 1, 0)))
    h_prev = jnp.moveaxis(h_prev, 0, 1)
    y_out = jnp.einsum("bcign,bcgrpn->bcigrp", cc, h_prev) * jnp.exp(acs)[..., None]
    return (y_in + y_out).reshape(bsz, L, G, R, P), h_last


def rope_tables(L):
    rows = L // GRID_W
    t = jnp.arange(rows * GRID_W)
    row = (t // GRID_W).astype(jnp.float32)
    col = (t % GRID_W).astype(jnp.float32)
    freq = ROPE_THETA ** (-jnp.arange(ROPE_PAIRS, dtype=jnp.float32) / ROPE_PAIRS)
    ar = row[:, None] * freq[None]
    ac = col[:, None] * freq[None]
    return (jnp.cos(ar), jnp.sin(ar), jnp.cos(ac), jnp.sin(ac))


def _rot(x, cos, sin):
    a, b = x[..., :ROPE_PAIRS], x[..., ROPE_PAIRS:]
    cos = cos[None, :, None, None, :]
    sin = sin[None, :, None, None, :]
    return jnp.concatenate([a * cos - b * sin, a * sin + b * cos], axis=-1)


def rope2d(x, rope):
    cr, sr, ccol, scol = rope
    half = ATT_HD // 2
    return jnp.concatenate([_rot(x[..., :half], cr, sr), _rot(x[..., half:], ccol, scol)],
                           axis=-1).astype(x.dtype)


def diff_attend(q, k, v, lam):
    bsz, lq = q.shape[:2]
    blk = math.gcd(lq, Q_BLOCK)
    nb = lq // blk
    qb = jnp.moveaxis(q.reshape(bsz, nb, blk, ATT_HEADS, 2, ATT_HD), 1, 0)
    scale = ATT_HD ** -0.5

    def one(qblk):
        s = jnp.einsum("bqhmd,bkhmd->bhmqk", qblk, k).astype(jnp.float32) * scale
        p = jax.nn.softmax(s, axis=-1)
        pd = p[:, :, 0] - lam * p[:, :, 1]
        return jnp.einsum("bhqk,bkhe->bqhe", pd.astype(v.dtype), v)

    o = lax.map(one, qb)
    return jnp.moveaxis(o, 0, 1).reshape(bsz, lq, ATT_HEADS, ATT_VD)


def ab_mixer(h, w_in, conv_w, conv_b, dt_bias, a_log, d_skip, ssd_norm, lam_qk, subln,
             w_out, lambda_init, h0_f, h0_b, ctx_k=None, ctx_v=None, rope=None):
    bsz, L, _ = h.shape
    G, R, P, N = SSD_GROUPS, HEADS_PER_GROUP, SSD_HD, D_STATE
    proj = h @ w_in
    o1 = SSD_INNER
    o2 = o1 + CONV_CH
    o3 = o2 + 2 * SSD_HEADS
    o4 = o3 + ATT_INNER
    o5 = o4 + ATT_INNER
    z, xbc, dt_raw, q, k, v = jnp.split(proj, [o1, o2, o3, o4, o5], axis=-1)

    xbc = jax.nn.silu(depthwise_conv(xbc, conv_w, conv_b))
    xs, bm, cm = jnp.split(xbc, [SSD_INNER, SSD_INNER + G * N], axis=-1)
    xs = xs.reshape(bsz, L, G, R, P)
    bm = bm.reshape(bsz, L, G, N)
    cm = cm.reshape(bsz, L, G, N)
    dt = jax.nn.softplus(dt_raw.astype(jnp.float32).reshape(bsz, L, 2, SSD_HEADS)
                         + dt_bias.astype(jnp.float32)).reshape(bsz, L, 2, G, R)
    a = -jnp.exp(a_log.astype(jnp.float32)).reshape(2, G, R)
    y_f, hf = ssd_scan(xs, dt[:, :, 0], a[0], bm, cm, h0_f.reshape(bsz, G, R, P, N))
    y_b, hb = ssd_scan(jnp.flip(xs, 1), jnp.flip(dt[:, :, 1], 1), a[1],
                       jnp.flip(bm, 1), jnp.flip(cm, 1), h0_b.reshape(bsz, G, R, P, N))
    y = y_f + jnp.flip(y_b, 1) + d_skip.reshape(G, R)[..., None] * xs
    y = y.reshape(bsz, L, SSD_INNER).astype(h.dtype)
    y_ssd = rmsnorm(y * jax.nn.silu(z), ssd_norm)

    q = q.reshape(bsz, L, ATT_HEADS, 2, ATT_HD)
    k = k.reshape(bsz, L, ATT_HEADS, 2, ATT_HD)
    v = v.reshape(bsz, L, ATT_HEADS, ATT_VD)
    if rope is not None:
        q_att = rope2d(q, rope)
        k_att = rope2d(k, rope)
    else:
        q_att, k_att = q, k
    if ctx_k is not None:
        k_all = jnp.concatenate([ctx_k.astype(k_att.dtype), k_att], axis=1)
        v_all = jnp.concatenate([ctx_v.astype(v.dtype), v], axis=1)
    else:
        k_all, v_all = k_att, v
    lq = lam_qk.astype(jnp.float32)
    lam = jnp.exp(jnp.sum(lq[0] * lq[1])) - jnp.exp(jnp.sum(lq[2] * lq[3])) + lambda_init
    o = diff_attend(q_att, k_all, v_all, lam)
    o = rmsnorm(o, subln) * (1.0 - lambda_init)

    out = jnp.concatenate([y_ssd, o.reshape(bsz, L, ATT_INNER)], axis=-1) @ w_out
    hf = hf.reshape(bsz, SSD_HEADS, P, N).astype(h.dtype)
    hb = hb.reshape(bsz, SSD_HEADS, P, N).astype(h.dtype)
    return out, k, v, hf, hb


def fourier_mix(h, w, b):
    bsz, L, D = h.shape
    hg = h.reshape(bsz, L, FOUR_GROUPS, D // FOUR_GROUPS).astype(jnp.float32)
    f = jnp.fft.fftn(hg, axes=(1, 3), norm="ortho").real
    return f.reshape(bsz, L, D).astype(h.dtype) @ w + b


def swiglu(h, w_in, w_out):
    g, u = jnp.split(h @ w_in, 2, axis=-1)
    return (jax.nn.silu(g) * u) @ w_out


def setup_inputs(seed: int = 0) -> dict:
    key = jax.random.key(seed)
    ks = jax.random.split(key, 32)
    f32 = jnp.float32

    def nrm(k, shape, scale):
        return jax.random.normal(k, shape, f32) * scale

    dt0 = jnp.exp(jax.random.uniform(ks[12], (N_EVEN, 2, SSD_HEADS), f32,
                                     math.log(1e-3), math.log(1e-1)))
    dt_bias = dt0 + jnp.log(-jnp.expm1(-dt0))
    a_log = jnp.log(jax.random.uniform(ks[13], (N_EVEN, 2, SSD_HEADS), f32, 1.0, 16.0))
    return {
        "x_prompt": nrm(ks[0], (BATCH, SEQ, D_MODEL), 1.0),
        "x_sample": nrm(ks[1], (DEC_BATCH, DEC_SEQ, D_MODEL), 1.0),
        "cache_k": nrm(ks[2], (DEC_BATCH, N_EVEN, PAST_LEN, ATT_HEADS, 2, ATT_HD), 1.0),
        "cache_v": nrm(ks[3], (DEC_BATCH, N_EVEN, PAST_LEN, ATT_HEADS, ATT_VD), 1.0),
        "state_ssd_fwd": nrm(ks[4], (DEC_BATCH, N_EVEN, SSD_HEADS, SSD_HD, D_STATE), 0.05),
        "state_ssd_bwd": nrm(ks[5], (DEC_BATCH, N_EVEN, SSD_HEADS, SSD_HD, D_STATE), 0.05),
        "c": nrm(ks[6], (DEC_BATCH, D_MODEL), 1.0),
        "c_ctx": nrm(ks[7], (D_MODEL,), 1.0),
        "w_ada": nrm(ks[8], (DEPTH, D_MODEL, 6 * D_MODEL), 0.5 * D_MODEL ** -0.5),
        "b_ada": nrm(ks[9], (DEPTH, 6 * D_MODEL), 0.01),
        "norm_mix": 1.0 + nrm(ks[10], (DEPTH, D_MODEL), 0.02),
        "norm_ffn": 1.0 + nrm(ks[11], (DEPTH, D_MODEL), 0.02),
        "w_in_ab": nrm(ks[14], (N_EVEN, D_MODEL, IN_AB), D_MODEL ** -0.5),
        "conv_w": nrm(ks[15], (N_EVEN, CONV_W, CONV_CH), CONV_W ** -0.5),
        "conv_b": nrm(ks[16], (N_EVEN, CONV_CH), 0.02),
        "dt_bias": dt_bias,
        "a_log": a_log,
        "d_skip": 1.0 + nrm(ks[17], (N_EVEN, SSD_HEADS), 0.1),
        "ssd_norm": 1.0 + nrm(ks[18], (N_EVEN, SSD_INNER), 0.02),
        "lambda_qk": nrm(ks[19], (N_EVEN, 4, ATT_HD), 0.1),
        "subln": 1.0 + nrm(ks[20], (N_EVEN, ATT_VD), 0.02),
        "w_out_ab": nrm(ks[21], (N_EVEN, OUT_AB, D_MODEL), OUT_AB ** -0.5),
        "w_four": nrm(ks[22], (N_ODD, D_MODEL, D_MODEL), D_MODEL ** -0.5),
        "b_four": nrm(ks[23], (N_ODD, D_MODEL), 0.01),
        "w_ffn_in": nrm(ks[24], (DEPTH, D_MODEL, 2 * FFN_HIDDEN), D_MODEL ** -0.5),
        "w_ffn_out": nrm(ks[25], (DEPTH, FFN_HIDDEN, D_MODEL), FFN_HIDDEN ** -0.5),
        "norm_final": 1.0 + nrm(ks[26], (D_MODEL,), 0.02),
    }


def reference(x_prompt, x_sample, cache_k, cache_v, state_ssd_fwd, state_ssd_bwd, c, c_ctx,
              w_ada, b_ada, norm_mix, norm_ffn, w_in_ab, conv_w, conv_b, dt_bias, a_log,
              d_skip, ssd_norm, lambda_qk, subln, w_out_ab, w_four, b_four, w_ffn_in,
              w_ffn_out, norm_final):
    xp, xs = x_prompt, x_sample
    bp = xp.shape[0]
    rope = rope_tables(xs.shape[1])
    silu_ctx = jax.nn.silu(c_ctx)
    silu_c = jax.nn.silu(c)
    zero_state = jnp.zeros((bp, SSD_HEADS, SSD_HD, D_STATE), xp.dtype)
    new_k, new_v, new_f, new_b = [], [], [], []
    for l in range(DEPTH):
        mp = silu_ctx @ w_ada[l] + b_ada[l]
        ms = (silu_c @ w_ada[l] + b_ada[l])[:, None, :]
        sh1p, sc1p, g1p, sh2p, sc2p, g2p = jnp.split(mp, 6, axis=-1)
        sh1s, sc1s, g1s, sh2s, sc2s, g2s = jnp.split(ms, 6, axis=-1)
        hp = modulate(xp, norm_mix[l], sh1p, sc1p)
        hs = modulate(xs, norm_mix[l], sh1s, sc1s)
        if l % 2 == 0:
            e = l // 2
            lambda_init = 0.8 - 0.6 * math.exp(-0.3 * l)
            w = (w_in_ab[e], conv_w[e], conv_b[e], dt_bias[e], a_log[e], d_skip[e],
                 ssd_norm[e], lambda_qk[e], subln[e], w_out_ab[e], lambda_init)
            op, kp, vp, hfp, hbp = ab_mixer(hp, *w, zero_state, zero_state)
            os_, _, _, _, _ = ab_mixer(hs, *w, state_ssd_fwd[:, e], state_ssd_bwd[:, e],
                                       cache_k[:, e], cache_v[:, e], rope)
            new_k.append(kp)
            new_v.append(vp)
            new_f.append(hfp)
            new_b.append(hbp)
        else:
            o_i = l // 2
            op = fourier_mix(hp, w_four[o_i], b_four[o_i])
            os_ = fourier_mix(hs, w_four[o_i], b_four[o_i])
        xp = xp + g1p * op
        xs = xs + g1s * os_
        xp = xp + g2p * swiglu(modulate(xp, norm_ffn[l], sh2p, sc2p), w_ffn_in[l], w_ffn_out[l])
        xs = xs + g2s * swiglu(modulate(xs, norm_ffn[l], sh2s, sc2s), w_ffn_in[l], w_ffn_out[l])
    y_prompt = rmsnorm(xp, norm_final)
    y_sample = rmsnorm(xs, norm_final)
    return (y_prompt, y_sample, jnp.stack(new_k, axis=1), jnp.stack(new_v, axis=1),
            jnp.stack(new_f, axis=1), jnp.stack(new_b, axis=1))
```

```python
import math
import numpy as np
from contextlib import ExitStack
import concourse.bass as bass
import concourse.mybir as mybir
from concourse.bass_utils import run_bass_kernel_spmd

F32 = mybir.dt.float32
BF16 = mybir.dt.bfloat16
AF = mybir.ActivationFunctionType
ALU = mybir.AluOpType

DEPTH_RUN = 4
STOP_AT = None
SKIP_KVO = False
DBG_CORES = 8


class _Stop(Exception):
    pass


_STOPPED = [False]


def checkpoint(name):
    if STOP_AT == name:
        _STOPPED[0] = True
D = 1024
T = 1536
NT = 12
EPS = 1e-6
SEQS = [(0, 2), (2, 2), (4, 8)]
FFH = 2816
NJ = 22
IN_AB = 6176
RING_SLOTS = 3
RING_ELEMS = 4096


def _layout(items):
    d, off = {}, 0
    for n, w in items:
        d[n] = (off, w)
        off += w
    return d, off


def param_layout():
    it = [("cvec", 16), ("nfin", 8)]
    for l in range(4):
        it += [(f"nm{l}", 8), (f"nf{l}", 8), (f"bada{l}", 48)]
    for e in range(2):
        it += [(f"cw{e}", 80), (f"cb{e}", 16), (f"ssdn{e}", 8), (f"subln{e}", 1), (f"dtb{e}", 32),
               (f"alog{e}", 32), (f"dsk{e}", 16), (f"lqk{e}", 256), (f"b4{e}", 8)]
    return _layout(it)


def const_layout():
    it = [("ident", 128), ("ones", 128), ("Uf", 128), ("Ub", 128), ("Vf", 128), ("Vb", 128),
          ("NEGf", 128), ("NEGb", 128), ("Psw", 128), ("Cc", 512), ("Sc", 512), ("CL256", 512), ("nSL256", 512)]
    return _layout(it)


PL, NPAR = param_layout()
CL, NCON = const_layout()


def conv_chan(sl):
    g, s = divmod(sl, 4)
    if s < 2:
        return g * 256 + s * 128 + np.arange(128)
    if s == 2:
        return 1024 + g * 128 + np.arange(128)
    return 1536 + g * 128 + np.arange(128)


def win_perm():
    o_z, o_x, o_dt, o_q, o_k, o_v = 0, 1024, 3072, 3104, 4128, 5152
    cols = list(o_dt + np.arange(32))
    for g in range(4):
        cols += list(o_x + g * 256 + np.arange(256))
        cols += list(o_x + 1024 + g * 128 + np.arange(128))
        cols += list(o_x + 1536 + g * 128 + np.arange(128))
        cols += list(o_z + g * 256 + np.arange(256))
    for h in range(8):
        cols += list(o_q + h * 128 + np.arange(128))
        cols += list(o_k + h * 128 + np.arange(128))
        cols += list(o_v + h * 128 + np.arange(128))
    return np.array(cols)


def ffi_perm():
    cols = []
    for t in range(11):
        for j in (2 * t, 2 * t + 1):
            cols += list(j * 128 + np.arange(128))
        for j in (2 * t, 2 * t + 1):
            cols += list(FFH + j * 128 + np.arange(128))
    return np.array(cols)


def make_consts():
    c = np.zeros((128, NCON), np.float32)
    t = np.arange(128)[:, None]
    j = np.arange(128)[None, :]

    def put(n, a):
        o, w = CL[n]
        c[:, o:o + w] = a.reshape(128, w)
    put("ident", (t == j).astype(np.float32))
    put("ones", np.ones((128, 128), np.float32))
    put("Uf", (t > j).astype(np.float32))
    put("Ub", (t < j).astype(np.float32))
    put("Vf", (t <= j).astype(np.float32))
    put("Vb", (t >= j).astype(np.float32))
    put("NEGf", -30000.0 * (j < t))
    put("NEGb", -30000.0 * (j > t))
    d = np.arange(128) % 64
    partner = np.where(d % 32 < 16, np.arange(128) + 16, np.arange(128) - 16)
    psw = np.zeros((128, 128), np.float32)
    psw[partner, np.arange(128)] = 1.0
    put("Psw", psw)
    n = np.arange(256)
    ang = 2 * np.pi * np.outer(n, n) / 256.0
    cc = (np.cos(ang) / 16.0).reshape(2, 128, 256).transpose(1, 0, 2)
    ss = (np.sin(ang) / 16.0).reshape(2, 128, 256).transpose(1, 0, 2)
    put("Cc", cc)
    put("Sc", ss)
    put("CL256", cc)
    put("nSL256", -ss)
    n = np.arange(1024)
    ang = 2 * np.pi * np.outer(n, n) / 1024.0
    cl = (np.cos(ang) / 32.0).reshape(8, 128, 1024).transpose(1, 0, 2)
    sl = (-np.sin(ang) / 32.0).reshape(8, 128, 1024).transpose(1, 0, 2)
    dft = np.ascontiguousarray(np.stack([cl, sl], 0).astype(np.float32))
    tok = np.arange(1024)
    row = (tok // 64).astype(np.float64)
    col = (tok % 64).astype(np.float64)
    freq = 10000.0 ** (-np.arange(16, dtype=np.float64) / 16.0)
    cosT = np.zeros((128, 1024), np.float32)
    sinT = np.zeros((128, 1024), np.float32)
    for p in range(128):
        dd = p % 64
        pos = row if dd < 32 else col
        a = pos * freq[dd % 16]
        cosT[p] = np.cos(a)
        sinT[p] = (-np.sin(a)) if (dd % 32) < 16 else np.sin(a)
    rope = np.ascontiguousarray(np.stack([cosT, sinT], 0))
    return c, dft, rope


class Buf:
    __slots__ = ("w", "r", "x", "fresh")

    def __init__(self, x=False):
        self.w = None
        self.r = {}
        self.fresh = True
        self.x = x


def BG(*shape):
    if len(shape) == 1:
        return [Buf() for _ in range(shape[0])]
    return [BG(*shape[1:]) for _ in range(shape[0])]


def flat(x):
    if isinstance(x, Buf):
        return [x]
    out = []
    for y in x:
        out += flat(y)
    return out


class Prog:
    def __init__(self, nc, es):
        self.nc = nc
        self.es = es
        self.E = {"pe": nc.tensor, "act": nc.scalar, "dve": nc.vector, "pool": nc.gpsimd, "sp": nc.sync}
        self.sems, self.cnt = {}, {}
        self.known = {e: {} for e in self.E}
        self.floor = {}
        self.isdma = set()
        for e in ("pe", "act", "dve", "pool"):
            self.sems[e] = es.enter_context(nc.semaphore("c_" + e))
            self.cnt[e] = 0
        self.nins = 0
        self.trace = {}

    def dsem(self, key):
        if key not in self.sems:
            self.sems[key] = self.es.enter_context(self.nc.semaphore("d_" + key))
            self.cnt[key] = 0
            self.isdma.add(key)
        return self.sems[key]

    def op(self, e, fn, reads=(), writes=(), sig=True, dkey=None, nofence=False):
        if _STOPPED[0]:
            return None
        reqs = {}

        def need(k, v):
            if k in self.isdma:
                v = self.cnt[k]
            elif k == e:
                if e == "pe":
                    return
            if reqs.get(k, 0) < v:
                reqs[k] = v
        reads = flat(reads)
        writes = flat(writes)
        if any(b.fresh for b in reads) or any(b.fresh for b in writes):
            for k, v in self.floor.items():
                if k != e:
                    need(k, v)
                elif e != "pe" and reqs.get(k, 0) < v:
                    reqs[k] = v
            for b in reads:
                b.fresh = False
            for b in writes:
                b.fresh = False
        for b in reads:
            if b.w is not None:
                need(*b.w)
            if b.x:
                for k, v in b.r.items():
                    if k != e:
                        need(k, v)
        for b in writes:
            if b.w is not None:
                need(*b.w)
            for k, v in b.r.items():
                need(k, v)
        kn = self.known[e]
        for k, v in reqs.items():
            if kn.get(k, 0) < v:
                self.E[e].wait_ge(self.sems[k], v)
                kn[k] = v
                self.trace.setdefault(e, []).append(("w", k, v))
        ins = fn()
        self.nins += 1
        self.trace.setdefault(e, []).append(("i", dkey if dkey is not None else (e if sig else None), 16 if dkey is not None else 1))
        if dkey is not None:
            sem = self.dsem(dkey)
            ins.then_inc(sem, 16)
            self.cnt[dkey] += 16
            stamp = (dkey, self.cnt[dkey])
        elif sig:
            ins.then_inc(self.sems[e], 1)
            self.cnt[e] += 1
            stamp = (e, self.cnt[e])
        else:
            stamp = (e, self.cnt[e] + 1)
        for b in writes:
            b.w = stamp
            b.r = {}
        k, v = stamp
        for b in reads:
            if b.r.get(k, 0) < v:
                b.r[k] = v
        return ins

    def fence(self):
        self.floor = {k: v for k, v in self.cnt.items() if not k.startswith("ring")}


def build_program():
    _STOPPED[0] = False
    nc = bass.Bass("TRN2", target_bir_lowering=False)
    dr = lambda n, s, kind="ExternalInput": nc.dram_tensor(n, s, F32, kind=kind).ap()
    xT_in = dr("xT_in", [D, T])
    par_in = dr("par", [128, NPAR])
    con_in = dr("con", [128, NCON])
    dft_in = dr("dft", [2, 128, 8, 1024])
    rope_in = dr("rope", [2, 128, 1024])
    ck_in = dr("ck", [2, 8, 128, 512])
    cv_in = dr("cv", [2, 512, 1024])
    sf_in = dr("sf", [2, 128, 1024])
    sb_in = dr("sb", [2, 128, 1024])
    w_ada = dr("w_ada", [4, D, 6144])
    w_inp = dr("w_inp", [2, D, IN_AB])
    w_out = dr("w_out", [2, 2048, D])
    w_four = dr("w_four", [2, D, D])
    w_ffi = dr("w_ffi", [4, D, 2 * FFH])
    w_ffo = dr("w_ffo", [4, FFH, D])
    yT_out = dr("yT", [D, T], "ExternalOutput")
    nk_out = dr("nk", [2, 512, 1024], "ExternalOutput")
    nv_out = dr("nv", [2, 512, 1024], "ExternalOutput")
    nf_out = dr("nf", [2, 2, 128, 1024], "ExternalOutput")
    nb_out = dr("nb", [2, 2, 128, 1024], "ExternalOutput")

    es = ExitStack()
    P = Prog(nc, es)
    op = P.op
    PE, ACT, DVE = nc.tensor, nc.scalar, nc.vector

    uid = {"n": 0}

    def sbt(stack, name, shape, dt):
        uid["n"] += 1
        return stack.enter_context(nc.sbuf_tensor(f"{name}_{uid['n']}", shape, dt))

    xT = sbt(es, "xT", [128, 8, T], F32)
    xB = BG(8, 3)
    par = sbt(es, "par_sb", [128, NPAR], F32)
    parB = Buf()
    cbf = sbt(es, "cbf", [128, 9 * 128], BF16)
    cbfB = Buf()
    cf32 = sbt(es, "cf32", [128, 4 * 128], F32)
    cfB = Buf()
    scT = sbt(es, "scT", [128, 8, 2], BF16)
    scB = Buf()
    mod = [sbt(es, f"mod{i}", [128, 48, 2], F32) for i in range(2)]
    modB = [Buf(), Buf()]
    gsc = [sbt(es, f"gsc{i}", [128, 2, 8, 2], F32) for i in range(2)]
    gscB = [Buf(), Buf()]
    ring = sbt(es, "ring", [128, RING_SLOTS, RING_ELEMS], BF16)
    ringB = BG(RING_SLOTS)
    psum = [es.enter_context(nc.psum_tensor(f"ps{i}", [128, 2, 512], F32)) for i in range(4)]
    PSB = [Buf(x=True) for _ in range(8)]

    def PS(i):
        return psum[i // 2][:, i % 2, :]

    def pv(name):
        o, w = PL[name]
        return par[:, o:o + w]

    def cb(name):
        o, w = CL[name]
        return cbf[:, o:o + w]

    plan = []

    def wtile(dram_ap, kc, ncols, tag):
        plan.append((dram_ap, kc, ncols, tag))

    for l in range(DEPTH_RUN):
        wa0 = w_ada[0].rearrange("(kc p) f -> p kc f", p=128)
        if l == 0:
            for t in range(4):
                wtile(wa0[:, :, t * 512:(t + 1) * 512], 8, 512, f"ada0_{t}")
        if l % 2 == 0:
            e = l // 2
            wi = w_inp[e].rearrange("(kc p) f -> p kc f", p=128)
            wo = w_out[e].rearrange("(kc p) f -> p kc f", p=128)
            wtile(wi[:, :, 0:32], 8, 32, f"dt{e}")
            for g in range(4):
                b0 = 32 + g * 768
                wtile(wi[:, :, b0:b0 + 512], 8, 512, f"A1_{e}_{g}")
                if l == 0:
                    t = 4 + 2 * g
                    wtile(wa0[:, :, t * 512:(t + 1) * 512], 8, 512, f"ada0_{t}")
                wtile(wi[:, :, b0 + 512:b0 + 768], 8, 256, f"A2_{e}_{g}")
                if l == 0:
                    t = 5 + 2 * g
                    wtile(wa0[:, :, t * 512:(t + 1) * 512], 8, 512, f"ada0_{t}")
            for t in range(2):
                wtile(wo[:, 0:8, t * 512:(t + 1) * 512], 8, 512, f"O1_{e}_{t}")
            for h in range(8):
                b0 = 3104 + h * 384
                wtile(wi[:, :, b0:b0 + 384], 8, 384, f"QKV_{e}_{h}")
            for t in range(2):
                wtile(wo[:, 8:16, t * 512:(t + 1) * 512], 8, 512, f"O2_{e}_{t}")
        else:
            o = l // 2
            w4 = w_four[o].rearrange("(kc p) f -> p kc f", p=128)
            for t in range(2):
                wtile(w4[:, :, t * 512:(t + 1) * 512], 8, 512, f"W4_{o}_{t}")
        wf = w_ffi[l].rearrange("(kc p) f -> p kc f", p=128)
        wa = w_ada[l + 1].rearrange("(kc p) f -> p kc f", p=128) if l + 1 < DEPTH_RUN else None
        for t in range(11):
            wtile(wf[:, :, t * 512:(t + 1) * 512], 8, 512, f"FI_{l}_{t}")
            if wa is not None:
                wtile(wa[:, :, t * 512:(t + 1) * 512], 8, 512, f"ada{l + 1}_{t}")
        if wa is not None:
            wtile(wa[:, :, 11 * 512:12 * 512], 8, 512, f"ada{l + 1}_11")
        wfo = w_ffo[l].rearrange("(kc p) f -> p kc f", p=128)
        for t in range(8):
            wtile(wfo[:, :, t * 128:(t + 1) * 128], 22, 128, f"FO_{l}_{t}")

    st = {"issued": 0, "used": 0}

    def ring_issue_upto(n):
        while st["issued"] <= n and st["issued"] < len(plan):
            m = st["issued"]
            ap, kc, ncols, tag = plan[m]
            s = m % RING_SLOTS
            dst = ring[:, s, 0:kc * ncols].rearrange("p (k c) -> p k c", k=kc)
            op("pool", lambda dst=dst, ap=ap: nc.gpsimd.dma_start(out=dst, in_=ap),
               writes=[ringB[s]], dkey=f"ring{s}", nofence=True)
            st["issued"] += 1

    def wget(tag):
        n = st["used"]
        ap, kc, ncols, tg = plan[n]
        assert tg == tag, (tg, tag)
        ring_issue_upto(n + RING_SLOTS - 1)
        st["used"] += 1
        s = n % RING_SLOTS
        return ring[:, s, 0:kc * ncols].rearrange("p (k c) -> p k c", k=kc), ringB[s]

    rot = {"i": 0}

    def nbank(lo=0, hi=6):
        b = lo + rot["i"] % (hi - lo)
        rot["i"] += 1
        return b

    op("sp", lambda: nc.sync.dma_start(out=par[:], in_=par_in[:, :]), writes=[parB], dkey="ld0")
    with ExitStack() as s0:
        con = sbt(s0, "con_sb", [128, NCON], F32)
        conB = Buf()
        op("sp", lambda: nc.sync.dma_start(out=con[:], in_=con_in[:, :]), writes=[conB], dkey="ld0")
        xin = xT_in.rearrange("(kc p) t -> p kc t", p=128)
        for kc in range(8):
            op("sp", lambda kc=kc: nc.sync.dma_start(out=xT[:, kc, :], in_=xin[:, kc, :]), writes=xB[kc], dkey="ld1")
        op("dve", lambda: DVE.tensor_copy(out=cbf[:], in_=con[:, 0:9 * 128]), reads=[conB], writes=[cbfB])
        o1 = CL["ones"][0]
        op("act", lambda: ACT.copy(out=cf32[:, 0:128], in_=con[:, o1:o1 + 128]), reads=[conB], writes=[cfB])
        o2 = CL["Vf"][0]
        op("act", lambda: ACT.copy(out=cf32[:, 128:384], in_=con[:, o2:o2 + 256]), reads=[conB], writes=[cfB])
        o3 = CL["ident"][0]
        op("act", lambda: ACT.copy(out=cf32[:, 384:512], in_=con[:, o3:o3 + 128]), reads=[conB], writes=[cfB])
        cv_ = pv("cvec").rearrange("p (k v) -> p k v", v=2)
        op("act", lambda: ACT.activation(out=scT[:], in_=cv_, func=AF.Silu), reads=[parB], writes=[scB])
        P.fence()
    kct = sbt(es, "kct", [128, 4], F32)
    kcB_ = Buf()
    op("dve", lambda: DVE.memset(kct[:, 0:1], 1024.0 * EPS), writes=[kcB_])
    op("dve", lambda: DVE.memset(kct[:, 1:2], EPS), writes=[kcB_])
    op("dve", lambda: DVE.memset(kct[:, 2:3], 1.0), writes=[kcB_])
    op("dve", lambda: DVE.memset(kct[:, 3:4], -30.0), writes=[kcB_])

    def KC(i):
        return kct[:, i:i + 1]
    ones_f = cf32[:, 0:128]
    Vf_f = cf32[:, 128:256]
    Vb_f = cf32[:, 256:384]
    ident_f = cf32[:, 384:512]
    ident = cb("ident")
    ones_b = cb("ones")

    ABANK = 7

    def ada_tile(l, t):
        wt, wb = wget(f"ada{l}_{t}")
        for s in range(4):
            j = 4 * t + s
            for kc in range(8):
                op("pe", lambda s=s, kc=kc, j=j: PE.matmul(PS(ABANK)[:, 2 * j:2 * j + 2], lhsT=wt[:, kc, s * 128:(s + 1) * 128],
                                                        rhs=scT[:, kc, :], start=(kc == 0), stop=(kc == 7)),
                   reads=[wb, scB], writes=[PSB[ABANK]], sig=(kc == 7))

    def ada_finish(l, part=None):
        m = mod[l % 2]
        mB = modB[l % 2]
        j0, j1 = {None: (0, 48), 0: (0, 16), 1: (16, 48)}[part]
        op("dve", lambda: DVE.tensor_tensor(out=m[:, j0:j1, :], in0=PS(ABANK)[:, 2 * j0:2 * j1].rearrange("p (j v) -> p j v", v=2),
                                            in1=pv(f"bada{l}")[:, j0:j1].unsqueeze(2).to_broadcast([128, j1 - j0, 2]), op=ALU.add),
           reads=[PSB[ABANK], parB], writes=[mB])
        g = gsc[l % 2]
        for wh, (nm, sc0) in enumerate(((f"nm{l}", 8), (f"nf{l}", 32))):
            if (part == 0 and wh == 1) or (part == 1 and wh == 0):
                continue
            op("dve", lambda wh=wh, nm=nm, sc0=sc0: DVE.scalar_tensor_tensor(
                out=g[:, wh], in0=m[:, sc0:sc0 + 8, :], scalar=1.0,
                in1=pv(nm).unsqueeze(2).to_broadcast([128, 8, 2]), op0=ALU.add, op1=ALU.mult),
               reads=[mB, parB], writes=[gscB[l % 2]])
            op("dve", lambda wh=wh: DVE.tensor_scalar(out=g[:, wh], in0=g[:, wh], scalar1=32.0, scalar2=None, op0=ALU.mult),
               reads=[gscB[l % 2]], writes=[gscB[l % 2]])

    def modulate(stack, dst, dstB, gs, sh, depB, out_dt_is_f32=False):
        with ExitStack() as s1:
            sq = sbt(s1, "sq", [128, 2, 8, 512], BF16)
            sqB = BG(2, 2)
            rs = sbt(s1, "rs", [128, 2, 512], F32)
            rsB = BG(2)
            tmp = sbt(s1, "mtmp", [128, 2, 8, 512], F32)
            tmpB = BG(2, 2)
            def stage1(tb):
                p = tb % 2
                bk = 6 + p
                ts = slice(tb * 512, (tb + 1) * 512)
                xb = [xB[kc][tb] for kc in range(8)]
                op("act", lambda: ACT.activation(out=sq[:, p, 0:5], in_=xT[:, 0:5, ts], func=AF.Square), reads=xb[0:5], writes=[sqB[p][0]])
                op("pool", lambda: nc.gpsimd.tensor_tensor(out=sq[:, p, 5:8], in0=xT[:, 5:8, ts], in1=xT[:, 5:8, ts], op=ALU.mult), reads=xb[5:8], writes=[sqB[p][1]])
                for kc in range(8):
                    op("pe", lambda: PE.matmul(PS(bk), lhsT=ones_b, rhs=sq[:, p, kc, :], start=(kc == 0), stop=(kc == 7)),
                       reads=[sqB[p][0 if kc < 5 else 1], cbfB], writes=[PSB[bk]], sig=(kc == 7))

            def stage2(tb):
                p = tb % 2
                bk = 6 + p
                v = 0 if tb == 0 else 1
                ts = slice(tb * 512, (tb + 1) * 512)
                xb = [xB[kc][tb] for kc in range(8)]
                op("act", lambda: ACT.activation(out=rs[:, p, :], in_=PS(bk), func=AF.Ln, bias=KC(0)), reads=[PSB[bk], kcB_], writes=[rsB[p]])
                op("act", lambda: ACT.activation(out=rs[:, p, :], in_=rs[:, p, :], func=AF.Exp, scale=-0.5), reads=[rsB[p]], writes=[rsB[p]])
                op("dve", lambda: DVE.tensor_tensor(out=tmp[:, p, 0:6], in0=xT[:, 0:6, ts], in1=rs[:, p, :].unsqueeze(1).to_broadcast([128, 6, 512]), op=ALU.mult),
                   reads=xb[0:6] + [rsB[p]], writes=[tmpB[p][0]])
                op("pool", lambda: nc.gpsimd.tensor_tensor(out=tmp[:, p, 6:8], in0=xT[:, 6:8, ts], in1=rs[:, p, :].unsqueeze(1).to_broadcast([128, 2, 512]), op=ALU.mult),
                   reads=xb[6:8] + [rsB[p]], writes=[tmpB[p][1]])
                for kc in range(8):
                    b = sh(kc, v)
                    if kc < 4:
                        if b is None:
                            op("act", lambda: ACT.activation(out=dst[:, kc, ts], in_=tmp[:, p, kc, :], func=AF.Identity, scale=gs(kc, v)),
                               reads=[tmpB[p][0 if kc < 6 else 1]] + depB, writes=[dstB[kc][tb]])
                        else:
                            op("act", lambda: ACT.activation(out=dst[:, kc, ts], in_=tmp[:, p, kc, :], func=AF.Identity, scale=gs(kc, v), bias=b),
                               reads=[tmpB[p][0 if kc < 6 else 1]] + depB, writes=[dstB[kc][tb]])
                    else:
                        if b is None:
                            op("dve", lambda: DVE.tensor_scalar(out=dst[:, kc, ts], in0=tmp[:, p, kc, :], scalar1=gs(kc, v), scalar2=None, op0=ALU.mult),
                               reads=[tmpB[p][0 if kc < 6 else 1]] + depB, writes=[dstB[kc][tb]])
                        else:
                            op("dve", lambda: DVE.tensor_scalar(out=dst[:, kc, ts], in0=tmp[:, p, kc, :], scalar1=gs(kc, v), scalar2=b, op0=ALU.mult, op1=ALU.add),
                               reads=[tmpB[p][0 if kc < 6 else 1]] + depB, writes=[dstB[kc][tb]])
            stage1(0)
            stage1(1)
            stage2(0)
            stage1(2)
            stage2(1)
            stage2(2)
            P.fence()

    def ffn(l):
        m = mod[l % 2]
        g = gsc[l % 2]
        mB = modB[l % 2]
        with ExitStack() as s1:
            h2 = sbt(s1, "h2", [128, 8, T], BF16)
            h2B = BG(8, 3)
            modulate(s1, h2, h2B, lambda kc, v: g[:, 1, kc, v:v + 1], lambda kc, v: m[:, 24 + kc, v:v + 1],
                     [mB, gscB[l % 2]])
            aT = sbt(s1, "aT", [128, NJ, T], BF16)
            aB = BG(NJ, 3)
            sg = sbt(s1, "sg", [128, 2, T], F32)
            sgB = BG(2, 3)
            for t in range(11):
                wt, wb = wget(f"FI_{l}_{t}")
                for s in range(2):
                    j = 2 * t + s
                    for half, b0 in ((0, 0), (1, 3)):
                        c0 = (half * 2 + s) * 128
                        for kc in range(8):
                            for tb in range(3):
                                op("pe", lambda c0=c0, kc=kc, tb=tb, b0=b0: PE.matmul(
                                    PS(b0 + tb), lhsT=wt[:, kc, c0:c0 + 128], rhs=h2[:, kc, tb * 512:(tb + 1) * 512],
                                    start=(kc == 0), stop=(kc == 7)),
                                   reads=[wb, h2B[kc][tb]], writes=[PSB[b0 + tb]], sig=(kc == 7))
                    for tb in range(3):
                        ts = slice(tb * 512, (tb + 1) * 512)
                        op("act", lambda tb=tb, ts=ts, s=s: ACT.activation(out=sg[:, s, ts], in_=PS(tb), func=AF.Silu),
                           reads=[PSB[tb]], writes=[sgB[s][tb]])
                        op("dve", lambda tb=tb, ts=ts, s=s, j=j: DVE.tensor_tensor(out=aT[:, j, ts], in0=sg[:, s, ts], in1=PS(3 + tb), op=ALU.mult),
                           reads=[sgB[s][tb], PSB[3 + tb]], writes=[aB[j][tb]])
                if l + 1 < DEPTH_RUN:
                    ada_tile(l + 1, t)
            if l + 1 < DEPTH_RUN:
                ada_tile(l + 1, 11)
                ada_finish(l + 1)
            for dsl in range(8):
                wt, wb = wget(f"FO_{l}_{dsl}")
                b0 = 0 if dsl % 2 == 0 else 3
                for j in range(NJ):
                    for tb in range(3):
                        op("pe", lambda j=j, tb=tb, b0=b0: PE.matmul(PS(b0 + tb), lhsT=wt[:, j, :], rhs=aT[:, j, tb * 512:(tb + 1) * 512],
                                                                    start=(j == 0), stop=(j == NJ - 1)),
                           reads=[wb, aB[j][tb]], writes=[PSB[b0 + tb]], sig=(j == NJ - 1))
                for tb in range(3):
                    v = 0 if tb == 0 else 1
                    ts = slice(tb * 512, (tb + 1) * 512)
                    op("dve", lambda tb=tb, ts=ts, v=v, dsl=dsl, b0=b0: DVE.scalar_tensor_tensor(
                        out=xT[:, dsl, ts], in0=PS(b0 + tb), scalar=m[:, 40 + dsl, v:v + 1], in1=xT[:, dsl, ts],
                        op0=ALU.mult, op1=ALU.add),
                       reads=[PSB[b0 + tb], mB, xB[dsl][tb]], writes=[xB[dsl][tb]])
            P.fence()

    def outproj(l, tags, srcT, srcB, bias=None, rowscale=None):
        m = mod[l % 2]
        mB = modB[l % 2]
        for t, tag in enumerate(tags):
            wt, wb = wget(tag)
            for s in range(4):
                dsl = 4 * t + s
                b0 = 0 if dsl % 2 == 0 else 3
                for kc in range(8):
                    for tb in range(3):
                        op("pe", lambda s=s, kc=kc, tb=tb, b0=b0: PE.matmul(PS(b0 + tb), lhsT=wt[:, kc, s * 128:(s + 1) * 128],
                                                                           rhs=srcT[:, kc, tb * 512:(tb + 1) * 512],
                                                                           start=(kc == 0), stop=(kc == 7)),
                           reads=[wb, srcB[kc][tb]], writes=[PSB[b0 + tb]], sig=(kc == 7))
                for tb in range(3):
                    v = 0 if tb == 0 else 1
                    ts = slice(tb * 512, (tb + 1) * 512)
                    if bias is not None:
                        op("act", lambda tb=tb, b0=b0, dsl=dsl: ACT.activation(out=PS(b0 + tb), in_=PS(b0 + tb), func=AF.Identity,
                                                                             bias=bias[:, dsl:dsl + 1]),
                           reads=[PSB[b0 + tb], parB], writes=[PSB[b0 + tb]])
                    if rowscale is not None:
                        op("dve", lambda tb=tb, ts=ts, b0=b0: DVE.tensor_tensor(out=PS(b0 + tb), in0=PS(b0 + tb), in1=rowscale[0][:, ts], op=ALU.mult),
                           reads=[PSB[b0 + tb], rowscale[1][tb]], writes=[PSB[b0 + tb]])
                    op("dve", lambda tb=tb, ts=ts, v=v, dsl=dsl, b0=b0: DVE.scalar_tensor_tensor(
                        out=xT[:, dsl, ts], in0=PS(b0 + tb), scalar=m[:, 16 + dsl, v:v + 1], in1=xT[:, dsl, ts],
                        op0=ALU.mult, op1=ALU.add),
                       reads=[PSB[b0 + tb], mB, xB[dsl][tb]], writes=[xB[dsl][tb]])

    def fourier_layer(l, hT, hB):
        o = l // 2
        with ExitStack() as s1:
            dftb = sbt(s1, "dftb", [128, 2, 8, 1024], BF16)
            dftB = Buf()
            for i in range(2):
                op("pool", lambda i=i: nc.gpsimd.dma_start(out=dftb[:, i], in_=dft_in[i]), writes=[dftB], dkey="dft")
            fT = sbt(s1, "fT", [128, 8, T], BF16)
            fB = BG(8, 3)
            AB = sbt(s1, "ABt", [128, 8, 2, 1024], BF16)
            ABB = BG(8)
            cdft = sbt(s1, "cdft", [128, 4, 2, 256], BF16)
            o_cc = CL["Cc"][0]
            op("pool", lambda: nc.gpsimd.dma_start(out=cdft[:], in_=con_in[:, o_cc:o_cc + 2048].rearrange("p (a k c) -> p a k c", a=4, k=2)),
               writes=[dftB], dkey="dft")
            Cc, Sc, c256, s256 = cdft[:, 0], cdft[:, 1], cdft[:, 2], cdft[:, 3]
            for (t0, ntl) in SEQS:
                for lt in range(ntl):
                    tt = t0 + lt
                    tb = tt // 4
                    tsl = slice(tt * 128, (tt + 1) * 128)
                    for ab, tab in ((0, Cc), (1, Sc)):
                        bk = [nbank(0, 6), nbank(0, 6)]
                        for g in range(4):
                            for cc in range(2):
                                op("pe", lambda g=g, cc=cc, tab=tab, bk=bk, tsl=tsl: PE.matmul(
                                    PS(bk[g // 2])[:, (g % 2) * 256:(g % 2) * 256 + 256], lhsT=hT[:, 2 * g + cc, tsl], rhs=tab[:, cc, :],
                                    start=(cc == 0), stop=(cc == 1)),
                                   reads=[hB[2 * g + cc][tb], dftB], writes=[PSB[bk[g // 2]]], sig=(cc == 1))
                        for hh in range(2):
                            eng = "act" if (hh + ab) % 2 == 0 else "dve"
                            if eng == "act":
                                op("act", lambda hh=hh, ab=ab, lt=lt, bk=bk: ACT.copy(out=AB[:, lt, ab, hh * 512:(hh + 1) * 512], in_=PS(bk[hh])),
                                   reads=[PSB[bk[hh]]], writes=[ABB[lt]])
                            else:
                                op("dve", lambda hh=hh, ab=ab, lt=lt, bk=bk: DVE.tensor_copy(out=AB[:, lt, ab, hh * 512:(hh + 1) * 512], in_=PS(bk[hh])),
                                   reads=[PSB[bk[hh]]], writes=[ABB[lt]])
                L = ntl * 128
                nkb = max(1, L // 512)
                kw = min(L, 512)
                for cs in range(8):
                    for kb in range(nkb):
                        bk = nbank(0, 6)
                        n_acc = 2 * ntl
                        i = 0
                        for lt in range(ntl):
                            for ab in range(2):
                                if ntl == 2:
                                    rhs = (c256 if ab == 0 else s256)[:, lt, :]
                                    rd = [dftB]
                                else:
                                    rhs = dftb[:, ab, lt, kb * 512:(kb + 1) * 512]
                                    rd = [dftB]
                                op("pe", lambda lt=lt, ab=ab, rhs=rhs, i=i, bk=bk, cs=cs: PE.matmul(
                                    PS(bk)[:, 0:kw], lhsT=AB[:, lt, ab, cs * 128:(cs + 1) * 128], rhs=rhs,
                                    start=(i == 0), stop=(i == n_acc - 1)),
                                   reads=[ABB[lt]] + rd, writes=[PSB[bk]], sig=(i == n_acc - 1))
                                i += 1
                        c0 = t0 * 128 + kb * 512
                        tbs = sorted(set([c0 // 512, (c0 + kw - 1) // 512]))
                        eng = "act" if (cs + kb) % 2 == 0 else "dve"
                        if eng == "act":
                            op("act", lambda bk=bk, cs=cs, c0=c0: ACT.copy(out=fT[:, cs, c0:c0 + kw], in_=PS(bk)[:, 0:kw]),
                               reads=[PSB[bk]], writes=[fB[cs][tb_] for tb_ in tbs])
                        else:
                            op("dve", lambda bk=bk, cs=cs, c0=c0: DVE.tensor_copy(out=fT[:, cs, c0:c0 + kw], in_=PS(bk)[:, 0:kw]),
                               reads=[PSB[bk]], writes=[fB[cs][tb_] for tb_ in tbs])
            outproj(l, [f"W4_{o}_0", f"W4_{o}_1"], fT, fB, bias=pv(f"b4{o}"))
            P.fence()

    def ab_layer(l, hT, hB):
        e = l // 2
        lambda_init = 0.8 - 0.6 * math.exp(-0.3 * l)
        Uf, Ub, Vf, Vb = cb("Uf"), cb("Ub"), cb("Vf"), cb("Vb")
        NEGf, NEGb = cb("NEGf"), cb("NEGb")
        with ExitStack() as s1:
            yoT = sbt(s1, "yoT", [128, 8, T], BF16)
            yoB = BG(8, 3)
            nlam = sbt(s1, "nlam", [128, 4], F32)
            nlB = Buf()
            with ExitStack() as s2:
                lt_ = sbt(s2, "lqtmp", [128, 2, 64], F32)
                ltB = Buf()
                lq = pv(f"lqk{e}").rearrange("p (a b d) -> p a b d", a=2, b=2)
                op("dve", lambda: DVE.tensor_tensor(out=lt_[:], in0=lq[:, :, 0, :], in1=lq[:, :, 1, :], op=ALU.mult), reads=[parB], writes=[ltB])
                op("dve", lambda: DVE.tensor_reduce(out=nlam[:, 0:2], in_=lt_[:], axis=mybir.AxisListType.X, op=ALU.add), reads=[ltB], writes=[nlB])
                op("act", lambda: ACT.activation(out=nlam[:, 0:2], in_=nlam[:, 0:2], func=AF.Exp), reads=[nlB], writes=[nlB])
                op("dve", lambda: DVE.tensor_tensor(out=nlam[:, 2:3], in0=nlam[:, 1:2], in1=nlam[:, 0:1], op=ALU.subtract), reads=[nlB], writes=[nlB])
                op("dve", lambda: DVE.tensor_scalar(out=nlam[:, 3:4], in0=nlam[:, 2:3], scalar1=-lambda_init, scalar2=None, op0=ALU.add), reads=[nlB], writes=[nlB])
                P.fence()

            with ExitStack() as s2:
                dtv = sbt(s2, "dtv", [128, NT, 32], F32)
                dta = sbt(s2, "dta", [128, NT, 32], F32)
                ea = sbt(s2, "ea", [128, NT, 32], F32)
                te = sbt(s2, "te", [128, NT, 32], F32)
                cdb = sbt(s2, "cdb", [128, NT, 32], F32)
                smB = BG(NT)
                aneg = sbt(s2, "aneg", [128, 32], F32)
                anB = Buf()
                dskd = sbt(s2, "dskd", [128, 16, 128], BF16)
                dskB = Buf()
                op("act", lambda: ACT.activation(out=aneg[:], in_=pv(f"alog{e}"), func=AF.Exp), reads=[parB], writes=[anB])
                for h in range(16):
                    op("dve", lambda h=h: DVE.tensor_scalar(out=dskd[:, h, :], in0=ident, scalar1=pv(f"dsk{e}")[:, h:h + 1], scalar2=None, op0=ALU.mult),
                       reads=[parB, cbfB], writes=[dskB])
                wt, wb = wget(f"dt{e}")
                bA, bB, bC = 3, 4, 5
                for tt in range(NT):
                    tb = tt // 4
                    tsl = slice(tt * 128, (tt + 1) * 128)
                    for kc in range(8):
                        op("pe", lambda: PE.matmul(PS(bA)[:, tt * 32:(tt + 1) * 32], lhsT=hT[:, kc, tsl], rhs=wt[:, kc, :], start=(kc == 0), stop=(kc == 7)),
                           reads=[wb, hB[kc][tb]], writes=[PSB[bA]], sig=(kc == 7))
                allsm = smB
                op("dve", lambda: DVE.tensor_tensor(out=dtv[:], in0=PS(bA)[:, 0:NT * 32].rearrange("p (t c) -> p t c", c=32),
                                                    in1=pv(f"dtb{e}").unsqueeze(1).to_broadcast([128, NT, 32]), op=ALU.add),
                   reads=[PSB[bA], parB], writes=allsm)
                op("act", lambda: ACT.activation(out=dtv[:], in_=dtv[:], func=AF.Exp), reads=allsm, writes=allsm)
                op("act", lambda: ACT.activation(out=dtv[:], in_=dtv[:], func=AF.Ln, bias=KC(2)), reads=allsm + [kcB_], writes=allsm)
                op("dve", lambda: DVE.scalar_tensor_tensor(out=dta[:], in0=dtv[:], scalar=-1.0, in1=aneg[:].unsqueeze(1).to_broadcast([128, NT, 32]),
                                                           op0=ALU.mult, op1=ALU.mult),
                   reads=allsm + [anB], writes=allsm)
                for tt in range(NT):
                    op("pe", lambda: PE.matmul(PS(bB)[:, tt * 32:tt * 32 + 16], lhsT=Vf_f, rhs=dta[:, tt, 0:16], start=True, stop=True),
                       reads=allsm + [cfB], writes=[PSB[bB]], sig=False)
                    op("pe", lambda: PE.matmul(PS(bB)[:, tt * 32 + 16:tt * 32 + 32], lhsT=Vb_f, rhs=dta[:, tt, 16:32], start=True, stop=True),
                       reads=allsm + [cfB], writes=[PSB[bB]], sig=False)
                    op("pe", lambda: PE.matmul(PS(bC)[:, tt * 32:(tt + 1) * 32], lhsT=ones_f, rhs=dta[:, tt, :], start=True, stop=True),
                       reads=allsm + [cfB], writes=[PSB[bC]], sig=True)
                vB = PS(bB)[:, 0:NT * 32].rearrange("p (t c) -> p t c", c=32)
                vC = PS(bC)[:, 0:NT * 32].rearrange("p (t c) -> p t c", c=32)
                op("act", lambda: ACT.activation(out=ea[:], in_=vB, func=AF.Exp), reads=[PSB[bB]], writes=allsm)
                op("act", lambda: ACT.copy(out=te[:], in_=vB), reads=[PSB[bB]], writes=allsm)
                op("act", lambda: ACT.activation(out=cdb[:], in_=vC, func=AF.Exp), reads=[PSB[bC]], writes=allsm)
                op("dve", lambda: DVE.tensor_tensor(out=te[:], in0=vC, in1=te[:], op=ALU.subtract), reads=[PSB[bC]] + allsm, writes=allsm)
                op("act", lambda: ACT.activation(out=te[:], in_=te[:], func=AF.Exp), reads=allsm, writes=allsm)
                op("dve", lambda: DVE.tensor_tensor(out=te[:], in0=te[:], in1=dtv[:], op=ALU.mult), reads=allsm, writes=allsm)

                checkpoint('dtprep')
                ssq = sbt(s2, "ssq", [128, NT, 4], F32)
                ssB = BG(NT)
                op("dve", lambda: DVE.memset(ssq[:], 0.0), writes=ssB)
                PADL = 1548
                pre = sbt(s2, "pre", [128, 2, PADL], BF16)
                preB = BG(2)
                op("dve", lambda: DVE.memset(pre[:], 0.0), writes=preB)
                post = sbt(s2, "post", [128, 4, T], BF16)
                postB = BG(4, 3)
                cwd = sbt(s2, "cwd", [128, 20, 128], BF16)
                cwB = Buf()
                xs_t = sbt(s2, "xs_t", [128, NT, 256], BF16)
                b_t = sbt(s2, "b_t", [128, NT, 128], BF16)
                zs = sbt(s2, "zs", [128, NT, 256], BF16)
                tkB = BG(NT)
                zB = BG(NT)
                xte = sbt(s2, "xte", [128, 2, 256], BF16)
                xteB = [Buf(), Buf()]
                xdt = sbt(s2, "xdt", [128, 2, 2, 256], BF16)
                xdB = BG(2, 2)
                hst = sbt(s2, "hst", [128, 2, 256], F32)
                hsB = [Buf(), Buf()]
                hbf = sbt(s2, "hbf", [128, 8, 2, 256], BF16)
                hbB = BG(8, 2)
                Ld = sbt(s2, "Ld", [128, 1, 8, 128], BF16)
                LdB = [Buf()]
                Et = sbt(s2, "Et", [128, 2, 2, 512], BF16)
                EtB = BG(2, 2)
                cbt = sbt(s2, "cbt", [128, 2, 128], F32)
                cbB = BG(2)
                Wt = sbt(s2, "Wt", [128, 2, 8, 128], BF16)
                WtB = BG(2, 2)
                ytmp = sbt(s2, "ytmp", [128, 2, 256], F32)
                ytB = [Buf(), Buf()]
                yg = sbt(s2, "yg", [128, 2, 256], BF16)
                ygB = BG(2)
                sqj = sbt(s2, "sqj", [128, 256], BF16)
                sqjB = Buf()
                for g in range(4):
                    for i in range(20):
                        ci = g * 20 + i
                        if i % 2 == 0:
                            op("dve", lambda i=i, ci=ci: DVE.tensor_scalar(out=cwd[:, i, :], in0=ident, scalar1=pv(f"cw{e}")[:, ci:ci + 1], scalar2=None, op0=ALU.mult),
                               reads=[parB, cbfB], writes=[cwB])
                        else:
                            op("act", lambda i=i, ci=ci: ACT.activation(out=cwd[:, i, :], in_=ident, func=AF.Identity, scale=pv(f"cw{e}")[:, ci:ci + 1]),
                               reads=[parB, cbfB], writes=[cwB])
                    wt1, wb1 = wget(f"A1_{e}_{g}")
                    ada_after_A1 = (l == 0)
                    def proj_s(s):
                        pp = s % 2
                        for kc in range(8):
                            for tb in range(3):
                                op("pe", lambda: PE.matmul(PS(tb), lhsT=wt1[:, kc, s * 128:(s + 1) * 128],
                                                           rhs=hT[:, kc, tb * 512:(tb + 1) * 512], start=(kc == 0), stop=(kc == 7)),
                                   reads=[wb1, hB[kc][tb]], writes=[PSB[tb]], sig=(kc == 7))
                        op("act", lambda: ACT.copy(out=pre[:, pp, 2:258], in_=PS(0)[:, 0:256]), reads=[PSB[0]], writes=[preB[pp]])
                        op("act", lambda: ACT.copy(out=pre[:, pp, 262:518], in_=PS(0)[:, 256:512]), reads=[PSB[0]], writes=[preB[pp]])
                        op("dve", lambda: DVE.tensor_copy(out=pre[:, pp, 522:1034], in_=PS(1)), reads=[PSB[1]], writes=[preB[pp]])
                        op("dve", lambda: DVE.tensor_copy(out=pre[:, pp, 1034:1546], in_=PS(2)), reads=[PSB[2]], writes=[preB[pp]])

                    def conv_s(s):
                        sl = g * 4 + s
                        pp = s % 2
                        for bi, (poff, toff, n) in enumerate(((2, 0, 256), (262, 256, 256), (522, 512, 512), (1034, 1024, 512))):
                            bk = 3 + bi % 3
                            for k in range(5):
                                op("pe", lambda: PE.matmul(PS(bk)[:, 0:n], lhsT=cwd[:, s * 5 + k, :], rhs=pre[:, pp, poff + k - 2:poff + k - 2 + n],
                                                           start=(k == 0), stop=(k == 4)),
                                   reads=[cwB, preB[pp]], writes=[PSB[bk]], sig=(k == 4))
                            tbs = sorted(set([toff // 512, (toff + n - 1) // 512]))
                            op("act", lambda: ACT.activation(out=post[:, s, toff:toff + n], in_=PS(bk)[:, 0:n], func=AF.Silu, bias=pv(f"cb{e}")[:, sl:sl + 1]),
                               reads=[PSB[bk], parB], writes=[postB[s][tb_] for tb_ in tbs])
                    proj_s(0)
                    for s in range(4):
                        if s + 1 < 4:
                            proj_s(s + 1)
                        conv_s(s)
                    for tt in range(NT):
                        tb = tt // 4
                        tsl = slice(tt * 128, (tt + 1) * 128)
                        bk = nbank(0, 6)
                        for s in range(3):
                            op("pe", lambda s=s, tsl=tsl, bk=bk: PE.matmul(PS(bk)[:, s * 128:(s + 1) * 128], lhsT=post[:, s, tsl], rhs=ident, start=True, stop=True),
                               reads=[postB[s][tb], cbfB], writes=[PSB[bk]], sig=(s == 2))
                        op("act", lambda tt=tt, bk=bk: ACT.copy(out=xs_t[:, tt, :], in_=PS(bk)[:, 0:256]), reads=[PSB[bk]], writes=[tkB[tt]])
                        op("act", lambda tt=tt, bk=bk: ACT.copy(out=b_t[:, tt, :], in_=PS(bk)[:, 256:384]), reads=[PSB[bk]], writes=[tkB[tt]])
                    if l == 0:
                        ada_tile(0, 4 + 2 * g)
                    wt2, wb2 = wget(f"A2_{e}_{g}")
                    for tt in range(NT):
                        tb = tt // 4
                        tsl = slice(tt * 128, (tt + 1) * 128)
                        bk = nbank(0, 6)
                        for kc in range(8):
                            op("pe", lambda kc=kc, tsl=tsl, bk=bk: PE.matmul(PS(bk)[:, 0:256], lhsT=hT[:, kc, tsl], rhs=wt2[:, kc, :], start=(kc == 0), stop=(kc == 7)),
                               reads=[wb2, hB[kc][tb]], writes=[PSB[bk]], sig=(kc == 7))
                        op("act", lambda tt=tt, bk=bk: ACT.activation(out=zs[:, tt, :], in_=PS(bk)[:, 0:256], func=AF.Silu), reads=[PSB[bk]], writes=[zB[tt]])
                    if l == 0:
                        ada_tile(0, 5 + 2 * g)
                        if g == 3:
                            ada_finish(0, 1)
                    checkpoint('g0conv')
                    for si, (t0, ntl) in enumerate(SEQS):
                        orders = [list(range(t0, t0 + ntl)), list(range(t0 + ntl - 1, t0 - 1, -1))]
                        for d in range(2):
                            if si < 2:
                                op("dve", lambda d=d: DVE.memset(hst[:, d, :], 0.0), writes=[hsB[d]])
                            else:
                                src = (sf_in if d == 0 else sb_in)[e][:, g * 256:(g + 1) * 256]
                                op("sp", lambda d=d, src=src: nc.sync.dma_start(out=hst[:, d, :], in_=src), writes=[hsB[d]], dkey=f"hin{d}")
                        for step in range(ntl):
                            for d in range(2):
                                tt = orders[d][step]
                                c0 = d * 16 + g * 4
                                ti = tt - t0
                                op("act", lambda ti=ti, d=d: ACT.copy(out=hbf[:, ti, d, :], in_=hst[:, d, :]), reads=[hsB[d]], writes=[hbB[ti][d]])
                                op("pool", lambda tt=tt, d=d, c0=c0: nc.gpsimd.tensor_tensor(
                                    out=xte[:, d, :].rearrange("p (r q) -> p r q", r=4),
                                    in0=xs_t[:, tt, :].rearrange("p (r q) -> p r q", r=4),
                                    in1=te[:, tt, c0:c0 + 4].unsqueeze(2).to_broadcast([128, 4, 64]), op=ALU.mult),
                                   reads=[tkB[tt], smB[tt]], writes=[xteB[d]])
                                bk = nbank(0, 6)
                                op("pe", lambda tt=tt, d=d, bk=bk: PE.matmul(PS(bk)[:, 0:256], lhsT=b_t[:, tt, :], rhs=xte[:, d, :], start=True, stop=True),
                                   reads=[tkB[tt], xteB[d]], writes=[PSB[bk]])
                                op("dve", lambda tt=tt, d=d, c0=c0: DVE.tensor_tensor(
                                    out=hst[:, d, :].rearrange("p (r q) -> p r q", r=4), in0=hst[:, d, :].rearrange("p (r q) -> p r q", r=4),
                                    in1=cdb[:, tt, c0:c0 + 4].unsqueeze(2).to_broadcast([128, 4, 64]), op=ALU.mult),
                                   reads=[hsB[d], smB[tt]], writes=[hsB[d]])
                                op("dve", lambda d=d, bk=bk: DVE.tensor_tensor(out=hst[:, d, :], in0=hst[:, d, :], in1=PS(bk)[:, 0:256], op=ALU.add),
                                   reads=[hsB[d], PSB[bk]], writes=[hsB[d]])
                        if si < 2:
                            for d in range(2):
                                dst = (nf_out if d == 0 else nb_out)[e, si][:, g * 256:(g + 1) * 256]
                                op("sp", lambda d=d, dst=dst: nc.sync.dma_start(out=dst, in_=hst[:, d, :]), reads=[hsB[d]], dkey=f"sto{d}")

                        def stageA(tt):
                            pb = tt % 2
                            tb = tt // 4
                            tsl = slice(tt * 128, (tt + 1) * 128)
                            bkc = nbank(0, 6)
                            op("pe", lambda: PE.matmul(PS(bkc)[:, 0:128], lhsT=post[:, 2, tsl], rhs=post[:, 3, tsl], start=True, stop=True),
                               reads=[postB[2][tb], postB[3][tb]], writes=[PSB[bkc]])
                            op("act", lambda: ACT.copy(out=cbt[:, pb, :], in_=PS(bkc)[:, 0:128]), reads=[PSB[bkc]], writes=[cbB[pb]])
                            for d, U in ((0, Uf), (1, Ub)):
                                c0 = d * 16 + g * 4
                                op("pool", lambda d=d, U=U, c0=c0: nc.gpsimd.tensor_tensor(
                                    out=Ld[:, 0, d * 4:(d + 1) * 4, :], in0=U.unsqueeze(1).to_broadcast([128, 4, 128]),
                                    in1=dta[:, tt, c0:c0 + 4].unsqueeze(2).to_broadcast([128, 4, 128]), op=ALU.mult),
                                   reads=[cbfB, smB[tt]], writes=[LdB[0]])
                            for d, V, NEG in ((0, Vf, NEGf), (1, Vb, NEGb)):
                                bks = nbank(0, 6)
                                for r in range(4):
                                    op("pe", lambda d=d, r=r, V=V: PE.matmul(PS(bks)[:, r * 128:(r + 1) * 128], lhsT=Ld[:, 0, d * 4 + r, :], rhs=V, start=True, stop=False),
                                       reads=[LdB[0], cbfB], writes=[PSB[bks]], sig=False)
                                    op("pe", lambda r=r, NEG=NEG: PE.matmul(PS(bks)[:, r * 128:(r + 1) * 128], lhsT=ident, rhs=NEG, start=False, stop=True),
                                       reads=[cbfB], writes=[PSB[bks]], sig=(r == 3))
                                op("act", lambda d=d: ACT.activation(out=Et[:, pb, d, :], in_=PS(bks), func=AF.Exp), reads=[PSB[bks]], writes=[EtB[pb][d]])

                        def stageA2(tt):
                            pb = tt % 2
                            for d in range(2):
                                c0 = d * 16 + g * 4
                                weng = "dve" if d == 0 else "pool"
                                wfn = DVE.tensor_tensor if d == 0 else nc.gpsimd.tensor_tensor
                                op(weng, lambda d=d, wfn=wfn: wfn(
                                    out=Wt[:, pb, d * 4:(d + 1) * 4, :], in0=Et[:, pb, d, :].rearrange("p (r i) -> p r i", r=4),
                                    in1=cbt[:, pb, :].unsqueeze(1).to_broadcast([128, 4, 128]), op=ALU.mult),
                                   reads=[EtB[pb][d], cbB[pb]], writes=[WtB[pb][d]])
                                op(weng, lambda d=d, c0=c0, wfn=wfn: wfn(
                                    out=xdt[:, pb, d, :].rearrange("p (r q) -> p r q", r=4), in0=xs_t[:, tt, :].rearrange("p (r q) -> p r q", r=4),
                                    in1=dtv[:, tt, c0:c0 + 4].unsqueeze(2).to_broadcast([128, 4, 64]), op=ALU.mult),
                                   reads=[tkB[tt], smB[tt]], writes=[xdB[pb][d]])

                        def stageB(tt):
                            pb = tt % 2
                            ti = tt - t0
                            tb = tt // 4
                            tsl = slice(tt * 128, (tt + 1) * 128)
                            bky = nbank(0, 6)
                            for r in range(4):
                                xr = xs_t[:, tt, r * 64:(r + 1) * 64]
                                yo_ = PS(bky)[:, r * 64:(r + 1) * 64]
                                op("pe", lambda: PE.matmul(yo_, lhsT=Wt[:, pb, r, :], rhs=xdt[:, pb, 0, r * 64:(r + 1) * 64], start=True, stop=False),
                                   reads=[WtB[pb][0], xdB[pb][0]], writes=[PSB[bky]], sig=False)
                                op("pe", lambda: PE.matmul(yo_, lhsT=Wt[:, pb, 4 + r, :], rhs=xdt[:, pb, 1, r * 64:(r + 1) * 64], start=False, stop=False),
                                   reads=[WtB[pb][1], xdB[pb][1]], writes=[PSB[bky]], sig=False)
                                op("pe", lambda: PE.matmul(yo_, lhsT=dskd[:, g * 4 + r, :], rhs=xr, start=False, stop=True),
                                   reads=[dskB, tkB[tt]], writes=[PSB[bky]], sig=(r == 3))
                            bko = [nbank(0, 6), nbank(0, 6)]
                            for d in range(2):
                                op("pe", lambda d=d: PE.matmul(PS(bko[d])[:, 0:256], lhsT=post[:, 3, tsl], rhs=hbf[:, ti, d, :], start=True, stop=True),
                                   reads=[postB[3][tb], hbB[ti][d]], writes=[PSB[bko[d]]])
                            for d in range(2):
                                c0 = d * 16 + g * 4
                                op("dve", lambda d=d, c0=c0: DVE.tensor_tensor(
                                    out=ytmp[:, d, :].rearrange("p (r q) -> p r q", r=4), in0=PS(bko[d])[:, 0:256].rearrange("p (r q) -> p r q", r=4),
                                    in1=ea[:, tt, c0:c0 + 4].unsqueeze(2).to_broadcast([128, 4, 64]), op=ALU.mult),
                                   reads=[PSB[bko[d]], smB[tt]], writes=[ytB[d]])
                            op("dve", lambda: DVE.tensor_tensor(out=ytmp[:, 0, :], in0=ytmp[:, 0, :], in1=ytmp[:, 1, :], op=ALU.add), reads=ytB, writes=[ytB[0]])
                            op("dve", lambda: DVE.tensor_tensor(out=ytmp[:, 0, :], in0=ytmp[:, 0, :], in1=PS(bky)[:, 0:256], op=ALU.add),
                               reads=[ytB[0], PSB[bky]], writes=[ytB[0]])
                            op("dve", lambda: DVE.tensor_tensor(out=yg[:, pb, :], in0=ytmp[:, 0, :], in1=zs[:, tt, :], op=ALU.mult),
                               reads=[ytB[0], zB[tt]], writes=[ygB[pb]])
                            op("act", lambda: ACT.activation(out=sqj[:], in_=yg[:, pb, :], func=AF.Square, accum_out=ssq[:, tt, g:g + 1]),
                               reads=[ygB[pb]], writes=[sqjB, ssB[tt]])
                            bkt = nbank(0, 6)
                            for cc in range(2):
                                op("pe", lambda cc=cc: PE.matmul(PS(bkt)[:, cc * 128:(cc + 1) * 128], lhsT=yg[:, pb, cc * 128:(cc + 1) * 128], rhs=ident, start=True, stop=True),
                                   reads=[ygB[pb], cbfB], writes=[PSB[bkt]], sig=(cc == 1))
                            for cc in range(2):
                                ck_ = 2 * g + cc
                                op("act", lambda cc=cc, ck_=ck_: ACT.activation(out=yoT[:, ck_, tsl], in_=PS(bkt)[:, cc * 128:(cc + 1) * 128], func=AF.Identity,
                                                                             scale=pv(f"ssdn{e}")[:, ck_:ck_ + 1]),
                                   reads=[PSB[bkt], parB], writes=[yoB[ck_][tb]])

                        tts = list(range(t0, t0 + ntl))
                        stageA(tts[0])
                        stageA2(tts[0])
                        for i_, tt in enumerate(tts):
                            if i_ + 1 < len(tts):
                                stageA(tts[i_ + 1])
                            stageB(tt)
                            if i_ + 1 < len(tts):
                                stageA2(tts[i_ + 1])
                    checkpoint('g0scan')
                checkpoint('ssdscan')
                rst = sbt(s2, "rst", [128, NT], F32)
                rstB = Buf()
                rsb = pre[:].rearrange("p a b -> p (a b)").bitcast(F32)
                rsbB = [preB, preB, preB]
                dg = sqj[:].bitcast(F32)
                dgB = sqjB
                op("dve", lambda: DVE.tensor_reduce(out=rst[:], in_=ssq[:], axis=mybir.AxisListType.X, op=ALU.add), reads=ssB, writes=[rstB])
                op("act", lambda: ACT.activation(out=rst[:], in_=rst[:], func=AF.Ln, scale=1.0 / 1024.0, bias=KC(1)), reads=[rstB, kcB_], writes=[rstB])
                op("act", lambda: ACT.activation(out=rst[:], in_=rst[:], func=AF.Exp, scale=-0.5), reads=[rstB], writes=[rstB])
                for tt in range(NT):
                    tb = tt // 4
                    op("dve", lambda tt=tt: DVE.tensor_scalar(out=dg, in0=ident_f, scalar1=rst[:, tt:tt + 1], scalar2=None, op0=ALU.mult),
                       reads=[rstB, cfB], writes=[dgB])
                    op("pe", lambda tt=tt: PE.matmul(PS(6)[:, 0:128], lhsT=ones_f, rhs=dg, start=True, stop=True), reads=[dgB, cfB], writes=[PSB[6]])
                    op("act", lambda tt=tt: ACT.copy(out=rsb[:, tt * 128:(tt + 1) * 128], in_=PS(6)[:, 0:128]), reads=[PSB[6]], writes=[rsbB[tb]])
                outproj(l, [f"O1_{e}_0", f"O1_{e}_1"], yoT, yoB, rowscale=(rsb, rsbB))
                P.fence()

            checkpoint('ssd')
            with ExitStack() as s2:
                rope = sbt(s2, "rope", [128, 2, 1024], F32)
                ropeB = Buf()
                for i in range(2):
                    op("sp", lambda i=i: nc.sync.dma_start(out=rope[:, i, :], in_=rope_in[i]), writes=[ropeB], dkey="rope")
                qT = sbt(s2, "qT", [128, 2, T], BF16)
                qB = BG(2, 3)
                qraw = sbt(s2, "qraw", [128, 2, 1024], BF16)
                qrB = BG(2, 2)
                rt = sbt(s2, "rt", [128, 2, 2, 512], BF16)
                rtB = BG(2, 2)
                kc32 = sbt(s2, "kc32", [128, 2, 512], F32)
                kcB = [Buf(), Buf()]
                vc32 = sbt(s2, "vc32", [128, 2, 4, 128], F32)
                vcB = [Buf(), Buf()]
                kcb = sbt(s2, "kcb", [128, 512], BF16)
                kcbB = Buf()
                va = sbt(s2, "va", [128, 16, 132], BF16)
                vaB = BG(16)
                op("dve", lambda: DVE.memset(va[:], 1.0), writes=vaB)
                kvo = sbt(s2, "kvo", [128, 2, 2, 4, 128], F32)
                kvB = BG(2, 2)
                ET = sbt(s2, "ET", [128, 2, 12, 2, 256], BF16)
                ETB = BG(2, 12)
                osb = sbt(s2, "osb", [128, 2, 128], F32)
                osB = BG(2)
                sm = sbt(s2, "sm", [128, 2, 3, 2], F32)
                smB_ = BG(2, 3)
                onb = sbt(s2, "onb", [128, 4, 128], BF16)
                onB = BG(4)
                oTs = sbt(s2, "oTs", [128, 512], F32)
                oTB = Buf()
                sqb = sbt(s2, "sqb", [128, 512], BF16)
                sqB_ = Buf()
                rsq = sbt(s2, "rsq", [128, 512], F32)
                rsqB = Buf()

                def load_cache(h):
                    pb = h % 2
                    op("sp", lambda: nc.sync.dma_start(out=kc32[:, pb, :], in_=ck_in[e, h]), writes=[kcB[pb]], dkey=f"kc{pb}")
                    src = cv_in[e].rearrange("(kt p) f -> p kt f", p=128)[:, :, h * 128:(h + 1) * 128]
                    op("sp", lambda: nc.sync.dma_start(out=vc32[:, pb], in_=src), writes=[vcB[pb]], dkey=f"vc{pb}")
                load_cache(0)
                tail_prev = [None]
                sk = list(range(0, 4)) + list(range(8, 16))
                blocks = [((0, 256), [4, 5]), ((256, 256), [6, 7]), ((512, 256), sk), ((768, 256), sk), ((1024, 256), sk), ((1280, 256), sk)]
                for h in range(8):
                    if h + 1 < 8:
                        load_cache(h + 1)
                    pb = h % 2
                    wt, wb = wget(f"QKV_{e}_{h}")
                    for qk in range(2):
                        for kc in range(8):
                            for tb in range(3):
                                op("pe", lambda: PE.matmul(PS(tb), lhsT=wt[:, kc, qk * 128:(qk + 1) * 128], rhs=hT[:, kc, tb * 512:(tb + 1) * 512],
                                                           start=(kc == 0), stop=(kc == 7)),
                                   reads=[wb, hB[kc][tb]], writes=[PSB[tb]], sig=(kc == 7))
                        op("dve", lambda: DVE.tensor_copy(out=qT[:, qk, 0:512], in_=PS(0)), reads=[PSB[0]], writes=[qB[qk][0]])
                        for sb_ in range(2):
                            op("dve", lambda: DVE.tensor_copy(out=qraw[:, qk, sb_ * 512:(sb_ + 1) * 512], in_=PS(1 + sb_)), reads=[PSB[1 + sb_]], writes=[qrB[qk][sb_]])
                    if tail_prev[0] is not None:
                        tail_prev[0]()
                        tail_prev[0] = None
                    for tt in range(NT):
                        tb = tt // 4
                        tsl = slice(tt * 128, (tt + 1) * 128)
                        bk = nbank(3, 6)
                        ncol = 256 if tt < 4 else 128
                        c0 = 128 if tt < 4 else 256
                        for kc in range(8):
                            op("pe", lambda: PE.matmul(PS(bk)[:, 0:ncol], lhsT=hT[:, kc, tsl], rhs=wt[:, kc, c0:c0 + ncol], start=(kc == 0), stop=(kc == 7)),
                               reads=[wb, hB[kc][tb]], writes=[PSB[bk]], sig=(kc == 7))
                        vo = ncol - 128
                        if tt < 4:
                            op("act", lambda: ACT.copy(out=va[:, 4 + tt, 0:128], in_=PS(bk)[:, vo:vo + 128]), reads=[PSB[bk]], writes=[vaB[4 + tt]])
                            op("act", lambda: ACT.copy(out=kvo[:, pb, 0, tt, :], in_=PS(bk)[:, 0:128]), reads=[PSB[bk]], writes=[kvB[pb][0]])
                            op("act", lambda: ACT.copy(out=kvo[:, pb, 1, tt, :], in_=PS(bk)[:, 128:256]), reads=[PSB[bk]], writes=[kvB[pb][1]])
                        else:
                            op("act", lambda: ACT.copy(out=va[:, 4 + tt, 0:128], in_=PS(bk)[:, vo:vo + 128]), reads=[PSB[bk]], writes=[vaB[4 + tt]])
                    for qk in range(2):
                        for sb_ in range(2):
                            bk = nbank(0, 3)
                            op("pe", lambda: PE.matmul(PS(bk), lhsT=cb("Psw"), rhs=qraw[:, qk, sb_ * 512:(sb_ + 1) * 512], start=True, stop=True),
                               reads=[qrB[qk][sb_], cbfB], writes=[PSB[bk]])
                            ss_ = slice(sb_ * 512, (sb_ + 1) * 512)
                            rp = (qk * 2 + sb_) % 2
                            op("pool", lambda: nc.gpsimd.tensor_tensor(out=rt[:, rp, 0, :], in0=qraw[:, qk, ss_], in1=rope[:, 0, ss_], op=ALU.mult),
                               reads=[qrB[qk][sb_], ropeB], writes=[rtB[rp][0]])
                            op("dve", lambda: DVE.tensor_tensor(out=rt[:, rp, 1, :], in0=PS(bk), in1=rope[:, 1, ss_], op=ALU.mult),
                               reads=[PSB[bk], ropeB], writes=[rtB[rp][1]])
                            op("pool", lambda: nc.gpsimd.tensor_tensor(out=qT[:, qk, 512 + sb_ * 512:1024 + sb_ * 512], in0=rt[:, rp, 0, :], in1=rt[:, rp, 1, :], op=ALU.add),
                               reads=rtB[rp], writes=[qB[qk][1 + sb_]])
                    for i in range(2):
                        dst = (nk_out if i == 0 else nv_out)[e].rearrange("(tt p) f -> p tt f", p=128)[:, :, h * 128:(h + 1) * 128]
                        op("sp", lambda: nc.sync.dma_start(out=dst, in_=kvo[:, pb, i]), reads=[kvB[pb][i]], dkey=f"kvo{pb}")
                    op("act", lambda: ACT.copy(out=kcb[:], in_=kc32[:, pb, :]), reads=[kcB[pb]], writes=[kcbB])
                    op("pool", lambda: nc.gpsimd.tensor_copy(out=va[:, 0:4, 0:128], in_=vc32[:, pb]), reads=[vcB[pb]], writes=vaB[0:4])

                    def pv_ops(bi):
                        (q0, qn), kts = blocks[bi]
                        eb = bi % 2
                        nk_ = len(kts)
                        lst = []
                        for qi in range(2):
                            for mm_ in range(2):
                                for ki, kt in enumerate(kts):
                                    def f(qi=qi, mm_=mm_, ki=ki, kt=kt):
                                        op("pe", lambda: PE.matmul(PS(4 + 2 * eb + qi)[:, mm_ * 132:mm_ * 132 + 129], lhsT=ET[:, eb, ki, mm_, qi * 128:(qi + 1) * 128],
                                                                   rhs=va[:, kt, 0:129], start=(ki == 0), stop=(ki == nk_ - 1)),
                                           reads=[ETB[eb][ki], vaB[kt]], writes=[PSB[4 + 2 * eb + qi]], sig=(mm_ == 1 and ki == nk_ - 1))
                                    lst.append(f)
                        return lst

                    def scores(bi, fill):
                        (q0, qn), kts = blocks[bi]
                        eb = bi % 2
                        qtb = q0 // 512
                        per = -(-len(fill) // len(kts)) if fill else 0
                        for ki, kt in enumerate(kts):
                            bp = nbank(0, 2)
                            for mm_ in range(2):
                                ps_ = slice(mm_ * 64, (mm_ + 1) * 64)
                                if kt < 4:
                                    lhs = kcb[ps_, kt * 128:(kt + 1) * 128]
                                    rd = [kcbB]
                                else:
                                    tk = kt - 4
                                    lhs = qT[ps_, 1, tk * 128:(tk + 1) * 128]
                                    rd = [qB[1][tk // 4]]
                                op("pe", lambda: PE.matmul(PS(2 * bp + mm_)[:, 0:qn], lhsT=lhs, rhs=qT[ps_, 0, q0:q0 + qn], start=True, stop=True),
                                   reads=rd + [qB[0][qtb]], writes=[PSB[2 * bp + mm_]], sig=(mm_ == 1))
                            op("act", lambda: ACT.activation(out=ET[:, eb, ki, :, 0:qn], in_=psum[bp][:, :, 0:qn], func=AF.Exp, scale=0.125, bias=KC(3)),
                               reads=[PSB[2 * bp], PSB[2 * bp + 1], kcB_], writes=[ETB[eb][ki]])
                            for _ in range(per):
                                if fill:
                                    fill.pop(0)()
                        while fill:
                            fill.pop(0)()

                    def chain(bi):
                        (q0, qn), kts = blocks[bi]
                        eb = bi % 2
                        t0_ = q0 // 128
                        sl4 = t0_ % 4
                        pvp = psum[2 + eb]
                        pq = [PSB[4 + 2 * eb], PSB[5 + 2 * eb]]
                        op("dve", lambda: DVE.reciprocal(out=sm[:, eb, 0, :].unsqueeze(2), in_=pvp[:, :, 128:129]), reads=pq, writes=[smB_[eb][0]])
                        op("dve", lambda: DVE.reciprocal(out=sm[:, eb, 1, :].unsqueeze(2), in_=pvp[:, :, 260:261]), reads=pq, writes=[smB_[eb][1]])
                        op("dve", lambda: DVE.tensor_scalar(out=sm[:, eb, 2, :], in0=sm[:, eb, 1, :], scalar1=nlam[:, 3:4], scalar2=None, op0=ALU.mult),
                           reads=[smB_[eb][1], nlB], writes=[smB_[eb][2]])
                        for qi in range(2):
                            bq = 4 + 2 * eb + qi
                            op("dve", lambda: DVE.tensor_scalar(out=osb[:, qi, :], in0=PS(bq)[:, 0:128], scalar1=sm[:, eb, 0, qi:qi + 1], scalar2=None, op0=ALU.mult),
                               reads=[PSB[bq], smB_[eb][0]], writes=[osB[qi]])
                            op("dve", lambda: DVE.scalar_tensor_tensor(out=onb[:, sl4 + qi, :], in0=PS(bq)[:, 132:260], scalar=sm[:, eb, 2, qi:qi + 1], in1=osb[:, qi, :],
                                                                       op0=ALU.mult, op1=ALU.add),
                               reads=[PSB[bq], smB_[eb][2], osB[qi]], writes=[onB[sl4 + qi]])

                    def norm1(tb):
                        bk = nbank(2, 4)
                        for q in range(4):
                            op("pe", lambda: PE.matmul(PS(bk)[:, q * 128:(q + 1) * 128], lhsT=onb[:, q, :], rhs=ident, start=True, stop=True),
                               reads=[onB[q], cbfB], writes=[PSB[bk]], sig=(q == 3))
                        op("dve", lambda: DVE.tensor_scalar(out=oTs[:], in0=PS(bk), scalar1=1.0, scalar2=None, op0=ALU.mult), reads=[PSB[bk]], writes=[oTB])
                        op("pool", lambda: nc.gpsimd.tensor_tensor(out=sqb[:], in0=oTs[:], in1=oTs[:], op=ALU.mult), reads=[oTB], writes=[sqB_])
                        bk2 = nbank(2, 4)
                        op("pe", lambda: PE.matmul(PS(bk2), lhsT=ones_b, rhs=sqb[:], start=True, stop=True), reads=[sqB_, cbfB], writes=[PSB[bk2]])
                        op("dve", lambda: DVE.tensor_scalar(out=rsq[:], in0=PS(bk2), scalar1=1.0 / 128.0, scalar2=None, op0=ALU.mult), reads=[PSB[bk2]], writes=[rsqB])
                        return bk2

                    def norm2(tb, bk2, h=h):
                        op("act", lambda: ACT.activation(out=rsq[:], in_=rsq[:], func=AF.Ln, bias=KC(1)), reads=[rsqB, kcB_], writes=[rsqB])
                        op("act", lambda: ACT.activation(out=rsq[:], in_=rsq[:], func=AF.Exp, scale=-0.5), reads=[rsqB], writes=[rsqB])
                        op("dve", lambda: DVE.scalar_tensor_tensor(out=yoT[:, h, tb * 512:(tb + 1) * 512], in0=oTs[:], scalar=sublnS[:, 0:1], in1=rsq[:],
                                                                   op0=ALU.mult, op1=ALU.mult),
                           reads=[oTB, rsqB, slB], writes=[yoB[h][tb]])

                    scores(0, [])
                    pend = None
                    nb_ = len(blocks)
                    for bi in range(nb_):
                        fill = pv_ops(bi)
                        if bi + 1 < nb_:
                            scores(bi + 1, fill)
                        else:
                            while fill:
                                fill.pop(0)()
                        if pend is not None:
                            norm2(*pend)
                            pend = None
                        if bi + 1 < nb_:
                            chain(bi)
                            if bi % 2 == 1:
                                pend = (bi // 2, norm1(bi // 2))

                    def tail(chain=chain, norm1=norm1, norm2=norm2, nb_=nb_):
                        chain(nb_ - 1)
                        norm2(2, norm1(2))
                    tail_prev[0] = tail
                if tail_prev[0] is not None:
                    tail_prev[0]()
                outproj(l, [f"O2_{e}_0", f"O2_{e}_1"], yoT, yoB)
                P.fence()
            P.fence()

    sublnS = sbt(es, "sublnS", [128, 1], F32)
    slB = Buf()

    try:
        if DEPTH_RUN > 0:
            for t in range(4):
                ada_tile(0, t)
            ada_finish(0, 0)
            checkpoint('ada0')
        for l in range(DEPTH_RUN):
            m = mod[l % 2]
            g = gsc[l % 2]
            with ExitStack() as sl_:
                hT = sbt(sl_, "hT", [128, 8, T], BF16)
                hB = BG(8, 3)
                modulate(sl_, hT, hB, lambda kc, v: g[:, 0, kc, v:v + 1], lambda kc, v: m[:, kc, v:v + 1], [modB[l % 2], gscB[l % 2]])
                checkpoint('mod')
                if l % 2 == 0:
                    li = 0.8 - 0.6 * math.exp(-0.3 * l)
                    op("dve", lambda: DVE.tensor_scalar(out=sublnS[:], in0=pv(f"subln{l // 2}"), scalar1=1.0 - li, scalar2=None, op0=ALU.mult), reads=[parB], writes=[slB])
                    ab_layer(l, hT, hB)
                else:
                    fourier_layer(l, hT, hB)
                P.fence()
            checkpoint('mixer')
            ffn(l)
            checkpoint('ffn')
        with ExitStack() as sl_:
            yo = sbt(sl_, "yo", [128, 8, T], F32)
            yB = BG(8, 3)
            nfs = sbt(sl_, "nfs", [128, 8], F32)
            nfB = Buf()
            op("dve", lambda: DVE.tensor_scalar(out=nfs[:], in0=pv("nfin"), scalar1=32.0, scalar2=None, op0=ALU.mult), reads=[parB], writes=[nfB])
            modulate(sl_, yo, yB, lambda kc, v: nfs[:, kc:kc + 1], lambda kc, v: None, [nfB])
            yout = yT_out.rearrange("(kc p) t -> p kc t", p=128)
            for kc in range(8):
                op("sp", lambda kc=kc: nc.sync.dma_start(out=yout[:, kc, :], in_=yo[:, kc, :]), reads=yB[kc], dkey="st2")

    except _Stop:
        pass
    for k in list(P.isdma):
        if not k.startswith('ring'):
            nc.sync.wait_ge(P.sems[k], P.cnt[k])
    assert STOP_AT is not None or st["used"] == len(plan), (st["used"], len(plan))
    es.close()
    _CACHE['trace'] = P.trace
    return nc, P.nins


_CACHE = {}


def kernel(x_prompt, x_sample, cache_k, cache_v, state_ssd_fwd, state_ssd_bwd, c, c_ctx, w_ada, b_ada, norm_mix,
           norm_ffn, w_in_ab, conv_w, conv_b, dt_bias, a_log, d_skip, ssd_norm, lambda_qk, subln, w_out_ab,
           w_four, b_four, w_ffn_in, w_ffn_out, norm_final):
    f = lambda a: np.ascontiguousarray(np.asarray(a, dtype=np.float32))
    x_prompt, x_sample, cache_k, cache_v = f(x_prompt), f(x_sample), f(cache_k), f(cache_v)
    state_ssd_fwd, state_ssd_bwd, c, c_ctx = f(state_ssd_fwd), f(state_ssd_bwd), f(c), f(c_ctx)
    if "nc" not in _CACHE:
        _CACHE["nc"] = build_program()[0]
        _CACHE["con"] = make_consts()
    nc = _CACHE["nc"]
    con, dft, rope = _CACHE["con"]
    w_inp = f(np.asarray(w_in_ab)[:, :, win_perm()])
    w_ffi = f(np.asarray(w_ffn_in)[:, :, ffi_perm()])
    shared = {"con": con, "dft": dft, "rope": rope, "w_ada": f(w_ada), "w_inp": w_inp, "w_out": f(w_out_ab),
              "w_four": f(w_four), "w_ffi": w_ffi, "w_ffo": f(w_ffn_out)}
    par0 = np.zeros((128, NPAR), np.float32)

    def put(name, a):
        o, w = PL[name]
        par0[:, o:o + w] = np.asarray(a, np.float32).reshape(128, w)
    fm = lambda v, n: np.asarray(v, np.float32).reshape(n, 128).T
    rb = lambda v: np.broadcast_to(np.asarray(v, np.float32).reshape(1, -1), (128, np.asarray(v).size))
    put("nfin", fm(norm_final, 8))
    for l in range(4):
        put(f"nm{l}", fm(norm_mix[l], 8))
        put(f"nf{l}", fm(norm_ffn[l], 8))
        put(f"bada{l}", fm(b_ada[l], 48))
    for e in range(2):
        cw = np.asarray(conv_w[e], np.float32)
        cbv = np.asarray(conv_b[e], np.float32)
        cwm = np.zeros((128, 80), np.float32)
        cbm = np.zeros((128, 16), np.float32)
        for sl in range(16):
            ch = conv_chan(sl)
            cwm[:, sl * 5:(sl + 1) * 5] = cw[:, ch].T
            cbm[:, sl] = cbv[ch]
        put(f"cw{e}", cwm)
        put(f"cb{e}", cbm)
        put(f"ssdn{e}", fm(ssd_norm[e], 8))
        put(f"subln{e}", np.asarray(subln[e], np.float32).reshape(128, 1))
        put(f"dtb{e}", rb(dt_bias[e]))
        put(f"alog{e}", rb(a_log[e]))
        put(f"dsk{e}", rb(d_skip[e]))
        put(f"lqk{e}", rb(lambda_qk[e]))
        put(f"b4{e}", fm(b_four[e], 8))
    in_maps = []
    for i in range(8):
        s = i // 2
        xt = np.concatenate([x_prompt[2 * i], x_prompt[2 * i + 1], x_sample[s]], 0)
        p = par0.copy()
        cvec = np.stack([c_ctx, c[s]], 0).reshape(2, 8, 128).transpose(2, 1, 0).reshape(128, 16)
        o, w = PL["cvec"]
        p[:, o:o + w] = cvec
        mp = dict(shared)
        mp["xT_in"] = np.ascontiguousarray(xt.T)
        mp["par"] = p
        mp["ck"] = np.ascontiguousarray(cache_k[s].reshape(2, 512, 8, 128).transpose(0, 2, 3, 1))
        mp["cv"] = np.ascontiguousarray(cache_v[s].reshape(2, 512, 1024))
        mp["sf"] = np.ascontiguousarray(state_ssd_fwd[s].reshape(2, 1024, 128).transpose(0, 2, 1))
        mp["sb"] = np.ascontiguousarray(state_ssd_bwd[s].reshape(2, 1024, 128).transpose(0, 2, 1))
        in_maps.append(mp)
    res = run_bass_kernel_spmd(nc, in_maps[:DBG_CORES], core_ids=list(range(DBG_CORES)))
    R = list(res.results) + [res.results[0]] * (8 - DBG_CORES)
    y_prompt = np.zeros((16, 256, 1024), np.float32)
    y_sample = np.zeros((4, 1024, 1024), np.float32)
    new_k = np.zeros((16, 2, 256, 8, 2, 64), np.float32)
    new_v = np.zeros((16, 2, 256, 8, 128), np.float32)
    new_f = np.zeros((16, 2, 16, 64, 128), np.float32)
    new_b = np.zeros((16, 2, 16, 64, 128), np.float32)
    for i in range(8):
        yT = np.asarray(R[i]["yT"])
        for q in range(2):
            b = 2 * i + q
            y_prompt[b] = yT[:, q * 256:(q + 1) * 256].T
            new_k[b] = np.asarray(R[i]["nk"])[:, q * 256:(q + 1) * 256].reshape(2, 256, 8, 2, 64)
            new_v[b] = np.asarray(R[i]["nv"])[:, q * 256:(q + 1) * 256].reshape(2, 256, 8, 128)
            new_f[b] = np.asarray(R[i]["nf"])[:, q].transpose(0, 2, 1).reshape(2, 16, 64, 128)
            new_b[b] = np.asarray(R[i]["nb"])[:, q].transpose(0, 2, 1).reshape(2, 16, 64, 128)
        if i % 2 == 0:
            y_sample[i // 2] = yT[:, 512:].T
    return (y_prompt, y_sample, new_k, new_v, new_f, new_b)
```

```python
import math
import numpy as np
from contextlib import ExitStack
import concourse.bass as bass
import concourse.mybir as mybir
from concourse.bass_utils import run_bass_kernel_spmd

F32 = mybir.dt.float32
BF16 = mybir.dt.bfloat16
AF = mybir.ActivationFunctionType
ALU = mybir.AluOpType

DEPTH_RUN = 4
STOP_AT = None
SKIP_KVO = False
DBG_CORES = 8


class _Stop(Exception):
    pass


_STOPPED = [False]


def checkpoint(name):
    if STOP_AT == name:
        _STOPPED[0] = True
D = 1024
T = 1536
NT = 12
EPS = 1e-6
SEQS = [(0, 2), (2, 2), (4, 8)]
FFH = 2816
NJ = 22
IN_AB = 6176
RING_SLOTS = 3
RING_ELEMS = 4096


def _layout(items):
    d, off = {}, 0
    for n, w in items:
        d[n] = (off, w)
        off += w
    return d, off


def param_layout():
    it = [("cvec", 16), ("nfin", 8)]
    for l in range(4):
        it += [(f"nm{l}", 8), (f"nf{l}", 8), (f"bada{l}", 48)]
    for e in range(2):
        it += [(f"cw{e}", 80), (f"cb{e}", 16), (f"ssdn{e}", 8), (f"subln{e}", 1), (f"dtb{e}", 32),
               (f"alog{e}", 32), (f"dsk{e}", 16), (f"lqk{e}", 256), (f"b4{e}", 8)]
    return _layout(it)


def const_layout():
    it = [("ident", 128), ("ones", 128), ("Uf", 128), ("Ub", 128), ("Vf", 128), ("Vb", 128),
          ("NEGf", 128), ("NEGb", 128), ("Psw", 128), ("Cc", 512), ("Sc", 512), ("CL256", 512), ("nSL256", 512)]
    return _layout(it)


PL, NPAR = param_layout()
CL, NCON = const_layout()


def conv_chan(sl):
    g, s = divmod(sl, 4)
    if s < 2:
        return g * 256 + s * 128 + np.arange(128)
    if s == 2:
        return 1024 + g * 128 + np.arange(128)
    return 1536 + g * 128 + np.arange(128)


def win_perm():
    o_z, o_x, o_dt, o_q, o_k, o_v = 0, 1024, 3072, 3104, 4128, 5152
    cols = list(o_dt + np.arange(32))
    for g in range(4):
        cols += list(o_x + g * 256 + np.arange(256))
        cols += list(o_x + 1024 + g * 128 + np.arange(128))
        cols += list(o_x + 1536 + g * 128 + np.arange(128))
        cols += list(o_z + g * 256 + np.arange(256))
    for h in range(8):
        cols += list(o_q + h * 128 + np.arange(128))
        cols += list(o_k + h * 128 + np.arange(128))
        cols += list(o_v + h * 128 + np.arange(128))
    return np.array(cols)


def ffi_perm():
    cols = []
    for t in range(11):
        for j in (2 * t, 2 * t + 1):
            cols += list(j * 128 + np.arange(128))
        for j in (2 * t, 2 * t + 1):
            cols += list(FFH + j * 128 + np.arange(128))
    return np.array(cols)


def make_consts():
    c = np.zeros((128, NCON), np.float32)
    t = np.arange(128)[:, None]
    j = np.arange(128)[None, :]

    def put(n, a):
        o, w = CL[n]
        c[:, o:o + w] = a.reshape(128, w)
    put("ident", (t == j).astype(np.float32))
    put("ones", np.ones((128, 128), np.float32))
    put("Uf", (t > j).astype(np.float32))
    put("Ub", (t < j).astype(np.float32))
    put("Vf", (t <= j).astype(np.float32))
    put("Vb", (t >= j).astype(np.float32))
    put("NEGf", -30000.0 * (j < t))
    put("NEGb", -30000.0 * (j > t))
    d = np.arange(128) % 64
    partner = np.where(d % 32 < 16, np.arange(128) + 16, np.arange(128) - 16)
    psw = np.zeros((128, 128), np.float32)
    psw[partner, np.arange(128)] = 1.0
    put("Psw", psw)
    n = np.arange(256)
    ang = 2 * np.pi * np.outer(n, n) / 256.0
    cc = (np.cos(ang) / 16.0).reshape(2, 128, 256).transpose(1, 0, 2)
    ss = (np.sin(ang) / 16.0).reshape(2, 128, 256).transpose(1, 0, 2)
    put("Cc", cc)
    put("Sc", ss)
    put("CL256", cc)
    put("nSL256", -ss)
    n = np.arange(1024)
    ang = 2 * np.pi * np.outer(n, n) / 1024.0
    cl = (np.cos(ang) / 32.0).reshape(8, 128, 1024).transpose(1, 0, 2)
    sl = (-np.sin(ang) / 32.0).reshape(8, 128, 1024).transpose(1, 0, 2)
    dft = np.ascontiguousarray(np.stack([cl, sl], 0).astype(np.float32))
    tok = np.arange(1024)
    row = (tok // 64).astype(np.float64)
    col = (tok % 64).astype(np.float64)
    freq = 10000.0 ** (-np.arange(16, dtype=np.float64) / 16.0)
    cosT = np.zeros((128, 1024), np.float32)
    sinT = np.zeros((128, 1024), np.float32)
    for p in range(128):
        dd = p % 64
        pos = row if dd < 32 else col
        a = pos * freq[dd % 16]
        cosT[p] = np.cos(a)
        sinT[p] = (-np.sin(a)) if (dd % 32) < 16 else np.sin(a)
    rope = np.ascontiguousarray(np.stack([cosT, sinT], 0))
    return c, dft, rope


class Buf:
    __slots__ = ("w", "r", "x", "fresh")

    def __init__(self, x=False):
        self.w = None
        self.r = {}
        self.fresh = True
        self.x = x


def BG(*shape):
    if len(shape) == 1:
        return [Buf() for _ in range(shape[0])]
    return [BG(*shape[1:]) for _ in range(shape[0])]


def flat(x):
    if isinstance(x, Buf):
        return [x]
    out = []
    for y in x:
        out += flat(y)
    return out


class Prog:
    def __init__(self, nc, es):
        self.nc = nc
        self.es = es
        self.E = {"pe": nc.tensor, "act": nc.scalar, "dve": nc.vector, "pool": nc.gpsimd, "sp": nc.sync}
        self.sems, self.cnt = {}, {}
        self.known = {e: {} for e in self.E}
        self.floor = {}
        self.isdma = set()
        for e in ("pe", "act", "dve", "pool"):
            self.sems[e] = es.enter_context(nc.semaphore("c_" + e))
            self.cnt[e] = 0
        self.nins = 0
        self.trace = {}

    def dsem(self, key):
        if key not in self.sems:
            self.sems[key] = self.es.enter_context(self.nc.semaphore("d_" + key))
            self.cnt[key] = 0
            self.isdma.add(key)
        return self.sems[key]

    def op(self, e, fn, reads=(), writes=(), sig=True, dkey=None, nofence=False):
        if _STOPPED[0]:
            return None
        reqs = {}

        def need(k, v):
            if k in self.isdma:
                v = self.cnt[k]
            elif k == e:
                if e == "pe":
                    return
            if reqs.get(k, 0) < v:
                reqs[k] = v
        reads = flat(reads)
        writes = flat(writes)
        if any(b.fresh for b in reads) or any(b.fresh for b in writes):
            for k, v in self.floor.items():
                if k != e:
                    need(k, v)
                elif e != "pe" and reqs.get(k, 0) < v:
                    reqs[k] = v
            for b in reads:
                b.fresh = False
            for b in writes:
                b.fresh = False
        for b in reads:
            if b.w is not None:
                need(*b.w)
            if b.x:
                for k, v in b.r.items():
                    if k != e:
                        need(k, v)
        for b in writes:
            if b.w is not None:
                need(*b.w)
            for k, v in b.r.items():
                need(k, v)
        kn = self.known[e]
        for k, v in reqs.items():
            if kn.get(k, 0) < v:
                self.E[e].wait_ge(self.sems[k], v)
                kn[k] = v
                self.trace.setdefault(e, []).append(("w", k, v))
        ins = fn()
        self.nins += 1
        self.trace.setdefault(e, []).append(("i", dkey if dkey is not None else (e if sig else None), 16 if dkey is not None else 1))
        if dkey is not None:
            sem = self.dsem(dkey)
            ins.then_inc(sem, 16)
            self.cnt[dkey] += 16
            stamp = (dkey, self.cnt[dkey])
        elif sig:
            ins.then_inc(self.sems[e], 1)
            self.cnt[e] += 1
            stamp = (e, self.cnt[e])
        else:
            stamp = (e, self.cnt[e] + 1)
        for b in writes:
            b.w = stamp
            b.r = {}
        k, v = stamp
        for b in reads:
            if b.r.get(k, 0) < v:
                b.r[k] = v
        return ins

    def fence(self):
        self.floor = {k: v for k, v in self.cnt.items() if not k.startswith("ring")}


def build_program():
    _STOPPED[0] = False
    nc = bass.Bass("TRN2", target_bir_lowering=False)
    dr = lambda n, s, kind="ExternalInput": nc.dram_tensor(n, s, F32, kind=kind).ap()
    xT_in = dr("xT_in", [D, T])
    par_in = dr("par", [128, NPAR])
    con_in = dr("con", [128, NCON])
    dft_in = dr("dft", [2, 128, 8, 1024])
    rope_in = dr("rope", [2, 128, 1024])
    ck_in = dr("ck", [2, 8, 128, 512])
    cv_in = dr("cv", [2, 512, 1024])
    sf_in = dr("sf", [2, 128, 1024])
    sb_in = dr("sb", [2, 128, 1024])
    w_ada = dr("w_ada", [4, D, 6144])
    w_inp = dr("w_inp", [2, D, IN_AB])
    w_out = dr("w_out", [2, 2048, D])
    w_four = dr("w_four", [2, D, D])
    w_ffi = dr("w_ffi", [4, D, 2 * FFH])
    w_ffo = dr("w_ffo", [4, FFH, D])
    yT_out = dr("yT", [D, T], "ExternalOutput")
    nk_out = dr("nk", [2, 512, 1024], "ExternalOutput")
    nv_out = dr("nv", [2, 512, 1024], "ExternalOutput")
    nf_out = dr("nf", [2, 2, 128, 1024], "ExternalOutput")
    nb_out = dr("nb", [2, 2, 128, 1024], "ExternalOutput")

    es = ExitStack()
    P = Prog(nc, es)
    op = P.op
    PE, ACT, DVE = nc.tensor, nc.scalar, nc.vector

    uid = {"n": 0}

    def sbt(stack, name, shape, dt):
        uid["n"] += 1
        return stack.enter_context(nc.sbuf_tensor(f"{name}_{uid['n']}", shape, dt))

    xT = sbt(es, "xT", [128, 8, T], F32)
    xB = BG(8, 3)
    par = sbt(es, "par_sb", [128, NPAR], F32)
    parB = Buf()
    cbf = sbt(es, "cbf", [128, 9 * 128], BF16)
    cbfB = Buf()
    cf32 = sbt(es, "cf32", [128, 4 * 128], F32)
    cfB = Buf()
    scT = sbt(es, "scT", [128, 8, 2], BF16)
    scB = Buf()
    mod = [sbt(es, f"mod{i}", [128, 48, 2], F32) for i in range(2)]
    modB = [Buf(), Buf()]
    gsc = [sbt(es, f"gsc{i}", [128, 2, 8, 2], F32) for i in range(2)]
    gscB = [Buf(), Buf()]
    ring = sbt(es, "ring", [128, RING_SLOTS, RING_ELEMS], BF16)
    ringB = BG(RING_SLOTS)
    psum = [es.enter_context(nc.psum_tensor(f"ps{i}", [128, 2, 512], F32)) for i in range(4)]
    PSB = [Buf(x=True) for _ in range(8)]

    def PS(i):
        return psum[i // 2][:, i % 2, :]

    def pv(name):
        o, w = PL[name]
        return par[:, o:o + w]

    def cb(name):
        o, w = CL[name]
        return cbf[:, o:o + w]

    plan = []

    def wtile(dram_ap, kc, ncols, tag):
        plan.append((dram_ap, kc, ncols, tag))

    for l in range(DEPTH_RUN):
        wa0 = w_ada[0].rearrange("(kc p) f -> p kc f", p=128)
        if l == 0:
            for t in range(4):
                wtile(wa0[:, :, t * 512:(t + 1) * 512], 8, 512, f"ada0_{t}")
        if l % 2 == 0:
            e = l // 2
            wi = w_inp[e].rearrange("(kc p) f -> p kc f", p=128)
            wo = w_out[e].rearrange("(kc p) f -> p kc f", p=128)
            wtile(wi[:, :, 0:32], 8, 32, f"dt{e}")
            for g in range(4):
                b0 = 32 + g * 768
                wtile(wi[:, :, b0:b0 + 512], 8, 512, f"A1_{e}_{g}")
                if l == 0:
                    t = 4 + 2 * g
                    wtile(wa0[:, :, t * 512:(t + 1) * 512], 8, 512, f"ada0_{t}")
                wtile(wi[:, :, b0 + 512:b0 + 768], 8, 256, f"A2_{e}_{g}")
                if l == 0:
                    t = 5 + 2 * g
                    wtile(wa0[:, :, t * 512:(t + 1) * 512], 8, 512, f"ada0_{t}")
            for t in range(2):
                wtile(wo[:, 0:8, t * 512:(t + 1) * 512], 8, 512, f"O1_{e}_{t}")
            for h in range(8):
                b0 = 3104 + h * 384
                wtile(wi[:, :, b0:b0 + 384], 8, 384, f"QKV_{e}_{h}")
            for t in range(2):
                wtile(wo[:, 8:16, t * 512:(t + 1) * 512], 8, 512, f"O2_{e}_{t}")
        else:
            o = l // 2
            w4 = w_four[o].rearrange("(kc p) f -> p kc f", p=128)
            for t in range(2):
                wtile(w4[:, :, t * 512:(t + 1) * 512], 8, 512, f"W4_{o}_{t}")
        wf = w_ffi[l].rearrange("(kc p) f -> p kc f", p=128)
        wa = w_ada[l + 1].rearrange("(kc p) f -> p kc f", p=128) if l + 1 < DEPTH_RUN else None
        for t in range(11):
            wtile(wf[:, :, t * 512:(t + 1) * 512], 8, 512, f"FI_{l}_{t}")
            if wa is not None:
                wtile(wa[:, :, t * 512:(t + 1) * 512], 8, 512, f"ada{l + 1}_{t}")
        if wa is not None:
            wtile(wa[:, :, 11 * 512:12 * 512], 8, 512, f"ada{l + 1}_11")
        wfo = w_ffo[l].rearrange("(kc p) f -> p kc f", p=128)
        for t in range(8):
            wtile(wfo[:, :, t * 128:(t + 1) * 128], 22, 128, f"FO_{l}_{t}")

    st = {"issued": 0, "used": 0}

    def ring_issue_upto(n):
        while st["issued"] <= n and st["issued"] < len(plan):
            m = st["issued"]
            ap, kc, ncols, tag = plan[m]
            s = m % RING_SLOTS
            dst = ring[:, s, 0:kc * ncols].rearrange("p (k c) -> p k c", k=kc)
            op("pool", lambda dst=dst, ap=ap: nc.gpsimd.dma_start(out=dst, in_=ap),
               writes=[ringB[s]], dkey=f"ring{s}", nofence=True)
            st["issued"] += 1

    def wget(tag):
        n = st["used"]
        ap, kc, ncols, tg = plan[n]
        assert tg == tag, (tg, tag)
        ring_issue_upto(n + RING_SLOTS - 1)
        st["used"] += 1
        s = n % RING_SLOTS
        return ring[:, s, 0:kc * ncols].rearrange("p (k c) -> p k c", k=kc), ringB[s]

    rot = {"i": 0}

    def nbank(lo=0, hi=6):
        b = lo + rot["i"] % (hi - lo)
        rot["i"] += 1
        return b

    op("sp", lambda: nc.sync.dma_start(out=par[:], in_=par_in[:, :]), writes=[parB], dkey="ld0")
    with ExitStack() as s0:
        con = sbt(s0, "con_sb", [128, NCON], F32)
        conB = Buf()
        op("sp", lambda: nc.sync.dma_start(out=con[:], in_=con_in[:, :]), writes=[conB], dkey="ld0")
        xin = xT_in.rearrange("(kc p) t -> p kc t", p=128)
        for kc in range(8):
            op("sp", lambda kc=kc: nc.sync.dma_start(out=xT[:, kc, :], in_=xin[:, kc, :]), writes=xB[kc], dkey="ld1")
        op("dve", lambda: DVE.tensor_copy(out=cbf[:], in_=con[:, 0:9 * 128]), reads=[conB], writes=[cbfB])
        o1 = CL["ones"][0]
        op("act", lambda: ACT.copy(out=cf32[:, 0:128], in_=con[:, o1:o1 + 128]), reads=[conB], writes=[cfB])
        o2 = CL["Vf"][0]
        op("act", lambda: ACT.copy(out=cf32[:, 128:384], in_=con[:, o2:o2 + 256]), reads=[conB], writes=[cfB])
        o3 = CL["ident"][0]
        op("act", lambda: ACT.copy(out=cf32[:, 384:512], in_=con[:, o3:o3 + 128]), reads=[conB], writes=[cfB])
        cv_ = pv("cvec").rearrange("p (k v) -> p k v", v=2)
        op("act", lambda: ACT.activation(out=scT[:], in_=cv_, func=AF.Silu), reads=[parB], writes=[scB])
        P.fence()
    kct = sbt(es, "kct", [128, 4], F32)
    kcB_ = Buf()
    op("dve", lambda: DVE.memset(kct[:, 0:1], 1024.0 * EPS), writes=[kcB_])
    op("dve", lambda: DVE.memset(kct[:, 1:2], EPS), writes=[kcB_])
    op("dve", lambda: DVE.memset(kct[:, 2:3], 1.0), writes=[kcB_])
    op("dve", lambda: DVE.memset(kct[:, 3:4], -30.0), writes=[kcB_])

    def KC(i):
        return kct[:, i:i + 1]
    ones_f = cf32[:, 0:128]
    Vf_f = cf32[:, 128:256]
    Vb_f = cf32[:, 256:384]
    ident_f = cf32[:, 384:512]
    ident = cb("ident")
    ones_b = cb("ones")

    ABANK = 7

    def ada_tile(l, t):
        wt, wb = wget(f"ada{l}_{t}")
        for s in range(4):
            j = 4 * t + s
            for kc in range(8):
                op("pe", lambda s=s, kc=kc, j=j: PE.matmul(PS(ABANK)[:, 2 * j:2 * j + 2], lhsT=wt[:, kc, s * 128:(s + 1) * 128],
                                                        rhs=scT[:, kc, :], start=(kc == 0), stop=(kc == 7)),
                   reads=[wb, scB], writes=[PSB[ABANK]], sig=(kc == 7))

    def ada_finish(l, part=None):
        m = mod[l % 2]
        mB = modB[l % 2]
        j0, j1 = {None: (0, 48), 0: (0, 16), 1: (16, 48)}[part]
        op("dve", lambda: DVE.tensor_tensor(out=m[:, j0:j1, :], in0=PS(ABANK)[:, 2 * j0:2 * j1].rearrange("p (j v) -> p j v", v=2),
                                            in1=pv(f"bada{l}")[:, j0:j1].unsqueeze(2).to_broadcast([128, j1 - j0, 2]), op=ALU.add),
           reads=[PSB[ABANK], parB], writes=[mB])
        g = gsc[l % 2]
        for wh, (nm, sc0) in enumerate(((f"nm{l}", 8), (f"nf{l}", 32))):
            if (part == 0 and wh == 1) or (part == 1 and wh == 0):
                continue
            op("dve", lambda wh=wh, nm=nm, sc0=sc0: DVE.scalar_tensor_tensor(
                out=g[:, wh], in0=m[:, sc0:sc0 + 8, :], scalar=1.0,
                in1=pv(nm).unsqueeze(2).to_broadcast([128, 8, 2]), op0=ALU.add, op1=ALU.mult),
               reads=[mB, parB], writes=[gscB[l % 2]])
            op("dve", lambda wh=wh: DVE.tensor_scalar(out=g[:, wh], in0=g[:, wh], scalar1=32.0, scalar2=None, op0=ALU.mult),
               reads=[gscB[l % 2]], writes=[gscB[l % 2]])

    def modulate(stack, dst, dstB, gs, sh, depB, out_dt_is_f32=False):
        with ExitStack() as s1:
            sq = sbt(s1, "sq", [128, 2, 8, 512], BF16)
            sqB = BG(2, 2)
            rs = sbt(s1, "rs", [128, 2, 512], F32)
            rsB = BG(2)
            tmp = sbt(s1, "mtmp", [128, 2, 8, 512], F32)
            tmpB = BG(2, 2)
            def stage1(tb):
                p = tb % 2
                bk = 6 + p
                ts = slice(tb * 512, (tb + 1) * 512)
                xb = [xB[kc][tb] for kc in range(8)]
                op("act", lambda: ACT.activation(out=sq[:, p, 0:5], in_=xT[:, 0:5, ts], func=AF.Square), reads=xb[0:5], writes=[sqB[p][0]])
                op("pool", lambda: nc.gpsimd.tensor_tensor(out=sq[:, p, 5:8], in0=xT[:, 5:8, ts], in1=xT[:, 5:8, ts], op=ALU.mult), reads=xb[5:8], writes=[sqB[p][1]])
                for kc in range(8):
                    op("pe", lambda: PE.matmul(PS(bk), lhsT=ones_b, rhs=sq[:, p, kc, :], start=(kc == 0), stop=(kc == 7)),
                       reads=[sqB[p][0 if kc < 5 else 1], cbfB], writes=[PSB[bk]], sig=(kc == 7))

            def stage2(tb):
                p = tb % 2
                bk = 6 + p
                v = 0 if tb == 0 else 1
                ts = slice(tb * 512, (tb + 1) * 512)
                xb = [xB[kc][tb] for kc in range(8)]
                op("act", lambda: ACT.activation(out=rs[:, p, :], in_=PS(bk), func=AF.Ln, bias=KC(0)), reads=[PSB[bk], kcB_], writes=[rsB[p]])
                op("act", lambda: ACT.activation(out=rs[:, p, :], in_=rs[:, p, :], func=AF.Exp, scale=-0.5), reads=[rsB[p]], writes=[rsB[p]])
                op("dve", lambda: DVE.tensor_tensor(out=tmp[:, p, 0:6], in0=xT[:, 0:6, ts], in1=rs[:, p, :].unsqueeze(1).to_broadcast([128, 6, 512]), op=ALU.mult),
                   reads=xb[0:6] + [rsB[p]], writes=[tmpB[p][0]])
                op("pool", lambda: nc.gpsimd.tensor_tensor(out=tmp[:, p, 6:8], in0=xT[:, 6:8, ts], in1=rs[:, p, :].unsqueeze(1).to_broadcast([128, 2, 512]), op=ALU.mult),
                   reads=xb[6:8] + [rsB[p]], writes=[tmpB[p][1]])
                for kc in range(8):
                    b = sh(kc, v)
                    if kc < 4:
                        if b is None:
                            op("act", lambda: ACT.activation(out=dst[:, kc, ts], in_=tmp[:, p, kc, :], func=AF.Identity, scale=gs(kc, v)),
                               reads=[tmpB[p][0 if kc < 6 else 1]] + depB, writes=[dstB[kc][tb]])
                        else:
                            op("act", lambda: ACT.activation(out=dst[:, kc, ts], in_=tmp[:, p, kc, :], func=AF.Identity, scale=gs(kc, v), bias=b),
                               reads=[tmpB[p][0 if kc < 6 else 1]] + depB, writes=[dstB[kc][tb]])
                    else:
                        if b is None:
                            op("dve", lambda: DVE.tensor_scalar(out=dst[:, kc, ts], in0=tmp[:, p, kc, :], scalar1=gs(kc, v), scalar2=None, op0=ALU.mult),
                               reads=[tmpB[p][0 if kc < 6 else 1]] + depB, writes=[dstB[kc][tb]])
                        else:
                            op("dve", lambda: DVE.tensor_scalar(out=dst[:, kc, ts], in0=tmp[:, p, kc, :], scalar1=gs(kc, v), scalar2=b, op0=ALU.mult, op1=ALU.add),
                               reads=[tmpB[p][0 if kc < 6 else 1]] + depB, writes=[dstB[kc][tb]])
            stage1(0)
            stage1(1)
            stage2(0)
            stage1(2)
            stage2(1)
            stage2(2)
            P.fence()

    def ffn(l):
        m = mod[l % 2]
        g = gsc[l % 2]
        mB = modB[l % 2]
        with ExitStack() as s1:
            h2 = sbt(s1, "h2", [128, 8, T], BF16)
            h2B = BG(8, 3)
            modulate(s1, h2, h2B, lambda kc, v: g[:, 1, kc, v:v + 1], lambda kc, v: m[:, 24 + kc, v:v + 1],
                     [mB, gscB[l % 2]])
            aT = sbt(s1, "aT", [128, NJ, T], BF16)
            aB = BG(NJ, 3)
            sg = sbt(s1, "sg", [128, 2, T], F32)
            sgB = BG(2, 3)
            for t in range(11):
                wt, wb = wget(f"FI_{l}_{t}")
                for s in range(2):
                    j = 2 * t + s
                    for half, b0 in ((0, 0), (1, 3)):
                        c0 = (half * 2 + s) * 128
                        for kc in range(8):
                            for tb in range(3):
                                op("pe", lambda c0=c0, kc=kc, tb=tb, b0=b0: PE.matmul(
                                    PS(b0 + tb), lhsT=wt[:, kc, c0:c0 + 128], rhs=h2[:, kc, tb * 512:(tb + 1) * 512],
                                    start=(kc == 0), stop=(kc == 7)),
                                   reads=[wb, h2B[kc][tb]], writes=[PSB[b0 + tb]], sig=(kc == 7))
                    for tb in range(3):
                        ts = slice(tb * 512, (tb + 1) * 512)
                        op("act", lambda tb=tb, ts=ts, s=s: ACT.activation(out=sg[:, s, ts], in_=PS(tb), func=AF.Silu),
                           reads=[PSB[tb]], writes=[sgB[s][tb]])
                        op("dve", lambda tb=tb, ts=ts, s=s, j=j: DVE.tensor_tensor(out=aT[:, j, ts], in0=sg[:, s, ts], in1=PS(3 + tb), op=ALU.mult),
                           reads=[sgB[s][tb], PSB[3 + tb]], writes=[aB[j][tb]])
                if l + 1 < DEPTH_RUN:
                    ada_tile(l + 1, t)
            if l + 1 < DEPTH_RUN:
                ada_tile(l + 1, 11)
                ada_finish(l + 1)
            for dsl in range(8):
                wt, wb = wget(f"FO_{l}_{dsl}")
                b0 = 0 if dsl % 2 == 0 else 3
                for j in range(NJ):
                    for tb in range(3):
                        op("pe", lambda j=j, tb=tb, b0=b0: PE.matmul(PS(b0 + tb), lhsT=wt[:, j, :], rhs=aT[:, j, tb * 512:(tb + 1) * 512],
                                                                    start=(j == 0), stop=(j == NJ - 1)),
                           reads=[wb, aB[j][tb]], writes=[PSB[b0 + tb]], sig=(j == NJ - 1))
                for tb in range(3):
                    v = 0 if tb == 0 else 1
                    ts = slice(tb * 512, (tb + 1) * 512)
                    op("dve", lambda tb=tb, ts=ts, v=v, dsl=dsl, b0=b0: DVE.scalar_tensor_tensor(
                        out=xT[:, dsl, ts], in0=PS(b0 + tb), scalar=m[:, 40 + dsl, v:v + 1], in1=xT[:, dsl, ts],
                        op0=ALU.mult, op1=ALU.add),
                       reads=[PSB[b0 + tb], mB, xB[dsl][tb]], writes=[xB[dsl][tb]])
            P.fence()

    def outproj(l, tags, srcT, srcB, bias=None, rowscale=None):
        m = mod[l % 2]
        mB = modB[l % 2]
        for t, tag in enumerate(tags):
            wt, wb = wget(tag)
            for s in range(4):
                dsl = 4 * t + s
                b0 = 0 if dsl % 2 == 0 else 3
                for kc in range(8):
                    for tb in range(3):
                        op("pe", lambda s=s, kc=kc, tb=tb, b0=b0: PE.matmul(PS(b0 + tb), lhsT=wt[:, kc, s * 128:(s + 1) * 128],
                                                                           rhs=srcT[:, kc, tb * 512:(tb + 1) * 512],
                                                                           start=(kc == 0), stop=(kc == 7)),
                           reads=[wb, srcB[kc][tb]], writes=[PSB[b0 + tb]], sig=(kc == 7))
                for tb in range(3):
                    v = 0 if tb == 0 else 1
                    ts = slice(tb * 512, (tb + 1) * 512)
                    if bias is not None:
                        op("act", lambda tb=tb, b0=b0, dsl=dsl: ACT.activation(out=PS(b0 + tb), in_=PS(b0 + tb), func=AF.Identity,
                                                                             bias=bias[:, dsl:dsl + 1]),
                           reads=[PSB[b0 + tb], parB], writes=[PSB[b0 + tb]])
                    if rowscale is not None:
                        op("dve", lambda tb=tb, ts=ts, b0=b0: DVE.tensor_tensor(out=PS(b0 + tb), in0=PS(b0 + tb), in1=rowscale[0][:, ts], op=ALU.mult),
                           reads=[PSB[b0 + tb], rowscale[1][tb]], writes=[PSB[b0 + tb]])
                    op("dve", lambda tb=tb, ts=ts, v=v, dsl=dsl, b0=b0: DVE.scalar_tensor_tensor(
                        out=xT[:, dsl, ts], in0=PS(b0 + tb), scalar=m[:, 16 + dsl, v:v + 1], in1=xT[:, dsl, ts],
                        op0=ALU.mult, op1=ALU.add),
                       reads=[PSB[b0 + tb], mB, xB[dsl][tb]], writes=[xB[dsl][tb]])

    def fourier_layer(l, hT, hB):
        o = l // 2
        with ExitStack() as s1:
            dftb = sbt(s1, "dftb", [128, 2, 8, 1024], BF16)
            dftB = Buf()
            for i in range(2):
                op("pool", lambda i=i: nc.gpsimd.dma_start(out=dftb[:, i], in_=dft_in[i]), writes=[dftB], dkey="dft")
            fT = sbt(s1, "fT", [128, 8, T], BF16)
            fB = BG(8, 3)
            AB = sbt(s1, "ABt", [128, 8, 2, 1024], BF16)
            ABB = BG(8)
            cdft = sbt(s1, "cdft", [128, 4, 2, 256], BF16)
            o_cc = CL["Cc"][0]
            op("pool", lambda: nc.gpsimd.dma_start(out=cdft[:], in_=con_in[:, o_cc:o_cc + 2048].rearrange("p (a k c) -> p a k c", a=4, k=2)),
               writes=[dftB], dkey="dft")
            Cc, Sc, c256, s256 = cdft[:, 0], cdft[:, 1], cdft[:, 2], cdft[:, 3]
            for (t0, ntl) in SEQS:
                for lt in range(ntl):
                    tt = t0 + lt
                    tb = tt // 4
                    tsl = slice(tt * 128, (tt + 1) * 128)
                    for ab, tab in ((0, Cc), (1, Sc)):
                        bk = [nbank(0, 6), nbank(0, 6)]
                        for g in range(4):
                            for cc in range(2):
                                op("pe", lambda g=g, cc=cc, tab=tab, bk=bk, tsl=tsl: PE.matmul(
                                    PS(bk[g // 2])[:, (g % 2) * 256:(g % 2) * 256 + 256], lhsT=hT[:, 2 * g + cc, tsl], rhs=tab[:, cc, :],
                                    start=(cc == 0), stop=(cc == 1)),
                                   reads=[hB[2 * g + cc][tb], dftB], writes=[PSB[bk[g // 2]]], sig=(cc == 1))
                        for hh in range(2):
                            eng = "act" if (hh + ab) % 2 == 0 else "dve"
                            if eng == "act":
                                op("act", lambda hh=hh, ab=ab, lt=lt, bk=bk: ACT.copy(out=AB[:, lt, ab, hh * 512:(hh + 1) * 512], in_=PS(bk[hh])),
                                   reads=[PSB[bk[hh]]], writes=[ABB[lt]])
                            else:
                                op("dve", lambda hh=hh, ab=ab, lt=lt, bk=bk: DVE.tensor_copy(out=AB[:, lt, ab, hh * 512:(hh + 1) * 512], in_=PS(bk[hh])),
                                   reads=[PSB[bk[hh]]], writes=[ABB[lt]])
                L = ntl * 128
                nkb = max(1, L // 512)
                kw = min(L, 512)
                for cs in range(8):
                    for kb in range(nkb):
                        bk = nbank(0, 6)
                        n_acc = 2 * ntl
                        i = 0
                        for lt in range(ntl):
                            for ab in range(2):
                                if ntl == 2:
                                    rhs = (c256 if ab == 0 else s256)[:, lt, :]
                                    rd = [dftB]
                                else:
                                    rhs = dftb[:, ab, lt, kb * 512:(kb + 1) * 512]
                                    rd = [dftB]
                                op("pe", lambda lt=lt, ab=ab, rhs=rhs, i=i, bk=bk, cs=cs: PE.matmul(
                                    PS(bk)[:, 0:kw], lhsT=AB[:, lt, ab, cs * 128:(cs + 1) * 128], rhs=rhs,
                                    start=(i == 0), stop=(i == n_acc - 1)),
                                   reads=[ABB[lt]] + rd, writes=[PSB[bk]], sig=(i == n_acc - 1))
                                i += 1
                        c0 = t0 * 128 + kb * 512
                        tbs = sorted(set([c0 // 512, (c0 + kw - 1) // 512]))
                        eng = "act" if (cs + kb) % 2 == 0 else "dve"
                        if eng == "act":
                            op("act", lambda bk=bk, cs=cs, c0=c0: ACT.copy(out=fT[:, cs, c0:c0 + kw], in_=PS(bk)[:, 0:kw]),
                               reads=[PSB[bk]], writes=[fB[cs][tb_] for tb_ in tbs])
                        else:
                            op("dve", lambda bk=bk, cs=cs, c0=c0: DVE.tensor_copy(out=fT[:, cs, c0:c0 + kw], in_=PS(bk)[:, 0:kw]),
                               reads=[PSB[bk]], writes=[fB[cs][tb_] for tb_ in tbs])
            outproj(l, [f"W4_{o}_0", f"W4_{o}_1"], fT, fB, bias=pv(f"b4{o}"))
            P.fence()

    def ab_layer(l, hT, hB):
        e = l // 2
        lambda_init = 0.8 - 0.6 * math.exp(-0.3 * l)
        Uf, Ub, Vf, Vb = cb("Uf"), cb("Ub"), cb("Vf"), cb("Vb")
        NEGf, NEGb = cb("NEGf"), cb("NEGb")
        with ExitStack() as s1:
            yoT = sbt(s1, "yoT", [128, 8, T], BF16)
            yoB = BG(8, 3)
            nlam = sbt(s1, "nlam", [128, 4], F32)
            nlB = Buf()
            with ExitStack() as s2:
                lt_ = sbt(s2, "lqtmp", [128, 2, 64], F32)
                ltB = Buf()
                lq = pv(f"lqk{e}").rearrange("p (a b d) -> p a b d", a=2, b=2)
                op("dve", lambda: DVE.tensor_tensor(out=lt_[:], in0=lq[:, :, 0, :], in1=lq[:, :, 1, :], op=ALU.mult), reads=[parB], writes=[ltB])
                op("dve", lambda: DVE.tensor_reduce(out=nlam[:, 0:2], in_=lt_[:], axis=mybir.AxisListType.X, op=ALU.add), reads=[ltB], writes=[nlB])
                op("act", lambda: ACT.activation(out=nlam[:, 0:2], in_=nlam[:, 0:2], func=AF.Exp), reads=[nlB], writes=[nlB])
                op("dve", lambda: DVE.tensor_tensor(out=nlam[:, 2:3], in0=nlam[:, 1:2], in1=nlam[:, 0:1], op=ALU.subtract), reads=[nlB], writes=[nlB])
                op("dve", lambda: DVE.tensor_scalar(out=nlam[:, 3:4], in0=nlam[:, 2:3], scalar1=-lambda_init, scalar2=None, op0=ALU.add), reads=[nlB], writes=[nlB])
                P.fence()

            with ExitStack() as s2:
                dtv = sbt(s2, "dtv", [128, NT, 32], F32)
                dta = sbt(s2, "dta", [128, NT, 32], F32)
                ea = sbt(s2, "ea", [128, NT, 32], F32)
                te = sbt(s2, "te", [128, NT, 32], F32)
                cdb = sbt(s2, "cdb", [128, NT, 32], F32)
                smB = BG(NT)
                aneg = sbt(s2, "aneg", [128, 32], F32)
                anB = Buf()
                dskd = sbt(s2, "dskd", [128, 16, 128], BF16)
                dskB = Buf()
                op("act", lambda: ACT.activation(out=aneg[:], in_=pv(f"alog{e}"), func=AF.Exp), reads=[parB], writes=[anB])
                for h in range(16):
                    op("dve", lambda h=h: DVE.tensor_scalar(out=dskd[:, h, :], in0=ident, scalar1=pv(f"dsk{e}")[:, h:h + 1], scalar2=None, op0=ALU.mult),
                       reads=[parB, cbfB], writes=[dskB])
                wt, wb = wget(f"dt{e}")
                bA, bB, bC = 3, 4, 5
                for tt in range(NT):
                    tb = tt // 4
                    tsl = slice(tt * 128, (tt + 1) * 128)
                    for kc in range(8):
                        op("pe", lambda: PE.matmul(PS(bA)[:, tt * 32:(tt + 1) * 32], lhsT=hT[:, kc, tsl], rhs=wt[:, kc, :], start=(kc == 0), stop=(kc == 7)),
                           reads=[wb, hB[kc][tb]], writes=[PSB[bA]], sig=(kc == 7))
                allsm = smB
                op("dve", lambda: DVE.tensor_tensor(out=dtv[:], in0=PS(bA)[:, 0:NT * 32].rearrange("p (t c) -> p t c", c=32),
                                                    in1=pv(f"dtb{e}").unsqueeze(1).to_broadcast([128, NT, 32]), op=ALU.add),
                   reads=[PSB[bA], parB], writes=allsm)
                op("act", lambda: ACT.activation(out=dtv[:], in_=dtv[:], func=AF.Exp), reads=allsm, writes=allsm)
                op("act", lambda: ACT.activation(out=dtv[:], in_=dtv[:], func=AF.Ln, bias=KC(2)), reads=allsm + [kcB_], writes=allsm)
                op("dve", lambda: DVE.scalar_tensor_tensor(out=dta[:], in0=dtv[:], scalar=-1.0, in1=aneg[:].unsqueeze(1).to_broadcast([128, NT, 32]),
                                                           op0=ALU.mult, op1=ALU.mult),
                   reads=allsm + [anB], writes=allsm)
                for tt in range(NT):
                    op("pe", lambda: PE.matmul(PS(bB)[:, tt * 32:tt * 32 + 16], lhsT=Vf_f, rhs=dta[:, tt, 0:16], start=True, stop=True),
                       reads=allsm + [cfB], writes=[PSB[bB]], sig=False)
                    op("pe", lambda: PE.matmul(PS(bB)[:, tt * 32 + 16:tt * 32 + 32], lhsT=Vb_f, rhs=dta[:, tt, 16:32], start=True, stop=True),
                       reads=allsm + [cfB], writes=[PSB[bB]], sig=False)
                    op("pe", lambda: PE.matmul(PS(bC)[:, tt * 32:(tt + 1) * 32], lhsT=ones_f, rhs=dta[:, tt, :], start=True, stop=True),
                       reads=allsm + [cfB], writes=[PSB[bC]], sig=True)
                vB = PS(bB)[:, 0:NT * 32].rearrange("p (t c) -> p t c", c=32)
                vC = PS(bC)[:, 0:NT * 32].rearrange("p (t c) -> p t c", c=32)
                op("act", lambda: ACT.activation(out=ea[:], in_=vB, func=AF.Exp), reads=[PSB[bB]], writes=allsm)
                op("act", lambda: ACT.copy(out=te[:], in_=vB), reads=[PSB[bB]], writes=allsm)
                op("act", lambda: ACT.activation(out=cdb[:], in_=vC, func=AF.Exp), reads=[PSB[bC]], writes=allsm)
                op("dve", lambda: DVE.tensor_tensor(out=te[:], in0=vC, in1=te[:], op=ALU.subtract), reads=[PSB[bC]] + allsm, writes=allsm)
                op("act", lambda: ACT.activation(out=te[:], in_=te[:], func=AF.Exp), reads=allsm, writes=allsm)
                op("dve", lambda: DVE.tensor_tensor(out=te[:], in0=te[:], in1=dtv[:], op=ALU.mult), reads=allsm, writes=allsm)

                checkpoint('dtprep')
                ssq = sbt(s2, "ssq", [128, NT, 4], F32)
                ssB = BG(NT)
                op("dve", lambda: DVE.memset(ssq[:], 0.0), writes=ssB)
                PADL = 1548
                pre = sbt(s2, "pre", [128, 2, PADL], BF16)
                preB = BG(2)
                op("dve", lambda: DVE.memset(pre[:], 0.0), writes=preB)
                post = sbt(s2, "post", [128, 4, T], BF16)
                postB = BG(4, 3)
                cwd = sbt(s2, "cwd", [128, 20, 128], BF16)
                cwB = Buf()
                xs_t = sbt(s2, "xs_t", [128, NT, 256], BF16)
                b_t = sbt(s2, "b_t", [128, NT, 128], BF16)
                zs = sbt(s2, "zs", [128, NT, 256], BF16)
                tkB = BG(NT)
                zB = BG(NT)
                xte = sbt(s2, "xte", [128, 2, 256], BF16)
                xteB = [Buf(), Buf()]
                xdt = sbt(s2, "xdt", [128, 2, 2, 256], BF16)
                xdB = BG(2, 2)
                hst = sbt(s2, "hst", [128, 2, 256], F32)
                hsB = [Buf(), Buf()]
                hbf = sbt(s2, "hbf", [128, 8, 2, 256], BF16)
                hbB = BG(8, 2)
                Ld = sbt(s2, "Ld", [128, 1, 8, 128], BF16)
                LdB = [Buf()]
                Et = sbt(s2, "Et", [128, 2, 2, 512], BF16)
                EtB = BG(2, 2)
                cbt = sbt(s2, "cbt", [128, 2, 128], F32)
                cbB = BG(2)
                Wt = sbt(s2, "Wt", [128, 2, 8, 128], BF16)
                WtB = BG(2, 2)
                ytmp = sbt(s2, "ytmp", [128, 2, 256], F32)
                ytB = [Buf(), Buf()]
                yg = sbt(s2, "yg", [128, 2, 256], BF16)
                ygB = BG(2)
                sqj = sbt(s2, "sqj", [128, 256], BF16)
                sqjB = Buf()
                for g in range(4):
                    for i in range(20):
                        ci = g * 20 + i
                        if i % 2 == 0:
                            op("dve", lambda i=i, ci=ci: DVE.tensor_scalar(out=cwd[:, i, :], in0=ident, scalar1=pv(f"cw{e}")[:, ci:ci + 1], scalar2=None, op0=ALU.mult),
                               reads=[parB, cbfB], writes=[cwB])
                        else:
                            op("act", lambda i=i, ci=ci: ACT.activation(out=cwd[:, i, :], in_=ident, func=AF.Identity, scale=pv(f"cw{e}")[:, ci:ci + 1]),
                               reads=[parB, cbfB], writes=[cwB])
                    wt1, wb1 = wget(f"A1_{e}_{g}")
                    ada_after_A1 = (l == 0)
                    def proj_s(s):
                        pp = s % 2
                        for kc in range(8):
                            for tb in range(3):
                                op("pe", lambda: PE.matmul(PS(tb), lhsT=wt1[:, kc, s * 128:(s + 1) * 128],
                                                           rhs=hT[:, kc, tb * 512:(tb + 1) * 512], start=(kc == 0), stop=(kc == 7)),
                                   reads=[wb1, hB[kc][tb]], writes=[PSB[tb]], sig=(kc == 7))
                        op("act", lambda: ACT.copy(out=pre[:, pp, 2:258], in_=PS(0)[:, 0:256]), reads=[PSB[0]], writes=[preB[pp]])
                        op("act", lambda: ACT.copy(out=pre[:, pp, 262:518], in_=PS(0)[:, 256:512]), reads=[PSB[0]], writes=[preB[pp]])
                        op("dve", lambda: DVE.tensor_copy(out=pre[:, pp, 522:1034], in_=PS(1)), reads=[PSB[1]], writes=[preB[pp]])
                        op("dve", lambda: DVE.tensor_copy(out=pre[:, pp, 1034:1546], in_=PS(2)), reads=[PSB[2]], writes=[preB[pp]])

                    def conv_s(s):
                        sl = g * 4 + s
                        pp = s % 2
                        for bi, (poff, toff, n) in enumerate(((2, 0, 256), (262, 256, 256), (522, 512, 512), (1034, 1024, 512))):
                            bk = 3 + bi % 3
                            for k in range(5):
                                op("pe", lambda: PE.matmul(PS(bk)[:, 0:n], lhsT=cwd[:, s * 5 + k, :], rhs=pre[:, pp, poff + k - 2:poff + k - 2 + n],
                                                           start=(k == 0), stop=(k == 4)),
                                   reads=[cwB, preB[pp]], writes=[PSB[bk]], sig=(k == 4))
                            tbs = sorted(set([toff // 512, (toff + n - 1) // 512]))
                            op("act", lambda: ACT.activation(out=post[:, s, toff:toff + n], in_=PS(bk)[:, 0:n], func=AF.Silu, bias=pv(f"cb{e}")[:, sl:sl + 1]),
                               reads=[PSB[bk], parB], writes=[postB[s][tb_] for tb_ in tbs])
                    proj_s(0)
                    for s in range(4):
                        if s + 1 < 4:
                            proj_s(s + 1)
                        conv_s(s)
                    for tt in range(NT):
                        tb = tt // 4
                        tsl = slice(tt * 128, (tt + 1) * 128)
                        bk = nbank(0, 6)
                        for s in range(3):
                            op("pe", lambda s=s, tsl=tsl, bk=bk: PE.matmul(PS(bk)[:, s * 128:(s + 1) * 128], lhsT=post[:, s, tsl], rhs=ident, start=True, stop=True),
                               reads=[postB[s][tb], cbfB], writes=[PSB[bk]], sig=(s == 2))
                        op("act", lambda tt=tt, bk=bk: ACT.copy(out=xs_t[:, tt, :], in_=PS(bk)[:, 0:256]), reads=[PSB[bk]], writes=[tkB[tt]])
                        op("act", lambda tt=tt, bk=bk: ACT.copy(out=b_t[:, tt, :], in_=PS(bk)[:, 256:384]), reads=[PSB[bk]], writes=[tkB[tt]])
                    if l == 0:
                        ada_tile(0, 4 + 2 * g)
                    wt2, wb2 = wget(f"A2_{e}_{g}")
                    for tt in range(NT):
                        tb = tt // 4
                        tsl = slice(tt * 128, (tt + 1) * 128)
                        bk = nbank(0, 6)
                        for kc in range(8):
                            op("pe", lambda kc=kc, tsl=tsl, bk=bk: PE.matmul(PS(bk)[:, 0:256], lhsT=hT[:, kc, tsl], rhs=wt2[:, kc, :], start=(kc == 0), stop=(kc == 7)),
                               reads=[wb2, hB[kc][tb]], writes=[PSB[bk]], sig=(kc == 7))
                        op("act", lambda tt=tt, bk=bk: ACT.activation(out=zs[:, tt, :], in_=PS(bk)[:, 0:256], func=AF.Silu), reads=[PSB[bk]], writes=[zB[tt]])
                    if l == 0:
                        ada_tile(0, 5 + 2 * g)
                        if g == 3:
                            ada_finish(0, 1)
                    checkpoint('g0conv')
                    for si, (t0, ntl) in enumerate(SEQS):
                        orders = [list(range(t0, t0 + ntl)), list(range(t0 + ntl - 1, t0 - 1, -1))]
                        for d in range(2):
                            if si < 2:
                                op("dve", lambda d=d: DVE.memset(hst[:, d, :], 0.0), writes=[hsB[d]])
                            else:
                                src = (sf_in if d == 0 else sb_in)[e][:, g * 256:(g + 1) * 256]
                                op("sp", lambda d=d, src=src: nc.sync.dma_start(out=hst[:, d, :], in_=src), writes=[hsB[d]], dkey=f"hin{d}")
                        for step in range(ntl):
                            for d in range(2):
                                tt = orders[d][step]
                                c0 = d * 16 + g * 4
                                ti = tt - t0
                                op("act", lambda ti=ti, d=d: ACT.copy(out=hbf[:, ti, d, :], in_=hst[:, d, :]), reads=[hsB[d]], writes=[hbB[ti][d]])
                                op("pool", lambda tt=tt, d=d, c0=c0: nc.gpsimd.tensor_tensor(
                                    out=xte[:, d, :].rearrange("p (r q) -> p r q", r=4),
                                    in0=xs_t[:, tt, :].rearrange("p (r q) -> p r q", r=4),
                                    in1=te[:, tt, c0:c0 + 4].unsqueeze(2).to_broadcast([128, 4, 64]), op=ALU.mult),
                                   reads=[tkB[tt], smB[tt]], writes=[xteB[d]])
                                bk = nbank(0, 6)
                                op("pe", lambda tt=tt, d=d, bk=bk: PE.matmul(PS(bk)[:, 0:256], lhsT=b_t[:, tt, :], rhs=xte[:, d, :], start=True, stop=True),
                                   reads=[tkB[tt], xteB[d]], writes=[PSB[bk]])
                                op("dve", lambda tt=tt, d=d, c0=c0: DVE.tensor_tensor(
                                    out=hst[:, d, :].rearrange("p (r q) -> p r q", r=4), in0=hst[:, d, :].rearrange("p (r q) -> p r q", r=4),
                                    in1=cdb[:, tt, c0:c0 + 4].unsqueeze(2).to_broadcast([128, 4, 64]), op=ALU.mult),
                                   reads=[hsB[d], smB[tt]], writes=[hsB[d]])
                                op("dve", lambda d=d, bk=bk: DVE.tensor_tensor(out=hst[:, d, :], in0=hst[:, d, :], in1=PS(bk)[:, 0:256], op=ALU.add),
                                   reads=[hsB[d], PSB[bk]], writes=[hsB[d]])
                        if si < 2:
                            for d in range(2):
                                dst = (nf_out if d == 0 else nb_out)[e, si][:, g * 256:(g + 1) * 256]
                                op("sp", lambda d=d, dst=dst: nc.sync.dma_start(out=dst, in_=hst[:, d, :]), reads=[hsB[d]], dkey=f"sto{d}")

                        def stageA(tt):
                            pb = tt % 2
                            tb = tt // 4
                            tsl = slice(tt * 128, (tt + 1) * 128)
                            bkc = nbank(0, 6)
                            op("pe", lambda: PE.matmul(PS(bkc)[:, 0:128], lhsT=post[:, 2, tsl], rhs=post[:, 3, tsl], start=True, stop=True),
                               reads=[postB[2][tb], postB[3][tb]], writes=[PSB[bkc]])
                            op("act", lambda: ACT.copy(out=cbt[:, pb, :], in_=PS(bkc)[:, 0:128]), reads=[PSB[bkc]], writes=[cbB[pb]])
                            for d, U in ((0, Uf), (1, Ub)):
                                c0 = d * 16 + g * 4
                                op("pool", lambda d=d, U=U, c0=c0: nc.gpsimd.tensor_tensor(
                                    out=Ld[:, 0, d * 4:(d + 1) * 4, :], in0=U.unsqueeze(1).to_broadcast([128, 4, 128]),
                                    in1=dta[:, tt, c0:c0 + 4].unsqueeze(2).to_broadcast([128, 4, 128]), op=ALU.mult),
                                   reads=[cbfB, smB[tt]], writes=[LdB[0]])
                            for d, V, NEG in ((0, Vf, NEGf), (1, Vb, NEGb)):
                                bks = nbank(0, 6)
                                for r in range(4):
                                    op("pe", lambda d=d, r=r, V=V: PE.matmul(PS(bks)[:, r * 128:(r + 1) * 128], lhsT=Ld[:, 0, d * 4 + r, :], rhs=V, start=True, stop=False),
                                       reads=[LdB[0], cbfB], writes=[PSB[bks]], sig=False)
                                    op("pe", lambda r=r, NEG=NEG: PE.matmul(PS(bks)[:, r * 128:(r + 1) * 128], lhsT=ident, rhs=NEG, start=False, stop=True),
                                       reads=[cbfB], writes=[PSB[bks]], sig=(r == 3))
                                op("act", lambda d=d: ACT.activation(out=Et[:, pb, d, :], in_=PS(bks), func=AF.Exp), reads=[PSB[bks]], writes=[EtB[pb][d]])

                        def stageA2(tt):
                            pb = tt % 2
                            for d in range(2):
                                c0 = d * 16 + g * 4
                                weng = "dve" if d == 0 else "pool"
                                wfn = DVE.tensor_tensor if d == 0 else nc.gpsimd.tensor_tensor
                                op(weng, lambda d=d, wfn=wfn: wfn(
                                    out=Wt[:, pb, d * 4:(d + 1) * 4, :], in0=Et[:, pb, d, :].rearrange("p (r i) -> p r i", r=4),
                                    in1=cbt[:, pb, :].unsqueeze(1).to_broadcast([128, 4, 128]), op=ALU.mult),
                                   reads=[EtB[pb][d], cbB[pb]], writes=[WtB[pb][d]])
                                op("dve", lambda d=d, c0=c0: DVE.tensor_tensor(
                                    out=xdt[:, pb, d, :].rearrange("p (r q) -> p r q", r=4), in0=xs_t[:, tt, :].rearrange("p (r q) -> p r q", r=4),
                                    in1=dtv[:, tt, c0:c0 + 4].unsqueeze(2).to_broadcast([128, 4, 64]), op=ALU.mult),
                                   reads=[tkB[tt], smB[tt]], writes=[xdB[pb][d]])

                        def stageB(tt):
                            pb = tt % 2
                            ti = tt - t0
                            tb = tt // 4
                            tsl = slice(tt * 128, (tt + 1) * 128)
                            bky = nbank(0, 6)
                            for r in range(4):
                                xr = xs_t[:, tt, r * 64:(r + 1) * 64]
                                yo_ = PS(bky)[:, r * 64:(r + 1) * 64]
                                op("pe", lambda: PE.matmul(yo_, lhsT=Wt[:, pb, r, :], rhs=xdt[:, pb, 0, r * 64:(r + 1) * 64], start=True, stop=False),
                                   reads=[WtB[pb][0], xdB[pb][0]], writes=[PSB[bky]], sig=False)
                                op("pe", lambda: PE.matmul(yo_, lhsT=Wt[:, pb, 4 + r, :], rhs=xdt[:, pb, 1, r * 64:(r + 1) * 64], start=False, stop=False),
                                   reads=[WtB[pb][1], xdB[pb][1]], writes=[PSB[bky]], sig=False)
                                op("pe", lambda: PE.matmul(yo_, lhsT=dskd[:, g * 4 + r, :], rhs=xr, start=False, stop=True),
                                   reads=[dskB, tkB[tt]], writes=[PSB[bky]], sig=(r == 3))
                            bko = [nbank(0, 6), nbank(0, 6)]
                            for d in range(2):
                                op("pe", lambda d=d: PE.matmul(PS(bko[d])[:, 0:256], lhsT=post[:, 3, tsl], rhs=hbf[:, ti, d, :], start=True, stop=True),
                                   reads=[postB[3][tb], hbB[ti][d]], writes=[PSB[bko[d]]])
                            for d in range(2):
                                c0 = d * 16 + g * 4
                                op("dve", lambda d=d, c0=c0: DVE.tensor_tensor(
                                    out=ytmp[:, d, :].rearrange("p (r q) -> p r q", r=4), in0=PS(bko[d])[:, 0:256].rearrange("p (r q) -> p r q", r=4),
                                    in1=ea[:, tt, c0:c0 + 4].unsqueeze(2).to_broadcast([128, 4, 64]), op=ALU.mult),
                                   reads=[PSB[bko[d]], smB[tt]], writes=[ytB[d]])
                            op("dve", lambda: DVE.tensor_tensor(out=ytmp[:, 0, :], in0=ytmp[:, 0, :], in1=ytmp[:, 1, :], op=ALU.add), reads=ytB, writes=[ytB[0]])
                            op("dve", lambda: DVE.tensor_tensor(out=ytmp[:, 0, :], in0=ytmp[:, 0, :], in1=PS(bky)[:, 0:256], op=ALU.add),
                               reads=[ytB[0], PSB[bky]], writes=[ytB[0]])
                            op("dve", lambda: DVE.tensor_tensor(out=yg[:, pb, :], in0=ytmp[:, 0, :], in1=zs[:, tt, :], op=ALU.mult),
                               reads=[ytB[0], zB[tt]], writes=[ygB[pb]])
                            op("act", lambda: ACT.activation(out=sqj[:], in_=yg[:, pb, :], func=AF.Square, accum_out=ssq[:, tt, g:g + 1]),
                               reads=[ygB[pb]], writes=[sqjB, ssB[tt]])
                            bkt = nbank(0, 6)
                            for cc in range(2):
                                op("pe", lambda cc=cc: PE.matmul(PS(bkt)[:, cc * 128:(cc + 1) * 128], lhsT=yg[:, pb, cc * 128:(cc + 1) * 128], rhs=ident, start=True, stop=True),
                                   reads=[ygB[pb], cbfB], writes=[PSB[bkt]], sig=(cc == 1))
                            for cc in range(2):
                                ck_ = 2 * g + cc
                                op("act", lambda cc=cc, ck_=ck_: ACT.activation(out=yoT[:, ck_, tsl], in_=PS(bkt)[:, cc * 128:(cc + 1) * 128], func=AF.Identity,
                                                                             scale=pv(f"ssdn{e}")[:, ck_:ck_ + 1]),
                                   reads=[PSB[bkt], parB], writes=[yoB[ck_][tb]])

                        tts = list(range(t0, t0 + ntl))
                        stageA(tts[0])
                        stageA2(tts[0])
                        for i_, tt in enumerate(tts):
                            if i_ + 1 < len(tts):
                                stageA(tts[i_ + 1])
                            stageB(tt)
                            if i_ + 1 < len(tts):
                                stageA2(tts[i_ + 1])
                    checkpoint('g0scan')
                checkpoint('ssdscan')
                rst = sbt(s2, "rst", [128, NT], F32)
                rstB = Buf()
                rsb = pre[:].rearrange("p a b -> p (a b)").bitcast(F32)
                rsbB = [preB, preB, preB]
                dg = sqj[:].bitcast(F32)
                dgB = sqjB
                op("dve", lambda: DVE.tensor_reduce(out=rst[:], in_=ssq[:], axis=mybir.AxisListType.X, op=ALU.add), reads=ssB, writes=[rstB])
                op("act", lambda: ACT.activation(out=rst[:], in_=rst[:], func=AF.Ln, scale=1.0 / 1024.0, bias=KC(1)), reads=[rstB, kcB_], writes=[rstB])
                op("act", lambda: ACT.activation(out=rst[:], in_=rst[:], func=AF.Exp, scale=-0.5), reads=[rstB], writes=[rstB])
                for tt in range(NT):
                    tb = tt // 4
                    op("dve", lambda tt=tt: DVE.tensor_scalar(out=dg, in0=ident_f, scalar1=rst[:, tt:tt + 1], scalar2=None, op0=ALU.mult),
                       reads=[rstB, cfB], writes=[dgB])
                    op("pe", lambda tt=tt: PE.matmul(PS(6)[:, 0:128], lhsT=ones_f, rhs=dg, start=True, stop=True), reads=[dgB, cfB], writes=[PSB[6]])
                    op("act", lambda tt=tt: ACT.copy(out=rsb[:, tt * 128:(tt + 1) * 128], in_=PS(6)[:, 0:128]), reads=[PSB[6]], writes=[rsbB[tb]])
                outproj(l, [f"O1_{e}_0", f"O1_{e}_1"], yoT, yoB, rowscale=(rsb, rsbB))
                P.fence()

            checkpoint('ssd')
            with ExitStack() as s2:
                rope = sbt(s2, "rope", [128, 2, 1024], F32)
                ropeB = Buf()
                for i in range(2):
                    op("sp", lambda i=i: nc.sync.dma_start(out=rope[:, i, :], in_=rope_in[i]), writes=[ropeB], dkey="rope")
                qT = sbt(s2, "qT", [128, 2, T], BF16)
                qB = BG(2, 3)
                qraw = sbt(s2, "qraw", [128, 2, 1024], BF16)
                qrB = BG(2, 2)
                rt = sbt(s2, "rt", [128, 2, 2, 512], BF16)
                rtB = BG(2, 2)
                kc32 = sbt(s2, "kc32", [128, 2, 512], F32)
                kcB = [Buf(), Buf()]
                vc32 = sbt(s2, "vc32", [128, 2, 4, 128], F32)
                vcB = [Buf(), Buf()]
                kcb = sbt(s2, "kcb", [128, 512], BF16)
                kcbB = Buf()
                va = sbt(s2, "va", [128, 16, 132], BF16)
                vaB = BG(16)
                op("dve", lambda: DVE.memset(va[:], 1.0), writes=vaB)
                kvo = sbt(s2, "kvo", [128, 2, 2, 4, 128], F32)
                kvB = BG(2, 2)
                ET = sbt(s2, "ET", [128, 2, 12, 2, 256], BF16)
                ETB = BG(2, 12)
                osb = sbt(s2, "osb", [128, 2, 128], F32)
                osB = BG(2)
                sm = sbt(s2, "sm", [128, 2, 3, 2], F32)
                smB_ = BG(2, 3)
                onb = sbt(s2, "onb", [128, 4, 128], BF16)
                onB = BG(4)
                oTs = sbt(s2, "oTs", [128, 512], F32)
                oTB = Buf()
                sqb = sbt(s2, "sqb", [128, 512], BF16)
                sqB_ = Buf()
                rsq = sbt(s2, "rsq", [128, 512], F32)
                rsqB = Buf()

                def load_cache(h):
                    pb = h % 2
                    op("sp", lambda: nc.sync.dma_start(out=kc32[:, pb, :], in_=ck_in[e, h]), writes=[kcB[pb]], dkey=f"kc{pb}")
                    src = cv_in[e].rearrange("(kt p) f -> p kt f", p=128)[:, :, h * 128:(h + 1) * 128]
                    op("sp", lambda: nc.sync.dma_start(out=vc32[:, pb], in_=src), writes=[vcB[pb]], dkey=f"vc{pb}")
                load_cache(0)
                tail_prev = [None]
                sk = list(range(0, 4)) + list(range(8, 16))
                blocks = [((0, 256), [4, 5]), ((256, 256), [6, 7]), ((512, 256), sk), ((768, 256), sk), ((1024, 256), sk), ((1280, 256), sk)]
                for h in range(8):
                    if h + 1 < 8:
                        load_cache(h + 1)
                    pb = h % 2
                    wt, wb = wget(f"QKV_{e}_{h}")
                    for qk in range(2):
                        for kc in range(8):
                            for tb in range(3):
                                op("pe", lambda: PE.matmul(PS(tb), lhsT=wt[:, kc, qk * 128:(qk + 1) * 128], rhs=hT[:, kc, tb * 512:(tb + 1) * 512],
                                                           start=(kc == 0), stop=(kc == 7)),
                                   reads=[wb, hB[kc][tb]], writes=[PSB[tb]], sig=(kc == 7))
                        op("dve", lambda: DVE.tensor_copy(out=qT[:, qk, 0:512], in_=PS(0)), reads=[PSB[0]], writes=[qB[qk][0]])
                        for sb_ in range(2):
                            op("dve", lambda: DVE.tensor_copy(out=qraw[:, qk, sb_ * 512:(sb_ + 1) * 512], in_=PS(1 + sb_)), reads=[PSB[1 + sb_]], writes=[qrB[qk][sb_]])
                    for tt in range(NT):
                        tb = tt // 4
                        tsl = slice(tt * 128, (tt + 1) * 128)
                        bk = nbank(3, 6)
                        ncol = 256 if tt < 4 else 128
                        c0 = 128 if tt < 4 else 256
                        for kc in range(8):
                            op("pe", lambda: PE.matmul(PS(bk)[:, 0:ncol], lhsT=hT[:, kc, tsl], rhs=wt[:, kc, c0:c0 + ncol], start=(kc == 0), stop=(kc == 7)),
                               reads=[wb, hB[kc][tb]], writes=[PSB[bk]], sig=(kc == 7))
                        vo = ncol - 128
                        if tt < 4:
                            op("act", lambda: ACT.copy(out=va[:, 4 + tt, 0:128], in_=PS(bk)[:, vo:vo + 128]), reads=[PSB[bk]], writes=[vaB[4 + tt]])
                            op("act", lambda: ACT.copy(out=kvo[:, pb, 0, tt, :], in_=PS(bk)[:, 0:128]), reads=[PSB[bk]], writes=[kvB[pb][0]])
                            op("act", lambda: ACT.copy(out=kvo[:, pb, 1, tt, :], in_=PS(bk)[:, 128:256]), reads=[PSB[bk]], writes=[kvB[pb][1]])
                        else:
                            op("act", lambda: ACT.copy(out=va[:, 4 + tt, 0:128], in_=PS(bk)[:, vo:vo + 128]), reads=[PSB[bk]], writes=[vaB[4 + tt]])
                    if tail_prev[0] is not None:
                        tail_prev[0]()
                        tail_prev[0] = None
                    for qk in range(2):
                        for sb_ in range(2):
                            bk = nbank(0, 3)
                            op("pe", lambda: PE.matmul(PS(bk), lhsT=cb("Psw"), rhs=qraw[:, qk, sb_ * 512:(sb_ + 1) * 512], start=True, stop=True),
                               reads=[qrB[qk][sb_], cbfB], writes=[PSB[bk]])
                            ss_ = slice(sb_ * 512, (sb_ + 1) * 512)
                            rp = (qk * 2 + sb_) % 2
                            op("pool", lambda: nc.gpsimd.tensor_tensor(out=rt[:, rp, 0, :], in0=qraw[:, qk, ss_], in1=rope[:, 0, ss_], op=ALU.mult),
                               reads=[qrB[qk][sb_], ropeB], writes=[rtB[rp][0]])
                            op("dve", lambda: DVE.tensor_tensor(out=rt[:, rp, 1, :], in0=PS(bk), in1=rope[:, 1, ss_], op=ALU.mult),
                               reads=[PSB[bk], ropeB], writes=[rtB[rp][1]])
                            op("pool", lambda: nc.gpsimd.tensor_tensor(out=qT[:, qk, 512 + sb_ * 512:1024 + sb_ * 512], in0=rt[:, rp, 0, :], in1=rt[:, rp, 1, :], op=ALU.add),
                               reads=rtB[rp], writes=[qB[qk][1 + sb_]])
                    for i in range(2):
                        dst = (nk_out if i == 0 else nv_out)[e].rearrange("(tt p) f -> p tt f", p=128)[:, :, h * 128:(h + 1) * 128]
                        op("sp", lambda: nc.sync.dma_start(out=dst, in_=kvo[:, pb, i]), reads=[kvB[pb][i]], dkey=f"kvo{pb}")
                    op("act", lambda: ACT.copy(out=kcb[:], in_=kc32[:, pb, :]), reads=[kcB[pb]], writes=[kcbB])
                    op("pool", lambda: nc.gpsimd.tensor_copy(out=va[:, 0:4, 0:128], in_=vc32[:, pb]), reads=[vcB[pb]], writes=vaB[0:4])

                    def pv_ops(bi):
                        (q0, qn), kts = blocks[bi]
                        eb = bi % 2
                        nk_ = len(kts)
                        lst = []
                        for qi in range(2):
                            for mm_ in range(2):
                                for ki, kt in enumerate(kts):
                                    def f(qi=qi, mm_=mm_, ki=ki, kt=kt):
                                        op("pe", lambda: PE.matmul(PS(4 + 2 * eb + qi)[:, mm_ * 132:mm_ * 132 + 129], lhsT=ET[:, eb, ki, mm_, qi * 128:(qi + 1) * 128],
                                                                   rhs=va[:, kt, 0:129], start=(ki == 0), stop=(ki == nk_ - 1)),
                                           reads=[ETB[eb][ki], vaB[kt]], writes=[PSB[4 + 2 * eb + qi]], sig=(mm_ == 1 and ki == nk_ - 1))
                                    lst.append(f)
                        return lst

                    def scores(bi, fill):
                        (q0, qn), kts = blocks[bi]
                        eb = bi % 2
                        qtb = q0 // 512
                        per = -(-len(fill) // len(kts)) if fill else 0
                        for ki, kt in enumerate(kts):
                            bp = nbank(0, 2)
                            for mm_ in range(2):
                                ps_ = slice(mm_ * 64, (mm_ + 1) * 64)
                                if kt < 4:
                                    lhs = kcb[ps_, kt * 128:(kt + 1) * 128]
                                    rd = [kcbB]
                                else:
                                    tk = kt - 4
                                    lhs = qT[ps_, 1, tk * 128:(tk + 1) * 128]
                                    rd = [qB[1][tk // 4]]
                                op("pe", lambda: PE.matmul(PS(2 * bp + mm_)[:, 0:qn], lhsT=lhs, rhs=qT[ps_, 0, q0:q0 + qn], start=True, stop=True),
                                   reads=rd + [qB[0][qtb]], writes=[PSB[2 * bp + mm_]], sig=(mm_ == 1))
                            op("act", lambda: ACT.activation(out=ET[:, eb, ki, :, 0:qn], in_=psum[bp][:, :, 0:qn], func=AF.Exp, scale=0.125, bias=KC(3)),
                               reads=[PSB[2 * bp], PSB[2 * bp + 1], kcB_], writes=[ETB[eb][ki]])
                            for _ in range(per):
                                if fill:
                                    fill.pop(0)()
                        while fill:
                            fill.pop(0)()

                    def chain(bi):
                        (q0, qn), kts = blocks[bi]
                        eb = bi % 2
                        t0_ = q0 // 128
                        sl4 = t0_ % 4
                        pvp = psum[2 + eb]
                        pq = [PSB[4 + 2 * eb], PSB[5 + 2 * eb]]
                        op("dve", lambda: DVE.reciprocal(out=sm[:, eb, 0, :].unsqueeze(2), in_=pvp[:, :, 128:129]), reads=pq, writes=[smB_[eb][0]])
                        op("dve", lambda: DVE.reciprocal(out=sm[:, eb, 1, :].unsqueeze(2), in_=pvp[:, :, 260:261]), reads=pq, writes=[smB_[eb][1]])
                        op("dve", lambda: DVE.tensor_scalar(out=sm[:, eb, 2, :], in0=sm[:, eb, 1, :], scalar1=nlam[:, 3:4], scalar2=None, op0=ALU.mult),
                           reads=[smB_[eb][1], nlB], writes=[smB_[eb][2]])
                        for qi in range(2):
                            bq = 4 + 2 * eb + qi
                            op("dve", lambda: DVE.tensor_scalar(out=osb[:, qi, :], in0=PS(bq)[:, 0:128], scalar1=sm[:, eb, 0, qi:qi + 1], scalar2=None, op0=ALU.mult),
                               reads=[PSB[bq], smB_[eb][0]], writes=[osB[qi]])
                            op("dve", lambda: DVE.scalar_tensor_tensor(out=onb[:, sl4 + qi, :], in0=PS(bq)[:, 132:260], scalar=sm[:, eb, 2, qi:qi + 1], in1=osb[:, qi, :],
                                                                       op0=ALU.mult, op1=ALU.add),
                               reads=[PSB[bq], smB_[eb][2], osB[qi]], writes=[onB[sl4 + qi]])

                    def norm1(tb):
                        bk = nbank(2, 4)
                        for q in range(4):
                            op("pe", lambda: PE.matmul(PS(bk)[:, q * 128:(q + 1) * 128], lhsT=onb[:, q, :], rhs=ident, start=True, stop=True),
                               reads=[onB[q], cbfB], writes=[PSB[bk]], sig=(q == 3))
                        op("dve", lambda: DVE.tensor_scalar(out=oTs[:], in0=PS(bk), scalar1=1.0, scalar2=None, op0=ALU.mult), reads=[PSB[bk]], writes=[oTB])
                        op("pool", lambda: nc.gpsimd.tensor_tensor(out=sqb[:], in0=oTs[:], in1=oTs[:], op=ALU.mult), reads=[oTB], writes=[sqB_])
                        bk2 = nbank(2, 4)
                        op("pe", lambda: PE.matmul(PS(bk2), lhsT=ones_b, rhs=sqb[:], start=True, stop=True), reads=[sqB_, cbfB], writes=[PSB[bk2]])
                        op("dve", lambda: DVE.tensor_scalar(out=rsq[:], in0=PS(bk2), scalar1=1.0 / 128.0, scalar2=None, op0=ALU.mult), reads=[PSB[bk2]], writes=[rsqB])
                        return bk2

                    def norm2(tb, bk2, h=h):
                        op("act", lambda: ACT.activation(out=rsq[:], in_=rsq[:], func=AF.Ln, bias=KC(1)), reads=[rsqB, kcB_], writes=[rsqB])
                        op("act", lambda: ACT.activation(out=rsq[:], in_=rsq[:], func=AF.Exp, scale=-0.5), reads=[rsqB], writes=[rsqB])
                        op("dve", lambda: DVE.scalar_tensor_tensor(out=yoT[:, h, tb * 512:(tb + 1) * 512], in0=oTs[:], scalar=sublnS[:, 0:1], in1=rsq[:],
                                                                   op0=ALU.mult, op1=ALU.mult),
                           reads=[oTB, rsqB, slB], writes=[yoB[h][tb]])

                    def norm1a(tb):
                        bk = nbank(2, 4)
                        for q in range(4):
                            op("pe", lambda: PE.matmul(PS(bk)[:, q * 128:(q + 1) * 128], lhsT=onb[:, q, :], rhs=ident, start=True, stop=True),
                               reads=[onB[q], cbfB], writes=[PSB[bk]], sig=(q == 3))
                        op("dve", lambda: DVE.tensor_scalar(out=oTs[:], in0=PS(bk), scalar1=1.0, scalar2=None, op0=ALU.mult), reads=[PSB[bk]], writes=[oTB])
                        op("pool", lambda: nc.gpsimd.tensor_tensor(out=sqb[:], in0=oTs[:], in1=oTs[:], op=ALU.mult), reads=[oTB], writes=[sqB_])

                    def norm1b(tb):
                        bk2 = nbank(2, 4)
                        op("pe", lambda: PE.matmul(PS(bk2), lhsT=ones_b, rhs=sqb[:], start=True, stop=True), reads=[sqB_, cbfB], writes=[PSB[bk2]])
                        op("dve", lambda: DVE.tensor_scalar(out=rsq[:], in0=PS(bk2), scalar1=1.0 / 128.0, scalar2=None, op0=ALU.mult), reads=[PSB[bk2]], writes=[rsqB])

                    scores(0, [])
                    pend = None
                    pend2 = None
                    nb_ = len(blocks)
                    for bi in range(nb_):
                        fill = pv_ops(bi)
                        if pend is not None and len(fill) >= 40:
                            tb_ = pend
                            fill.insert(8, lambda tb_=tb_: norm1a(tb_))
                            fill.insert(36, lambda tb_=tb_: norm1b(tb_))
                            pend2 = tb_
                            pend = None
                        if bi + 1 < nb_:
                            scores(bi + 1, fill)
                        else:
                            while fill:
                                fill.pop(0)()
                        if pend2 is not None:
                            norm2(pend2, None)
                            pend2 = None
                        if bi + 1 < nb_:
                            chain(bi)
                            if bi % 2 == 1:
                                pend = bi // 2
                    assert pend is None and pend2 is None

                    def tail(chain=chain, norm1=norm1, norm2=norm2, nb_=nb_):
                        chain(nb_ - 1)
                        norm2(2, norm1(2))
                    tail_prev[0] = tail
                if tail_prev[0] is not None:
                    tail_prev[0]()
                outproj(l, [f"O2_{e}_0", f"O2_{e}_1"], yoT, yoB)
                P.fence()
            P.fence()

    sublnS = sbt(es, "sublnS", [128, 1], F32)
    slB = Buf()

    try:
        if DEPTH_RUN > 0:
            for t in range(4):
                ada_tile(0, t)
            ada_finish(0, 0)
            checkpoint('ada0')
        for l in range(DEPTH_RUN):
            m = mod[l % 2]
            g = gsc[l % 2]
            with ExitStack() as sl_:
                hT = sbt(sl_, "hT", [128, 8, T], BF16)
                hB = BG(8, 3)
                modulate(sl_, hT, hB, lambda kc, v: g[:, 0, kc, v:v + 1], lambda kc, v: m[:, kc, v:v + 1], [modB[l % 2], gscB[l % 2]])
                checkpoint('mod')
                if l % 2 == 0:
                    li = 0.8 - 0.6 * math.exp(-0.3 * l)
                    op("dve", lambda: DVE.tensor_scalar(out=sublnS[:], in0=pv(f"subln{l // 2}"), scalar1=1.0 - li, scalar2=None, op0=ALU.mult), reads=[parB], writes=[slB])
                    ab_layer(l, hT, hB)
                else:
                    fourier_layer(l, hT, hB)
                P.fence()
            checkpoint('mixer')
            ffn(l)
            checkpoint('ffn')
        with ExitStack() as sl_:
            yo = sbt(sl_, "yo", [128, 8, T], F32)
            yB = BG(8, 3)
            nfs = sbt(sl_, "nfs", [128, 8], F32)
            nfB = Buf()
            op("dve", lambda: DVE.tensor_scalar(out=nfs[:], in0=pv("nfin"), scalar1=32.0, scalar2=None, op0=ALU.mult), reads=[parB], writes=[nfB])
            modulate(sl_, yo, yB, lambda kc, v: nfs[:, kc:kc + 1], lambda kc, v: None, [nfB])
            yout = yT_out.rearrange("(kc p) t -> p kc t", p=128)
            for kc in range(8):
                op("sp", lambda kc=kc: nc.sync.dma_start(out=yout[:, kc, :], in_=yo[:, kc, :]), reads=yB[kc], dkey="st2")

    except _Stop:
        pass
    for k in list(P.isdma):
        if not k.startswith('ring'):
            nc.sync.wait_ge(P.sems[k], P.cnt[k])
    assert STOP_AT is not None or st["used"] == len(plan), (st["used"], len(plan))
    es.close()
    _CACHE['trace'] = P.trace
    return nc, P.nins


_CACHE = {}


def kernel(x_prompt, x_sample, cache_k, cache_v, state_ssd_fwd, state_ssd_bwd, c, c_ctx, w_ada, b_ada, norm_mix,
           norm_ffn, w_in_ab, conv_w, conv_b, dt_bias, a_log, d_skip, ssd_norm, lambda_qk, subln, w_out_ab,
           w_four, b_four, w_ffn_in, w_ffn_out, norm_final):
    f = lambda a: np.ascontiguousarray(np.asarray(a, dtype=np.float32))
    x_prompt, x_sample, cache_k, cache_v = f(x_prompt), f(x_sample), f(cache_k), f(cache_v)
    state_ssd_fwd, state_ssd_bwd, c, c_ctx = f(state_ssd_fwd), f(state_ssd_bwd), f(c), f(c_ctx)
    if "nc" not in _CACHE:
        _CACHE["nc"] = build_program()[0]
        _CACHE["con"] = make_consts()
    nc = _CACHE["nc"]
    con, dft, rope = _CACHE["con"]
    w_inp = f(np.asarray(w_in_ab)[:, :, win_perm()])
    w_ffi = f(np.asarray(w_ffn_in)[:, :, ffi_perm()])
    shared = {"con": con, "dft": dft, "rope": rope, "w_ada": f(w_ada), "w_inp": w_inp, "w_out": f(w_out_ab),
              "w_four": f(w_four), "w_ffi": w_ffi, "w_ffo": f(w_ffn_out)}
    par0 = np.zeros((128, NPAR), np.float32)

    def put(name, a):
        o, w = PL[name]
        par0[:, o:o + w] = np.asarray(a, np.float32).reshape(128, w)
    fm = lambda v, n: np.asarray(v, np.float32).reshape(n, 128).T
    rb = lambda v: np.broadcast_to(np.asarray(v, np.float32).reshape(1, -1), (128, np.asarray(v).size))
    put("nfin", fm(norm_final, 8))
    for l in range(4):
        put(f"nm{l}", fm(norm_mix[l], 8))
        put(f"nf{l}", fm(norm_ffn[l], 8))
        put(f"bada{l}", fm(b_ada[l], 48))
    for e in range(2):
        cw = np.asarray(conv_w[e], np.float32)
        cbv = np.asarray(conv_b[e], np.float32)
        cwm = np.zeros((128, 80), np.float32)
        cbm = np.zeros((128, 16), np.float32)
        for sl in range(16):
            ch = conv_chan(sl)
            cwm[:, sl * 5:(sl + 1) * 5] = cw[:, ch].T
            cbm[:, sl] = cbv[ch]
        put(f"cw{e}", cwm)
        put(f"cb{e}", cbm)
        put(f"ssdn{e}", fm(ssd_norm[e], 8))
        put(f"subln{e}", np.asarray(subln[e], np.float32).reshape(128, 1))
        put(f"dtb{e}", rb(dt_bias[e]))
        put(f"alog{e}", rb(a_log[e]))
        put(f"dsk{e}", rb(d_skip[e]))
        put(f"lqk{e}", rb(lambda_qk[e]))
        put(f"b4{e}", fm(b_four[e], 8))
    in_maps = []
    for i in range(8):
        s = i // 2
        xt = np.concatenate([x_prompt[2 * i], x_prompt[2 * i + 1], x_sample[s]], 0)
        p = par0.copy()
        cvec = np.stack([c_ctx, c[s]], 0).reshape(2, 8, 128).transpose(2, 1, 0).reshape(128, 16)
        o, w = PL["cvec"]
        p[:, o:o + w] = cvec
        mp = dict(shared)
        mp["xT_in"] = np.ascontiguousarray(xt.T)
        mp["par"] = p
        mp["ck"] = np.ascontiguousarray(cache_k[s].reshape(2, 512, 8, 128).transpose(0, 2, 3, 1))
        mp["cv"] = np.ascontiguousarray(cache_v[s].reshape(2, 512, 1024))
        mp["sf"] = np.ascontiguousarray(state_ssd_fwd[s].reshape(2, 1024, 128).transpose(0, 2, 1))
        mp["sb"] = np.ascontiguousarray(state_ssd_bwd[s].reshape(2, 1024, 128).transpose(0, 2, 1))
        in_maps.append(mp)
    res = run_bass_kernel_spmd(nc, in_maps[:DBG_CORES], core_ids=list(range(DBG_CORES)))
    R = list(res.results) + [res.results[0]] * (8 - DBG_CORES)
    y_prompt = np.zeros((16, 256, 1024), np.float32)
    y_sample = np.zeros((4, 1024, 1024), np.float32)
    new_k = np.zeros((16, 2, 256, 8, 2, 64), np.float32)
    new_v = np.zeros((16, 2, 256, 8, 128), np.float32)
    new_f = np.zeros((16, 2, 16, 64, 128), np.float32)
    new_b = np.zeros((16, 2, 16, 64, 128), np.float32)
    for i in range(8):
        yT = np.asarray(R[i]["yT"])
        for q in range(2):
            b = 2 * i + q
            y_prompt[b] = yT[:, q * 256:(q + 1) * 256].T
            new_k[b] = np.asarray(R[i]["nk"])[:, q * 256:(q + 1) * 256].reshape(2, 256, 8, 2, 64)
            new_v[b] = np.asarray(R[i]["nv"])[:, q * 256:(q + 1) * 256].reshape(2, 256, 8, 128)
            new_f[b] = np.asarray(R[i]["nf"])[:, q].transpose(0, 2, 1).reshape(2, 16, 64, 128)
            new_b[b] = np.asarray(R[i]["nb"])[:, q].transpose(0, 2, 1).reshape(2, 16, 64, 128)
        if i % 2 == 0:
            y_sample[i // 2] = yT[:, 512:].T
    return (y_prompt, y_sample, new_k, new_v, new_f, new_b)
```

```python
import math
import numpy as np
from contextlib import ExitStack
import concourse.bass as bass
import concourse.mybir as mybir
from concourse.bass_utils import run_bass_kernel_spmd

F32 = mybir.dt.float32
BF16 = mybir.dt.bfloat16
AF = mybir.ActivationFunctionType
ALU = mybir.AluOpType

DEPTH_RUN = 4
STOP_AT = None
SKIP_KVO = False
DBG_CORES = 8


class _Stop(Exception):
    pass


_STOPPED = [False]


def checkpoint(name):
    if STOP_AT == name:
        _STOPPED[0] = True
D = 1024
T = 1536
NT = 12
EPS = 1e-6
SEQS = [(0, 2), (2, 2), (4, 8)]
FFH = 2816
NJ = 22
IN_AB = 6176
RING_SLOTS = 3
RING_ELEMS = 4096


def _layout(items):
    d, off = {}, 0
    for n, w in items:
        d[n] = (off, w)
        off += w
    return d, off


def param_layout():
    it = [("cvec", 16), ("nfin", 8)]
    for l in range(4):
        it += [(f"nm{l}", 8), (f"nf{l}", 8), (f"bada{l}", 48)]
    for e in range(2):
        it += [(f"cw{e}", 80), (f"cb{e}", 16), (f"ssdn{e}", 8), (f"subln{e}", 1), (f"dtb{e}", 32),
               (f"alog{e}", 32), (f"dsk{e}", 16), (f"lqk{e}", 256), (f"b4{e}", 8)]
    return _layout(it)


def const_layout():
    it = [("ident", 128), ("ones", 128), ("Uf", 128), ("Ub", 128), ("Vf", 128), ("Vb", 128),
          ("NEGf", 128), ("NEGb", 128), ("Psw", 128), ("Cc", 512), ("Sc", 512), ("CL256", 512), ("nSL256", 512)]
    return _layout(it)


PL, NPAR = param_layout()
CL, NCON = const_layout()


def conv_chan(sl):
    g, s = divmod(sl, 4)
    if s < 2:
        return g * 256 + s * 128 + np.arange(128)
    if s == 2:
        return 1024 + g * 128 + np.arange(128)
    return 1536 + g * 128 + np.arange(128)


def win_perm():
    o_z, o_x, o_dt, o_q, o_k, o_v = 0, 1024, 3072, 3104, 4128, 5152
    cols = list(o_dt + np.arange(32))
    for g in range(4):
        cols += list(o_x + g * 256 + np.arange(256))
        cols += list(o_x + 1024 + g * 128 + np.arange(128))
        cols += list(o_x + 1536 + g * 128 + np.arange(128))
        cols += list(o_z + g * 256 + np.arange(256))
    for h in range(8):
        cols += list(o_q + h * 128 + np.arange(128))
        cols += list(o_k + h * 128 + np.arange(128))
        cols += list(o_v + h * 128 + np.arange(128))
    return np.array(cols)


def ffi_perm():
    cols = []
    for t in range(11):
        for j in (2 * t, 2 * t + 1):
            cols += list(j * 128 + np.arange(128))
        for j in (2 * t, 2 * t + 1):
            cols += list(FFH + j * 128 + np.arange(128))
    return np.array(cols)


def make_consts():
    c = np.zeros((128, NCON), np.float32)
    t = np.arange(128)[:, None]
    j = np.arange(128)[None, :]

    def put(n, a):
        o, w = CL[n]
        c[:, o:o + w] = a.reshape(128, w)
    put("ident", (t == j).astype(np.float32))
    put("ones", np.ones((128, 128), np.float32))
    put("Uf", (t > j).astype(np.float32))
    put("Ub", (t < j).astype(np.float32))
    put("Vf", (t <= j).astype(np.float32))
    put("Vb", (t >= j).astype(np.float32))
    put("NEGf", -30000.0 * (j < t))
    put("NEGb", -30000.0 * (j > t))
    d = np.arange(128) % 64
    partner = np.where(d % 32 < 16, np.arange(128) + 16, np.arange(128) - 16)
    psw = np.zeros((128, 128), np.float32)
    psw[partner, np.arange(128)] = 1.0
    put("Psw", psw)
    n = np.arange(256)
    ang = 2 * np.pi * np.outer(n, n) / 256.0
    cc = (np.cos(ang) / 16.0).reshape(2, 128, 256).transpose(1, 0, 2)
    ss = (np.sin(ang) / 16.0).reshape(2, 128, 256).transpose(1, 0, 2)
    put("Cc", cc)
    put("Sc", ss)
    put("CL256", cc)
    put("nSL256", -ss)
    n = np.arange(1024)
    ang = 2 * np.pi * np.outer(n, n) / 1024.0
    cl = (np.cos(ang) / 32.0).reshape(8, 128, 1024).transpose(1, 0, 2)
    sl = (-np.sin(ang) / 32.0).reshape(8, 128, 1024).transpose(1, 0, 2)
    dft = np.ascontiguousarray(np.stack([cl, sl], 0).astype(np.float32))
    tok = np.arange(1024)
    row = (tok // 64).astype(np.float64)
    col = (tok % 64).astype(np.float64)
    freq = 10000.0 ** (-np.arange(16, dtype=np.float64) / 16.0)
    cosT = np.zeros((128, 1024), np.float32)
    sinT = np.zeros((128, 1024), np.float32)
    for p in range(128):
        dd = p % 64
        pos = row if dd < 32 else col
        a = pos * freq[dd % 16]
        cosT[p] = np.cos(a)
        sinT[p] = (-np.sin(a)) if (dd % 32) < 16 else np.sin(a)
    rope = np.ascontiguousarray(np.stack([cosT, sinT], 0))
    return c, dft, rope


class Buf:
    __slots__ = ("w", "r", "x", "fresh")

    def __init__(self, x=False):
        self.w = None
        self.r = {}
        self.fresh = True
        self.x = x


def BG(*shape):
    if len(shape) == 1:
        return [Buf() for _ in range(shape[0])]
    return [BG(*shape[1:]) for _ in range(shape[0])]


def flat(x):
    if isinstance(x, Buf):
        return [x]
    out = []
    for y in x:
        out += flat(y)
    return out


class Prog:
    def __init__(self, nc, es):
        self.nc = nc
        self.es = es
        self.E = {"pe": nc.tensor, "act": nc.scalar, "dve": nc.vector, "pool": nc.gpsimd, "sp": nc.sync}
        self.sems, self.cnt = {}, {}
        self.known = {e: {} for e in self.E}
        self.floor = {}
        self.isdma = set()
        for e in ("pe", "act", "dve", "pool"):
            self.sems[e] = es.enter_context(nc.semaphore("c_" + e))
            self.cnt[e] = 0
        self.nins = 0
        self.trace = {}

    def dsem(self, key):
        if key not in self.sems:
            self.sems[key] = self.es.enter_context(self.nc.semaphore("d_" + key))
            self.cnt[key] = 0
            self.isdma.add(key)
        return self.sems[key]

    def op(self, e, fn, reads=(), writes=(), sig=True, dkey=None, nofence=False):
        if _STOPPED[0]:
            return None
        reqs = {}

        def need(k, v):
            if k in self.isdma:
                v = self.cnt[k]
            elif k == e:
                if e == "pe":
                    return
            if reqs.get(k, 0) < v:
                reqs[k] = v
        reads = flat(reads)
        writes = flat(writes)
        if any(b.fresh for b in reads) or any(b.fresh for b in writes):
            for k, v in self.floor.items():
                if k != e:
                    need(k, v)
                elif e != "pe" and reqs.get(k, 0) < v:
                    reqs[k] = v
            for b in reads:
                b.fresh = False
            for b in writes:
                b.fresh = False
        for b in reads:
            if b.w is not None:
                need(*b.w)
            if b.x:
                for k, v in b.r.items():
                    if k != e:
                        need(k, v)
        for b in writes:
            if b.w is not None:
                need(*b.w)
            for k, v in b.r.items():
                need(k, v)
        kn = self.known[e]
        for k, v in reqs.items():
            if kn.get(k, 0) < v:
                self.E[e].wait_ge(self.sems[k], v)
                kn[k] = v
                self.trace.setdefault(e, []).append(("w", k, v))
        ins = fn()
        self.nins += 1
        self.trace.setdefault(e, []).append(("i", dkey if dkey is not None else (e if sig else None), 16 if dkey is not None else 1))
        if dkey is not None:
            sem = self.dsem(dkey)
            ins.then_inc(sem, 16)
            self.cnt[dkey] += 16
            stamp = (dkey, self.cnt[dkey])
        elif sig:
            ins.then_inc(self.sems[e], 1)
            self.cnt[e] += 1
            stamp = (e, self.cnt[e])
        else:
            stamp = (e, self.cnt[e] + 1)
        for b in writes:
            b.w = stamp
            b.r = {}
        k, v = stamp
        for b in reads:
            if b.r.get(k, 0) < v:
                b.r[k] = v
        return ins

    def fence(self):
        self.floor = {k: v for k, v in self.cnt.items() if not k.startswith("ring")}


def build_program():
    _STOPPED[0] = False
    nc = bass.Bass("TRN2", target_bir_lowering=False)
    dr = lambda n, s, kind="ExternalInput": nc.dram_tensor(n, s, F32, kind=kind).ap()
    xT_in = dr("xT_in", [D, T])
    par_in = dr("par", [128, NPAR])
    con_in = dr("con", [128, NCON])
    dft_in = dr("dft", [2, 128, 8, 1024])
    rope_in = dr("rope", [2, 128, 1024])
    ck_in = dr("ck", [2, 8, 128, 512])
    cv_in = dr("cv", [2, 512, 1024])
    sf_in = dr("sf", [2, 128, 1024])
    sb_in = dr("sb", [2, 128, 1024])
    w_ada = dr("w_ada", [4, D, 6144])
    w_inp = dr("w_inp", [2, D, IN_AB])
    w_out = dr("w_out", [2, 2048, D])
    w_four = dr("w_four", [2, D, D])
    w_ffi = dr("w_ffi", [4, D, 2 * FFH])
    w_ffo = dr("w_ffo", [4, FFH, D])
    yT_out = dr("yT", [D, T], "ExternalOutput")
    nk_out = dr("nk", [2, 512, 1024], "ExternalOutput")
    nv_out = dr("nv", [2, 512, 1024], "ExternalOutput")
    nf_out = dr("nf", [2, 2, 128, 1024], "ExternalOutput")
    nb_out = dr("nb", [2, 2, 128, 1024], "ExternalOutput")

    es = ExitStack()
    P = Prog(nc, es)
    op = P.op
    PE, ACT, DVE = nc.tensor, nc.scalar, nc.vector

    uid = {"n": 0}

    def sbt(stack, name, shape, dt):
        uid["n"] += 1
        return stack.enter_context(nc.sbuf_tensor(f"{name}_{uid['n']}", shape, dt))

    xT = sbt(es, "xT", [128, 8, T], F32)
    xB = BG(8, 3)
    par = sbt(es, "par_sb", [128, NPAR], F32)
    parB = Buf()
    cbf = sbt(es, "cbf", [128, 9 * 128], BF16)
    cbfB = Buf()
    cf32 = sbt(es, "cf32", [128, 4 * 128], F32)
    cfB = Buf()
    scT = sbt(es, "scT", [128, 8, 2], BF16)
    scB = Buf()
    mod = [sbt(es, f"mod{i}", [128, 48, 2], F32) for i in range(2)]
    modB = [Buf(), Buf()]
    gsc = [sbt(es, f"gsc{i}", [128, 2, 8, 2], F32) for i in range(2)]
    gscB = [Buf(), Buf()]
    ring = sbt(es, "ring", [128, RING_SLOTS, RING_ELEMS], BF16)
    ringB = BG(RING_SLOTS)
    psum = [es.enter_context(nc.psum_tensor(f"ps{i}", [128, 2, 512], F32)) for i in range(4)]
    PSB = [Buf(x=True) for _ in range(8)]

    def PS(i):
        return psum[i // 2][:, i % 2, :]

    def pv(name):
        o, w = PL[name]
        return par[:, o:o + w]

    def cb(name):
        o, w = CL[name]
        return cbf[:, o:o + w]

    plan = []

    def wtile(dram_ap, kc, ncols, tag):
        plan.append((dram_ap, kc, ncols, tag))

    for l in range(DEPTH_RUN):
        wa0 = w_ada[0].rearrange("(kc p) f -> p kc f", p=128)
        if l == 0:
            for t in range(4):
                wtile(wa0[:, :, t * 512:(t + 1) * 512], 8, 512, f"ada0_{t}")
        if l % 2 == 0:
            e = l // 2
            wi = w_inp[e].rearrange("(kc p) f -> p kc f", p=128)
            wo = w_out[e].rearrange("(kc p) f -> p kc f", p=128)
            wtile(wi[:, :, 0:32], 8, 32, f"dt{e}")
            for g in range(4):
                b0 = 32 + g * 768
                wtile(wi[:, :, b0:b0 + 512], 8, 512, f"A1_{e}_{g}")
                if l == 0:
                    t = 4 + 2 * g
                    wtile(wa0[:, :, t * 512:(t + 1) * 512], 8, 512, f"ada0_{t}")
                wtile(wi[:, :, b0 + 512:b0 + 768], 8, 256, f"A2_{e}_{g}")
                if l == 0:
                    t = 5 + 2 * g
                    wtile(wa0[:, :, t * 512:(t + 1) * 512], 8, 512, f"ada0_{t}")
            for t in range(2):
                wtile(wo[:, 0:8, t * 512:(t + 1) * 512], 8, 512, f"O1_{e}_{t}")
            for h in range(8):
                b0 = 3104 + h * 384
                wtile(wi[:, :, b0:b0 + 384], 8, 384, f"QKV_{e}_{h}")
            for t in range(2):
                wtile(wo[:, 8:16, t * 512:(t + 1) * 512], 8, 512, f"O2_{e}_{t}")
        else:
            o = l // 2
            w4 = w_four[o].rearrange("(kc p) f -> p kc f", p=128)
            for t in range(2):
                wtile(w4[:, :, t * 512:(t + 1) * 512], 8, 512, f"W4_{o}_{t}")
        wf = w_ffi[l].rearrange("(kc p) f -> p kc f", p=128)
        wa = w_ada[l + 1].rearrange("(kc p) f -> p kc f", p=128) if l + 1 < DEPTH_RUN else None
        for t in range(11):
            wtile(wf[:, :, t * 512:(t + 1) * 512], 8, 512, f"FI_{l}_{t}")
            if wa is not None:
                wtile(wa[:, :, t * 512:(t + 1) * 512], 8, 512, f"ada{l + 1}_{t}")
        if wa is not None:
            wtile(wa[:, :, 11 * 512:12 * 512], 8, 512, f"ada{l + 1}_11")
        wfo = w_ffo[l].rearrange("(kc p) f -> p kc f", p=128)
        for t in range(8):
            wtile(wfo[:, :, t * 128:(t + 1) * 128], 22, 128, f"FO_{l}_{t}")

    st = {"issued": 0, "used": 0}

    def ring_issue_upto(n):
        while st["issued"] <= n and st["issued"] < len(plan):
            m = st["issued"]
            ap, kc, ncols, tag = plan[m]
            s = m % RING_SLOTS
            dst = ring[:, s, 0:kc * ncols].rearrange("p (k c) -> p k c", k=kc)
            op("pool", lambda dst=dst, ap=ap: nc.gpsimd.dma_start(out=dst, in_=ap),
               writes=[ringB[s]], dkey=f"ring{s}", nofence=True)
            st["issued"] += 1

    def wget(tag):
        n = st["used"]
        ap, kc, ncols, tg = plan[n]
        assert tg == tag, (tg, tag)
        ring_issue_upto(n + RING_SLOTS - 1)
        st["used"] += 1
        s = n % RING_SLOTS
        return ring[:, s, 0:kc * ncols].rearrange("p (k c) -> p k c", k=kc), ringB[s]

    rot = {"i": 0}

    def nbank(lo=0, hi=6):
        b = lo + rot["i"] % (hi - lo)
        rot["i"] += 1
        return b

    op("sp", lambda: nc.sync.dma_start(out=par[:], in_=par_in[:, :]), writes=[parB], dkey="ld0")
    with ExitStack() as s0:
        con = sbt(s0, "con_sb", [128, NCON], F32)
        conB = Buf()
        op("sp", lambda: nc.sync.dma_start(out=con[:], in_=con_in[:, :]), writes=[conB], dkey="ld0")
        xin = xT_in.rearrange("(kc p) t -> p kc t", p=128)
        for kc in range(8):
            op("sp", lambda kc=kc: nc.sync.dma_start(out=xT[:, kc, :], in_=xin[:, kc, :]), writes=xB[kc], dkey="ld1")
        op("dve", lambda: DVE.tensor_copy(out=cbf[:], in_=con[:, 0:9 * 128]), reads=[conB], writes=[cbfB])
        o1 = CL["ones"][0]
        op("act", lambda: ACT.copy(out=cf32[:, 0:128], in_=con[:, o1:o1 + 128]), reads=[conB], writes=[cfB])
        o2 = CL["Vf"][0]
        op("act", lambda: ACT.copy(out=cf32[:, 128:384], in_=con[:, o2:o2 + 256]), reads=[conB], writes=[cfB])
        o3 = CL["ident"][0]
        op("act", lambda: ACT.copy(out=cf32[:, 384:512], in_=con[:, o3:o3 + 128]), reads=[conB], writes=[cfB])
        cv_ = pv("cvec").rearrange("p (k v) -> p k v", v=2)
        op("act", lambda: ACT.activation(out=scT[:], in_=cv_, func=AF.Silu), reads=[parB], writes=[scB])
        P.fence()
    kct = sbt(es, "kct", [128, 4], F32)
    kcB_ = Buf()
    op("dve", lambda: DVE.memset(kct[:, 0:1], 1024.0 * EPS), writes=[kcB_])
    op("dve", lambda: DVE.memset(kct[:, 1:2], EPS), writes=[kcB_])
    op("dve", lambda: DVE.memset(kct[:, 2:3], 1.0), writes=[kcB_])
    op("dve", lambda: DVE.memset(kct[:, 3:4], -30.0), writes=[kcB_])

    def KC(i):
        return kct[:, i:i + 1]
    ones_f = cf32[:, 0:128]
    Vf_f = cf32[:, 128:256]
    Vb_f = cf32[:, 256:384]
    ident_f = cf32[:, 384:512]
    ident = cb("ident")
    ones_b = cb("ones")

    ABANK = 7

    def ada_tile(l, t):
        wt, wb = wget(f"ada{l}_{t}")
        for s in range(4):
            j = 4 * t + s
            for kc in range(8):
                op("pe", lambda s=s, kc=kc, j=j: PE.matmul(PS(ABANK)[:, 2 * j:2 * j + 2], lhsT=wt[:, kc, s * 128:(s + 1) * 128],
                                                        rhs=scT[:, kc, :], start=(kc == 0), stop=(kc == 7)),
                   reads=[wb, scB], writes=[PSB[ABANK]], sig=(kc == 7))

    def ada_finish(l, part=None):
        m = mod[l % 2]
        mB = modB[l % 2]
        j0, j1 = {None: (0, 48), 0: (0, 16), 1: (16, 48)}[part]
        op("dve", lambda: DVE.tensor_tensor(out=m[:, j0:j1, :], in0=PS(ABANK)[:, 2 * j0:2 * j1].rearrange("p (j v) -> p j v", v=2),
                                            in1=pv(f"bada{l}")[:, j0:j1].unsqueeze(2).to_broadcast([128, j1 - j0, 2]), op=ALU.add),
           reads=[PSB[ABANK], parB], writes=[mB])
        g = gsc[l % 2]
        for wh, (nm, sc0) in enumerate(((f"nm{l}", 8), (f"nf{l}", 32))):
            if (part == 0 and wh == 1) or (part == 1 and wh == 0):
                continue
            op("dve", lambda wh=wh, nm=nm, sc0=sc0: DVE.scalar_tensor_tensor(
                out=g[:, wh], in0=m[:, sc0:sc0 + 8, :], scalar=1.0,
                in1=pv(nm).unsqueeze(2).to_broadcast([128, 8, 2]), op0=ALU.add, op1=ALU.mult),
               reads=[mB, parB], writes=[gscB[l % 2]])
            op("dve", lambda wh=wh: DVE.tensor_scalar(out=g[:, wh], in0=g[:, wh], scalar1=32.0, scalar2=None, op0=ALU.mult),
               reads=[gscB[l % 2]], writes=[gscB[l % 2]])

    def modulate(stack, dst, dstB, gs, sh, depB, out_dt_is_f32=False):
        with ExitStack() as s1:
            sq = sbt(s1, "sq", [128, 2, 8, 512], BF16)
            sqB = BG(2, 2)
            rs = sbt(s1, "rs", [128, 2, 512], F32)
            rsB = BG(2)
            tmp = sbt(s1, "mtmp", [128, 2, 8, 512], F32)
            tmpB = BG(2, 2)
            def stage1(tb):
                p = tb % 2
                bk = 6 + p
                ts = slice(tb * 512, (tb + 1) * 512)
                xb = [xB[kc][tb] for kc in range(8)]
                op("act", lambda: ACT.activation(out=sq[:, p, 0:5], in_=xT[:, 0:5, ts], func=AF.Square), reads=xb[0:5], writes=[sqB[p][0]])
                op("pool", lambda: nc.gpsimd.tensor_tensor(out=sq[:, p, 5:8], in0=xT[:, 5:8, ts], in1=xT[:, 5:8, ts], op=ALU.mult), reads=xb[5:8], writes=[sqB[p][1]])
                for kc in range(8):
                    op("pe", lambda: PE.matmul(PS(bk), lhsT=ones_b, rhs=sq[:, p, kc, :], start=(kc == 0), stop=(kc == 7)),
                       reads=[sqB[p][0 if kc < 5 else 1], cbfB], writes=[PSB[bk]], sig=(kc == 7))

            def stage2(tb):
                p = tb % 2
                bk = 6 + p
                v = 0 if tb == 0 else 1
                ts = slice(tb * 512, (tb + 1) * 512)
                xb = [xB[kc][tb] for kc in range(8)]
                op("act", lambda: ACT.activation(out=rs[:, p, :], in_=PS(bk), func=AF.Ln, bias=KC(0)), reads=[PSB[bk], kcB_], writes=[rsB[p]])
                op("act", lambda: ACT.activation(out=rs[:, p, :], in_=rs[:, p, :], func=AF.Exp, scale=-0.5), reads=[rsB[p]], writes=[rsB[p]])
                op("dve", lambda: DVE.tensor_tensor(out=tmp[:, p, 0:6], in0=xT[:, 0:6, ts], in1=rs[:, p, :].unsqueeze(1).to_broadcast([128, 6, 512]), op=ALU.mult),
                   reads=xb[0:6] + [rsB[p]], writes=[tmpB[p][0]])
                op("pool", lambda: nc.gpsimd.tensor_tensor(out=tmp[:, p, 6:8], in0=xT[:, 6:8, ts], in1=rs[:, p, :].unsqueeze(1).to_broadcast([128, 2, 512]), op=ALU.mult),
                   reads=xb[6:8] + [rsB[p]], writes=[tmpB[p][1]])
                for kc in range(8):
                    b = sh(kc, v)
                    if kc < 4:
                        if b is None:
                            op("act", lambda: ACT.activation(out=dst[:, kc, ts], in_=tmp[:, p, kc, :], func=AF.Identity, scale=gs(kc, v)),
                               reads=[tmpB[p][0 if kc < 6 else 1]] + depB, writes=[dstB[kc][tb]])
                        else:
                            op("act", lambda: ACT.activation(out=dst[:, kc, ts], in_=tmp[:, p, kc, :], func=AF.Identity, scale=gs(kc, v), bias=b),
                               reads=[tmpB[p][0 if kc < 6 else 1]] + depB, writes=[dstB[kc][tb]])
                    else:
                        if b is None:
                            op("dve", lambda: DVE.tensor_scalar(out=dst[:, kc, ts], in0=tmp[:, p, kc, :], scalar1=gs(kc, v), scalar2=None, op0=ALU.mult),
                               reads=[tmpB[p][0 if kc < 6 else 1]] + depB, writes=[dstB[kc][tb]])
                        else:
                            op("dve", lambda: DVE.tensor_scalar(out=dst[:, kc, ts], in0=tmp[:, p, kc, :], scalar1=gs(kc, v), scalar2=b, op0=ALU.mult, op1=ALU.add),
                               reads=[tmpB[p][0 if kc < 6 else 1]] + depB, writes=[dstB[kc][tb]])
            stage1(0)
            stage1(1)
            stage2(0)
            stage1(2)
            stage2(1)
            stage2(2)
            P.fence()

    def ffn(l):
        m = mod[l % 2]
        g = gsc[l % 2]
        mB = modB[l % 2]
        with ExitStack() as s1:
            h2 = sbt(s1, "h2", [128, 8, T], BF16)
            h2B = BG(8, 3)
            modulate(s1, h2, h2B, lambda kc, v: g[:, 1, kc, v:v + 1], lambda kc, v: m[:, 24 + kc, v:v + 1],
                     [mB, gscB[l % 2]])
            aT = sbt(s1, "aT", [128, NJ, T], BF16)
            aB = BG(NJ, 3)
            sg = sbt(s1, "sg", [128, 2, T], F32)
            sgB = BG(2, 3)
            for t in range(11):
                wt, wb = wget(f"FI_{l}_{t}")
                for s in range(2):
                    j = 2 * t + s
                    for half, b0 in ((0, 0), (1, 3)):
                        c0 = (half * 2 + s) * 128
                        for kc in range(8):
                            for tb in range(3):
                                op("pe", lambda c0=c0, kc=kc, tb=tb, b0=b0: PE.matmul(
                                    PS(b0 + tb), lhsT=wt[:, kc, c0:c0 + 128], rhs=h2[:, kc, tb * 512:(tb + 1) * 512],
                                    start=(kc == 0), stop=(kc == 7)),
                                   reads=[wb, h2B[kc][tb]], writes=[PSB[b0 + tb]], sig=(kc == 7))
                    for tb in range(3):
                        ts = slice(tb * 512, (tb + 1) * 512)
                        op("act", lambda tb=tb, ts=ts, s=s: ACT.activation(out=sg[:, s, ts], in_=PS(tb), func=AF.Silu),
                           reads=[PSB[tb]], writes=[sgB[s][tb]])
                        op("dve", lambda tb=tb, ts=ts, s=s, j=j: DVE.tensor_tensor(out=aT[:, j, ts], in0=sg[:, s, ts], in1=PS(3 + tb), op=ALU.mult),
                           reads=[sgB[s][tb], PSB[3 + tb]], writes=[aB[j][tb]])
                if l + 1 < DEPTH_RUN:
                    ada_tile(l + 1, t)
            if l + 1 < DEPTH_RUN:
                ada_tile(l + 1, 11)
                ada_finish(l + 1)
            for dsl in range(8):
                wt, wb = wget(f"FO_{l}_{dsl}")
                b0 = 0 if dsl % 2 == 0 else 3
                for j in range(NJ):
                    for tb in range(3):
                        op("pe", lambda j=j, tb=tb, b0=b0: PE.matmul(PS(b0 + tb), lhsT=wt[:, j, :], rhs=aT[:, j, tb * 512:(tb + 1) * 512],
                                                                    start=(j == 0), stop=(j == NJ - 1)),
                           reads=[wb, aB[j][tb]], writes=[PSB[b0 + tb]], sig=(j == NJ - 1))
                for tb in range(3):
                    v = 0 if tb == 0 else 1
                    ts = slice(tb * 512, (tb + 1) * 512)
                    op("dve", lambda tb=tb, ts=ts, v=v, dsl=dsl, b0=b0: DVE.scalar_tensor_tensor(
                        out=xT[:, dsl, ts], in0=PS(b0 + tb), scalar=m[:, 40 + dsl, v:v + 1], in1=xT[:, dsl, ts],
                        op0=ALU.mult, op1=ALU.add),
                       reads=[PSB[b0 + tb], mB, xB[dsl][tb]], writes=[xB[dsl][tb]])
            P.fence()

    def outproj(l, tags, srcT, srcB, bias=None, rowscale=None):
        m = mod[l % 2]
        mB = modB[l % 2]
        for t, tag in enumerate(tags):
            wt, wb = wget(tag)
            for s in range(4):
                dsl = 4 * t + s
                b0 = 0 if dsl % 2 == 0 else 3
                for kc in range(8):
                    for tb in range(3):
                        op("pe", lambda s=s, kc=kc, tb=tb, b0=b0: PE.matmul(PS(b0 + tb), lhsT=wt[:, kc, s * 128:(s + 1) * 128],
                                                                           rhs=srcT[:, kc, tb * 512:(tb + 1) * 512],
                                                                           start=(kc == 0), stop=(kc == 7)),
                           reads=[wb, srcB[kc][tb]], writes=[PSB[b0 + tb]], sig=(kc == 7))
                for tb in range(3):
                    v = 0 if tb == 0 else 1
                    ts = slice(tb * 512, (tb + 1) * 512)
                    if bias is not None:
                        op("act", lambda tb=tb, b0=b0, dsl=dsl: ACT.activation(out=PS(b0 + tb), in_=PS(b0 + tb), func=AF.Identity,
                                                                             bias=bias[:, dsl:dsl + 1]),
                           reads=[PSB[b0 + tb], parB], writes=[PSB[b0 + tb]])
                    if rowscale is not None:
                        op("dve", lambda tb=tb, ts=ts, b0=b0: DVE.tensor_tensor(out=PS(b0 + tb), in0=PS(b0 + tb), in1=rowscale[0][:, ts], op=ALU.mult),
                           reads=[PSB[b0 + tb], rowscale[1][tb]], writes=[PSB[b0 + tb]])
                    op("dve", lambda tb=tb, ts=ts, v=v, dsl=dsl, b0=b0: DVE.scalar_tensor_tensor(
                        out=xT[:, dsl, ts], in0=PS(b0 + tb), scalar=m[:, 16 + dsl, v:v + 1], in1=xT[:, dsl, ts],
                        op0=ALU.mult, op1=ALU.add),
                       reads=[PSB[b0 + tb], mB, xB[dsl][tb]], writes=[xB[dsl][tb]])

    def fourier_layer(l, hT, hB):
        o = l // 2
        with ExitStack() as s1:
            dftb = sbt(s1, "dftb", [128, 2, 8, 1024], BF16)
            dftB = Buf()
            for i in range(2):
                op("pool", lambda i=i: nc.gpsimd.dma_start(out=dftb[:, i], in_=dft_in[i]), writes=[dftB], dkey="dft")
            fT = sbt(s1, "fT", [128, 8, T], BF16)
            fB = BG(8, 3)
            AB = sbt(s1, "ABt", [128, 8, 2, 1024], BF16)
            ABB = BG(8)
            cdft = sbt(s1, "cdft", [128, 4, 2, 256], BF16)
            o_cc = CL["Cc"][0]
            op("pool", lambda: nc.gpsimd.dma_start(out=cdft[:], in_=con_in[:, o_cc:o_cc + 2048].rearrange("p (a k c) -> p a k c", a=4, k=2)),
               writes=[dftB], dkey="dft")
            Cc, Sc, c256, s256 = cdft[:, 0], cdft[:, 1], cdft[:, 2], cdft[:, 3]
            for (t0, ntl) in SEQS:
                for lt in range(ntl):
                    tt = t0 + lt
                    tb = tt // 4
                    tsl = slice(tt * 128, (tt + 1) * 128)
                    for ab, tab in ((0, Cc), (1, Sc)):
                        bk = [nbank(0, 6), nbank(0, 6)]
                        for g in range(4):
                            for cc in range(2):
                                op("pe", lambda g=g, cc=cc, tab=tab, bk=bk, tsl=tsl: PE.matmul(
                                    PS(bk[g // 2])[:, (g % 2) * 256:(g % 2) * 256 + 256], lhsT=hT[:, 2 * g + cc, tsl], rhs=tab[:, cc, :],
                                    start=(cc == 0), stop=(cc == 1)),
                                   reads=[hB[2 * g + cc][tb], dftB], writes=[PSB[bk[g // 2]]], sig=(cc == 1))
                        for hh in range(2):
                            eng = "act" if (hh + ab) % 2 == 0 else "dve"
                            if eng == "act":
                                op("act", lambda hh=hh, ab=ab, lt=lt, bk=bk: ACT.copy(out=AB[:, lt, ab, hh * 512:(hh + 1) * 512], in_=PS(bk[hh])),
                                   reads=[PSB[bk[hh]]], writes=[ABB[lt]])
                            else:
                                op("dve", lambda hh=hh, ab=ab, lt=lt, bk=bk: DVE.tensor_copy(out=AB[:, lt, ab, hh * 512:(hh + 1) * 512], in_=PS(bk[hh])),
                                   reads=[PSB[bk[hh]]], writes=[ABB[lt]])
                L = ntl * 128
                nkb = max(1, L // 512)
                kw = min(L, 512)
                for cs in range(8):
                    for kb in range(nkb):
                        bk = nbank(0, 6)
                        n_acc = 2 * ntl
                        i = 0
                        for lt in range(ntl):
                            for ab in range(2):
                                if ntl == 2:
                                    rhs = (c256 if ab == 0 else s256)[:, lt, :]
                                    rd = [dftB]
                                else:
                                    rhs = dftb[:, ab, lt, kb * 512:(kb + 1) * 512]
                                    rd = [dftB]
                                op("pe", lambda lt=lt, ab=ab, rhs=rhs, i=i, bk=bk, cs=cs: PE.matmul(
                                    PS(bk)[:, 0:kw], lhsT=AB[:, lt, ab, cs * 128:(cs + 1) * 128], rhs=rhs,
                                    start=(i == 0), stop=(i == n_acc - 1)),
                                   reads=[ABB[lt]] + rd, writes=[PSB[bk]], sig=(i == n_acc - 1))
                                i += 1
                        c0 = t0 * 128 + kb * 512
                        tbs = sorted(set([c0 // 512, (c0 + kw - 1) // 512]))
                        eng = "act" if (cs + kb) % 2 == 0 else "dve"
                        if eng == "act":
                            op("act", lambda bk=bk, cs=cs, c0=c0: ACT.copy(out=fT[:, cs, c0:c0 + kw], in_=PS(bk)[:, 0:kw]),
                               reads=[PSB[bk]], writes=[fB[cs][tb_] for tb_ in tbs])
                        else:
                            op("dve", lambda bk=bk, cs=cs, c0=c0: DVE.tensor_copy(out=fT[:, cs, c0:c0 + kw], in_=PS(bk)[:, 0:kw]),
                               reads=[PSB[bk]], writes=[fB[cs][tb_] for tb_ in tbs])
            outproj(l, [f"W4_{o}_0", f"W4_{o}_1"], fT, fB, bias=pv(f"b4{o}"))
            P.fence()

    def ab_layer(l, hT, hB):
        e = l // 2
        lambda_init = 0.8 - 0.6 * math.exp(-0.3 * l)
        Uf, Ub, Vf, Vb = cb("Uf"), cb("Ub"), cb("Vf"), cb("Vb")
        NEGf, NEGb = cb("NEGf"), cb("NEGb")
        with ExitStack() as s1:
            yoT = sbt(s1, "yoT", [128, 8, T], BF16)
            yoB = BG(8, 3)
            nlam = sbt(s1, "nlam", [128, 4], F32)
            nlB = Buf()
            with ExitStack() as s2:
                lt_ = sbt(s2, "lqtmp", [128, 2, 64], F32)
                ltB = Buf()
                lq = pv(f"lqk{e}").rearrange("p (a b d) -> p a b d", a=2, b=2)
                op("dve", lambda: DVE.tensor_tensor(out=lt_[:], in0=lq[:, :, 0, :], in1=lq[:, :, 1, :], op=ALU.mult), reads=[parB], writes=[ltB])
                op("dve", lambda: DVE.tensor_reduce(out=nlam[:, 0:2], in_=lt_[:], axis=mybir.AxisListType.X, op=ALU.add), reads=[ltB], writes=[nlB])
                op("act", lambda: ACT.activation(out=nlam[:, 0:2], in_=nlam[:, 0:2], func=AF.Exp), reads=[nlB], writes=[nlB])
                op("dve", lambda: DVE.tensor_tensor(out=nlam[:, 2:3], in0=nlam[:, 1:2], in1=nlam[:, 0:1], op=ALU.subtract), reads=[nlB], writes=[nlB])
                op("dve", lambda: DVE.tensor_scalar(out=nlam[:, 3:4], in0=nlam[:, 2:3], scalar1=-lambda_init, scalar2=None, op0=ALU.add), reads=[nlB], writes=[nlB])
                P.fence()

            with ExitStack() as s2:
                dtv = sbt(s2, "dtv", [128, NT, 32], F32)
                dta = sbt(s2, "dta", [128, NT, 32], F32)
                ea = sbt(s2, "ea", [128, NT, 32], F32)
                te = sbt(s2, "te", [128, NT, 32], F32)
                cdb = sbt(s2, "cdb", [128, NT, 32], F32)
                smB = BG(NT)
                aneg = sbt(s2, "aneg", [128, 32], F32)
                anB = Buf()
                dskd = sbt(s2, "dskd", [128, 16, 128], BF16)
                dskB = Buf()
                op("act", lambda: ACT.activation(out=aneg[:], in_=pv(f"alog{e}"), func=AF.Exp), reads=[parB], writes=[anB])
                for h in range(16):
                    op("dve", lambda h=h: DVE.tensor_scalar(out=dskd[:, h, :], in0=ident, scalar1=pv(f"dsk{e}")[:, h:h + 1], scalar2=None, op0=ALU.mult),
                       reads=[parB, cbfB], writes=[dskB])
                wt, wb = wget(f"dt{e}")
                bA, bB, bC = 3, 4, 5
                for tt in range(NT):
                    tb = tt // 4
                    tsl = slice(tt * 128, (tt + 1) * 128)
                    for kc in range(8):
                        op("pe", lambda: PE.matmul(PS(bA)[:, tt * 32:(tt + 1) * 32], lhsT=hT[:, kc, tsl], rhs=wt[:, kc, :], start=(kc == 0), stop=(kc == 7)),
                           reads=[wb, hB[kc][tb]], writes=[PSB[bA]], sig=(kc == 7))
                allsm = smB
                op("dve", lambda: DVE.tensor_tensor(out=dtv[:], in0=PS(bA)[:, 0:NT * 32].rearrange("p (t c) -> p t c", c=32),
                                                    in1=pv(f"dtb{e}").unsqueeze(1).to_broadcast([128, NT, 32]), op=ALU.add),
                   reads=[PSB[bA], parB], writes=allsm)
                op("act", lambda: ACT.activation(out=dtv[:], in_=dtv[:], func=AF.Exp), reads=allsm, writes=allsm)
                op("act", lambda: ACT.activation(out=dtv[:], in_=dtv[:], func=AF.Ln, bias=KC(2)), reads=allsm + [kcB_], writes=allsm)
                op("dve", lambda: DVE.scalar_tensor_tensor(out=dta[:], in0=dtv[:], scalar=-1.0, in1=aneg[:].unsqueeze(1).to_broadcast([128, NT, 32]),
                                                           op0=ALU.mult, op1=ALU.mult),
                   reads=allsm + [anB], writes=allsm)
                for tt in range(NT):
                    op("pe", lambda: PE.matmul(PS(bB)[:, tt * 32:tt * 32 + 16], lhsT=Vf_f, rhs=dta[:, tt, 0:16], start=True, stop=True),
                       reads=allsm + [cfB], writes=[PSB[bB]], sig=False)
                    op("pe", lambda: PE.matmul(PS(bB)[:, tt * 32 + 16:tt * 32 + 32], lhsT=Vb_f, rhs=dta[:, tt, 16:32], start=True, stop=True),
                       reads=allsm + [cfB], writes=[PSB[bB]], sig=False)
                    op("pe", lambda: PE.matmul(PS(bC)[:, tt * 32:(tt + 1) * 32], lhsT=ones_f, rhs=dta[:, tt, :], start=True, stop=True),
                       reads=allsm + [cfB], writes=[PSB[bC]], sig=True)
                vB = PS(bB)[:, 0:NT * 32].rearrange("p (t c) -> p t c", c=32)
                vC = PS(bC)[:, 0:NT * 32].rearrange("p (t c) -> p t c", c=32)
                op("act", lambda: ACT.activation(out=ea[:], in_=vB, func=AF.Exp), reads=[PSB[bB]], writes=allsm)
                op("act", lambda: ACT.copy(out=te[:], in_=vB), reads=[PSB[bB]], writes=allsm)
                op("act", lambda: ACT.activation(out=cdb[:], in_=vC, func=AF.Exp), reads=[PSB[bC]], writes=allsm)
                op("dve", lambda: DVE.tensor_tensor(out=te[:], in0=vC, in1=te[:], op=ALU.subtract), reads=[PSB[bC]] + allsm, writes=allsm)
                op("act", lambda: ACT.activation(out=te[:], in_=te[:], func=AF.Exp), reads=allsm, writes=allsm)
                op("dve", lambda: DVE.tensor_tensor(out=te[:], in0=te[:], in1=dtv[:], op=ALU.mult), reads=allsm, writes=allsm)

                checkpoint('dtprep')
                ssq = sbt(s2, "ssq", [128, NT, 4], F32)
                ssB = BG(NT)
                op("dve", lambda: DVE.memset(ssq[:], 0.0), writes=ssB)
                PADL = 1548
                pre = sbt(s2, "pre", [128, 2, PADL], BF16)
                preB = BG(2)
                op("dve", lambda: DVE.memset(pre[:], 0.0), writes=preB)
                post = sbt(s2, "post", [128, 4, T], BF16)
                postB = BG(4, 3)
                cwd = sbt(s2, "cwd", [128, 20, 128], BF16)
                cwB = Buf()
                xs_t = sbt(s2, "xs_t", [128, NT, 256], BF16)
                b_t = sbt(s2, "b_t", [128, NT, 128], BF16)
                zs = sbt(s2, "zs", [128, NT, 256], BF16)
                tkB = BG(NT)
                zB = BG(NT)
                xte = sbt(s2, "xte", [128, 2, 256], BF16)
                xteB = [Buf(), Buf()]
                xdt = sbt(s2, "xdt", [128, 2, 2, 256], BF16)
                xdB = BG(2, 2)
                hst = sbt(s2, "hst", [128, 2, 256], F32)
                hsB = [Buf(), Buf()]
                hbf = sbt(s2, "hbf", [128, 8, 2, 256], BF16)
                hbB = BG(8, 2)
                Ld = sbt(s2, "Ld", [128, 1, 8, 128], BF16)
                LdB = [Buf()]
                Et = sbt(s2, "Et", [128, 2, 2, 512], BF16)
                EtB = BG(2, 2)
                cbt = sbt(s2, "cbt", [128, 2, 128], F32)
                cbB = BG(2)
                Wt = sbt(s2, "Wt", [128, 2, 8, 128], BF16)
                WtB = BG(2, 2)
                ytmp = sbt(s2, "ytmp", [128, 2, 256], F32)
                ytB = [Buf(), Buf()]
                yg = sbt(s2, "yg", [128, 2, 256], BF16)
                ygB = BG(2)
                sqj = sbt(s2, "sqj", [128, 256], BF16)
                sqjB = Buf()
                for g in range(4):
                    for i in range(20):
                        ci = g * 20 + i
                        if i % 2 == 0:
                            op("dve", lambda i=i, ci=ci: DVE.tensor_scalar(out=cwd[:, i, :], in0=ident, scalar1=pv(f"cw{e}")[:, ci:ci + 1], scalar2=None, op0=ALU.mult),
                               reads=[parB, cbfB], writes=[cwB])
                        else:
                            op("act", lambda i=i, ci=ci: ACT.activation(out=cwd[:, i, :], in_=ident, func=AF.Identity, scale=pv(f"cw{e}")[:, ci:ci + 1]),
                               reads=[parB, cbfB], writes=[cwB])
                    wt1, wb1 = wget(f"A1_{e}_{g}")
                    ada_after_A1 = (l == 0)
                    def proj_s(s):
                        pp = s % 2
                        for kc in range(8):
                            for tb in range(3):
                                op("pe", lambda: PE.matmul(PS(tb), lhsT=wt1[:, kc, s * 128:(s + 1) * 128],
                                                           rhs=hT[:, kc, tb * 512:(tb + 1) * 512], start=(kc == 0), stop=(kc == 7)),
                                   reads=[wb1, hB[kc][tb]], writes=[PSB[tb]], sig=(kc == 7))
                        op("act", lambda: ACT.copy(out=pre[:, pp, 2:258], in_=PS(0)[:, 0:256]), reads=[PSB[0]], writes=[preB[pp]])
                        op("act", lambda: ACT.copy(out=pre[:, pp, 262:518], in_=PS(0)[:, 256:512]), reads=[PSB[0]], writes=[preB[pp]])
                        op("dve", lambda: DVE.tensor_copy(out=pre[:, pp, 522:1034], in_=PS(1)), reads=[PSB[1]], writes=[preB[pp]])
                        op("dve", lambda: DVE.tensor_copy(out=pre[:, pp, 1034:1546], in_=PS(2)), reads=[PSB[2]], writes=[preB[pp]])

                    def conv_s(s):
                        sl = g * 4 + s
                        pp = s % 2
                        for bi, (poff, toff, n) in enumerate(((2, 0, 256), (262, 256, 256), (522, 512, 512), (1034, 1024, 512))):
                            bk = 3 + bi % 3
                            for k in range(5):
                                op("pe", lambda: PE.matmul(PS(bk)[:, 0:n], lhsT=cwd[:, s * 5 + k, :], rhs=pre[:, pp, poff + k - 2:poff + k - 2 + n],
                                                           start=(k == 0), stop=(k == 4)),
                                   reads=[cwB, preB[pp]], writes=[PSB[bk]], sig=(k == 4))
                            tbs = sorted(set([toff // 512, (toff + n - 1) // 512]))
                            op("act", lambda: ACT.activation(out=post[:, s, toff:toff + n], in_=PS(bk)[:, 0:n], func=AF.Silu, bias=pv(f"cb{e}")[:, sl:sl + 1]),
                               reads=[PSB[bk], parB], writes=[postB[s][tb_] for tb_ in tbs])
                    proj_s(0)
                    for s in range(4):
                        if s + 1 < 4:
                            proj_s(s + 1)
                        conv_s(s)
                    for tt in range(NT):
                        tb = tt // 4
                        tsl = slice(tt * 128, (tt + 1) * 128)
                        bk = nbank(0, 6)
                        for s in range(3):
                            op("pe", lambda s=s, tsl=tsl, bk=bk: PE.matmul(PS(bk)[:, s * 128:(s + 1) * 128], lhsT=post[:, s, tsl], rhs=ident, start=True, stop=True),
                               reads=[postB[s][tb], cbfB], writes=[PSB[bk]], sig=(s == 2))
                        op("act", lambda tt=tt, bk=bk: ACT.copy(out=xs_t[:, tt, :], in_=PS(bk)[:, 0:256]), reads=[PSB[bk]], writes=[tkB[tt]])
                        op("act", lambda tt=tt, bk=bk: ACT.copy(out=b_t[:, tt, :], in_=PS(bk)[:, 256:384]), reads=[PSB[bk]], writes=[tkB[tt]])
                    if l == 0:
                        ada_tile(0, 4 + 2 * g)
                    wt2, wb2 = wget(f"A2_{e}_{g}")
                    for tt in range(NT):
                        tb = tt // 4
                        tsl = slice(tt * 128, (tt + 1) * 128)
                        bk = nbank(0, 6)
                        for kc in range(8):
                            op("pe", lambda kc=kc, tsl=tsl, bk=bk: PE.matmul(PS(bk)[:, 0:256], lhsT=hT[:, kc, tsl], rhs=wt2[:, kc, :], start=(kc == 0), stop=(kc == 7)),
                               reads=[wb2, hB[kc][tb]], writes=[PSB[bk]], sig=(kc == 7))
                        op("act", lambda tt=tt, bk=bk: ACT.activation(out=zs[:, tt, :], in_=PS(bk)[:, 0:256], func=AF.Silu), reads=[PSB[bk]], writes=[zB[tt]])
                    if l == 0:
                        ada_tile(0, 5 + 2 * g)
                        if g == 3:
                            ada_finish(0, 1)
                    checkpoint('g0conv')
                    for si, (t0, ntl) in enumerate(SEQS):
                        orders = [list(range(t0, t0 + ntl)), list(range(t0 + ntl - 1, t0 - 1, -1))]
                        for d in range(2):
                            if si < 2:
                                op("dve", lambda d=d: DVE.memset(hst[:, d, :], 0.0), writes=[hsB[d]])
                            else:
                                src = (sf_in if d == 0 else sb_in)[e][:, g * 256:(g + 1) * 256]
                                op("sp", lambda d=d, src=src: nc.sync.dma_start(out=hst[:, d, :], in_=src), writes=[hsB[d]], dkey=f"hin{d}")
                        for step in range(ntl):
                            for d in range(2):
                                tt = orders[d][step]
                                c0 = d * 16 + g * 4
                                ti = tt - t0
                                op("act", lambda ti=ti, d=d: ACT.copy(out=hbf[:, ti, d, :], in_=hst[:, d, :]), reads=[hsB[d]], writes=[hbB[ti][d]])
                                op("pool", lambda tt=tt, d=d, c0=c0: nc.gpsimd.tensor_tensor(
                                    out=xte[:, d, :].rearrange("p (r q) -> p r q", r=4),
                                    in0=xs_t[:, tt, :].rearrange("p (r q) -> p r q", r=4),
                                    in1=te[:, tt, c0:c0 + 4].unsqueeze(2).to_broadcast([128, 4, 64]), op=ALU.mult),
                                   reads=[tkB[tt], smB[tt]], writes=[xteB[d]])
                                bk = nbank(0, 6)
                                op("pe", lambda tt=tt, d=d, bk=bk: PE.matmul(PS(bk)[:, 0:256], lhsT=b_t[:, tt, :], rhs=xte[:, d, :], start=True, stop=True),
                                   reads=[tkB[tt], xteB[d]], writes=[PSB[bk]])
                                op("dve", lambda tt=tt, d=d, c0=c0: DVE.tensor_tensor(
                                    out=hst[:, d, :].rearrange("p (r q) -> p r q", r=4), in0=hst[:, d, :].rearrange("p (r q) -> p r q", r=4),
                                    in1=cdb[:, tt, c0:c0 + 4].unsqueeze(2).to_broadcast([128, 4, 64]), op=ALU.mult),
                                   reads=[hsB[d], smB[tt]], writes=[hsB[d]])
                                op("dve", lambda d=d, bk=bk: DVE.tensor_tensor(out=hst[:, d, :], in0=hst[:, d, :], in1=PS(bk)[:, 0:256], op=ALU.add),
                                   reads=[hsB[d], PSB[bk]], writes=[hsB[d]])
                        if si < 2:
                            for d in range(2):
                                dst = (nf_out if d == 0 else nb_out)[e, si][:, g * 256:(g + 1) * 256]
                                op("sp", lambda d=d, dst=dst: nc.sync.dma_start(out=dst, in_=hst[:, d, :]), reads=[hsB[d]], dkey=f"sto{d}")

                        def stageA(tt):
                            pb = tt % 2
                            tb = tt // 4
                            tsl = slice(tt * 128, (tt + 1) * 128)
                            bkc = nbank(0, 6)
                            op("pe", lambda: PE.matmul(PS(bkc)[:, 0:128], lhsT=post[:, 2, tsl], rhs=post[:, 3, tsl], start=True, stop=True),
                               reads=[postB[2][tb], postB[3][tb]], writes=[PSB[bkc]])
                            op("act", lambda: ACT.copy(out=cbt[:, pb, :], in_=PS(bkc)[:, 0:128]), reads=[PSB[bkc]], writes=[cbB[pb]])
                            for d, U in ((0, Uf), (1, Ub)):
                                c0 = d * 16 + g * 4
                                op("pool", lambda d=d, U=U, c0=c0: nc.gpsimd.tensor_tensor(
                                    out=Ld[:, 0, d * 4:(d + 1) * 4, :], in0=U.unsqueeze(1).to_broadcast([128, 4, 128]),
                                    in1=dta[:, tt, c0:c0 + 4].unsqueeze(2).to_broadcast([128, 4, 128]), op=ALU.mult),
                                   reads=[cbfB, smB[tt]], writes=[LdB[0]])
                            for d, V, NEG in ((0, Vf, NEGf), (1, Vb, NEGb)):
                                bks = nbank(0, 6)
                                for r in range(4):
                                    op("pe", lambda d=d, r=r, V=V: PE.matmul(PS(bks)[:, r * 128:(r + 1) * 128], lhsT=Ld[:, 0, d * 4 + r, :], rhs=V, start=True, stop=False),
                                       reads=[LdB[0], cbfB], writes=[PSB[bks]], sig=False)
                                    op("pe", lambda r=r, NEG=NEG: PE.matmul(PS(bks)[:, r * 128:(r + 1) * 128], lhsT=ident, rhs=NEG, start=False, stop=True),
                                       reads=[cbfB], writes=[PSB[bks]], sig=(r == 3))
                                op("act", lambda d=d: ACT.activation(out=Et[:, pb, d, :], in_=PS(bks), func=AF.Exp), reads=[PSB[bks]], writes=[EtB[pb][d]])

                        def stageA2(tt):
                            pb = tt % 2
                            for d in range(2):
                                c0 = d * 16 + g * 4
                                weng = "dve" if d == 0 else "pool"
                                wfn = DVE.tensor_tensor if d == 0 else nc.gpsimd.tensor_tensor
                                op(weng, lambda d=d, wfn=wfn: wfn(
                                    out=Wt[:, pb, d * 4:(d + 1) * 4, :], in0=Et[:, pb, d, :].rearrange("p (r i) -> p r i", r=4),
                                    in1=cbt[:, pb, :].unsqueeze(1).to_broadcast([128, 4, 128]), op=ALU.mult),
                                   reads=[EtB[pb][d], cbB[pb]], writes=[WtB[pb][d]])
                                op("dve", lambda d=d, c0=c0: DVE.tensor_tensor(
                                    out=xdt[:, pb, d, :].rearrange("p (r q) -> p r q", r=4), in0=xs_t[:, tt, :].rearrange("p (r q) -> p r q", r=4),
                                    in1=dtv[:, tt, c0:c0 + 4].unsqueeze(2).to_broadcast([128, 4, 64]), op=ALU.mult),
                                   reads=[tkB[tt], smB[tt]], writes=[xdB[pb][d]])

                        def stageB(tt):
                            pb = tt % 2
                            ti = tt - t0
                            tb = tt // 4
                            tsl = slice(tt * 128, (tt + 1) * 128)
                            bky = nbank(0, 6)
                            for r in range(4):
                                xr = xs_t[:, tt, r * 64:(r + 1) * 64]
                                yo_ = PS(bky)[:, r * 64:(r + 1) * 64]
                                op("pe", lambda: PE.matmul(yo_, lhsT=Wt[:, pb, r, :], rhs=xdt[:, pb, 0, r * 64:(r + 1) * 64], start=True, stop=False),
                                   reads=[WtB[pb][0], xdB[pb][0]], writes=[PSB[bky]], sig=False)
                                op("pe", lambda: PE.matmul(yo_, lhsT=Wt[:, pb, 4 + r, :], rhs=xdt[:, pb, 1, r * 64:(r + 1) * 64], start=False, stop=False),
                                   reads=[WtB[pb][1], xdB[pb][1]], writes=[PSB[bky]], sig=False)
                                op("pe", lambda: PE.matmul(yo_, lhsT=dskd[:, g * 4 + r, :], rhs=xr, start=False, stop=True),
                                   reads=[dskB, tkB[tt]], writes=[PSB[bky]], sig=(r == 3))
                            bko = [nbank(0, 6), nbank(0, 6)]
                            for d in range(2):
                                op("pe", lambda d=d: PE.matmul(PS(bko[d])[:, 0:256], lhsT=post[:, 3, tsl], rhs=hbf[:, ti, d, :], start=True, stop=True),
                                   reads=[postB[3][tb], hbB[ti][d]], writes=[PSB[bko[d]]])
                            for d in range(2):
                                c0 = d * 16 + g * 4
                                op("dve", lambda d=d, c0=c0: DVE.tensor_tensor(
                                    out=ytmp[:, d, :].rearrange("p (r q) -> p r q", r=4), in0=PS(bko[d])[:, 0:256].rearrange("p (r q) -> p r q", r=4),
                                    in1=ea[:, tt, c0:c0 + 4].unsqueeze(2).to_broadcast([128, 4, 64]), op=ALU.mult),
                                   reads=[PSB[bko[d]], smB[tt]], writes=[ytB[d]])
                            op("dve", lambda: DVE.tensor_tensor(out=ytmp[:, 0, :], in0=ytmp[:, 0, :], in1=ytmp[:, 1, :], op=ALU.add), reads=ytB, writes=[ytB[0]])
                            op("dve", lambda: DVE.tensor_tensor(out=ytmp[:, 0, :], in0=ytmp[:, 0, :], in1=PS(bky)[:, 0:256], op=ALU.add),
                               reads=[ytB[0], PSB[bky]], writes=[ytB[0]])
                            op("dve", lambda: DVE.tensor_tensor(out=yg[:, pb, :], in0=ytmp[:, 0, :], in1=zs[:, tt, :], op=ALU.mult),
                               reads=[ytB[0], zB[tt]], writes=[ygB[pb]])
                            op("act", lambda: ACT.activation(out=sqj[:], in_=yg[:, pb, :], func=AF.Square, accum_out=ssq[:, tt, g:g + 1]),
                               reads=[ygB[pb]], writes=[sqjB, ssB[tt]])
                            bkt = nbank(0, 6)
                            for cc in range(2):
                                op("pe", lambda cc=cc: PE.matmul(PS(bkt)[:, cc * 128:(cc + 1) * 128], lhsT=yg[:, pb, cc * 128:(cc + 1) * 128], rhs=ident, start=True, stop=True),
                                   reads=[ygB[pb], cbfB], writes=[PSB[bkt]], sig=(cc == 1))
                            for cc in range(2):
                                ck_ = 2 * g + cc
                                op("act", lambda cc=cc, ck_=ck_: ACT.activation(out=yoT[:, ck_, tsl], in_=PS(bkt)[:, cc * 128:(cc + 1) * 128], func=AF.Identity,
                                                                             scale=pv(f"ssdn{e}")[:, ck_:ck_ + 1]),
                                   reads=[PSB[bkt], parB], writes=[yoB[ck_][tb]])

                        tts = list(range(t0, t0 + ntl))
                        stageA(tts[0])
                        stageA2(tts[0])
                        for i_, tt in enumerate(tts):
                            if i_ + 1 < len(tts):
                                stageA(tts[i_ + 1])
                            stageB(tt)
                            if i_ + 1 < len(tts):
                                stageA2(tts[i_ + 1])
                    checkpoint('g0scan')
                checkpoint('ssdscan')
                rst = sbt(s2, "rst", [128, NT], F32)
                rstB = Buf()
                rsb = pre[:].rearrange("p a b -> p (a b)").bitcast(F32)
                rsbB = [preB, preB, preB]
                dg = sqj[:].bitcast(F32)
                dgB = sqjB
                op("dve", lambda: DVE.tensor_reduce(out=rst[:], in_=ssq[:], axis=mybir.AxisListType.X, op=ALU.add), reads=ssB, writes=[rstB])
                op("act", lambda: ACT.activation(out=rst[:], in_=rst[:], func=AF.Ln, scale=1.0 / 1024.0, bias=KC(1)), reads=[rstB, kcB_], writes=[rstB])
                op("act", lambda: ACT.activation(out=rst[:], in_=rst[:], func=AF.Exp, scale=-0.5), reads=[rstB], writes=[rstB])
                for tt in range(NT):
                    tb = tt // 4
                    op("dve", lambda tt=tt: DVE.tensor_scalar(out=dg, in0=ident_f, scalar1=rst[:, tt:tt + 1], scalar2=None, op0=ALU.mult),
                       reads=[rstB, cfB], writes=[dgB])
                    op("pe", lambda tt=tt: PE.matmul(PS(6)[:, 0:128], lhsT=ones_f, rhs=dg, start=True, stop=True), reads=[dgB, cfB], writes=[PSB[6]])
                    op("act", lambda tt=tt: ACT.copy(out=rsb[:, tt * 128:(tt + 1) * 128], in_=PS(6)[:, 0:128]), reads=[PSB[6]], writes=[rsbB[tb]])
                outproj(l, [f"O1_{e}_0", f"O1_{e}_1"], yoT, yoB, rowscale=(rsb, rsbB))
                P.fence()

            checkpoint('ssd')
            with ExitStack() as s2:
                rope = sbt(s2, "rope", [128, 2, 1024], F32)
                ropeB = Buf()
                for i in range(2):
                    op("sp", lambda i=i: nc.sync.dma_start(out=rope[:, i, :], in_=rope_in[i]), writes=[ropeB], dkey="rope")
                qT = sbt(s2, "qT", [128, 2, T], BF16)
                qB = BG(2, 3)
                qraw = sbt(s2, "qraw", [128, 2, 1024], BF16)
                qrB = BG(2, 2)
                rt = sbt(s2, "rt", [128, 2, 2, 512], BF16)
                rtB = BG(2, 2)
                kc32 = sbt(s2, "kc32", [128, 2, 512], F32)
                kcB = [Buf(), Buf()]
                vc32 = sbt(s2, "vc32", [128, 2, 4, 128], F32)
                vcB = [Buf(), Buf()]
                kcb = sbt(s2, "kcb", [128, 512], BF16)
                kcbB = Buf()
                va = sbt(s2, "va", [128, 16, 132], BF16)
                vaB = BG(16)
                op("dve", lambda: DVE.memset(va[:], 1.0), writes=vaB)
                kvo = sbt(s2, "kvo", [128, 2, 2, 4, 128], F32)
                kvB = BG(2, 2)
                ET = sbt(s2, "ET", [128, 2, 12, 2, 256], BF16)
                ETB = BG(2, 12)
                osb = sbt(s2, "osb", [128, 2, 128], F32)
                osB = BG(2)
                sm = sbt(s2, "sm", [128, 2, 3, 2], F32)
                smB_ = BG(2, 3)
                onb = sbt(s2, "onb", [128, 4, 128], BF16)
                onB = BG(4)
                oTs = sbt(s2, "oTs", [128, 512], F32)
                oTB = Buf()
                sqb = sbt(s2, "sqb", [128, 512], BF16)
                sqB_ = Buf()
                rsq = sbt(s2, "rsq", [128, 512], F32)
                rsqB = Buf()

                def load_cache(h):
                    pb = h % 2
                    op("sp", lambda: nc.sync.dma_start(out=kc32[:, pb, :], in_=ck_in[e, h]), writes=[kcB[pb]], dkey=f"kc{pb}")
                    src = cv_in[e].rearrange("(kt p) f -> p kt f", p=128)[:, :, h * 128:(h + 1) * 128]
                    op("sp", lambda: nc.sync.dma_start(out=vc32[:, pb], in_=src), writes=[vcB[pb]], dkey=f"vc{pb}")
                load_cache(0)
                tail_prev = [None]
                sk = list(range(0, 4)) + list(range(8, 16))
                blocks = [((0, 256), [4, 5]), ((256, 256), [6, 7]), ((512, 256), sk), ((768, 256), sk), ((1024, 256), sk), ((1280, 256), sk)]
                for h in range(8):
                    if h + 1 < 8:
                        load_cache(h + 1)
                    pb = h % 2
                    wt, wb = wget(f"QKV_{e}_{h}")
                    for qk in range(2):
                        for kc in range(8):
                            for tb in range(3):
                                op("pe", lambda: PE.matmul(PS(tb), lhsT=wt[:, kc, qk * 128:(qk + 1) * 128], rhs=hT[:, kc, tb * 512:(tb + 1) * 512],
                                                           start=(kc == 0), stop=(kc == 7)),
                                   reads=[wb, hB[kc][tb]], writes=[PSB[tb]], sig=(kc == 7))
                        op("dve", lambda: DVE.tensor_copy(out=qT[:, qk, 0:512], in_=PS(0)), reads=[PSB[0]], writes=[qB[qk][0]])
                        for sb_ in range(2):
                            op("dve", lambda: DVE.tensor_copy(out=qraw[:, qk, sb_ * 512:(sb_ + 1) * 512], in_=PS(1 + sb_)), reads=[PSB[1 + sb_]], writes=[qrB[qk][sb_]])
                    if tail_prev[0] is not None:
                        tail_prev[0]["c"]()
                    for tt in range(NT):
                        tb = tt // 4
                        tsl = slice(tt * 128, (tt + 1) * 128)
                        bk = nbank(3, 6)
                        ncol = 256 if tt < 4 else 128
                        c0 = 128 if tt < 4 else 256
                        for kc in range(8):
                            op("pe", lambda: PE.matmul(PS(bk)[:, 0:ncol], lhsT=hT[:, kc, tsl], rhs=wt[:, kc, c0:c0 + ncol], start=(kc == 0), stop=(kc == 7)),
                               reads=[wb, hB[kc][tb]], writes=[PSB[bk]], sig=(kc == 7))
                        vo = ncol - 128
                        if tt < 4:
                            op("act", lambda: ACT.copy(out=va[:, 4 + tt, 0:128], in_=PS(bk)[:, vo:vo + 128]), reads=[PSB[bk]], writes=[vaB[4 + tt]])
                            op("act", lambda: ACT.copy(out=kvo[:, pb, 0, tt, :], in_=PS(bk)[:, 0:128]), reads=[PSB[bk]], writes=[kvB[pb][0]])
                            op("act", lambda: ACT.copy(out=kvo[:, pb, 1, tt, :], in_=PS(bk)[:, 128:256]), reads=[PSB[bk]], writes=[kvB[pb][1]])
                        else:
                            op("act", lambda: ACT.copy(out=va[:, 4 + tt, 0:128], in_=PS(bk)[:, vo:vo + 128]), reads=[PSB[bk]], writes=[vaB[4 + tt]])
                        if tail_prev[0] is not None and tt == 5:
                            tail_prev[0]["a"]()
                        if tail_prev[0] is not None and tt == 11:
                            tail_prev[0]["b"]()
                    if tail_prev[0] is not None:
                        tail_prev[0]["n"]()
                        tail_prev[0] = None
                    for qk in range(2):
                        for sb_ in range(2):
                            bk = nbank(0, 3)
                            op("pe", lambda: PE.matmul(PS(bk), lhsT=cb("Psw"), rhs=qraw[:, qk, sb_ * 512:(sb_ + 1) * 512], start=True, stop=True),
                               reads=[qrB[qk][sb_], cbfB], writes=[PSB[bk]])
                            ss_ = slice(sb_ * 512, (sb_ + 1) * 512)
                            rp = (qk * 2 + sb_) % 2
                            op("pool", lambda: nc.gpsimd.tensor_tensor(out=rt[:, rp, 0, :], in0=qraw[:, qk, ss_], in1=rope[:, 0, ss_], op=ALU.mult),
                               reads=[qrB[qk][sb_], ropeB], writes=[rtB[rp][0]])
                            op("dve", lambda: DVE.tensor_tensor(out=rt[:, rp, 1, :], in0=PS(bk), in1=rope[:, 1, ss_], op=ALU.mult),
                               reads=[PSB[bk], ropeB], writes=[rtB[rp][1]])
                            op("pool", lambda: nc.gpsimd.tensor_tensor(out=qT[:, qk, 512 + sb_ * 512:1024 + sb_ * 512], in0=rt[:, rp, 0, :], in1=rt[:, rp, 1, :], op=ALU.add),
                               reads=rtB[rp], writes=[qB[qk][1 + sb_]])
                    for i in range(2):
                        dst = (nk_out if i == 0 else nv_out)[e].rearrange("(tt p) f -> p tt f", p=128)[:, :, h * 128:(h + 1) * 128]
                        op("sp", lambda: nc.sync.dma_start(out=dst, in_=kvo[:, pb, i]), reads=[kvB[pb][i]], dkey=f"kvo{pb}")
                    op("act", lambda: ACT.copy(out=kcb[:], in_=kc32[:, pb, :]), reads=[kcB[pb]], writes=[kcbB])
                    op("pool", lambda: nc.gpsimd.tensor_copy(out=va[:, 0:4, 0:128], in_=vc32[:, pb]), reads=[vcB[pb]], writes=vaB[0:4])

                    def pv_ops(bi):
                        (q0, qn), kts = blocks[bi]
                        eb = bi % 2
                        nk_ = len(kts)
                        lst = []
                        for qi in range(2):
                            for mm_ in range(2):
                                for ki, kt in enumerate(kts):
                                    def f(qi=qi, mm_=mm_, ki=ki, kt=kt):
                                        op("pe", lambda: PE.matmul(PS(4 + 2 * eb + qi)[:, mm_ * 132:mm_ * 132 + 129], lhsT=ET[:, eb, ki, mm_, qi * 128:(qi + 1) * 128],
                                                                   rhs=va[:, kt, 0:129], start=(ki == 0), stop=(ki == nk_ - 1)),
                                           reads=[ETB[eb][ki], vaB[kt]], writes=[PSB[4 + 2 * eb + qi]], sig=(mm_ == 1 and ki == nk_ - 1))
                                    lst.append(f)
                        return lst

                    def scores(bi, fill):
                        (q0, qn), kts = blocks[bi]
                        eb = bi % 2
                        qtb = q0 // 512
                        per = -(-len(fill) // len(kts)) if fill else 0
                        for ki, kt in enumerate(kts):
                            bp = nbank(0, 2)
                            for mm_ in range(2):
                                ps_ = slice(mm_ * 64, (mm_ + 1) * 64)
                                if kt < 4:
                                    lhs = kcb[ps_, kt * 128:(kt + 1) * 128]
                                    rd = [kcbB]
                                else:
                                    tk = kt - 4
                                    lhs = qT[ps_, 1, tk * 128:(tk + 1) * 128]
                                    rd = [qB[1][tk // 4]]
                                op("pe", lambda: PE.matmul(PS(2 * bp + mm_)[:, 0:qn], lhsT=lhs, rhs=qT[ps_, 0, q0:q0 + qn], start=True, stop=True),
                                   reads=rd + [qB[0][qtb]], writes=[PSB[2 * bp + mm_]], sig=(mm_ == 1))
                            op("act", lambda: ACT.activation(out=ET[:, eb, ki, :, 0:qn], in_=psum[bp][:, :, 0:qn], func=AF.Exp, scale=0.125, bias=KC(3)),
                               reads=[PSB[2 * bp], PSB[2 * bp + 1], kcB_], writes=[ETB[eb][ki]])
                            for _ in range(per):
                                if fill:
                                    fill.pop(0)()
                        while fill:
                            fill.pop(0)()

                    def chain(bi):
                        (q0, qn), kts = blocks[bi]
                        eb = bi % 2
                        t0_ = q0 // 128
                        sl4 = t0_ % 4
                        pvp = psum[2 + eb]
                        pq = [PSB[4 + 2 * eb], PSB[5 + 2 * eb]]
                        op("dve", lambda: DVE.reciprocal(out=sm[:, eb, 0, :].unsqueeze(2), in_=pvp[:, :, 128:129]), reads=pq, writes=[smB_[eb][0]])
                        op("dve", lambda: DVE.reciprocal(out=sm[:, eb, 1, :].unsqueeze(2), in_=pvp[:, :, 260:261]), reads=pq, writes=[smB_[eb][1]])
                        op("dve", lambda: DVE.tensor_scalar(out=sm[:, eb, 2, :], in0=sm[:, eb, 1, :], scalar1=nlam[:, 3:4], scalar2=None, op0=ALU.mult),
                           reads=[smB_[eb][1], nlB], writes=[smB_[eb][2]])
                        for qi in range(2):
                            bq = 4 + 2 * eb + qi
                            op("dve", lambda: DVE.tensor_scalar(out=osb[:, qi, :], in0=PS(bq)[:, 0:128], scalar1=sm[:, eb, 0, qi:qi + 1], scalar2=None, op0=ALU.mult),
                               reads=[PSB[bq], smB_[eb][0]], writes=[osB[qi]])
                            op("dve", lambda: DVE.scalar_tensor_tensor(out=onb[:, sl4 + qi, :], in0=PS(bq)[:, 132:260], scalar=sm[:, eb, 2, qi:qi + 1], in1=osb[:, qi, :],
                                                                       op0=ALU.mult, op1=ALU.add),
                               reads=[PSB[bq], smB_[eb][2], osB[qi]], writes=[onB[sl4 + qi]])

                    def norm1(tb):
                        bk = nbank(2, 4)
                        for q in range(4):
                            op("pe", lambda: PE.matmul(PS(bk)[:, q * 128:(q + 1) * 128], lhsT=onb[:, q, :], rhs=ident, start=True, stop=True),
                               reads=[onB[q], cbfB], writes=[PSB[bk]], sig=(q == 3))
                        op("dve", lambda: DVE.tensor_scalar(out=oTs[:], in0=PS(bk), scalar1=1.0, scalar2=None, op0=ALU.mult), reads=[PSB[bk]], writes=[oTB])
                        op("pool", lambda: nc.gpsimd.tensor_tensor(out=sqb[:], in0=oTs[:], in1=oTs[:], op=ALU.mult), reads=[oTB], writes=[sqB_])
                        bk2 = nbank(2, 4)
                        op("pe", lambda: PE.matmul(PS(bk2), lhsT=ones_b, rhs=sqb[:], start=True, stop=True), reads=[sqB_, cbfB], writes=[PSB[bk2]])
                        op("dve", lambda: DVE.tensor_scalar(out=rsq[:], in0=PS(bk2), scalar1=1.0 / 128.0, scalar2=None, op0=ALU.mult), reads=[PSB[bk2]], writes=[rsqB])
                        return bk2

                    def norm2(tb, bk2, h=h):
                        op("act", lambda: ACT.activation(out=rsq[:], in_=rsq[:], func=AF.Ln, bias=KC(1)), reads=[rsqB, kcB_], writes=[rsqB])
                        op("act", lambda: ACT.activation(out=rsq[:], in_=rsq[:], func=AF.Exp, scale=-0.5), reads=[rsqB], writes=[rsqB])
                        op("dve", lambda: DVE.scalar_tensor_tensor(out=yoT[:, h, tb * 512:(tb + 1) * 512], in0=oTs[:], scalar=sublnS[:, 0:1], in1=rsq[:],
                                                                   op0=ALU.mult, op1=ALU.mult),
                           reads=[oTB, rsqB, slB], writes=[yoB[h][tb]])

                    def norm1a(tb):
                        bk = nbank(2, 4)
                        for q in range(4):
                            op("pe", lambda: PE.matmul(PS(bk)[:, q * 128:(q + 1) * 128], lhsT=onb[:, q, :], rhs=ident, start=True, stop=True),
                               reads=[onB[q], cbfB], writes=[PSB[bk]], sig=(q == 3))
                        op("dve", lambda: DVE.tensor_scalar(out=oTs[:], in0=PS(bk), scalar1=1.0, scalar2=None, op0=ALU.mult), reads=[PSB[bk]], writes=[oTB])
                        op("pool", lambda: nc.gpsimd.tensor_tensor(out=sqb[:], in0=oTs[:], in1=oTs[:], op=ALU.mult), reads=[oTB], writes=[sqB_])

                    def norm1b(tb):
                        bk2 = nbank(2, 4)
                        op("pe", lambda: PE.matmul(PS(bk2), lhsT=ones_b, rhs=sqb[:], start=True, stop=True), reads=[sqB_, cbfB], writes=[PSB[bk2]])
                        op("dve", lambda: DVE.tensor_scalar(out=rsq[:], in0=PS(bk2), scalar1=1.0 / 128.0, scalar2=None, op0=ALU.mult), reads=[PSB[bk2]], writes=[rsqB])

                    scores(0, [])
                    pend = None
                    pend2 = None
                    nb_ = len(blocks)
                    for bi in range(nb_):
                        fill = pv_ops(bi)
                        if pend is not None and len(fill) >= 40:
                            tb_ = pend
                            fill.insert(8, lambda tb_=tb_: norm1a(tb_))
                            fill.insert(36, lambda tb_=tb_: norm1b(tb_))
                            pend2 = tb_
                            pend = None
                        if bi + 1 < nb_:
                            scores(bi + 1, fill)
                        else:
                            while fill:
                                fill.pop(0)()
                        if pend2 is not None:
                            norm2(pend2, None)
                            pend2 = None
                        if bi + 1 < nb_:
                            chain(bi)
                            if bi % 2 == 1:
                                pend = bi // 2
                    assert pend is None and pend2 is None

                    tail_prev[0] = {"c": (lambda chain=chain, nb_=nb_: chain(nb_ - 1)), "a": (lambda f=norm1a: f(2)),
                                    "b": (lambda f=norm1b: f(2)), "n": (lambda f=norm2: f(2, None))}
                if tail_prev[0] is not None:
                    for k_ in ("c", "a", "b", "n"):
                        tail_prev[0][k_]()
                outproj(l, [f"O2_{e}_0", f"O2_{e}_1"], yoT, yoB)
                P.fence()
            P.fence()

    sublnS = sbt(es, "sublnS", [128, 1], F32)
    slB = Buf()

    try:
        if DEPTH_RUN > 0:
            for t in range(4):
                ada_tile(0, t)
            ada_finish(0, 0)
            checkpoint('ada0')
        for l in range(DEPTH_RUN):
            m = mod[l % 2]
            g = gsc[l % 2]
            with ExitStack() as sl_:
                hT = sbt(sl_, "hT", [128, 8, T], BF16)
                hB = BG(8, 3)
                modulate(sl_, hT, hB, lambda kc, v: g[:, 0, kc, v:v + 1], lambda kc, v: m[:, kc, v:v + 1], [modB[l % 2], gscB[l % 2]])
                checkpoint('mod')
                if l % 2 == 0:
                    li = 0.8 - 0.6 * math.exp(-0.3 * l)
                    op("dve", lambda: DVE.tensor_scalar(out=sublnS[:], in0=pv(f"subln{l // 2}"), scalar1=1.0 - li, scalar2=None, op0=ALU.mult), reads=[parB], writes=[slB])
                    ab_layer(l, hT, hB)
                else:
                    fourier_layer(l, hT, hB)
                P.fence()
            checkpoint('mixer')
            ffn(l)
            checkpoint('ffn')
        with ExitStack() as sl_:
            yo = sbt(sl_, "yo", [128, 8, T], F32)
            yB = BG(8, 3)
            nfs = sbt(sl_, "nfs", [128, 8], F32)
            nfB = Buf()
            op("dve", lambda: DVE.tensor_scalar(out=nfs[:], in0=pv("nfin"), scalar1=32.0, scalar2=None, op0=ALU.mult), reads=[parB], writes=[nfB])
            modulate(sl_, yo, yB, lambda kc, v: nfs[:, kc:kc + 1], lambda kc, v: None, [nfB])
            yout = yT_out.rearrange("(kc p) t -> p kc t", p=128)
            for kc in range(8):
                op("sp", lambda kc=kc: nc.sync.dma_start(out=yout[:, kc, :], in_=yo[:, kc, :]), reads=yB[kc], dkey="st2")

    except _Stop:
        pass
    for k in list(P.isdma):
        if not k.startswith('ring'):
            nc.sync.wait_ge(P.sems[k], P.cnt[k])
    assert STOP_AT is not None or st["used"] == len(plan), (st["used"], len(plan))
    es.close()
    _CACHE['trace'] = P.trace
    return nc, P.nins


_CACHE = {}


def kernel(x_prompt, x_sample, cache_k, cache_v, state_ssd_fwd, state_ssd_bwd, c, c_ctx, w_ada, b_ada, norm_mix,
           norm_ffn, w_in_ab, conv_w, conv_b, dt_bias, a_log, d_skip, ssd_norm, lambda_qk, subln, w_out_ab,
           w_four, b_four, w_ffn_in, w_ffn_out, norm_final):
    f = lambda a: np.ascontiguousarray(np.asarray(a, dtype=np.float32))
    x_prompt, x_sample, cache_k, cache_v = f(x_prompt), f(x_sample), f(cache_k), f(cache_v)
    state_ssd_fwd, state_ssd_bwd, c, c_ctx = f(state_ssd_fwd), f(state_ssd_bwd), f(c), f(c_ctx)
    if "nc" not in _CACHE:
        _CACHE["nc"] = build_program()[0]
        _CACHE["con"] = make_consts()
    nc = _CACHE["nc"]
    con, dft, rope = _CACHE["con"]
    w_inp = f(np.asarray(w_in_ab)[:, :, win_perm()])
    w_ffi = f(np.asarray(w_ffn_in)[:, :, ffi_perm()])
    shared = {"con": con, "dft": dft, "rope": rope, "w_ada": f(w_ada), "w_inp": w_inp, "w_out": f(w_out_ab),
              "w_four": f(w_four), "w_ffi": w_ffi, "w_ffo": f(w_ffn_out)}
    par0 = np.zeros((128, NPAR), np.float32)

    def put(name, a):
        o, w = PL[name]
        par0[:, o:o + w] = np.asarray(a, np.float32).reshape(128, w)
    fm = lambda v, n: np.asarray(v, np.float32).reshape(n, 128).T
    rb = lambda v: np.broadcast_to(np.asarray(v, np.float32).reshape(1, -1), (128, np.asarray(v).size))
    put("nfin", fm(norm_final, 8))
    for l in range(4):
        put(f"nm{l}", fm(norm_mix[l], 8))
        put(f"nf{l}", fm(norm_ffn[l], 8))
        put(f"bada{l}", fm(b_ada[l], 48))
    for e in range(2):
        cw = np.asarray(conv_w[e], np.float32)
        cbv = np.asarray(conv_b[e], np.float32)
        cwm = np.zeros((128, 80), np.float32)
        cbm = np.zeros((128, 16), np.float32)
        for sl in range(16):
            ch = conv_chan(sl)
            cwm[:, sl * 5:(sl + 1) * 5] = cw[:, ch].T
            cbm[:, sl] = cbv[ch]
        put(f"cw{e}", cwm)
        put(f"cb{e}", cbm)
        put(f"ssdn{e}", fm(ssd_norm[e], 8))
        put(f"subln{e}", np.asarray(subln[e], np.float32).reshape(128, 1))
        put(f"dtb{e}", rb(dt_bias[e]))
        put(f"alog{e}", rb(a_log[e]))
        put(f"dsk{e}", rb(d_skip[e]))
        put(f"lqk{e}", rb(lambda_qk[e]))
        put(f"b4{e}", fm(b_four[e], 8))
    in_maps = []
    for i in range(8):
        s = i // 2
        xt = np.concatenate([x_prompt[2 * i], x_prompt[2 * i + 1], x_sample[s]], 0)
        p = par0.copy()
        cvec = np.stack([c_ctx, c[s]], 0).reshape(2, 8, 128).transpose(2, 1, 0).reshape(128, 16)
        o, w = PL["cvec"]
        p[:, o:o + w] = cvec
        mp = dict(shared)
        mp["xT_in"] = np.ascontiguousarray(xt.T)
        mp["par"] = p
        mp["ck"] = np.ascontiguousarray(cache_k[s].reshape(2, 512, 8, 128).transpose(0, 2, 3, 1))
        mp["cv"] = np.ascontiguousarray(cache_v[s].reshape(2, 512, 1024))
        mp["sf"] = np.ascontiguousarray(state_ssd_fwd[s].reshape(2, 1024, 128).transpose(0, 2, 1))
        mp["sb"] = np.ascontiguousarray(state_ssd_bwd[s].reshape(2, 1024, 128).transpose(0, 2, 1))
        in_maps.append(mp)
    res = run_bass_kernel_spmd(nc, in_maps[:DBG_CORES], core_ids=list(range(DBG_CORES)))
    R = list(res.results) + [res.results[0]] * (8 - DBG_CORES)
    y_prompt = np.zeros((16, 256, 1024), np.float32)
    y_sample = np.zeros((4, 1024, 1024), np.float32)
    new_k = np.zeros((16, 2, 256, 8, 2, 64), np.float32)
    new_v = np.zeros((16, 2, 256, 8, 128), np.float32)
    new_f = np.zeros((16, 2, 16, 64, 128), np.float32)
    new_b = np.zeros((16, 2, 16, 64, 128), np.float32)
    for i in range(8):
        yT = np.asarray(R[i]["yT"])
        for q in range(2):
            b = 2 * i + q
            y_prompt[b] = yT[:, q * 256:(q + 1) * 256].T
            new_k[b] = np.asarray(R[i]["nk"])[:, q * 256:(q + 1) * 256].reshape(2, 256, 8, 2, 64)
            new_v[b] = np.asarray(R[i]["nv"])[:, q * 256:(q + 1) * 256].reshape(2, 256, 8, 128)
            new_f[b] = np.asarray(R[i]["nf"])[:, q].transpose(0, 2, 1).reshape(2, 16, 64, 128)
            new_b[b] = np.asarray(R[i]["nb"])[:, q].transpose(0, 2, 1).reshape(2, 16, 64, 128)
        if i % 2 == 0:
            y_sample[i // 2] = yT[:, 512:].T
    return (y_prompt, y_sample, new_k, new_v, new_f, new_b)
```

```python
import math
import numpy as np
from contextlib import ExitStack
import concourse.bass as bass
import concourse.mybir as mybir
from concourse.bass_utils import run_bass_kernel_spmd

F32 = mybir.dt.float32
BF16 = mybir.dt.bfloat16
AF = mybir.ActivationFunctionType
ALU = mybir.AluOpType

DEPTH_RUN = 4
STOP_AT = None
SKIP_KVO = False
DBG_CORES = 8


class _Stop(Exception):
    pass


_STOPPED = [False]


def checkpoint(name):
    if STOP_AT == name:
        _STOPPED[0] = True
D = 1024
T = 1536
NT = 12
EPS = 1e-6
SEQS = [(0, 2), (2, 2), (4, 8)]
FFH = 2816
NJ = 22
IN_AB = 6176
RING_SLOTS = 3
RING_ELEMS = 4096


def _layout(items):
    d, off = {}, 0
    for n, w in items:
        d[n] = (off, w)
        off += w
    return d, off


def param_layout():
    it = [("cvec", 16), ("nfin", 8)]
    for l in range(4):
        it += [(f"nm{l}", 8), (f"nf{l}", 8), (f"bada{l}", 48)]
    for e in range(2):
        it += [(f"cw{e}", 80), (f"cb{e}", 16), (f"ssdn{e}", 8), (f"subln{e}", 1), (f"dtb{e}", 32),
               (f"alog{e}", 32), (f"dsk{e}", 16), (f"lqk{e}", 256), (f"b4{e}", 8)]
    return _layout(it)


def const_layout():
    it = [("ident", 128), ("ones", 128), ("Uf", 128), ("Ub", 128), ("Vf", 128), ("Vb", 128),
          ("NEGf", 128), ("NEGb", 128), ("Psw", 128), ("Cc", 512), ("Sc", 512), ("CL256", 512), ("nSL256", 512)]
    return _layout(it)


PL, NPAR = param_layout()
CL, NCON = const_layout()


def conv_chan(sl):
    g, s = divmod(sl, 4)
    if s < 2:
        return g * 256 + s * 128 + np.arange(128)
    if s == 2:
        return 1024 + g * 128 + np.arange(128)
    return 1536 + g * 128 + np.arange(128)


def win_perm():
    o_z, o_x, o_dt, o_q, o_k, o_v = 0, 1024, 3072, 3104, 4128, 5152
    cols = list(o_dt + np.arange(32))
    for g in range(4):
        cols += list(o_x + g * 256 + np.arange(256))
        cols += list(o_x + 1024 + g * 128 + np.arange(128))
        cols += list(o_x + 1536 + g * 128 + np.arange(128))
        cols += list(o_z + g * 256 + np.arange(256))
    for h in range(8):
        cols += list(o_q + h * 128 + np.arange(128))
        cols += list(o_k + h * 128 + np.arange(128))
        cols += list(o_v + h * 128 + np.arange(128))
    return np.array(cols)


def ffi_perm():
    cols = []
    for t in range(11):
        for j in (2 * t, 2 * t + 1):
            cols += list(j * 128 + np.arange(128))
        for j in (2 * t, 2 * t + 1):
            cols += list(FFH + j * 128 + np.arange(128))
    return np.array(cols)


def make_consts():
    c = np.zeros((128, NCON), np.float32)
    t = np.arange(128)[:, None]
    j = np.arange(128)[None, :]

    def put(n, a):
        o, w = CL[n]
        c[:, o:o + w] = a.reshape(128, w)
    put("ident", (t == j).astype(np.float32))
    put("ones", np.ones((128, 128), np.float32))
    put("Uf", (t > j).astype(np.float32))
    put("Ub", (t < j).astype(np.float32))
    put("Vf", (t <= j).astype(np.float32))
    put("Vb", (t >= j).astype(np.float32))
    put("NEGf", -30000.0 * (j < t))
    put("NEGb", -30000.0 * (j > t))
    d = np.arange(128) % 64
    partner = np.where(d % 32 < 16, np.arange(128) + 16, np.arange(128) - 16)
    psw = np.zeros((128, 128), np.float32)
    psw[partner, np.arange(128)] = 1.0
    put("Psw", psw)
    n = np.arange(256)
    ang = 2 * np.pi * np.outer(n, n) / 256.0
    cc = (np.cos(ang) / 16.0).reshape(2, 128, 256).transpose(1, 0, 2)
    ss = (np.sin(ang) / 16.0).reshape(2, 128, 256).transpose(1, 0, 2)
    put("Cc", cc)
    put("Sc", ss)
    put("CL256", cc)
    put("nSL256", -ss)
    n = np.arange(1024)
    ang = 2 * np.pi * np.outer(n, n) / 1024.0
    cl = (np.cos(ang) / 32.0).reshape(8, 128, 1024).transpose(1, 0, 2)
    sl = (-np.sin(ang) / 32.0).reshape(8, 128, 1024).transpose(1, 0, 2)
    dft = np.ascontiguousarray(np.stack([cl, sl], 0).astype(np.float32))
    tok = np.arange(1024)
    row = (tok // 64).astype(np.float64)
    col = (tok % 64).astype(np.float64)
    freq = 10000.0 ** (-np.arange(16, dtype=np.float64) / 16.0)
    cosT = np.zeros((128, 1024), np.float32)
    sinT = np.zeros((128, 1024), np.float32)
    for p in range(128):
        dd = p % 64
        pos = row if dd < 32 else col
        a = pos * freq[dd % 16]
        cosT[p] = np.cos(a)
        sinT[p] = (-np.sin(a)) if (dd % 32) < 16 else np.sin(a)
    rope = np.ascontiguousarray(np.stack([cosT, sinT], 0))
    return c, dft, rope


class Buf:
    __slots__ = ("w", "r", "x", "fresh")

    def __init__(self, x=False):
        self.w = None
        self.r = {}
        self.fresh = True
        self.x = x


def BG(*shape):
    if len(shape) == 1:
        return [Buf() for _ in range(shape[0])]
    return [BG(*shape[1:]) for _ in range(shape[0])]


def flat(x):
    if isinstance(x, Buf):
        return [x]
    out = []
    for y in x:
        out += flat(y)
    return out


class Prog:
    def __init__(self, nc, es):
        self.nc = nc
        self.es = es
        self.E = {"pe": nc.tensor, "act": nc.scalar, "dve": nc.vector, "pool": nc.gpsimd, "sp": nc.sync}
        self.sems, self.cnt = {}, {}
        self.known = {e: {} for e in self.E}
        self.floor = {}
        self.isdma = set()
        for e in ("pe", "act", "dve", "pool"):
            self.sems[e] = es.enter_context(nc.semaphore("c_" + e))
            self.cnt[e] = 0
        self.nins = 0
        self.trace = {}

    def dsem(self, key):
        if key not in self.sems:
            self.sems[key] = self.es.enter_context(self.nc.semaphore("d_" + key))
            self.cnt[key] = 0
            self.isdma.add(key)
        return self.sems[key]

    def op(self, e, fn, reads=(), writes=(), sig=True, dkey=None, nofence=False):
        if _STOPPED[0]:
            return None
        reqs = {}

        def need(k, v):
            if k in self.isdma:
                v = self.cnt[k]
            elif k == e:
                if e == "pe":
                    return
            if reqs.get(k, 0) < v:
                reqs[k] = v
        reads = flat(reads)
        writes = flat(writes)
        if any(b.fresh for b in reads) or any(b.fresh for b in writes):
            for k, v in self.floor.items():
                if k != e:
                    need(k, v)
                elif e != "pe" and reqs.get(k, 0) < v:
                    reqs[k] = v
            for b in reads:
                b.fresh = False
            for b in writes:
                b.fresh = False
        for b in reads:
            if b.w is not None:
                need(*b.w)
            if b.x:
                for k, v in b.r.items():
                    if k != e:
                        need(k, v)
        for b in writes:
            if b.w is not None:
                need(*b.w)
            for k, v in b.r.items():
                need(k, v)
        kn = self.known[e]
        for k, v in reqs.items():
            if kn.get(k, 0) < v:
                self.E[e].wait_ge(self.sems[k], v)
                kn[k] = v
                self.trace.setdefault(e, []).append(("w", k, v))
        ins = fn()
        self.nins += 1
        self.trace.setdefault(e, []).append(("i", dkey if dkey is not None else (e if sig else None), 16 if dkey is not None else 1))
        if dkey is not None:
            sem = self.dsem(dkey)
            ins.then_inc(sem, 16)
            self.cnt[dkey] += 16
            stamp = (dkey, self.cnt[dkey])
        elif sig:
            ins.then_inc(self.sems[e], 1)
            self.cnt[e] += 1
            stamp = (e, self.cnt[e])
        else:
            stamp = (e, self.cnt[e] + 1)
        for b in writes:
            b.w = stamp
            b.r = {}
        k, v = stamp
        for b in reads:
            if b.r.get(k, 0) < v:
                b.r[k] = v
        return ins

    def fence(self):
        self.floor = {k: v for k, v in self.cnt.items() if not k.startswith("ring")}


def build_program():
    _STOPPED[0] = False
    nc = bass.Bass("TRN2", target_bir_lowering=False)
    dr = lambda n, s, kind="ExternalInput": nc.dram_tensor(n, s, F32, kind=kind).ap()
    xT_in = dr("xT_in", [D, T])
    par_in = dr("par", [128, NPAR])
    con_in = dr("con", [128, NCON])
    dft_in = dr("dft", [2, 128, 8, 1024])
    rope_in = dr("rope", [2, 128, 1024])
    ck_in = dr("ck", [2, 8, 128, 512])
    cv_in = dr("cv", [2, 512, 1024])
    sf_in = dr("sf", [2, 128, 1024])
    sb_in = dr("sb", [2, 128, 1024])
    w_ada = dr("w_ada", [4, D, 6144])
    w_inp = dr("w_inp", [2, D, IN_AB])
    w_out = dr("w_out", [2, 2048, D])
    w_four = dr("w_four", [2, D, D])
    w_ffi = dr("w_ffi", [4, D, 2 * FFH])
    w_ffo = dr("w_ffo", [4, FFH, D])
    yT_out = dr("yT", [D, T], "ExternalOutput")
    nk_out = dr("nk", [2, 512, 1024], "ExternalOutput")
    nv_out = dr("nv", [2, 512, 1024], "ExternalOutput")
    nf_out = dr("nf", [2, 2, 128, 1024], "ExternalOutput")
    nb_out = dr("nb", [2, 2, 128, 1024], "ExternalOutput")

    es = ExitStack()
    P = Prog(nc, es)
    op = P.op
    PE, ACT, DVE = nc.tensor, nc.scalar, nc.vector

    uid = {"n": 0}

    def sbt(stack, name, shape, dt):
        uid["n"] += 1
        return stack.enter_context(nc.sbuf_tensor(f"{name}_{uid['n']}", shape, dt))

    xT = sbt(es, "xT", [128, 8, T], F32)
    xB = BG(8, 3)
    par = sbt(es, "par_sb", [128, NPAR], F32)
    parB = Buf()
    cbf = sbt(es, "cbf", [128, 9 * 128], BF16)
    cbfB = Buf()
    cf32 = sbt(es, "cf32", [128, 4 * 128], F32)
    cfB = Buf()
    scT = sbt(es, "scT", [128, 8, 2], BF16)
    scB = Buf()
    mod = [sbt(es, f"mod{i}", [128, 48, 2], F32) for i in range(2)]
    modB = [Buf(), Buf()]
    gsc = [sbt(es, f"gsc{i}", [128, 2, 8, 2], F32) for i in range(2)]
    gscB = [Buf(), Buf()]
    ring = sbt(es, "ring", [128, RING_SLOTS, RING_ELEMS], BF16)
    ringB = BG(RING_SLOTS)
    psum = [es.enter_context(nc.psum_tensor(f"ps{i}", [128, 2, 512], F32)) for i in range(4)]
    PSB = [Buf(x=True) for _ in range(8)]

    def PS(i):
        return psum[i // 2][:, i % 2, :]

    def pv(name):
        o, w = PL[name]
        return par[:, o:o + w]

    def cb(name):
        o, w = CL[name]
        return cbf[:, o:o + w]

    plan = []

    def wtile(dram_ap, kc, ncols, tag):
        plan.append((dram_ap, kc, ncols, tag))

    for l in range(DEPTH_RUN):
        wa0 = w_ada[0].rearrange("(kc p) f -> p kc f", p=128)
        if l == 0:
            for t in range(4):
                wtile(wa0[:, :, t * 512:(t + 1) * 512], 8, 512, f"ada0_{t}")
        if l % 2 == 0:
            e = l // 2
            wi = w_inp[e].rearrange("(kc p) f -> p kc f", p=128)
            wo = w_out[e].rearrange("(kc p) f -> p kc f", p=128)
            wtile(wi[:, :, 0:32], 8, 32, f"dt{e}")
            for g in range(4):
                b0 = 32 + g * 768
                wtile(wi[:, :, b0:b0 + 512], 8, 512, f"A1_{e}_{g}")
                if l == 0:
                    t = 4 + 2 * g
                    wtile(wa0[:, :, t * 512:(t + 1) * 512], 8, 512, f"ada0_{t}")
                wtile(wi[:, :, b0 + 512:b0 + 768], 8, 256, f"A2_{e}_{g}")
                if l == 0:
                    t = 5 + 2 * g
                    wtile(wa0[:, :, t * 512:(t + 1) * 512], 8, 512, f"ada0_{t}")
            for t in range(2):
                wtile(wo[:, 0:8, t * 512:(t + 1) * 512], 8, 512, f"O1_{e}_{t}")
            for h in range(8):
                b0 = 3104 + h * 384
                wtile(wi[:, :, b0:b0 + 384], 8, 384, f"QKV_{e}_{h}")
            for t in range(2):
                wtile(wo[:, 8:16, t * 512:(t + 1) * 512], 8, 512, f"O2_{e}_{t}")
        else:
            o = l // 2
            w4 = w_four[o].rearrange("(kc p) f -> p kc f", p=128)
            for t in range(2):
                wtile(w4[:, :, t * 512:(t + 1) * 512], 8, 512, f"W4_{o}_{t}")
        wf = w_ffi[l].rearrange("(kc p) f -> p kc f", p=128)
        wa = w_ada[l + 1].rearrange("(kc p) f -> p kc f", p=128) if l + 1 < DEPTH_RUN else None
        for t in range(11):
            wtile(wf[:, :, t * 512:(t + 1) * 512], 8, 512, f"FI_{l}_{t}")
            if wa is not None:
                wtile(wa[:, :, t * 512:(t + 1) * 512], 8, 512, f"ada{l + 1}_{t}")
        if wa is not None:
            wtile(wa[:, :, 11 * 512:12 * 512], 8, 512, f"ada{l + 1}_11")
        wfo = w_ffo[l].rearrange("(kc p) f -> p kc f", p=128)
        for t in range(8):
            wtile(wfo[:, :, t * 128:(t + 1) * 128], 22, 128, f"FO_{l}_{t}")

    st = {"issued": 0, "used": 0}

    def ring_issue_upto(n):
        while st["issued"] <= n and st["issued"] < len(plan):
            m = st["issued"]
            ap, kc, ncols, tag = plan[m]
            s = m % RING_SLOTS
            dst = ring[:, s, 0:kc * ncols].rearrange("p (k c) -> p k c", k=kc)
            op("pool", lambda dst=dst, ap=ap: nc.gpsimd.dma_start(out=dst, in_=ap),
               writes=[ringB[s]], dkey=f"ring{s}", nofence=True)
            st["issued"] += 1

    def wget(tag):
        n = st["used"]
        ap, kc, ncols, tg = plan[n]
        assert tg == tag, (tg, tag)
        ring_issue_upto(n + RING_SLOTS - 1)
        st["used"] += 1
        s = n % RING_SLOTS
        return ring[:, s, 0:kc * ncols].rearrange("p (k c) -> p k c", k=kc), ringB[s]

    rot = {"i": 0}

    def nbank(lo=0, hi=6):
        b = lo + rot["i"] % (hi - lo)
        rot["i"] += 1
        return b

    op("sp", lambda: nc.sync.dma_start(out=par[:], in_=par_in[:, :]), writes=[parB], dkey="ld0")
    with ExitStack() as s0:
        con = sbt(s0, "con_sb", [128, NCON], F32)
        conB = Buf()
        op("sp", lambda: nc.sync.dma_start(out=con[:], in_=con_in[:, :]), writes=[conB], dkey="ld0")
        xin = xT_in.rearrange("(kc p) t -> p kc t", p=128)
        for kc in range(8):
            op("sp", lambda kc=kc: nc.sync.dma_start(out=xT[:, kc, :], in_=xin[:, kc, :]), writes=xB[kc], dkey="ld1")
        op("dve", lambda: DVE.tensor_copy(out=cbf[:], in_=con[:, 0:9 * 128]), reads=[conB], writes=[cbfB])
        o1 = CL["ones"][0]
        op("act", lambda: ACT.copy(out=cf32[:, 0:128], in_=con[:, o1:o1 + 128]), reads=[conB], writes=[cfB])
        o2 = CL["Vf"][0]
        op("act", lambda: ACT.copy(out=cf32[:, 128:384], in_=con[:, o2:o2 + 256]), reads=[conB], writes=[cfB])
        o3 = CL["ident"][0]
        op("act", lambda: ACT.copy(out=cf32[:, 384:512], in_=con[:, o3:o3 + 128]), reads=[conB], writes=[cfB])
        cv_ = pv("cvec").rearrange("p (k v) -> p k v", v=2)
        op("act", lambda: ACT.activation(out=scT[:], in_=cv_, func=AF.Silu), reads=[parB], writes=[scB])
        P.fence()
    kct = sbt(es, "kct", [128, 4], F32)
    kcB_ = Buf()
    op("dve", lambda: DVE.memset(kct[:, 0:1], 1024.0 * EPS), writes=[kcB_])
    op("dve", lambda: DVE.memset(kct[:, 1:2], EPS), writes=[kcB_])
    op("dve", lambda: DVE.memset(kct[:, 2:3], 1.0), writes=[kcB_])
    op("dve", lambda: DVE.memset(kct[:, 3:4], -30.0), writes=[kcB_])

    def KC(i):
        return kct[:, i:i + 1]
    ones_f = cf32[:, 0:128]
    Vf_f = cf32[:, 128:256]
    Vb_f = cf32[:, 256:384]
    ident_f = cf32[:, 384:512]
    ident = cb("ident")
    ones_b = cb("ones")

    ABANK = 7

    def ada_tile(l, t):
        wt, wb = wget(f"ada{l}_{t}")
        for s in range(4):
            j = 4 * t + s
            for kc in range(8):
                op("pe", lambda s=s, kc=kc, j=j: PE.matmul(PS(ABANK)[:, 2 * j:2 * j + 2], lhsT=wt[:, kc, s * 128:(s + 1) * 128],
                                                        rhs=scT[:, kc, :], start=(kc == 0), stop=(kc == 7)),
                   reads=[wb, scB], writes=[PSB[ABANK]], sig=(kc == 7))

    def ada_finish(l, part=None):
        m = mod[l % 2]
        mB = modB[l % 2]
        j0, j1 = {None: (0, 48), 0: (0, 16), 1: (16, 48)}[part]
        op("dve", lambda: DVE.tensor_tensor(out=m[:, j0:j1, :], in0=PS(ABANK)[:, 2 * j0:2 * j1].rearrange("p (j v) -> p j v", v=2),
                                            in1=pv(f"bada{l}")[:, j0:j1].unsqueeze(2).to_broadcast([128, j1 - j0, 2]), op=ALU.add),
           reads=[PSB[ABANK], parB], writes=[mB])
        g = gsc[l % 2]
        for wh, (nm, sc0) in enumerate(((f"nm{l}", 8), (f"nf{l}", 32))):
            if (part == 0 and wh == 1) or (part == 1 and wh == 0):
                continue
            op("dve", lambda wh=wh, nm=nm, sc0=sc0: DVE.scalar_tensor_tensor(
                out=g[:, wh], in0=m[:, sc0:sc0 + 8, :], scalar=1.0,
                in1=pv(nm).unsqueeze(2).to_broadcast([128, 8, 2]), op0=ALU.add, op1=ALU.mult),
               reads=[mB, parB], writes=[gscB[l % 2]])
            op("dve", lambda wh=wh: DVE.tensor_scalar(out=g[:, wh], in0=g[:, wh], scalar1=32.0, scalar2=None, op0=ALU.mult),
               reads=[gscB[l % 2]], writes=[gscB[l % 2]])

    def modulate(stack, dst, dstB, gs, sh, depB, out_dt_is_f32=False):
        with ExitStack() as s1:
            sq = sbt(s1, "sq", [128, 2, 8, 512], BF16)
            sqB = BG(2, 2)
            rs = sbt(s1, "rs", [128, 2, 512], F32)
            rsB = BG(2)
            tmp = sbt(s1, "mtmp", [128, 2, 8, 512], F32)
            tmpB = BG(2, 2)
            def stage1(tb):
                p = tb % 2
                bk = 6 + p
                ts = slice(tb * 512, (tb + 1) * 512)
                xb = [xB[kc][tb] for kc in range(8)]
                op("act", lambda: ACT.activation(out=sq[:, p, 0:5], in_=xT[:, 0:5, ts], func=AF.Square), reads=xb[0:5], writes=[sqB[p][0]])
                op("pool", lambda: nc.gpsimd.tensor_tensor(out=sq[:, p, 5:8], in0=xT[:, 5:8, ts], in1=xT[:, 5:8, ts], op=ALU.mult), reads=xb[5:8], writes=[sqB[p][1]])
                for kc in range(8):
                    op("pe", lambda: PE.matmul(PS(bk), lhsT=ones_b, rhs=sq[:, p, kc, :], start=(kc == 0), stop=(kc == 7)),
                       reads=[sqB[p][0 if kc < 5 else 1], cbfB], writes=[PSB[bk]], sig=(kc == 7))

            def stage2(tb):
                p = tb % 2
                bk = 6 + p
                v = 0 if tb == 0 else 1
                ts = slice(tb * 512, (tb + 1) * 512)
                xb = [xB[kc][tb] for kc in range(8)]
                op("act", lambda: ACT.activation(out=rs[:, p, :], in_=PS(bk), func=AF.Ln, bias=KC(0)), reads=[PSB[bk], kcB_], writes=[rsB[p]])
                op("act", lambda: ACT.activation(out=rs[:, p, :], in_=rs[:, p, :], func=AF.Exp, scale=-0.5), reads=[rsB[p]], writes=[rsB[p]])
                op("dve", lambda: DVE.tensor_tensor(out=tmp[:, p, 0:6], in0=xT[:, 0:6, ts], in1=rs[:, p, :].unsqueeze(1).to_broadcast([128, 6, 512]), op=ALU.mult),
                   reads=xb[0:6] + [rsB[p]], writes=[tmpB[p][0]])
                op("pool", lambda: nc.gpsimd.tensor_tensor(out=tmp[:, p, 6:8], in0=xT[:, 6:8, ts], in1=rs[:, p, :].unsqueeze(1).to_broadcast([128, 2, 512]), op=ALU.mult),
                   reads=xb[6:8] + [rsB[p]], writes=[tmpB[p][1]])
                for kc in range(8):
                    b = sh(kc, v)
                    if kc < 4:
                        if b is None:
                            op("act", lambda: ACT.activation(out=dst[:, kc, ts], in_=tmp[:, p, kc, :], func=AF.Identity, scale=gs(kc, v)),
                               reads=[tmpB[p][0 if kc < 6 else 1]] + depB, writes=[dstB[kc][tb]])
                        else:
                            op("act", lambda: ACT.activation(out=dst[:, kc, ts], in_=tmp[:, p, kc, :], func=AF.Identity, scale=gs(kc, v), bias=b),
                               reads=[tmpB[p][0 if kc < 6 else 1]] + depB, writes=[dstB[kc][tb]])
                    else:
                        if b is None:
                            op("dve", lambda: DVE.tensor_scalar(out=dst[:, kc, ts], in0=tmp[:, p, kc, :], scalar1=gs(kc, v), scalar2=None, op0=ALU.mult),
                               reads=[tmpB[p][0 if kc < 6 else 1]] + depB, writes=[dstB[kc][tb]])
                        else:
                            op("dve", lambda: DVE.tensor_scalar(out=dst[:, kc, ts], in0=tmp[:, p, kc, :], scalar1=gs(kc, v), scalar2=b, op0=ALU.mult, op1=ALU.add),
                               reads=[tmpB[p][0 if kc < 6 else 1]] + depB, writes=[dstB[kc][tb]])
            stage1(0)
            stage1(1)
            stage2(0)
            stage1(2)
            stage2(1)
            stage2(2)
            P.fence()

    def ffn(l):
        m = mod[l % 2]
        g = gsc[l % 2]
        mB = modB[l % 2]
        with ExitStack() as s1:
            h2 = sbt(s1, "h2", [128, 8, T], BF16)
            h2B = BG(8, 3)
            modulate(s1, h2, h2B, lambda kc, v: g[:, 1, kc, v:v + 1], lambda kc, v: m[:, 24 + kc, v:v + 1],
                     [mB, gscB[l % 2]])
            aT = sbt(s1, "aT", [128, NJ, T], BF16)
            aB = BG(NJ, 3)
            sg = sbt(s1, "sg", [128, 2, T], F32)
            sgB = BG(2, 3)
            for t in range(11):
                wt, wb = wget(f"FI_{l}_{t}")
                for s in range(2):
                    j = 2 * t + s
                    for half, b0 in ((0, 0), (1, 3)):
                        c0 = (half * 2 + s) * 128
                        for kc in range(8):
                            for tb in range(3):
                                op("pe", lambda c0=c0, kc=kc, tb=tb, b0=b0: PE.matmul(
                                    PS(b0 + tb), lhsT=wt[:, kc, c0:c0 + 128], rhs=h2[:, kc, tb * 512:(tb + 1) * 512],
                                    start=(kc == 0), stop=(kc == 7)),
                                   reads=[wb, h2B[kc][tb]], writes=[PSB[b0 + tb]], sig=(kc == 7))
                    for tb in range(3):
                        ts = slice(tb * 512, (tb + 1) * 512)
                        op("act", lambda tb=tb, ts=ts, s=s: ACT.activation(out=sg[:, s, ts], in_=PS(tb), func=AF.Silu),
                           reads=[PSB[tb]], writes=[sgB[s][tb]])
                        op("dve", lambda tb=tb, ts=ts, s=s, j=j: DVE.tensor_tensor(out=aT[:, j, ts], in0=sg[:, s, ts], in1=PS(3 + tb), op=ALU.mult),
                           reads=[sgB[s][tb], PSB[3 + tb]], writes=[aB[j][tb]])
                if l + 1 < DEPTH_RUN:
                    ada_tile(l + 1, t)
            if l + 1 < DEPTH_RUN:
                ada_tile(l + 1, 11)
                ada_finish(l + 1)
            for dsl in range(8):
                wt, wb = wget(f"FO_{l}_{dsl}")
                b0 = 0 if dsl % 2 == 0 else 3
                for j in range(NJ):
                    for tb in range(3):
                        op("pe", lambda j=j, tb=tb, b0=b0: PE.matmul(PS(b0 + tb), lhsT=wt[:, j, :], rhs=aT[:, j, tb * 512:(tb + 1) * 512],
                                                                    start=(j == 0), stop=(j == NJ - 1)),
                           reads=[wb, aB[j][tb]], writes=[PSB[b0 + tb]], sig=(j == NJ - 1))
                for tb in range(3):
                    v = 0 if tb == 0 else 1
                    ts = slice(tb * 512, (tb + 1) * 512)
                    op("dve", lambda tb=tb, ts=ts, v=v, dsl=dsl, b0=b0: DVE.scalar_tensor_tensor(
                        out=xT[:, dsl, ts], in0=PS(b0 + tb), scalar=m[:, 40 + dsl, v:v + 1], in1=xT[:, dsl, ts],
                        op0=ALU.mult, op1=ALU.add),
                       reads=[PSB[b0 + tb], mB, xB[dsl][tb]], writes=[xB[dsl][tb]])
            P.fence()

    def outproj(l, tags, srcT, srcB, bias=None, rowscale=None):
        m = mod[l % 2]
        mB = modB[l % 2]
        for t, tag in enumerate(tags):
            wt, wb = wget(tag)
            for s in range(4):
                dsl = 4 * t + s
                b0 = 0 if dsl % 2 == 0 else 3
                for kc in range(8):
                    for tb in range(3):
                        op("pe", lambda s=s, kc=kc, tb=tb, b0=b0: PE.matmul(PS(b0 + tb), lhsT=wt[:, kc, s * 128:(s + 1) * 128],
                                                                           rhs=srcT[:, kc, tb * 512:(tb + 1) * 512],
                                                                           start=(kc == 0), stop=(kc == 7)),
                           reads=[wb, srcB[kc][tb]], writes=[PSB[b0 + tb]], sig=(kc == 7))
                for tb in range(3):
                    v = 0 if tb == 0 else 1
                    ts = slice(tb * 512, (tb + 1) * 512)
                    if bias is not None:
                        op("act", lambda tb=tb, b0=b0, dsl=dsl: ACT.activation(out=PS(b0 + tb), in_=PS(b0 + tb), func=AF.Identity,
                                                                             bias=bias[:, dsl:dsl + 1]),
                           reads=[PSB[b0 + tb], parB], writes=[PSB[b0 + tb]])
                    if rowscale is not None:
                        op("dve", lambda tb=tb, ts=ts, b0=b0: DVE.tensor_tensor(out=PS(b0 + tb), in0=PS(b0 + tb), in1=rowscale[0][:, ts], op=ALU.mult),
                           reads=[PSB[b0 + tb], rowscale[1][tb]], writes=[PSB[b0 + tb]])
                    op("dve", lambda tb=tb, ts=ts, v=v, dsl=dsl, b0=b0: DVE.scalar_tensor_tensor(
                        out=xT[:, dsl, ts], in0=PS(b0 + tb), scalar=m[:, 16 + dsl, v:v + 1], in1=xT[:, dsl, ts],
                        op0=ALU.mult, op1=ALU.add),
                       reads=[PSB[b0 + tb], mB, xB[dsl][tb]], writes=[xB[dsl][tb]])

    def fourier_layer(l, hT, hB):
        o = l // 2
        with ExitStack() as s1:
            dftb = sbt(s1, "dftb", [128, 2, 8, 1024], BF16)
            dftB = Buf()
            for i in range(2):
                op("pool", lambda i=i: nc.gpsimd.dma_start(out=dftb[:, i], in_=dft_in[i]), writes=[dftB], dkey="dft")
            fT = sbt(s1, "fT", [128, 8, T], BF16)
            fB = BG(8, 3)
            AB = sbt(s1, "ABt", [128, 8, 2, 1024], BF16)
            ABB = BG(8)
            cdft = sbt(s1, "cdft", [128, 4, 2, 256], BF16)
            o_cc = CL["Cc"][0]
            op("pool", lambda: nc.gpsimd.dma_start(out=cdft[:], in_=con_in[:, o_cc:o_cc + 2048].rearrange("p (a k c) -> p a k c", a=4, k=2)),
               writes=[dftB], dkey="dft")
            Cc, Sc, c256, s256 = cdft[:, 0], cdft[:, 1], cdft[:, 2], cdft[:, 3]
            for (t0, ntl) in SEQS:
                for lt in range(ntl):
                    tt = t0 + lt
                    tb = tt // 4
                    tsl = slice(tt * 128, (tt + 1) * 128)
                    for ab, tab in ((0, Cc), (1, Sc)):
                        bk = [nbank(0, 6), nbank(0, 6)]
                        for g in range(4):
                            for cc in range(2):
                                op("pe", lambda g=g, cc=cc, tab=tab, bk=bk, tsl=tsl: PE.matmul(
                                    PS(bk[g // 2])[:, (g % 2) * 256:(g % 2) * 256 + 256], lhsT=hT[:, 2 * g + cc, tsl], rhs=tab[:, cc, :],
                                    start=(cc == 0), stop=(cc == 1)),
                                   reads=[hB[2 * g + cc][tb], dftB], writes=[PSB[bk[g // 2]]], sig=(cc == 1))
                        for hh in range(2):
                            eng = "act" if (hh + ab) % 2 == 0 else "dve"
                            if eng == "act":
                                op("act", lambda hh=hh, ab=ab, lt=lt, bk=bk: ACT.copy(out=AB[:, lt, ab, hh * 512:(hh + 1) * 512], in_=PS(bk[hh])),
                                   reads=[PSB[bk[hh]]], writes=[ABB[lt]])
                            else:
                                op("dve", lambda hh=hh, ab=ab, lt=lt, bk=bk: DVE.tensor_copy(out=AB[:, lt, ab, hh * 512:(hh + 1) * 512], in_=PS(bk[hh])),
                                   reads=[PSB[bk[hh]]], writes=[ABB[lt]])
                L = ntl * 128
                nkb = max(1, L // 512)
                kw = min(L, 512)
                for cs in range(8):
                    for kb in range(nkb):
                        bk = nbank(0, 6)
                        n_acc = 2 * ntl
                        i = 0
                        for lt in range(ntl):
                            for ab in range(2):
                                if ntl == 2:
                                    rhs = (c256 if ab == 0 else s256)[:, lt, :]
                                    rd = [dftB]
                                else:
                                    rhs = dftb[:, ab, lt, kb * 512:(kb + 1) * 512]
                                    rd = [dftB]
                                op("pe", lambda lt=lt, ab=ab, rhs=rhs, i=i, bk=bk, cs=cs: PE.matmul(
                                    PS(bk)[:, 0:kw], lhsT=AB[:, lt, ab, cs * 128:(cs + 1) * 128], rhs=rhs,
                                    start=(i == 0), stop=(i == n_acc - 1)),
                                   reads=[ABB[lt]] + rd, writes=[PSB[bk]], sig=(i == n_acc - 1))
                                i += 1
                        c0 = t0 * 128 + kb * 512
                        tbs = sorted(set([c0 // 512, (c0 + kw - 1) // 512]))
                        eng = "act" if (cs + kb) % 2 == 0 else "dve"
                        if eng == "act":
                            op("act", lambda bk=bk, cs=cs, c0=c0: ACT.copy(out=fT[:, cs, c0:c0 + kw], in_=PS(bk)[:, 0:kw]),
                               reads=[PSB[bk]], writes=[fB[cs][tb_] for tb_ in tbs])
                        else:
                            op("dve", lambda bk=bk, cs=cs, c0=c0: DVE.tensor_copy(out=fT[:, cs, c0:c0 + kw], in_=PS(bk)[:, 0:kw]),
                               reads=[PSB[bk]], writes=[fB[cs][tb_] for tb_ in tbs])
            outproj(l, [f"W4_{o}_0", f"W4_{o}_1"], fT, fB, bias=pv(f"b4{o}"))
            P.fence()

    def ab_layer(l, hT, hB):
        e = l // 2
        lambda_init = 0.8 - 0.6 * math.exp(-0.3 * l)
        Uf, Ub, Vf, Vb = cb("Uf"), cb("Ub"), cb("Vf"), cb("Vb")
        NEGf, NEGb = cb("NEGf"), cb("NEGb")
        with ExitStack() as s1:
            yoT = sbt(s1, "yoT", [128, 8, T], BF16)
            yoB = BG(8, 3)
            nlam = sbt(s1, "nlam", [128, 4], F32)
            nlB = Buf()
            with ExitStack() as s2:
                lt_ = sbt(s2, "lqtmp", [128, 2, 64], F32)
                ltB = Buf()
                lq = pv(f"lqk{e}").rearrange("p (a b d) -> p a b d", a=2, b=2)
                op("dve", lambda: DVE.tensor_tensor(out=lt_[:], in0=lq[:, :, 0, :], in1=lq[:, :, 1, :], op=ALU.mult), reads=[parB], writes=[ltB])
                op("dve", lambda: DVE.tensor_reduce(out=nlam[:, 0:2], in_=lt_[:], axis=mybir.AxisListType.X, op=ALU.add), reads=[ltB], writes=[nlB])
                op("act", lambda: ACT.activation(out=nlam[:, 0:2], in_=nlam[:, 0:2], func=AF.Exp), reads=[nlB], writes=[nlB])
                op("dve", lambda: DVE.tensor_tensor(out=nlam[:, 2:3], in0=nlam[:, 1:2], in1=nlam[:, 0:1], op=ALU.subtract), reads=[nlB], writes=[nlB])
                op("dve", lambda: DVE.tensor_scalar(out=nlam[:, 3:4], in0=nlam[:, 2:3], scalar1=-lambda_init, scalar2=None, op0=ALU.add), reads=[nlB], writes=[nlB])
                P.fence()

            with ExitStack() as s2:
                dtv = sbt(s2, "dtv", [128, NT, 32], F32)
                dta = sbt(s2, "dta", [128, NT, 32], F32)
                ea = sbt(s2, "ea", [128, NT, 32], F32)
                te = sbt(s2, "te", [128, NT, 32], F32)
                cdb = sbt(s2, "cdb", [128, NT, 32], F32)
                smB = BG(NT)
                aneg = sbt(s2, "aneg", [128, 32], F32)
                anB = Buf()
                dskd = sbt(s2, "dskd", [128, 16, 128], BF16)
                dskB = Buf()
                op("act", lambda: ACT.activation(out=aneg[:], in_=pv(f"alog{e}"), func=AF.Exp), reads=[parB], writes=[anB])
                for h in range(16):
                    op("dve", lambda h=h: DVE.tensor_scalar(out=dskd[:, h, :], in0=ident, scalar1=pv(f"dsk{e}")[:, h:h + 1], scalar2=None, op0=ALU.mult),
                       reads=[parB, cbfB], writes=[dskB])
                wt, wb = wget(f"dt{e}")
                bA, bB, bC = 3, 4, 5
                for tt in range(NT):
                    tb = tt // 4
                    tsl = slice(tt * 128, (tt + 1) * 128)
                    for kc in range(8):
                        op("pe", lambda: PE.matmul(PS(bA)[:, tt * 32:(tt + 1) * 32], lhsT=hT[:, kc, tsl], rhs=wt[:, kc, :], start=(kc == 0), stop=(kc == 7)),
                           reads=[wb, hB[kc][tb]], writes=[PSB[bA]], sig=(kc == 7))
                allsm = smB
                op("dve", lambda: DVE.tensor_tensor(out=dtv[:], in0=PS(bA)[:, 0:NT * 32].rearrange("p (t c) -> p t c", c=32),
                                                    in1=pv(f"dtb{e}").unsqueeze(1).to_broadcast([128, NT, 32]), op=ALU.add),
                   reads=[PSB[bA], parB], writes=allsm)
                op("act", lambda: ACT.activation(out=dtv[:], in_=dtv[:], func=AF.Exp), reads=allsm, writes=allsm)
                op("act", lambda: ACT.activation(out=dtv[:], in_=dtv[:], func=AF.Ln, bias=KC(2)), reads=allsm + [kcB_], writes=allsm)
                op("dve", lambda: DVE.scalar_tensor_tensor(out=dta[:], in0=dtv[:], scalar=-1.0, in1=aneg[:].unsqueeze(1).to_broadcast([128, NT, 32]),
                                                           op0=ALU.mult, op1=ALU.mult),
                   reads=allsm + [anB], writes=allsm)
                for tt in range(NT):
                    op("pe", lambda: PE.matmul(PS(bB)[:, tt * 32:tt * 32 + 16], lhsT=Vf_f, rhs=dta[:, tt, 0:16], start=True, stop=True),
                       reads=allsm + [cfB], writes=[PSB[bB]], sig=False)
                    op("pe", lambda: PE.matmul(PS(bB)[:, tt * 32 + 16:tt * 32 + 32], lhsT=Vb_f, rhs=dta[:, tt, 16:32], start=True, stop=True),
                       reads=allsm + [cfB], writes=[PSB[bB]], sig=False)
                    op("pe", lambda: PE.matmul(PS(bC)[:, tt * 32:(tt + 1) * 32], lhsT=ones_f, rhs=dta[:, tt, :], start=True, stop=True),
                       reads=allsm + [cfB], writes=[PSB[bC]], sig=True)
                vB = PS(bB)[:, 0:NT * 32].rearrange("p (t c) -> p t c", c=32)
                vC = PS(bC)[:, 0:NT * 32].rearrange("p (t c) -> p t c", c=32)
                op("act", lambda: ACT.activation(out=ea[:], in_=vB, func=AF.Exp), reads=[PSB[bB]], writes=allsm)
                op("act", lambda: ACT.copy(out=te[:], in_=vB), reads=[PSB[bB]], writes=allsm)
                op("act", lambda: ACT.activation(out=cdb[:], in_=vC, func=AF.Exp), reads=[PSB[bC]], writes=allsm)
                op("dve", lambda: DVE.tensor_tensor(out=te[:], in0=vC, in1=te[:], op=ALU.subtract), reads=[PSB[bC]] + allsm, writes=allsm)
                op("act", lambda: ACT.activation(out=te[:], in_=te[:], func=AF.Exp), reads=allsm, writes=allsm)
                op("dve", lambda: DVE.tensor_tensor(out=te[:], in0=te[:], in1=dtv[:], op=ALU.mult), reads=allsm, writes=allsm)

                checkpoint('dtprep')
                ssq = sbt(s2, "ssq", [128, NT, 4], F32)
                ssB = BG(NT)
                op("dve", lambda: DVE.memset(ssq[:], 0.0), writes=ssB)
                PADL = 1548
                pre = sbt(s2, "pre", [128, 2, PADL], BF16)
                preB = BG(2)
                op("dve", lambda: DVE.memset(pre[:], 0.0), writes=preB)
                post = sbt(s2, "post", [128, 4, T], BF16)
                postB = BG(4, 3)
                cwd = sbt(s2, "cwd", [128, 20, 128], BF16)
                cwB = Buf()
                xs_t = sbt(s2, "xs_t", [128, NT, 256], BF16)
                b_t = sbt(s2, "b_t", [128, NT, 128], BF16)
                zs = sbt(s2, "zs", [128, NT, 256], BF16)
                tkB = BG(NT)
                zB = BG(NT)
                xte = sbt(s2, "xte", [128, 2, 256], BF16)
                xteB = [Buf(), Buf()]
                xdt = sbt(s2, "xdt", [128, 2, 2, 256], BF16)
                xdB = BG(2, 2)
                hst = sbt(s2, "hst", [128, 2, 256], F32)
                hsB = [Buf(), Buf()]
                hbf = sbt(s2, "hbf", [128, 8, 2, 256], BF16)
                hbB = BG(8, 2)
                Ld = sbt(s2, "Ld", [128, 1, 8, 128], BF16)
                LdB = [Buf()]
                Et = sbt(s2, "Et", [128, 2, 2, 512], BF16)
                EtB = BG(2, 2)
                cbt = sbt(s2, "cbt", [128, 2, 128], F32)
                cbB = BG(2)
                Wt = sbt(s2, "Wt", [128, 2, 8, 128], BF16)
                WtB = BG(2, 2)
                ytmp = sbt(s2, "ytmp", [128, 2, 256], F32)
                ytB = [Buf(), Buf()]
                yg = sbt(s2, "yg", [128, 2, 256], BF16)
                ygB = BG(2)
                sqj = sbt(s2, "sqj", [128, 256], BF16)
                sqjB = Buf()
                for g in range(4):
                    for i in range(20):
                        ci = g * 20 + i
                        if i % 2 == 0:
                            op("dve", lambda i=i, ci=ci: DVE.tensor_scalar(out=cwd[:, i, :], in0=ident, scalar1=pv(f"cw{e}")[:, ci:ci + 1], scalar2=None, op0=ALU.mult),
                               reads=[parB, cbfB], writes=[cwB])
                        else:
                            op("act", lambda i=i, ci=ci: ACT.activation(out=cwd[:, i, :], in_=ident, func=AF.Identity, scale=pv(f"cw{e}")[:, ci:ci + 1]),
                               reads=[parB, cbfB], writes=[cwB])
                    wt1, wb1 = wget(f"A1_{e}_{g}")
                    ada_after_A1 = (l == 0)
                    def proj_s(s):
                        pp = s % 2
                        for kc in range(8):
                            for tb in range(3):
                                op("pe", lambda: PE.matmul(PS(tb), lhsT=wt1[:, kc, s * 128:(s + 1) * 128],
                                                           rhs=hT[:, kc, tb * 512:(tb + 1) * 512], start=(kc == 0), stop=(kc == 7)),
                                   reads=[wb1, hB[kc][tb]], writes=[PSB[tb]], sig=(kc == 7))
                        op("act", lambda: ACT.copy(out=pre[:, pp, 2:258], in_=PS(0)[:, 0:256]), reads=[PSB[0]], writes=[preB[pp]])
                        op("act", lambda: ACT.copy(out=pre[:, pp, 262:518], in_=PS(0)[:, 256:512]), reads=[PSB[0]], writes=[preB[pp]])
                        op("dve", lambda: DVE.tensor_copy(out=pre[:, pp, 522:1034], in_=PS(1)), reads=[PSB[1]], writes=[preB[pp]])
                        op("dve", lambda: DVE.tensor_copy(out=pre[:, pp, 1034:1546], in_=PS(2)), reads=[PSB[2]], writes=[preB[pp]])

                    def conv_s(s):
                        sl = g * 4 + s
                        pp = s % 2
                        for bi, (poff, toff, n) in enumerate(((2, 0, 256), (262, 256, 256), (522, 512, 512), (1034, 1024, 512))):
                            bk = 3 + bi % 3
                            for k in range(5):
                                op("pe", lambda: PE.matmul(PS(bk)[:, 0:n], lhsT=cwd[:, s * 5 + k, :], rhs=pre[:, pp, poff + k - 2:poff + k - 2 + n],
                                                           start=(k == 0), stop=(k == 4)),
                                   reads=[cwB, preB[pp]], writes=[PSB[bk]], sig=(k == 4))
                            tbs = sorted(set([toff // 512, (toff + n - 1) // 512]))
                            op("act", lambda: ACT.activation(out=post[:, s, toff:toff + n], in_=PS(bk)[:, 0:n], func=AF.Silu, bias=pv(f"cb{e}")[:, sl:sl + 1]),
                               reads=[PSB[bk], parB], writes=[postB[s][tb_] for tb_ in tbs])
                    proj_s(0)
                    for s in range(4):
                        if s + 1 < 4:
                            proj_s(s + 1)
                        conv_s(s)
                    for tt in range(NT):
                        tb = tt // 4
                        tsl = slice(tt * 128, (tt + 1) * 128)
                        bk = nbank(0, 6)
                        for s in range(3):
                            op("pe", lambda s=s, tsl=tsl, bk=bk: PE.matmul(PS(bk)[:, s * 128:(s + 1) * 128], lhsT=post[:, s, tsl], rhs=ident, start=True, stop=True),
                               reads=[postB[s][tb], cbfB], writes=[PSB[bk]], sig=(s == 2))
                        op("act", lambda tt=tt, bk=bk: ACT.copy(out=xs_t[:, tt, :], in_=PS(bk)[:, 0:256]), reads=[PSB[bk]], writes=[tkB[tt]])
                        op("act", lambda tt=tt, bk=bk: ACT.copy(out=b_t[:, tt, :], in_=PS(bk)[:, 256:384]), reads=[PSB[bk]], writes=[tkB[tt]])
                    if l == 0:
                        ada_tile(0, 4 + 2 * g)
                    wt2, wb2 = wget(f"A2_{e}_{g}")
                    for tt in range(NT):
                        tb = tt // 4
                        tsl = slice(tt * 128, (tt + 1) * 128)
                        bk = nbank(0, 6)
                        for kc in range(8):
                            op("pe", lambda kc=kc, tsl=tsl, bk=bk: PE.matmul(PS(bk)[:, 0:256], lhsT=hT[:, kc, tsl], rhs=wt2[:, kc, :], start=(kc == 0), stop=(kc == 7)),
                               reads=[wb2, hB[kc][tb]], writes=[PSB[bk]], sig=(kc == 7))
                        op("act", lambda tt=tt, bk=bk: ACT.activation(out=zs[:, tt, :], in_=PS(bk)[:, 0:256], func=AF.Silu), reads=[PSB[bk]], writes=[zB[tt]])
                    if l == 0:
                        ada_tile(0, 5 + 2 * g)
                        if g == 3:
                            ada_finish(0, 1)
                    checkpoint('g0conv')
                    for si, (t0, ntl) in enumerate(SEQS):
                        orders = [list(range(t0, t0 + ntl)), list(range(t0 + ntl - 1, t0 - 1, -1))]
                        for d in range(2):
                            if si < 2:
                                op("dve", lambda d=d: DVE.memset(hst[:, d, :], 0.0), writes=[hsB[d]])
                            else:
                                src = (sf_in if d == 0 else sb_in)[e][:, g * 256:(g + 1) * 256]
                                op("sp", lambda d=d, src=src: nc.sync.dma_start(out=hst[:, d, :], in_=src), writes=[hsB[d]], dkey=f"hin{d}")
                        for step in range(ntl):
                            for d in range(2):
                                tt = orders[d][step]
                                c0 = d * 16 + g * 4
                                ti = tt - t0
                                op("act", lambda ti=ti, d=d: ACT.copy(out=hbf[:, ti, d, :], in_=hst[:, d, :]), reads=[hsB[d]], writes=[hbB[ti][d]])
                                op("pool", lambda tt=tt, d=d, c0=c0: nc.gpsimd.tensor_tensor(
                                    out=xte[:, d, :].rearrange("p (r q) -> p r q", r=4),
                                    in0=xs_t[:, tt, :].rearrange("p (r q) -> p r q", r=4),
                                    in1=te[:, tt, c0:c0 + 4].unsqueeze(2).to_broadcast([128, 4, 64]), op=ALU.mult),
                                   reads=[tkB[tt], smB[tt]], writes=[xteB[d]])
                                bk = nbank(0, 6)
                                op("pe", lambda tt=tt, d=d, bk=bk: PE.matmul(PS(bk)[:, 0:256], lhsT=b_t[:, tt, :], rhs=xte[:, d, :], start=True, stop=True),
                                   reads=[tkB[tt], xteB[d]], writes=[PSB[bk]])
                                op("dve", lambda tt=tt, d=d, c0=c0: DVE.tensor_tensor(
                                    out=hst[:, d, :].rearrange("p (r q) -> p r q", r=4), in0=hst[:, d, :].rearrange("p (r q) -> p r q", r=4),
                                    in1=cdb[:, tt, c0:c0 + 4].unsqueeze(2).to_broadcast([128, 4, 64]), op=ALU.mult),
                                   reads=[hsB[d], smB[tt]], writes=[hsB[d]])
                                op("dve", lambda d=d, bk=bk: DVE.tensor_tensor(out=hst[:, d, :], in0=hst[:, d, :], in1=PS(bk)[:, 0:256], op=ALU.add),
                                   reads=[hsB[d], PSB[bk]], writes=[hsB[d]])
                        if si < 2:
                            for d in range(2):
                                dst = (nf_out if d == 0 else nb_out)[e, si][:, g * 256:(g + 1) * 256]
                                op("sp", lambda d=d, dst=dst: nc.sync.dma_start(out=dst, in_=hst[:, d, :]), reads=[hsB[d]], dkey=f"sto{d}")

                        def stageA(tt):
                            pb = tt % 2
                            tb = tt // 4
                            tsl = slice(tt * 128, (tt + 1) * 128)
                            bkc = nbank(0, 6)
                            op("pe", lambda: PE.matmul(PS(bkc)[:, 0:128], lhsT=post[:, 2, tsl], rhs=post[:, 3, tsl], start=True, stop=True),
                               reads=[postB[2][tb], postB[3][tb]], writes=[PSB[bkc]])
                            op("act", lambda: ACT.copy(out=cbt[:, pb, :], in_=PS(bkc)[:, 0:128]), reads=[PSB[bkc]], writes=[cbB[pb]])
                            for d, U in ((0, Uf), (1, Ub)):
                                c0 = d * 16 + g * 4
                                op("pool", lambda d=d, U=U, c0=c0: nc.gpsimd.tensor_tensor(
                                    out=Ld[:, 0, d * 4:(d + 1) * 4, :], in0=U.unsqueeze(1).to_broadcast([128, 4, 128]),
                                    in1=dta[:, tt, c0:c0 + 4].unsqueeze(2).to_broadcast([128, 4, 128]), op=ALU.mult),
                                   reads=[cbfB, smB[tt]], writes=[LdB[0]])
                            for d, V, NEG in ((0, Vf, NEGf), (1, Vb, NEGb)):
                                bks = nbank(0, 6)
                                for r in range(4):
                                    op("pe", lambda d=d, r=r, V=V: PE.matmul(PS(bks)[:, r * 128:(r + 1) * 128], lhsT=Ld[:, 0, d * 4 + r, :], rhs=V, start=True, stop=False),
                                       reads=[LdB[0], cbfB], writes=[PSB[bks]], sig=False)
                                    op("pe", lambda r=r, NEG=NEG: PE.matmul(PS(bks)[:, r * 128:(r + 1) * 128], lhsT=ident, rhs=NEG, start=False, stop=True),
                                       reads=[cbfB], writes=[PSB[bks]], sig=(r == 3))
                                op("act", lambda d=d: ACT.activation(out=Et[:, pb, d, :], in_=PS(bks), func=AF.Exp), reads=[PSB[bks]], writes=[EtB[pb][d]])

                        def stageA2(tt):
                            pb = tt % 2
                            for d in range(2):
                                c0 = d * 16 + g * 4
                                weng = "dve" if d == 0 else "pool"
                                wfn = DVE.tensor_tensor if d == 0 else nc.gpsimd.tensor_tensor
                                op(weng, lambda d=d, wfn=wfn: wfn(
                                    out=Wt[:, pb, d * 4:(d + 1) * 4, :], in0=Et[:, pb, d, :].rearrange("p (r i) -> p r i", r=4),
                                    in1=cbt[:, pb, :].unsqueeze(1).to_broadcast([128, 4, 128]), op=ALU.mult),
                                   reads=[EtB[pb][d], cbB[pb]], writes=[WtB[pb][d]])
                                op("dve", lambda d=d, c0=c0: DVE.tensor_tensor(
                                    out=xdt[:, pb, d, :].rearrange("p (r q) -> p r q", r=4), in0=xs_t[:, tt, :].rearrange("p (r q) -> p r q", r=4),
                                    in1=dtv[:, tt, c0:c0 + 4].unsqueeze(2).to_broadcast([128, 4, 64]), op=ALU.mult),
                                   reads=[tkB[tt], smB[tt]], writes=[xdB[pb][d]])

                        def stageB(tt):
                            pb = tt % 2
                            ti = tt - t0
                            tb = tt // 4
                            tsl = slice(tt * 128, (tt + 1) * 128)
                            bky = nbank(0, 6)
                            for r in range(4):
                                xr = xs_t[:, tt, r * 64:(r + 1) * 64]
                                yo_ = PS(bky)[:, r * 64:(r + 1) * 64]
                                op("pe", lambda: PE.matmul(yo_, lhsT=Wt[:, pb, r, :], rhs=xdt[:, pb, 0, r * 64:(r + 1) * 64], start=True, stop=False),
                                   reads=[WtB[pb][0], xdB[pb][0]], writes=[PSB[bky]], sig=False)
                                op("pe", lambda: PE.matmul(yo_, lhsT=Wt[:, pb, 4 + r, :], rhs=xdt[:, pb, 1, r * 64:(r + 1) * 64], start=False, stop=False),
                                   reads=[WtB[pb][1], xdB[pb][1]], writes=[PSB[bky]], sig=False)
                                op("pe", lambda: PE.matmul(yo_, lhsT=dskd[:, g * 4 + r, :], rhs=xr, start=False, stop=True),
                                   reads=[dskB, tkB[tt]], writes=[PSB[bky]], sig=(r == 3))
                            bko = [nbank(0, 6), nbank(0, 6)]
                            for d in range(2):
                                op("pe", lambda d=d: PE.matmul(PS(bko[d])[:, 0:256], lhsT=post[:, 3, tsl], rhs=hbf[:, ti, d, :], start=True, stop=True),
                                   reads=[postB[3][tb], hbB[ti][d]], writes=[PSB[bko[d]]])
                            for d in range(2):
                                c0 = d * 16 + g * 4
                                op("dve", lambda d=d, c0=c0: DVE.tensor_tensor(
                                    out=ytmp[:, d, :].rearrange("p (r q) -> p r q", r=4), in0=PS(bko[d])[:, 0:256].rearrange("p (r q) -> p r q", r=4),
                                    in1=ea[:, tt, c0:c0 + 4].unsqueeze(2).to_broadcast([128, 4, 64]), op=ALU.mult),
                                   reads=[PSB[bko[d]], smB[tt]], writes=[ytB[d]])
                            op("dve", lambda: DVE.tensor_tensor(out=ytmp[:, 0, :], in0=ytmp[:, 0, :], in1=ytmp[:, 1, :], op=ALU.add), reads=ytB, writes=[ytB[0]])
                            op("dve", lambda: DVE.tensor_tensor(out=ytmp[:, 0, :], in0=ytmp[:, 0, :], in1=PS(bky)[:, 0:256], op=ALU.add),
                               reads=[ytB[0], PSB[bky]], writes=[ytB[0]])
                            op("dve", lambda: DVE.tensor_tensor(out=yg[:, pb, :], in0=ytmp[:, 0, :], in1=zs[:, tt, :], op=ALU.mult),
                               reads=[ytB[0], zB[tt]], writes=[ygB[pb]])
                            op("act", lambda: ACT.activation(out=sqj[:], in_=yg[:, pb, :], func=AF.Square, accum_out=ssq[:, tt, g:g + 1]),
                               reads=[ygB[pb]], writes=[sqjB, ssB[tt]])

                        def stageB2(tt):
                            pb = tt % 2
                            tb = tt // 4
                            tsl = slice(tt * 128, (tt + 1) * 128)
                            bkt = nbank(0, 6)
                            for cc in range(2):
                                op("pe", lambda cc=cc: PE.matmul(PS(bkt)[:, cc * 128:(cc + 1) * 128], lhsT=yg[:, pb, cc * 128:(cc + 1) * 128], rhs=ident, start=True, stop=True),
                                   reads=[ygB[pb], cbfB], writes=[PSB[bkt]], sig=(cc == 1))
                            for cc in range(2):
                                ck_ = 2 * g + cc
                                op("act", lambda cc=cc, ck_=ck_: ACT.activation(out=yoT[:, ck_, tsl], in_=PS(bkt)[:, cc * 128:(cc + 1) * 128], func=AF.Identity,
                                                                             scale=pv(f"ssdn{e}")[:, ck_:ck_ + 1]),
                                   reads=[PSB[bkt], parB], writes=[yoB[ck_][tb]])

                        tts = list(range(t0, t0 + ntl))
                        stageA(tts[0])
                        stageA2(tts[0])
                        for i_, tt in enumerate(tts):
                            if i_ + 1 < len(tts):
                                stageA(tts[i_ + 1])
                            if i_ > 0:
                                stageB2(tts[i_ - 1])
                            stageB(tt)
                            if i_ + 1 < len(tts):
                                stageA2(tts[i_ + 1])
                        stageB2(tts[-1])
                    checkpoint('g0scan')
                checkpoint('ssdscan')
                rst = sbt(s2, "rst", [128, NT], F32)
                rstB = Buf()
                rsb = pre[:].rearrange("p a b -> p (a b)").bitcast(F32)
                rsbB = [preB, preB, preB]
                dg = sqj[:].bitcast(F32)
                dgB = sqjB
                op("dve", lambda: DVE.tensor_reduce(out=rst[:], in_=ssq[:], axis=mybir.AxisListType.X, op=ALU.add), reads=ssB, writes=[rstB])
                op("act", lambda: ACT.activation(out=rst[:], in_=rst[:], func=AF.Ln, scale=1.0 / 1024.0, bias=KC(1)), reads=[rstB, kcB_], writes=[rstB])
                op("act", lambda: ACT.activation(out=rst[:], in_=rst[:], func=AF.Exp, scale=-0.5), reads=[rstB], writes=[rstB])
                for tt in range(NT):
                    tb = tt // 4
                    op("dve", lambda tt=tt: DVE.tensor_scalar(out=dg, in0=ident_f, scalar1=rst[:, tt:tt + 1], scalar2=None, op0=ALU.mult),
                       reads=[rstB, cfB], writes=[dgB])
                    op("pe", lambda tt=tt: PE.matmul(PS(6)[:, 0:128], lhsT=ones_f, rhs=dg, start=True, stop=True), reads=[dgB, cfB], writes=[PSB[6]])
                    op("act", lambda tt=tt: ACT.copy(out=rsb[:, tt * 128:(tt + 1) * 128], in_=PS(6)[:, 0:128]), reads=[PSB[6]], writes=[rsbB[tb]])
                outproj(l, [f"O1_{e}_0", f"O1_{e}_1"], yoT, yoB, rowscale=(rsb, rsbB))
                P.fence()

            checkpoint('ssd')
            with ExitStack() as s2:
                rope = sbt(s2, "rope", [128, 2, 1024], F32)
                ropeB = Buf()
                for i in range(2):
                    op("sp", lambda i=i: nc.sync.dma_start(out=rope[:, i, :], in_=rope_in[i]), writes=[ropeB], dkey="rope")
                qT = sbt(s2, "qT", [128, 2, T], BF16)
                qB = BG(2, 3)
                qraw = sbt(s2, "qraw", [128, 2, 1024], BF16)
                qrB = BG(2, 2)
                rt = sbt(s2, "rt", [128, 2, 2, 512], BF16)
                rtB = BG(2, 2)
                kc32 = sbt(s2, "kc32", [128, 2, 512], F32)
                kcB = [Buf(), Buf()]
                vc32 = sbt(s2, "vc32", [128, 2, 4, 128], F32)
                vcB = [Buf(), Buf()]
                kcb = sbt(s2, "kcb", [128, 512], BF16)
                kcbB = Buf()
                va = sbt(s2, "va", [128, 16, 132], BF16)
                vaB = BG(16)
                op("dve", lambda: DVE.memset(va[:], 1.0), writes=vaB)
                kvo = sbt(s2, "kvo", [128, 2, 2, 4, 128], F32)
                kvB = BG(2, 2)
                ET = sbt(s2, "ET", [128, 2, 12, 2, 256], BF16)
                ETB = BG(2, 12)
                osb = sbt(s2, "osb", [128, 2, 128], F32)
                osB = BG(2)
                sm = sbt(s2, "sm", [128, 2, 3, 2], F32)
                smB_ = BG(2, 3)
                onb = sbt(s2, "onb", [128, 4, 128], BF16)
                onB = BG(4)
                oTs = sbt(s2, "oTs", [128, 512], F32)
                oTB = Buf()
                sqb = sbt(s2, "sqb", [128, 512], BF16)
                sqB_ = Buf()
                rsq = sbt(s2, "rsq", [128, 512], F32)
                rsqB = Buf()

                def load_cache(h):
                    pb = h % 2
                    op("sp", lambda: nc.sync.dma_start(out=kc32[:, pb, :], in_=ck_in[e, h]), writes=[kcB[pb]], dkey=f"kc{pb}")
                    src = cv_in[e].rearrange("(kt p) f -> p kt f", p=128)[:, :, h * 128:(h + 1) * 128]
                    op("sp", lambda: nc.sync.dma_start(out=vc32[:, pb], in_=src), writes=[vcB[pb]], dkey=f"vc{pb}")
                load_cache(0)
                tail_prev = [None]
                sk = list(range(0, 4)) + list(range(8, 16))
                blocks = [((0, 256), [4, 5]), ((256, 256), [6, 7]), ((512, 256), sk), ((768, 256), sk), ((1024, 256), sk), ((1280, 256), sk)]
                for h in range(8):
                    if h + 1 < 8:
                        load_cache(h + 1)
                    pb = h % 2
                    wt, wb = wget(f"QKV_{e}_{h}")
                    for qk in range(2):
                        for kc in range(8):
                            for tb in range(3):
                                op("pe", lambda: PE.matmul(PS(tb), lhsT=wt[:, kc, qk * 128:(qk + 1) * 128], rhs=hT[:, kc, tb * 512:(tb + 1) * 512],
                                                           start=(kc == 0), stop=(kc == 7)),
                                   reads=[wb, hB[kc][tb]], writes=[PSB[tb]], sig=(kc == 7))
                        op("dve", lambda: DVE.tensor_copy(out=qT[:, qk, 0:512], in_=PS(0)), reads=[PSB[0]], writes=[qB[qk][0]])
                        for sb_ in range(2):
                            op("dve", lambda: DVE.tensor_copy(out=qraw[:, qk, sb_ * 512:(sb_ + 1) * 512], in_=PS(1 + sb_)), reads=[PSB[1 + sb_]], writes=[qrB[qk][sb_]])
                    if tail_prev[0] is not None:
                        tail_prev[0]["c"]()
                    for tt in range(NT):
                        tb = tt // 4
                        tsl = slice(tt * 128, (tt + 1) * 128)
                        bk = nbank(3, 6)
                        ncol = 256 if tt < 4 else 128
                        c0 = 128 if tt < 4 else 256
                        for kc in range(8):
                            op("pe", lambda: PE.matmul(PS(bk)[:, 0:ncol], lhsT=hT[:, kc, tsl], rhs=wt[:, kc, c0:c0 + ncol], start=(kc == 0), stop=(kc == 7)),
                               reads=[wb, hB[kc][tb]], writes=[PSB[bk]], sig=(kc == 7))
                        vo = ncol - 128
                        if tt < 4:
                            op("act", lambda: ACT.copy(out=va[:, 4 + tt, 0:128], in_=PS(bk)[:, vo:vo + 128]), reads=[PSB[bk]], writes=[vaB[4 + tt]])
                            op("act", lambda: ACT.copy(out=kvo[:, pb, 0, tt, :], in_=PS(bk)[:, 0:128]), reads=[PSB[bk]], writes=[kvB[pb][0]])
                            op("act", lambda: ACT.copy(out=kvo[:, pb, 1, tt, :], in_=PS(bk)[:, 128:256]), reads=[PSB[bk]], writes=[kvB[pb][1]])
                        else:
                            op("act", lambda: ACT.copy(out=va[:, 4 + tt, 0:128], in_=PS(bk)[:, vo:vo + 128]), reads=[PSB[bk]], writes=[vaB[4 + tt]])
                        if tail_prev[0] is not None and tt == 5:
                            tail_prev[0]["a"]()
                        if tail_prev[0] is not None and tt == 11:
                            tail_prev[0]["b"]()
                    if tail_prev[0] is not None:
                        tail_prev[0]["n"]()
                        tail_prev[0] = None
                    for qk in range(2):
                        for sb_ in range(2):
                            bk = nbank(0, 3)
                            op("pe", lambda: PE.matmul(PS(bk), lhsT=cb("Psw"), rhs=qraw[:, qk, sb_ * 512:(sb_ + 1) * 512], start=True, stop=True),
                               reads=[qrB[qk][sb_], cbfB], writes=[PSB[bk]])
                            ss_ = slice(sb_ * 512, (sb_ + 1) * 512)
                            rp = (qk * 2 + sb_) % 2
                            op("pool", lambda: nc.gpsimd.tensor_tensor(out=rt[:, rp, 0, :], in0=qraw[:, qk, ss_], in1=rope[:, 0, ss_], op=ALU.mult),
                               reads=[qrB[qk][sb_], ropeB], writes=[rtB[rp][0]])
                            op("dve", lambda: DVE.tensor_tensor(out=rt[:, rp, 1, :], in0=PS(bk), in1=rope[:, 1, ss_], op=ALU.mult),
                               reads=[PSB[bk], ropeB], writes=[rtB[rp][1]])
                            op("pool", lambda: nc.gpsimd.tensor_tensor(out=qT[:, qk, 512 + sb_ * 512:1024 + sb_ * 512], in0=rt[:, rp, 0, :], in1=rt[:, rp, 1, :], op=ALU.add),
                               reads=rtB[rp], writes=[qB[qk][1 + sb_]])
                    for i in range(2):
                        dst = (nk_out if i == 0 else nv_out)[e].rearrange("(tt p) f -> p tt f", p=128)[:, :, h * 128:(h + 1) * 128]
                        op("sp", lambda: nc.sync.dma_start(out=dst, in_=kvo[:, pb, i]), reads=[kvB[pb][i]], dkey=f"kvo{pb}")
                    op("act", lambda: ACT.copy(out=kcb[:], in_=kc32[:, pb, :]), reads=[kcB[pb]], writes=[kcbB])
                    op("pool", lambda: nc.gpsimd.tensor_copy(out=va[:, 0:4, 0:128], in_=vc32[:, pb]), reads=[vcB[pb]], writes=vaB[0:4])

                    def pv_ops(bi):
                        (q0, qn), kts = blocks[bi]
                        eb = bi % 2
                        nk_ = len(kts)
                        lst = []
                        for qi in range(2):
                            for mm_ in range(2):
                                for ki, kt in enumerate(kts):
                                    def f(qi=qi, mm_=mm_, ki=ki, kt=kt):
                                        op("pe", lambda: PE.matmul(PS(4 + 2 * eb + qi)[:, mm_ * 132:mm_ * 132 + 129], lhsT=ET[:, eb, ki, mm_, qi * 128:(qi + 1) * 128],
                                                                   rhs=va[:, kt, 0:129], start=(ki == 0), stop=(ki == nk_ - 1)),
                                           reads=[ETB[eb][ki], vaB[kt]], writes=[PSB[4 + 2 * eb + qi]], sig=(mm_ == 1 and ki == nk_ - 1))
                                    lst.append(f)
                        return lst

                    def scores(bi, fill):
                        (q0, qn), kts = blocks[bi]
                        eb = bi % 2
                        qtb = q0 // 512
                        per = -(-len(fill) // len(kts)) if fill else 0
                        for ki, kt in enumerate(kts):
                            bp = nbank(0, 2)
                            for mm_ in range(2):
                                ps_ = slice(mm_ * 64, (mm_ + 1) * 64)
                                if kt < 4:
                                    lhs = kcb[ps_, kt * 128:(kt + 1) * 128]
                                    rd = [kcbB]
                                else:
                                    tk = kt - 4
                                    lhs = qT[ps_, 1, tk * 128:(tk + 1) * 128]
                                    rd = [qB[1][tk // 4]]
                                op("pe", lambda: PE.matmul(PS(2 * bp + mm_)[:, 0:qn], lhsT=lhs, rhs=qT[ps_, 0, q0:q0 + qn], start=True, stop=True),
                                   reads=rd + [qB[0][qtb]], writes=[PSB[2 * bp + mm_]], sig=(mm_ == 1))
                            op("act", lambda: ACT.activation(out=ET[:, eb, ki, :, 0:qn], in_=psum[bp][:, :, 0:qn], func=AF.Exp, scale=0.125, bias=KC(3)),
                               reads=[PSB[2 * bp], PSB[2 * bp + 1], kcB_], writes=[ETB[eb][ki]])
                            for _ in range(per):
                                if fill:
                                    fill.pop(0)()
                        while fill:
                            fill.pop(0)()

                    def chain(bi):
                        (q0, qn), kts = blocks[bi]
                        eb = bi % 2
                        t0_ = q0 // 128
                        sl4 = t0_ % 4
                        pvp = psum[2 + eb]
                        pq = [PSB[4 + 2 * eb], PSB[5 + 2 * eb]]
                        op("dve", lambda: DVE.reciprocal(out=sm[:, eb, 0, :].unsqueeze(2), in_=pvp[:, :, 128:129]), reads=pq, writes=[smB_[eb][0]])
                        op("dve", lambda: DVE.reciprocal(out=sm[:, eb, 1, :].unsqueeze(2), in_=pvp[:, :, 260:261]), reads=pq, writes=[smB_[eb][1]])
                        op("dve", lambda: DVE.tensor_scalar(out=sm[:, eb, 2, :], in0=sm[:, eb, 1, :], scalar1=nlam[:, 3:4], scalar2=None, op0=ALU.mult),
                           reads=[smB_[eb][1], nlB], writes=[smB_[eb][2]])
                        for qi in range(2):
                            bq = 4 + 2 * eb + qi
                            op("dve", lambda: DVE.tensor_scalar(out=osb[:, qi, :], in0=PS(bq)[:, 0:128], scalar1=sm[:, eb, 0, qi:qi + 1], scalar2=None, op0=ALU.mult),
                               reads=[PSB[bq], smB_[eb][0]], writes=[osB[qi]])
                            op("dve", lambda: DVE.scalar_tensor_tensor(out=onb[:, sl4 + qi, :], in0=PS(bq)[:, 132:260], scalar=sm[:, eb, 2, qi:qi + 1], in1=osb[:, qi, :],
                                                                       op0=ALU.mult, op1=ALU.add),
                               reads=[PSB[bq], smB_[eb][2], osB[qi]], writes=[onB[sl4 + qi]])

                    def norm1(tb):
                        bk = nbank(2, 4)
                        for q in range(4):
                            op("pe", lambda: PE.matmul(PS(bk)[:, q * 128:(q + 1) * 128], lhsT=onb[:, q, :], rhs=ident, start=True, stop=True),
                               reads=[onB[q], cbfB], writes=[PSB[bk]], sig=(q == 3))
                        op("dve", lambda: DVE.tensor_scalar(out=oTs[:], in0=PS(bk), scalar1=1.0, scalar2=None, op0=ALU.mult), reads=[PSB[bk]], writes=[oTB])
                        op("pool", lambda: nc.gpsimd.tensor_tensor(out=sqb[:], in0=oTs[:], in1=oTs[:], op=ALU.mult), reads=[oTB], writes=[sqB_])
                        bk2 = nbank(2, 4)
                        op("pe", lambda: PE.matmul(PS(bk2), lhsT=ones_b, rhs=sqb[:], start=True, stop=True), reads=[sqB_, cbfB], writes=[PSB[bk2]])
                        op("dve", lambda: DVE.tensor_scalar(out=rsq[:], in0=PS(bk2), scalar1=1.0 / 128.0, scalar2=None, op0=ALU.mult), reads=[PSB[bk2]], writes=[rsqB])
                        return bk2

                    def norm2(tb, bk2, h=h):
                        op("act", lambda: ACT.activation(out=rsq[:], in_=rsq[:], func=AF.Ln, bias=KC(1)), reads=[rsqB, kcB_], writes=[rsqB])
                        op("act", lambda: ACT.activation(out=rsq[:], in_=rsq[:], func=AF.Exp, scale=-0.5), reads=[rsqB], writes=[rsqB])
                        op("dve", lambda: DVE.scalar_tensor_tensor(out=yoT[:, h, tb * 512:(tb + 1) * 512], in0=oTs[:], scalar=sublnS[:, 0:1], in1=rsq[:],
                                                                   op0=ALU.mult, op1=ALU.mult),
                           reads=[oTB, rsqB, slB], writes=[yoB[h][tb]])

                    def norm1a(tb):
                        bk = nbank(2, 4)
                        for q in range(4):
                            op("pe", lambda: PE.matmul(PS(bk)[:, q * 128:(q + 1) * 128], lhsT=onb[:, q, :], rhs=ident, start=True, stop=True),
                               reads=[onB[q], cbfB], writes=[PSB[bk]], sig=(q == 3))
                        op("dve", lambda: DVE.tensor_scalar(out=oTs[:], in0=PS(bk), scalar1=1.0, scalar2=None, op0=ALU.mult), reads=[PSB[bk]], writes=[oTB])
                        op("pool", lambda: nc.gpsimd.tensor_tensor(out=sqb[:], in0=oTs[:], in1=oTs[:], op=ALU.mult), reads=[oTB], writes=[sqB_])

                    def norm1b(tb):
                        bk2 = nbank(2, 4)
                        op("pe", lambda: PE.matmul(PS(bk2), lhsT=ones_b, rhs=sqb[:], start=True, stop=True), reads=[sqB_, cbfB], writes=[PSB[bk2]])
                        op("dve", lambda: DVE.tensor_scalar(out=rsq[:], in0=PS(bk2), scalar1=1.0 / 128.0, scalar2=None, op0=ALU.mult), reads=[PSB[bk2]], writes=[rsqB])

                    scores(0, [])
                    pend = None
                    pend2 = None
                    nb_ = len(blocks)
                    for bi in range(nb_):
                        fill = pv_ops(bi)
                        if pend is not None and len(fill) >= 40:
                            tb_ = pend
                            fill.insert(8, lambda tb_=tb_: norm1a(tb_))
                            fill.insert(36, lambda tb_=tb_: norm1b(tb_))
                            pend2 = tb_
                            pend = None
                        if bi + 1 < nb_:
                            scores(bi + 1, fill)
                        else:
                            while fill:
                                fill.pop(0)()
                        if pend2 is not None:
                            norm2(pend2, None)
                            pend2 = None
                        if bi + 1 < nb_:
                            chain(bi)
                            if bi % 2 == 1:
                                pend = bi // 2
                    assert pend is None and pend2 is None

                    tail_prev[0] = {"c": (lambda chain=chain, nb_=nb_: chain(nb_ - 1)), "a": (lambda f=norm1a: f(2)),
                                    "b": (lambda f=norm1b: f(2)), "n": (lambda f=norm2: f(2, None))}
                if tail_prev[0] is not None:
                    for k_ in ("c", "a", "b", "n"):
                        tail_prev[0][k_]()
                outproj(l, [f"O2_{e}_0", f"O2_{e}_1"], yoT, yoB)
                P.fence()
            P.fence()

    sublnS = sbt(es, "sublnS", [128, 1], F32)
    slB = Buf()

    try:
        if DEPTH_RUN > 0:
            for t in range(4):
                ada_tile(0, t)
            ada_finish(0, 0)
            checkpoint('ada0')
        for l in range(DEPTH_RUN):
            m = mod[l % 2]
            g = gsc[l % 2]
            with ExitStack() as sl_:
                hT = sbt(sl_, "hT", [128, 8, T], BF16)
                hB = BG(8, 3)
                modulate(sl_, hT, hB, lambda kc, v: g[:, 0, kc, v:v + 1], lambda kc, v: m[:, kc, v:v + 1], [modB[l % 2], gscB[l % 2]])
                checkpoint('mod')
                if l % 2 == 0:
                    li = 0.8 - 0.6 * math.exp(-0.3 * l)
                    op("dve", lambda: DVE.tensor_scalar(out=sublnS[:], in0=pv(f"subln{l // 2}"), scalar1=1.0 - li, scalar2=None, op0=ALU.mult), reads=[parB], writes=[slB])
                    ab_layer(l, hT, hB)
                else:
                    fourier_layer(l, hT, hB)
                P.fence()
            checkpoint('mixer')
            ffn(l)
            checkpoint('ffn')
        with ExitStack() as sl_:
            yo = sbt(sl_, "yo", [128, 8, T], F32)
            yB = BG(8, 3)
            nfs = sbt(sl_, "nfs", [128, 8], F32)
            nfB = Buf()
            op("dve", lambda: DVE.tensor_scalar(out=nfs[:], in0=pv("nfin"), scalar1=32.0, scalar2=None, op0=ALU.mult), reads=[parB], writes=[nfB])
            modulate(sl_, yo, yB, lambda kc, v: nfs[:, kc:kc + 1], lambda kc, v: None, [nfB])
            yout = yT_out.rearrange("(kc p) t -> p kc t", p=128)
            for kc in range(8):
                op("sp", lambda kc=kc: nc.sync.dma_start(out=yout[:, kc, :], in_=yo[:, kc, :]), reads=yB[kc], dkey="st2")

    except _Stop:
        pass
    for k in list(P.isdma):
        if not k.startswith('ring'):
            nc.sync.wait_ge(P.sems[k], P.cnt[k])
    assert STOP_AT is not None or st["used"] == len(plan), (st["used"], len(plan))
    es.close()
    _CACHE['trace'] = P.trace
    return nc, P.nins


_CACHE = {}


def kernel(x_prompt, x_sample, cache_k, cache_v, state_ssd_fwd, state_ssd_bwd, c, c_ctx, w_ada, b_ada, norm_mix,
           norm_ffn, w_in_ab, conv_w, conv_b, dt_bias, a_log, d_skip, ssd_norm, lambda_qk, subln, w_out_ab,
           w_four, b_four, w_ffn_in, w_ffn_out, norm_final):
    f = lambda a: np.ascontiguousarray(np.asarray(a, dtype=np.float32))
    x_prompt, x_sample, cache_k, cache_v = f(x_prompt), f(x_sample), f(cache_k), f(cache_v)
    state_ssd_fwd, state_ssd_bwd, c, c_ctx = f(state_ssd_fwd), f(state_ssd_bwd), f(c), f(c_ctx)
    if "nc" not in _CACHE:
        _CACHE["nc"] = build_program()[0]
        _CACHE["con"] = make_consts()
    nc = _CACHE["nc"]
    con, dft, rope = _CACHE["con"]
    w_inp = f(np.asarray(w_in_ab)[:, :, win_perm()])
    w_ffi = f(np.asarray(w_ffn_in)[:, :, ffi_perm()])
    shared = {"con": con, "dft": dft, "rope": rope, "w_ada": f(w_ada), "w_inp": w_inp, "w_out": f(w_out_ab),
              "w_four": f(w_four), "w_ffi": w_ffi, "w_ffo": f(w_ffn_out)}
    par0 = np.zeros((128, NPAR), np.float32)

    def put(name, a):
        o, w = PL[name]
        par0[:, o:o + w] = np.asarray(a, np.float32).reshape(128, w)
    fm = lambda v, n: np.asarray(v, np.float32).reshape(n, 128).T
    rb = lambda v: np.broadcast_to(np.asarray(v, np.float32).reshape(1, -1), (128, np.asarray(v).size))
    put("nfin", fm(norm_final, 8))
    for l in range(4):
        put(f"nm{l}", fm(norm_mix[l], 8))
        put(f"nf{l}", fm(norm_ffn[l], 8))
        put(f"bada{l}", fm(b_ada[l], 48))
    for e in range(2):
        cw = np.asarray(conv_w[e], np.float32)
        cbv = np.asarray(conv_b[e], np.float32)
        cwm = np.zeros((128, 80), np.float32)
        cbm = np.zeros((128, 16), np.float32)
        for sl in range(16):
            ch = conv_chan(sl)
            cwm[:, sl * 5:(sl + 1) * 5] = cw[:, ch].T
            cbm[:, sl] = cbv[ch]
        put(f"cw{e}", cwm)
        put(f"cb{e}", cbm)
        put(f"ssdn{e}", fm(ssd_norm[e], 8))
        put(f"subln{e}", np.asarray(subln[e], np.float32).reshape(128, 1))
        put(f"dtb{e}", rb(dt_bias[e]))
        put(f"alog{e}", rb(a_log[e]))
        put(f"dsk{e}", rb(d_skip[e]))
        put(f"lqk{e}", rb(lambda_qk[e]))
        put(f"b4{e}", fm(b_four[e], 8))
    in_maps = []
    for i in range(8):
        s = i // 2
        xt = np.concatenate([x_prompt[2 * i], x_prompt[2 * i + 1], x_sample[s]], 0)
        p = par0.copy()
        cvec = np.stack([c_ctx, c[s]], 0).reshape(2, 8, 128).transpose(2, 1, 0).reshape(128, 16)
        o, w = PL["cvec"]
        p[:, o:o + w] = cvec
        mp = dict(shared)
        mp["xT_in"] = np.ascontiguousarray(xt.T)
        mp["par"] = p
        mp["ck"] = np.ascontiguousarray(cache_k[s].reshape(2, 512, 8, 128).transpose(0, 2, 3, 1))
        mp["cv"] = np.ascontiguousarray(cache_v[s].reshape(2, 512, 1024))
        mp["sf"] = np.ascontiguousarray(state_ssd_fwd[s].reshape(2, 1024, 128).transpose(0, 2, 1))
        mp["sb"] = np.ascontiguousarray(state_ssd_bwd[s].reshape(2, 1024, 128).transpose(0, 2, 1))
        in_maps.append(mp)
    res = run_bass_kernel_spmd(nc, in_maps[:DBG_CORES], core_ids=list(range(DBG_CORES)))
    R = list(res.results) + [res.results[0]] * (8 - DBG_CORES)
    y_prompt = np.zeros((16, 256, 1024), np.float32)
    y_sample = np.zeros((4, 1024, 1024), np.float32)
    new_k = np.zeros((16, 2, 256, 8, 2, 64), np.float32)
    new_v = np.zeros((16, 2, 256, 8, 128), np.float32)
    new_f = np.zeros((16, 2, 16, 64, 128), np.float32)
    new_b = np.zeros((16, 2, 16, 64, 128), np.float32)
    for i in range(8):
        yT = np.asarray(R[i]["yT"])
        for q in range(2):
            b = 2 * i + q
            y_prompt[b] = yT[:, q * 256:(q + 1) * 256].T
            new_k[b] = np.asarray(R[i]["nk"])[:, q * 256:(q + 1) * 256].reshape(2, 256, 8, 2, 64)
            new_v[b] = np.asarray(R[i]["nv"])[:, q * 256:(q + 1) * 256].reshape(2, 256, 8, 128)
            new_f[b] = np.asarray(R[i]["nf"])[:, q].transpose(0, 2, 1).reshape(2, 16, 64, 128)
            new_b[b] = np.asarray(R[i]["nb"])[:, q].transpose(0, 2, 1).reshape(2, 16, 64, 128)
        if i % 2 == 0:
            y_sample[i // 2] = yT[:, 512:].T
    return (y_prompt, y_sample, new_k, new_v, new_f, new_b)
```

```python
import math
import numpy as np
from contextlib import ExitStack
import concourse.bass as bass
import concourse.mybir as mybir
from concourse.bass_utils import run_bass_kernel_spmd

F32 = mybir.dt.float32
BF16 = mybir.dt.bfloat16
AF = mybir.ActivationFunctionType
ALU = mybir.AluOpType

DEPTH_RUN = 4
STOP_AT = None
SKIP_KVO = False
DBG_CORES = 8


class _Stop(Exception):
    pass


_STOPPED = [False]


def checkpoint(name):
    if STOP_AT == name:
        _STOPPED[0] = True
D = 1024
T = 1536
NT = 12
EPS = 1e-6
SEQS = [(0, 2), (2, 2), (4, 8)]
FFH = 2816
NJ = 22
IN_AB = 6176
RING_SLOTS = 3
RING_ELEMS = 4096


def _layout(items):
    d, off = {}, 0
    for n, w in items:
        d[n] = (off, w)
        off += w
    return d, off


def param_layout():
    it = [("cvec", 16), ("nfin", 8)]
    for l in range(4):
        it += [(f"nm{l}", 8), (f"nf{l}", 8), (f"bada{l}", 48)]
    for e in range(2):
        it += [(f"cw{e}", 80), (f"cb{e}", 16), (f"ssdn{e}", 8), (f"subln{e}", 1), (f"dtb{e}", 32),
               (f"alog{e}", 32), (f"dsk{e}", 16), (f"lqk{e}", 256), (f"b4{e}", 8)]
    return _layout(it)


def const_layout():
    it = [("ident", 128), ("ones", 128), ("Uf", 128), ("Ub", 128), ("Vf", 128), ("Vb", 128),
          ("NEGf", 128), ("NEGb", 128), ("Psw", 128), ("Cc", 512), ("Sc", 512), ("CL256", 512), ("nSL256", 512)]
    return _layout(it)


PL, NPAR = param_layout()
CL, NCON = const_layout()


def conv_chan(sl):
    g, s = divmod(sl, 4)
    if s < 2:
        return g * 256 + s * 128 + np.arange(128)
    if s == 2:
        return 1024 + g * 128 + np.arange(128)
    return 1536 + g * 128 + np.arange(128)


def win_perm():
    o_z, o_x, o_dt, o_q, o_k, o_v = 0, 1024, 3072, 3104, 4128, 5152
    cols = list(o_dt + np.arange(32))
    for g in range(4):
        cols += list(o_x + g * 256 + np.arange(256))
        cols += list(o_x + 1024 + g * 128 + np.arange(128))
        cols += list(o_x + 1536 + g * 128 + np.arange(128))
        cols += list(o_z + g * 256 + np.arange(256))
    for h in range(8):
        cols += list(o_q + h * 128 + np.arange(128))
        cols += list(o_k + h * 128 + np.arange(128))
        cols += list(o_v + h * 128 + np.arange(128))
    return np.array(cols)


def ffi_perm():
    cols = []
    for t in range(11):
        for j in (2 * t, 2 * t + 1):
            cols += list(j * 128 + np.arange(128))
        for j in (2 * t, 2 * t + 1):
            cols += list(FFH + j * 128 + np.arange(128))
    return np.array(cols)


def make_consts():
    c = np.zeros((128, NCON), np.float32)
    t = np.arange(128)[:, None]
    j = np.arange(128)[None, :]

    def put(n, a):
        o, w = CL[n]
        c[:, o:o + w] = a.reshape(128, w)
    put("ident", (t == j).astype(np.float32))
    put("ones", np.ones((128, 128), np.float32))
    put("Uf", (t > j).astype(np.float32))
    put("Ub", (t < j).astype(np.float32))
    put("Vf", (t <= j).astype(np.float32))
    put("Vb", (t >= j).astype(np.float32))
    put("NEGf", -30000.0 * (j < t))
    put("NEGb", -30000.0 * (j > t))
    d = np.arange(128) % 64
    partner = np.where(d % 32 < 16, np.arange(128) + 16, np.arange(128) - 16)
    psw = np.zeros((128, 128), np.float32)
    psw[partner, np.arange(128)] = 1.0
    put("Psw", psw)
    n = np.arange(256)
    ang = 2 * np.pi * np.outer(n, n) / 256.0
    cc = (np.cos(ang) / 16.0).reshape(2, 128, 256).transpose(1, 0, 2)
    ss = (np.sin(ang) / 16.0).reshape(2, 128, 256).transpose(1, 0, 2)
    put("Cc", cc)
    put("Sc", ss)
    put("CL256", cc)
    put("nSL256", -ss)
    n = np.arange(1024)
    ang = 2 * np.pi * np.outer(n, n) / 1024.0
    cl = (np.cos(ang) / 32.0).reshape(8, 128, 1024).transpose(1, 0, 2)
    sl = (-np.sin(ang) / 32.0).reshape(8, 128, 1024).transpose(1, 0, 2)
    dft = np.ascontiguousarray(np.stack([cl, sl], 0).astype(np.float32))
    tok = np.arange(1024)
    row = (tok // 64).astype(np.float64)
    col = (tok % 64).astype(np.float64)
    freq = 10000.0 ** (-np.arange(16, dtype=np.float64) / 16.0)
    cosT = np.zeros((128, 1024), np.float32)
    sinT = np.zeros((128, 1024), np.float32)
    for p in range(128):
        dd = p % 64
        pos = row if dd < 32 else col
        a = pos * freq[dd % 16]
        cosT[p] = np.cos(a)
        sinT[p] = (-np.sin(a)) if (dd % 32) < 16 else np.sin(a)
    rope = np.ascontiguousarray(np.stack([cosT, sinT], 0))
    return c, dft, rope


class Buf:
    __slots__ = ("w", "r", "x", "fresh")

    def __init__(self, x=False):
        self.w = None
        self.r = {}
        self.fresh = True
        self.x = x


def BG(*shape):
    if len(shape) == 1:
        return [Buf() for _ in range(shape[0])]
    return [BG(*shape[1:]) for _ in range(shape[0])]


def flat(x):
    if isinstance(x, Buf):
        return [x]
    out = []
    for y in x:
        out += flat(y)
    return out


class Prog:
    def __init__(self, nc, es):
        self.nc = nc
        self.es = es
        self.E = {"pe": nc.tensor, "act": nc.scalar, "dve": nc.vector, "pool": nc.gpsimd, "sp": nc.sync}
        self.sems, self.cnt = {}, {}
        self.known = {e: {} for e in self.E}
        self.floor = {}
        self.isdma = set()
        for e in ("pe", "act", "dve", "pool"):
            self.sems[e] = es.enter_context(nc.semaphore("c_" + e))
            self.cnt[e] = 0
        self.nins = 0
        self.trace = {}

    def dsem(self, key):
        if key not in self.sems:
            self.sems[key] = self.es.enter_context(self.nc.semaphore("d_" + key))
            self.cnt[key] = 0
            self.isdma.add(key)
        return self.sems[key]

    def op(self, e, fn, reads=(), writes=(), sig=True, dkey=None, nofence=False):
        if _STOPPED[0]:
            return None
        reqs = {}

        def need(k, v):
            if k in self.isdma:
                v = self.cnt[k]
            elif k == e:
                if e == "pe":
                    return
            if reqs.get(k, 0) < v:
                reqs[k] = v
        reads = flat(reads)
        writes = flat(writes)
        if any(b.fresh for b in reads) or any(b.fresh for b in writes):
            for k, v in self.floor.items():
                if k != e:
                    need(k, v)
                elif e != "pe" and reqs.get(k, 0) < v:
                    reqs[k] = v
            for b in reads:
                b.fresh = False
            for b in writes:
                b.fresh = False
        for b in reads:
            if b.w is not None:
                need(*b.w)
            if b.x:
                for k, v in b.r.items():
                    if k != e:
                        need(k, v)
        for b in writes:
            if b.w is not None:
                need(*b.w)
            for k, v in b.r.items():
                need(k, v)
        kn = self.known[e]
        for k, v in reqs.items():
            if kn.get(k, 0) < v:
                self.E[e].wait_ge(self.sems[k], v)
                kn[k] = v
                self.trace.setdefault(e, []).append(("w", k, v))
        ins = fn()
        self.nins += 1
        self.trace.setdefault(e, []).append(("i", dkey if dkey is not None else (e if sig else None), 16 if dkey is not None else 1))
        if dkey is not None:
            sem = self.dsem(dkey)
            ins.then_inc(sem, 16)
            self.cnt[dkey] += 16
            stamp = (dkey, self.cnt[dkey])
        elif sig:
            ins.then_inc(self.sems[e], 1)
            self.cnt[e] += 1
            stamp = (e, self.cnt[e])
        else:
            stamp = (e, self.cnt[e] + 1)
        for b in writes:
            b.w = stamp
            b.r = {}
        k, v = stamp
        for b in reads:
            if b.r.get(k, 0) < v:
                b.r[k] = v
        return ins

    def fence(self):
        self.floor = {k: v for k, v in self.cnt.items() if not k.startswith("ring")}


def build_program():
    _STOPPED[0] = False
    nc = bass.Bass("TRN2", target_bir_lowering=False)
    dr = lambda n, s, kind="ExternalInput": nc.dram_tensor(n, s, F32, kind=kind).ap()
    xT_in = dr("xT_in", [D, T])
    par_in = dr("par", [128, NPAR])
    con_in = dr("con", [128, NCON])
    dft_in = dr("dft", [2, 128, 8, 1024])
    rope_in = dr("rope", [2, 128, 1024])
    ck_in = dr("ck", [2, 8, 128, 512])
    cv_in = dr("cv", [2, 512, 1024])
    sf_in = dr("sf", [2, 128, 1024])
    sb_in = dr("sb", [2, 128, 1024])
    w_ada = dr("w_ada", [4, D, 6144])
    w_inp = dr("w_inp", [2, D, IN_AB])
    w_out = dr("w_out", [2, 2048, D])
    w_four = dr("w_four", [2, D, D])
    w_ffi = dr("w_ffi", [4, D, 2 * FFH])
    w_ffo = dr("w_ffo", [4, FFH, D])
    yT_out = dr("yT", [D, T], "ExternalOutput")
    nk_out = dr("nk", [2, 512, 1024], "ExternalOutput")
    nv_out = dr("nv", [2, 512, 1024], "ExternalOutput")
    nf_out = dr("nf", [2, 2, 128, 1024], "ExternalOutput")
    nb_out = dr("nb", [2, 2, 128, 1024], "ExternalOutput")

    es = ExitStack()
    P = Prog(nc, es)
    op = P.op
    PE, ACT, DVE = nc.tensor, nc.scalar, nc.vector

    uid = {"n": 0}

    def sbt(stack, name, shape, dt):
        uid["n"] += 1
        return stack.enter_context(nc.sbuf_tensor(f"{name}_{uid['n']}", shape, dt))

    xT = sbt(es, "xT", [128, 8, T], F32)
    xB = BG(8, 3)
    par = sbt(es, "par_sb", [128, NPAR], F32)
    parB = Buf()
    cbf = sbt(es, "cbf", [128, 9 * 128], BF16)
    cbfB = Buf()
    cf32 = sbt(es, "cf32", [128, 4 * 128], F32)
    cfB = Buf()
    scT = sbt(es, "scT", [128, 8, 2], BF16)
    scB = Buf()
    mod = [sbt(es, f"mod{i}", [128, 48, 2], F32) for i in range(2)]
    modB = [Buf(), Buf()]
    gsc = [sbt(es, f"gsc{i}", [128, 2, 8, 2], F32) for i in range(2)]
    gscB = [Buf(), Buf()]
    ring = sbt(es, "ring", [128, RING_SLOTS, RING_ELEMS], BF16)
    ringB = BG(RING_SLOTS)
    psum = [es.enter_context(nc.psum_tensor(f"ps{i}", [128, 2, 512], F32)) for i in range(4)]
    PSB = [Buf(x=True) for _ in range(8)]

    def PS(i):
        return psum[i // 2][:, i % 2, :]

    def pv(name):
        o, w = PL[name]
        return par[:, o:o + w]

    def cb(name):
        o, w = CL[name]
        return cbf[:, o:o + w]

    plan = []

    def wtile(dram_ap, kc, ncols, tag):
        plan.append((dram_ap, kc, ncols, tag))

    for l in range(DEPTH_RUN):
        wa0 = w_ada[0].rearrange("(kc p) f -> p kc f", p=128)
        if l == 0:
            for t in range(4):
                wtile(wa0[:, :, t * 512:(t + 1) * 512], 8, 512, f"ada0_{t}")
        if l % 2 == 0:
            e = l // 2
            wi = w_inp[e].rearrange("(kc p) f -> p kc f", p=128)
            wo = w_out[e].rearrange("(kc p) f -> p kc f", p=128)
            wtile(wi[:, :, 0:32], 8, 32, f"dt{e}")
            for g in range(4):
                b0 = 32 + g * 768
                wtile(wi[:, :, b0:b0 + 512], 8, 512, f"A1_{e}_{g}")
                if l == 0:
                    t = 4 + 2 * g
                    wtile(wa0[:, :, t * 512:(t + 1) * 512], 8, 512, f"ada0_{t}")
                wtile(wi[:, :, b0 + 512:b0 + 768], 8, 256, f"A2_{e}_{g}")
                if l == 0:
                    t = 5 + 2 * g
                    wtile(wa0[:, :, t * 512:(t + 1) * 512], 8, 512, f"ada0_{t}")
            for t in range(2):
                wtile(wo[:, 0:8, t * 512:(t + 1) * 512], 8, 512, f"O1_{e}_{t}")
            for h in range(8):
                b0 = 3104 + h * 384
                wtile(wi[:, :, b0:b0 + 384], 8, 384, f"QKV_{e}_{h}")
            for t in range(2):
                wtile(wo[:, 8:16, t * 512:(t + 1) * 512], 8, 512, f"O2_{e}_{t}")
        else:
            o = l // 2
            w4 = w_four[o].rearrange("(kc p) f -> p kc f", p=128)
            for t in range(2):
                wtile(w4[:, :, t * 512:(t + 1) * 512], 8, 512, f"W4_{o}_{t}")
        wf = w_ffi[l].rearrange("(kc p) f -> p kc f", p=128)
        wa = w_ada[l + 1].rearrange("(kc p) f -> p kc f", p=128) if l + 1 < DEPTH_RUN else None
        for t in range(11):
            wtile(wf[:, :, t * 512:(t + 1) * 512], 8, 512, f"FI_{l}_{t}")
            if wa is not None:
                wtile(wa[:, :, t * 512:(t + 1) * 512], 8, 512, f"ada{l + 1}_{t}")
        if wa is not None:
            wtile(wa[:, :, 11 * 512:12 * 512], 8, 512, f"ada{l + 1}_11")
        wfo = w_ffo[l].rearrange("(kc p) f -> p kc f", p=128)
        for t in range(8):
            wtile(wfo[:, :, t * 128:(t + 1) * 128], 22, 128, f"FO_{l}_{t}")

    st = {"issued": 0, "used": 0}

    def ring_issue_upto(n):
        while st["issued"] <= n and st["issued"] < len(plan):
            m = st["issued"]
            ap, kc, ncols, tag = plan[m]
            s = m % RING_SLOTS
            dst = ring[:, s, 0:kc * ncols].rearrange("p (k c) -> p k c", k=kc)
            op("pool", lambda dst=dst, ap=ap: nc.gpsimd.dma_start(out=dst, in_=ap),
               writes=[ringB[s]], dkey=f"ring{s}", nofence=True)
            st["issued"] += 1

    def wget(tag):
        n = st["used"]
        ap, kc, ncols, tg = plan[n]
        assert tg == tag, (tg, tag)
        ring_issue_upto(n + RING_SLOTS - 1)
        st["used"] += 1
        s = n % RING_SLOTS
        return ring[:, s, 0:kc * ncols].rearrange("p (k c) -> p k c", k=kc), ringB[s]

    rot = {"i": 0}

    def nbank(lo=0, hi=6):
        b = lo + rot["i"] % (hi - lo)
        rot["i"] += 1
        return b

    op("sp", lambda: nc.sync.dma_start(out=par[:], in_=par_in[:, :]), writes=[parB], dkey="ld0")
    with ExitStack() as s0:
        con = sbt(s0, "con_sb", [128, NCON], F32)
        conB = Buf()
        op("sp", lambda: nc.sync.dma_start(out=con[:], in_=con_in[:, :]), writes=[conB], dkey="ld0")
        xin = xT_in.rearrange("(kc p) t -> p kc t", p=128)
        for kc in range(8):
            op("sp", lambda kc=kc: nc.sync.dma_start(out=xT[:, kc, :], in_=xin[:, kc, :]), writes=xB[kc], dkey="ld1")
        op("dve", lambda: DVE.tensor_copy(out=cbf[:], in_=con[:, 0:9 * 128]), reads=[conB], writes=[cbfB])
        o1 = CL["ones"][0]
        op("act", lambda: ACT.copy(out=cf32[:, 0:128], in_=con[:, o1:o1 + 128]), reads=[conB], writes=[cfB])
        o2 = CL["Vf"][0]
        op("act", lambda: ACT.copy(out=cf32[:, 128:384], in_=con[:, o2:o2 + 256]), reads=[conB], writes=[cfB])
        o3 = CL["ident"][0]
        op("act", lambda: ACT.copy(out=cf32[:, 384:512], in_=con[:, o3:o3 + 128]), reads=[conB], writes=[cfB])
        cv_ = pv("cvec").rearrange("p (k v) -> p k v", v=2)
        op("act", lambda: ACT.activation(out=scT[:], in_=cv_, func=AF.Silu), reads=[parB], writes=[scB])
        P.fence()
    kct = sbt(es, "kct", [128, 4], F32)
    kcB_ = Buf()
    op("dve", lambda: DVE.memset(kct[:, 0:1], 1024.0 * EPS), writes=[kcB_])
    op("dve", lambda: DVE.memset(kct[:, 1:2], EPS), writes=[kcB_])
    op("dve", lambda: DVE.memset(kct[:, 2:3], 1.0), writes=[kcB_])
    op("dve", lambda: DVE.memset(kct[:, 3:4], -30.0), writes=[kcB_])

    def KC(i):
        return kct[:, i:i + 1]
    ones_f = cf32[:, 0:128]
    Vf_f = cf32[:, 128:256]
    Vb_f = cf32[:, 256:384]
    ident_f = cf32[:, 384:512]
    ident = cb("ident")
    ones_b = cb("ones")

    ABANK = 7

    def ada_tile(l, t):
        wt, wb = wget(f"ada{l}_{t}")
        for s in range(4):
            j = 4 * t + s
            for kc in range(8):
                op("pe", lambda s=s, kc=kc, j=j: PE.matmul(PS(ABANK)[:, 2 * j:2 * j + 2], lhsT=wt[:, kc, s * 128:(s + 1) * 128],
                                                        rhs=scT[:, kc, :], start=(kc == 0), stop=(kc == 7)),
                   reads=[wb, scB], writes=[PSB[ABANK]], sig=(kc == 7))

    def ada_finish(l, part=None):
        m = mod[l % 2]
        mB = modB[l % 2]
        j0, j1 = {None: (0, 48), 0: (0, 16), 1: (16, 48)}[part]
        op("dve", lambda: DVE.tensor_tensor(out=m[:, j0:j1, :], in0=PS(ABANK)[:, 2 * j0:2 * j1].rearrange("p (j v) -> p j v", v=2),
                                            in1=pv(f"bada{l}")[:, j0:j1].unsqueeze(2).to_broadcast([128, j1 - j0, 2]), op=ALU.add),
           reads=[PSB[ABANK], parB], writes=[mB])
        g = gsc[l % 2]
        for wh, (nm, sc0) in enumerate(((f"nm{l}", 8), (f"nf{l}", 32))):
            if (part == 0 and wh == 1) or (part == 1 and wh == 0):
                continue
            op("dve", lambda wh=wh, nm=nm, sc0=sc0: DVE.scalar_tensor_tensor(
                out=g[:, wh], in0=m[:, sc0:sc0 + 8, :], scalar=1.0,
                in1=pv(nm).unsqueeze(2).to_broadcast([128, 8, 2]), op0=ALU.add, op1=ALU.mult),
               reads=[mB, parB], writes=[gscB[l % 2]])
            op("dve", lambda wh=wh: DVE.tensor_scalar(out=g[:, wh], in0=g[:, wh], scalar1=32.0, scalar2=None, op0=ALU.mult),
               reads=[gscB[l % 2]], writes=[gscB[l % 2]])

    def modulate(stack, dst, dstB, gs, sh, depB, out_dt_is_f32=False):
        with ExitStack() as s1:
            sq = sbt(s1, "sq", [128, 2, 8, 512], BF16)
            sqB = BG(2, 2)
            rs = sbt(s1, "rs", [128, 2, 512], F32)
            rsB = BG(2)
            tmp = sbt(s1, "mtmp", [128, 2, 8, 512], F32)
            tmpB = BG(2, 2)
            def stage1(tb):
                p = tb % 2
                bk = 6 + p
                ts = slice(tb * 512, (tb + 1) * 512)
                xb = [xB[kc][tb] for kc in range(8)]
                op("act", lambda: ACT.activation(out=sq[:, p, 0:5], in_=xT[:, 0:5, ts], func=AF.Square), reads=xb[0:5], writes=[sqB[p][0]])
                op("pool", lambda: nc.gpsimd.tensor_tensor(out=sq[:, p, 5:8], in0=xT[:, 5:8, ts], in1=xT[:, 5:8, ts], op=ALU.mult), reads=xb[5:8], writes=[sqB[p][1]])
                for kc in range(8):
                    op("pe", lambda: PE.matmul(PS(bk), lhsT=ones_b, rhs=sq[:, p, kc, :], start=(kc == 0), stop=(kc == 7)),
                       reads=[sqB[p][0 if kc < 5 else 1], cbfB], writes=[PSB[bk]], sig=(kc == 7))

            def stage2(tb):
                p = tb % 2
                bk = 6 + p
                v = 0 if tb == 0 else 1
                ts = slice(tb * 512, (tb + 1) * 512)
                xb = [xB[kc][tb] for kc in range(8)]
                op("act", lambda: ACT.activation(out=rs[:, p, :], in_=PS(bk), func=AF.Ln, bias=KC(0)), reads=[PSB[bk], kcB_], writes=[rsB[p]])
                op("act", lambda: ACT.activation(out=rs[:, p, :], in_=rs[:, p, :], func=AF.Exp, scale=-0.5), reads=[rsB[p]], writes=[rsB[p]])
                op("dve", lambda: DVE.tensor_tensor(out=tmp[:, p, 0:6], in0=xT[:, 0:6, ts], in1=rs[:, p, :].unsqueeze(1).to_broadcast([128, 6, 512]), op=ALU.mult),
                   reads=xb[0:6] + [rsB[p]], writes=[tmpB[p][0]])
                op("pool", lambda: nc.gpsimd.tensor_tensor(out=tmp[:, p, 6:8], in0=xT[:, 6:8, ts], in1=rs[:, p, :].unsqueeze(1).to_broadcast([128, 2, 512]), op=ALU.mult),
                   reads=xb[6:8] + [rsB[p]], writes=[tmpB[p][1]])
                for kc in range(8):
                    b = sh(kc, v)
                    if kc < 4:
                        if b is None:
                            op("act", lambda: ACT.activation(out=dst[:, kc, ts], in_=tmp[:, p, kc, :], func=AF.Identity, scale=gs(kc, v)),
                               reads=[tmpB[p][0 if kc < 6 else 1]] + depB, writes=[dstB[kc][tb]])
                        else:
                            op("act", lambda: ACT.activation(out=dst[:, kc, ts], in_=tmp[:, p, kc, :], func=AF.Identity, scale=gs(kc, v), bias=b),
                               reads=[tmpB[p][0 if kc < 6 else 1]] + depB, writes=[dstB[kc][tb]])
                    else:
                        if b is None:
                            op("dve", lambda: DVE.tensor_scalar(out=dst[:, kc, ts], in0=tmp[:, p, kc, :], scalar1=gs(kc, v), scalar2=None, op0=ALU.mult),
                               reads=[tmpB[p][0 if kc < 6 else 1]] + depB, writes=[dstB[kc][tb]])
                        else:
                            op("dve", lambda: DVE.tensor_scalar(out=dst[:, kc, ts], in0=tmp[:, p, kc, :], scalar1=gs(kc, v), scalar2=b, op0=ALU.mult, op1=ALU.add),
                               reads=[tmpB[p][0 if kc < 6 else 1]] + depB, writes=[dstB[kc][tb]])
            stage1(0)
            stage1(1)
            stage2(0)
            stage1(2)
            stage2(1)
            stage2(2)
            P.fence()

    def ffn(l):
        m = mod[l % 2]
        g = gsc[l % 2]
        mB = modB[l % 2]
        with ExitStack() as s1:
            h2 = sbt(s1, "h2", [128, 8, T], BF16)
            h2B = BG(8, 3)
            modulate(s1, h2, h2B, lambda kc, v: g[:, 1, kc, v:v + 1], lambda kc, v: m[:, 24 + kc, v:v + 1],
                     [mB, gscB[l % 2]])
            aT = sbt(s1, "aT", [128, NJ, T], BF16)
            aB = BG(NJ, 3)
            sg = sbt(s1, "sg", [128, 2, T], F32)
            sgB = BG(2, 3)
            for t in range(11):
                wt, wb = wget(f"FI_{l}_{t}")
                for s in range(2):
                    j = 2 * t + s
                    for half, b0 in ((0, 0), (1, 3)):
                        c0 = (half * 2 + s) * 128
                        for kc in range(8):
                            for tb in range(3):
                                op("pe", lambda c0=c0, kc=kc, tb=tb, b0=b0: PE.matmul(
                                    PS(b0 + tb), lhsT=wt[:, kc, c0:c0 + 128], rhs=h2[:, kc, tb * 512:(tb + 1) * 512],
                                    start=(kc == 0), stop=(kc == 7)),
                                   reads=[wb, h2B[kc][tb]], writes=[PSB[b0 + tb]], sig=(kc == 7))
                    for tb in range(3):
                        ts = slice(tb * 512, (tb + 1) * 512)
                        op("act", lambda tb=tb, ts=ts, s=s: ACT.activation(out=sg[:, s, ts], in_=PS(tb), func=AF.Silu),
                           reads=[PSB[tb]], writes=[sgB[s][tb]])
                        op("dve", lambda tb=tb, ts=ts, s=s, j=j: DVE.tensor_tensor(out=aT[:, j, ts], in0=sg[:, s, ts], in1=PS(3 + tb), op=ALU.mult),
                           reads=[sgB[s][tb], PSB[3 + tb]], writes=[aB[j][tb]])
                if l + 1 < DEPTH_RUN:
                    ada_tile(l + 1, t)
            if l + 1 < DEPTH_RUN:
                ada_tile(l + 1, 11)
                ada_finish(l + 1)
            for dsl in range(8):
                wt, wb = wget(f"FO_{l}_{dsl}")
                b0 = 0 if dsl % 2 == 0 else 3
                for j in range(NJ):
                    for tb in range(3):
                        op("pe", lambda j=j, tb=tb, b0=b0: PE.matmul(PS(b0 + tb), lhsT=wt[:, j, :], rhs=aT[:, j, tb * 512:(tb + 1) * 512],
                                                                    start=(j == 0), stop=(j == NJ - 1)),
                           reads=[wb, aB[j][tb]], writes=[PSB[b0 + tb]], sig=(j == NJ - 1))
                for tb in range(3):
                    v = 0 if tb == 0 else 1
                    ts = slice(tb * 512, (tb + 1) * 512)
                    op("dve", lambda tb=tb, ts=ts, v=v, dsl=dsl, b0=b0: DVE.scalar_tensor_tensor(
                        out=xT[:, dsl, ts], in0=PS(b0 + tb), scalar=m[:, 40 + dsl, v:v + 1], in1=xT[:, dsl, ts],
                        op0=ALU.mult, op1=ALU.add),
                       reads=[PSB[b0 + tb], mB, xB[dsl][tb]], writes=[xB[dsl][tb]])
            P.fence()

    def outproj(l, tags, srcT, srcB, bias=None, rowscale=None):
        m = mod[l % 2]
        mB = modB[l % 2]
        for t, tag in enumerate(tags):
            wt, wb = wget(tag)
            for s in range(4):
                dsl = 4 * t + s
                b0 = 0 if dsl % 2 == 0 else 3
                for kc in range(8):
                    for tb in range(3):
                        op("pe", lambda s=s, kc=kc, tb=tb, b0=b0: PE.matmul(PS(b0 + tb), lhsT=wt[:, kc, s * 128:(s + 1) * 128],
                                                                           rhs=srcT[:, kc, tb * 512:(tb + 1) * 512],
                                                                           start=(kc == 0), stop=(kc == 7)),
                           reads=[wb, srcB[kc][tb]], writes=[PSB[b0 + tb]], sig=(kc == 7))
                for tb in range(3):
                    v = 0 if tb == 0 else 1
                    ts = slice(tb * 512, (tb + 1) * 512)
                    if bias is not None:
                        op("act", lambda tb=tb, b0=b0, dsl=dsl: ACT.activation(out=PS(b0 + tb), in_=PS(b0 + tb), func=AF.Identity,
                                                                             bias=bias[:, dsl:dsl + 1]),
                           reads=[PSB[b0 + tb], parB], writes=[PSB[b0 + tb]])
                    if rowscale is not None:
                        op("dve", lambda tb=tb, ts=ts, b0=b0: DVE.tensor_tensor(out=PS(b0 + tb), in0=PS(b0 + tb), in1=rowscale[0][:, ts], op=ALU.mult),
                           reads=[PSB[b0 + tb], rowscale[1][tb]], writes=[PSB[b0 + tb]])
                    op("dve", lambda tb=tb, ts=ts, v=v, dsl=dsl, b0=b0: DVE.scalar_tensor_tensor(
                        out=xT[:, dsl, ts], in0=PS(b0 + tb), scalar=m[:, 16 + dsl, v:v + 1], in1=xT[:, dsl, ts],
                        op0=ALU.mult, op1=ALU.add),
                       reads=[PSB[b0 + tb], mB, xB[dsl][tb]], writes=[xB[dsl][tb]])

    def fourier_layer(l, hT, hB):
        o = l // 2
        with ExitStack() as s1:
            dftb = sbt(s1, "dftb", [128, 2, 8, 1024], BF16)
            dftB = Buf()
            for i in range(2):
                op("pool", lambda i=i: nc.gpsimd.dma_start(out=dftb[:, i], in_=dft_in[i]), writes=[dftB], dkey="dft")
            fT = sbt(s1, "fT", [128, 8, T], BF16)
            fB = BG(8, 3)
            AB = sbt(s1, "ABt", [128, 8, 2, 1024], BF16)
            ABB = BG(8)
            cdft = sbt(s1, "cdft", [128, 4, 2, 256], BF16)
            o_cc = CL["Cc"][0]
            op("pool", lambda: nc.gpsimd.dma_start(out=cdft[:], in_=con_in[:, o_cc:o_cc + 2048].rearrange("p (a k c) -> p a k c", a=4, k=2)),
               writes=[dftB], dkey="dft")
            Cc, Sc, c256, s256 = cdft[:, 0], cdft[:, 1], cdft[:, 2], cdft[:, 3]
            for (t0, ntl) in SEQS:
                for lt in range(ntl):
                    tt = t0 + lt
                    tb = tt // 4
                    tsl = slice(tt * 128, (tt + 1) * 128)
                    for ab, tab in ((0, Cc), (1, Sc)):
                        bk = [nbank(0, 6), nbank(0, 6)]
                        for g in range(4):
                            for cc in range(2):
                                op("pe", lambda g=g, cc=cc, tab=tab, bk=bk, tsl=tsl: PE.matmul(
                                    PS(bk[g // 2])[:, (g % 2) * 256:(g % 2) * 256 + 256], lhsT=hT[:, 2 * g + cc, tsl], rhs=tab[:, cc, :],
                                    start=(cc == 0), stop=(cc == 1)),
                                   reads=[hB[2 * g + cc][tb], dftB], writes=[PSB[bk[g // 2]]], sig=(cc == 1))
                        for hh in range(2):
                            eng = "act" if (hh + ab) % 2 == 0 else "dve"
                            if eng == "act":
                                op("act", lambda hh=hh, ab=ab, lt=lt, bk=bk: ACT.copy(out=AB[:, lt, ab, hh * 512:(hh + 1) * 512], in_=PS(bk[hh])),
                                   reads=[PSB[bk[hh]]], writes=[ABB[lt]])
                            else:
                                op("dve", lambda hh=hh, ab=ab, lt=lt, bk=bk: DVE.tensor_copy(out=AB[:, lt, ab, hh * 512:(hh + 1) * 512], in_=PS(bk[hh])),
                                   reads=[PSB[bk[hh]]], writes=[ABB[lt]])
                L = ntl * 128
                nkb = max(1, L // 512)
                kw = min(L, 512)
                for cs in range(8):
                    for kb in range(nkb):
                        bk = nbank(0, 6)
                        n_acc = 2 * ntl
                        i = 0
                        for lt in range(ntl):
                            for ab in range(2):
                                if ntl == 2:
                                    rhs = (c256 if ab == 0 else s256)[:, lt, :]
                                    rd = [dftB]
                                else:
                                    rhs = dftb[:, ab, lt, kb * 512:(kb + 1) * 512]
                                    rd = [dftB]
                                op("pe", lambda lt=lt, ab=ab, rhs=rhs, i=i, bk=bk, cs=cs: PE.matmul(
                                    PS(bk)[:, 0:kw], lhsT=AB[:, lt, ab, cs * 128:(cs + 1) * 128], rhs=rhs,
                                    start=(i == 0), stop=(i == n_acc - 1)),
                                   reads=[ABB[lt]] + rd, writes=[PSB[bk]], sig=(i == n_acc - 1))
                                i += 1
                        c0 = t0 * 128 + kb * 512
                        tbs = sorted(set([c0 // 512, (c0 + kw - 1) // 512]))
                        eng = "act" if (cs + kb) % 2 == 0 else "dve"
                        if eng == "act":
                            op("act", lambda bk=bk, cs=cs, c0=c0: ACT.copy(out=fT[:, cs, c0:c0 + kw], in_=PS(bk)[:, 0:kw]),
                               reads=[PSB[bk]], writes=[fB[cs][tb_] for tb_ in tbs])
                        else:
                            op("dve", lambda bk=bk, cs=cs, c0=c0: DVE.tensor_copy(out=fT[:, cs, c0:c0 + kw], in_=PS(bk)[:, 0:kw]),
                               reads=[PSB[bk]], writes=[fB[cs][tb_] for tb_ in tbs])
            outproj(l, [f"W4_{o}_0", f"W4_{o}_1"], fT, fB, bias=pv(f"b4{o}"))
            P.fence()

    def ab_layer(l, hT, hB):
        e = l // 2
        lambda_init = 0.8 - 0.6 * math.exp(-0.3 * l)
        Uf, Ub, Vf, Vb = cb("Uf"), cb("Ub"), cb("Vf"), cb("Vb")
        NEGf, NEGb = cb("NEGf"), cb("NEGb")
        with ExitStack() as s1:
            yoT = sbt(s1, "yoT", [128, 8, T], BF16)
            yoB = BG(8, 3)
            nlam = sbt(s1, "nlam", [128, 4], F32)
            nlB = Buf()
            with ExitStack() as s2:
                lt_ = sbt(s2, "lqtmp", [128, 2, 64], F32)
                ltB = Buf()
                lq = pv(f"lqk{e}").rearrange("p (a b d) -> p a b d", a=2, b=2)
                op("dve", lambda: DVE.tensor_tensor(out=lt_[:], in0=lq[:, :, 0, :], in1=lq[:, :, 1, :], op=ALU.mult), reads=[parB], writes=[ltB])
                op("dve", lambda: DVE.tensor_reduce(out=nlam[:, 0:2], in_=lt_[:], axis=mybir.AxisListType.X, op=ALU.add), reads=[ltB], writes=[nlB])
                op("act", lambda: ACT.activation(out=nlam[:, 0:2], in_=nlam[:, 0:2], func=AF.Exp), reads=[nlB], writes=[nlB])
                op("dve", lambda: DVE.tensor_tensor(out=nlam[:, 2:3], in0=nlam[:, 1:2], in1=nlam[:, 0:1], op=ALU.subtract), reads=[nlB], writes=[nlB])
                op("dve", lambda: DVE.tensor_scalar(out=nlam[:, 3:4], in0=nlam[:, 2:3], scalar1=-lambda_init, scalar2=None, op0=ALU.add), reads=[nlB], writes=[nlB])
                P.fence()

            with ExitStack() as s2:
                dtv = sbt(s2, "dtv", [128, NT, 32], F32)
                dta = sbt(s2, "dta", [128, NT, 32], F32)
                ea = sbt(s2, "ea", [128, NT, 32], F32)
                te = sbt(s2, "te", [128, NT, 32], F32)
                cdb = sbt(s2, "cdb", [128, NT, 32], F32)
                smB = BG(NT)
                aneg = sbt(s2, "aneg", [128, 32], F32)
                anB = Buf()
                dskd = sbt(s2, "dskd", [128, 16, 128], BF16)
                dskB = Buf()
                op("act", lambda: ACT.activation(out=aneg[:], in_=pv(f"alog{e}"), func=AF.Exp), reads=[parB], writes=[anB])
                for h in range(16):
                    op("dve", lambda h=h: DVE.tensor_scalar(out=dskd[:, h, :], in0=ident, scalar1=pv(f"dsk{e}")[:, h:h + 1], scalar2=None, op0=ALU.mult),
                       reads=[parB, cbfB], writes=[dskB])
                wt, wb = wget(f"dt{e}")
                bA, bB, bC = 3, 4, 5
                for tt in range(NT):
                    tb = tt // 4
                    tsl = slice(tt * 128, (tt + 1) * 128)
                    for kc in range(8):
                        op("pe", lambda: PE.matmul(PS(bA)[:, tt * 32:(tt + 1) * 32], lhsT=hT[:, kc, tsl], rhs=wt[:, kc, :], start=(kc == 0), stop=(kc == 7)),
                           reads=[wb, hB[kc][tb]], writes=[PSB[bA]], sig=(kc == 7))
                allsm = smB
                op("dve", lambda: DVE.tensor_tensor(out=dtv[:], in0=PS(bA)[:, 0:NT * 32].rearrange("p (t c) -> p t c", c=32),
                                                    in1=pv(f"dtb{e}").unsqueeze(1).to_broadcast([128, NT, 32]), op=ALU.add),
                   reads=[PSB[bA], parB], writes=allsm)
                op("act", lambda: ACT.activation(out=dtv[:], in_=dtv[:], func=AF.Exp), reads=allsm, writes=allsm)
                op("act", lambda: ACT.activation(out=dtv[:], in_=dtv[:], func=AF.Ln, bias=KC(2)), reads=allsm + [kcB_], writes=allsm)
                op("dve", lambda: DVE.scalar_tensor_tensor(out=dta[:], in0=dtv[:], scalar=-1.0, in1=aneg[:].unsqueeze(1).to_broadcast([128, NT, 32]),
                                                           op0=ALU.mult, op1=ALU.mult),
                   reads=allsm + [anB], writes=allsm)
                for tt in range(NT):
                    op("pe", lambda: PE.matmul(PS(bB)[:, tt * 32:tt * 32 + 16], lhsT=Vf_f, rhs=dta[:, tt, 0:16], start=True, stop=True),
                       reads=allsm + [cfB], writes=[PSB[bB]], sig=False)
                    op("pe", lambda: PE.matmul(PS(bB)[:, tt * 32 + 16:tt * 32 + 32], lhsT=Vb_f, rhs=dta[:, tt, 16:32], start=True, stop=True),
                       reads=allsm + [cfB], writes=[PSB[bB]], sig=False)
                    op("pe", lambda: PE.matmul(PS(bC)[:, tt * 32:(tt + 1) * 32], lhsT=ones_f, rhs=dta[:, tt, :], start=True, stop=True),
                       reads=allsm + [cfB], writes=[PSB[bC]], sig=True)
                vB = PS(bB)[:, 0:NT * 32].rearrange("p (t c) -> p t c", c=32)
                vC = PS(bC)[:, 0:NT * 32].rearrange("p (t c) -> p t c", c=32)
                op("act", lambda: ACT.activation(out=ea[:], in_=vB, func=AF.Exp), reads=[PSB[bB]], writes=allsm)
                op("act", lambda: ACT.copy(out=te[:], in_=vB), reads=[PSB[bB]], writes=allsm)
                op("act", lambda: ACT.activation(out=cdb[:], in_=vC, func=AF.Exp), reads=[PSB[bC]], writes=allsm)
                op("dve", lambda: DVE.tensor_tensor(out=te[:], in0=vC, in1=te[:], op=ALU.subtract), reads=[PSB[bC]] + allsm, writes=allsm)
                op("act", lambda: ACT.activation(out=te[:], in_=te[:], func=AF.Exp), reads=allsm, writes=allsm)
                op("dve", lambda: DVE.tensor_tensor(out=te[:], in0=te[:], in1=dtv[:], op=ALU.mult), reads=allsm, writes=allsm)

                checkpoint('dtprep')
                ssq = sbt(s2, "ssq", [128, NT, 4], F32)
                ssB = BG(NT)
                op("dve", lambda: DVE.memset(ssq[:], 0.0), writes=ssB)
                PADL = 1548
                pre = sbt(s2, "pre", [128, 2, PADL], BF16)
                preB = BG(2)
                op("dve", lambda: DVE.memset(pre[:], 0.0), writes=preB)
                post = sbt(s2, "post", [128, 4, T], BF16)
                postB = BG(4, 3)
                cwd = sbt(s2, "cwd", [128, 20, 128], BF16)
                cwB = Buf()
                xs_t = sbt(s2, "xs_t", [128, NT, 256], BF16)
                b_t = sbt(s2, "b_t", [128, NT, 128], BF16)
                zs = sbt(s2, "zs", [128, NT, 256], BF16)
                tkB = BG(NT)
                zB = BG(NT)
                xte = sbt(s2, "xte", [128, 2, 256], BF16)
                xteB = [Buf(), Buf()]
                xdt = sbt(s2, "xdt", [128, 2, 2, 256], BF16)
                xdB = BG(2, 2)
                hst = sbt(s2, "hst", [128, 2, 256], F32)
                hsB = [Buf(), Buf()]
                hbf = sbt(s2, "hbf", [128, 8, 2, 256], BF16)
                hbB = BG(8, 2)
                Ld = sbt(s2, "Ld", [128, 1, 8, 128], BF16)
                LdB = [Buf()]
                Et = sbt(s2, "Et", [128, 2, 2, 512], BF16)
                EtB = BG(2, 2)
                cbt = sbt(s2, "cbt", [128, 2, 128], F32)
                cbB = BG(2)
                Wt = sbt(s2, "Wt", [128, 2, 8, 128], BF16)
                WtB = BG(2, 2)
                ytmp = sbt(s2, "ytmp", [128, 2, 256], F32)
                ytB = [Buf(), Buf()]
                yg = sbt(s2, "yg", [128, 2, 256], BF16)
                ygB = BG(2)
                sqj = sbt(s2, "sqj", [128, 256], BF16)
                sqjB = Buf()
                for g in range(4):
                    for i in range(20):
                        ci = g * 20 + i
                        if i % 2 == 0:
                            op("dve", lambda i=i, ci=ci: DVE.tensor_scalar(out=cwd[:, i, :], in0=ident, scalar1=pv(f"cw{e}")[:, ci:ci + 1], scalar2=None, op0=ALU.mult),
                               reads=[parB, cbfB], writes=[cwB])
                        else:
                            op("act", lambda i=i, ci=ci: ACT.activation(out=cwd[:, i, :], in_=ident, func=AF.Identity, scale=pv(f"cw{e}")[:, ci:ci + 1]),
                               reads=[parB, cbfB], writes=[cwB])
                    wt1, wb1 = wget(f"A1_{e}_{g}")
                    ada_after_A1 = (l == 0)
                    def proj_s(s):
                        pp = s % 2
                        for kc in range(8):
                            for tb in range(3):
                                op("pe", lambda: PE.matmul(PS(tb), lhsT=wt1[:, kc, s * 128:(s + 1) * 128],
                                                           rhs=hT[:, kc, tb * 512:(tb + 1) * 512], start=(kc == 0), stop=(kc == 7)),
                                   reads=[wb1, hB[kc][tb]], writes=[PSB[tb]], sig=(kc == 7))
                        op("act", lambda: ACT.copy(out=pre[:, pp, 2:258], in_=PS(0)[:, 0:256]), reads=[PSB[0]], writes=[preB[pp]])
                        op("act", lambda: ACT.copy(out=pre[:, pp, 262:518], in_=PS(0)[:, 256:512]), reads=[PSB[0]], writes=[preB[pp]])
                        op("dve", lambda: DVE.tensor_copy(out=pre[:, pp, 522:1034], in_=PS(1)), reads=[PSB[1]], writes=[preB[pp]])
                        op("dve", lambda: DVE.tensor_copy(out=pre[:, pp, 1034:1546], in_=PS(2)), reads=[PSB[2]], writes=[preB[pp]])

                    def conv_s(s):
                        sl = g * 4 + s
                        pp = s % 2
                        for bi, (poff, toff, n) in enumerate(((2, 0, 256), (262, 256, 256), (522, 512, 512), (1034, 1024, 512))):
                            bk = 3 + bi % 3
                            for k in range(5):
                                op("pe", lambda: PE.matmul(PS(bk)[:, 0:n], lhsT=cwd[:, s * 5 + k, :], rhs=pre[:, pp, poff + k - 2:poff + k - 2 + n],
                                                           start=(k == 0), stop=(k == 4)),
                                   reads=[cwB, preB[pp]], writes=[PSB[bk]], sig=(k == 4))
                            tbs = sorted(set([toff // 512, (toff + n - 1) // 512]))
                            op("act", lambda: ACT.activation(out=post[:, s, toff:toff + n], in_=PS(bk)[:, 0:n], func=AF.Silu, bias=pv(f"cb{e}")[:, sl:sl + 1]),
                               reads=[PSB[bk], parB], writes=[postB[s][tb_] for tb_ in tbs])
                    proj_s(0)
                    for s in range(4):
                        if s + 1 < 4:
                            proj_s(s + 1)
                        conv_s(s)
                    for tt in range(NT):
                        tb = tt // 4
                        tsl = slice(tt * 128, (tt + 1) * 128)
                        bk = nbank(0, 6)
                        for s in range(3):
                            op("pe", lambda s=s, tsl=tsl, bk=bk: PE.matmul(PS(bk)[:, s * 128:(s + 1) * 128], lhsT=post[:, s, tsl], rhs=ident, start=True, stop=True),
                               reads=[postB[s][tb], cbfB], writes=[PSB[bk]], sig=(s == 2))
                        op("act", lambda tt=tt, bk=bk: ACT.copy(out=xs_t[:, tt, :], in_=PS(bk)[:, 0:256]), reads=[PSB[bk]], writes=[tkB[tt]])
                        op("act", lambda tt=tt, bk=bk: ACT.copy(out=b_t[:, tt, :], in_=PS(bk)[:, 256:384]), reads=[PSB[bk]], writes=[tkB[tt]])
                    if l == 0:
                        ada_tile(0, 4 + 2 * g)
                    wt2, wb2 = wget(f"A2_{e}_{g}")
                    for tt in range(NT):
                        tb = tt // 4
                        tsl = slice(tt * 128, (tt + 1) * 128)
                        bk = nbank(0, 6)
                        for kc in range(8):
                            op("pe", lambda kc=kc, tsl=tsl, bk=bk: PE.matmul(PS(bk)[:, 0:256], lhsT=hT[:, kc, tsl], rhs=wt2[:, kc, :], start=(kc == 0), stop=(kc == 7)),
                               reads=[wb2, hB[kc][tb]], writes=[PSB[bk]], sig=(kc == 7))
                        op("act", lambda tt=tt, bk=bk: ACT.activation(out=zs[:, tt, :], in_=PS(bk)[:, 0:256], func=AF.Silu), reads=[PSB[bk]], writes=[zB[tt]])
                    if l == 0:
                        ada_tile(0, 5 + 2 * g)
                        if g == 3:
                            ada_finish(0, 1)
                    checkpoint('g0conv')
                    for si, (t0, ntl) in enumerate(SEQS):
                        orders = [list(range(t0, t0 + ntl)), list(range(t0 + ntl - 1, t0 - 1, -1))]
                        for d in range(2):
                            if si < 2:
                                op("dve", lambda d=d: DVE.memset(hst[:, d, :], 0.0), writes=[hsB[d]])
                            else:
                                src = (sf_in if d == 0 else sb_in)[e][:, g * 256:(g + 1) * 256]
                                op("sp", lambda d=d, src=src: nc.sync.dma_start(out=hst[:, d, :], in_=src), writes=[hsB[d]], dkey=f"hin{d}")
                        for step in range(ntl):
                            for d in range(2):
                                tt = orders[d][step]
                                c0 = d * 16 + g * 4
                                ti = tt - t0
                                op("act", lambda ti=ti, d=d: ACT.copy(out=hbf[:, ti, d, :], in_=hst[:, d, :]), reads=[hsB[d]], writes=[hbB[ti][d]])
                                op("pool", lambda tt=tt, d=d, c0=c0: nc.gpsimd.tensor_tensor(
                                    out=xte[:, d, :].rearrange("p (r q) -> p r q", r=4),
                                    in0=xs_t[:, tt, :].rearrange("p (r q) -> p r q", r=4),
                                    in1=te[:, tt, c0:c0 + 4].unsqueeze(2).to_broadcast([128, 4, 64]), op=ALU.mult),
                                   reads=[tkB[tt], smB[tt]], writes=[xteB[d]])
                                bk = nbank(0, 6)
                                op("pe", lambda tt=tt, d=d, bk=bk: PE.matmul(PS(bk)[:, 0:256], lhsT=b_t[:, tt, :], rhs=xte[:, d, :], start=True, stop=True),
                                   reads=[tkB[tt], xteB[d]], writes=[PSB[bk]])
                                op("dve", lambda tt=tt, d=d, c0=c0: DVE.tensor_tensor(
                                    out=hst[:, d, :].rearrange("p (r q) -> p r q", r=4), in0=hst[:, d, :].rearrange("p (r q) -> p r q", r=4),
                                    in1=cdb[:, tt, c0:c0 + 4].unsqueeze(2).to_broadcast([128, 4, 64]), op=ALU.mult),
                                   reads=[hsB[d], smB[tt]], writes=[hsB[d]])
                                op("dve", lambda d=d, bk=bk: DVE.tensor_tensor(out=hst[:, d, :], in0=hst[:, d, :], in1=PS(bk)[:, 0:256], op=ALU.add),
                                   reads=[hsB[d], PSB[bk]], writes=[hsB[d]])
                        if si < 2:
                            for d in range(2):
                                dst = (nf_out if d == 0 else nb_out)[e, si][:, g * 256:(g + 1) * 256]
                                op("sp", lambda d=d, dst=dst: nc.sync.dma_start(out=dst, in_=hst[:, d, :]), reads=[hsB[d]], dkey=f"sto{d}")

                        def stageA(tt):
                            pb = tt % 2
                            tb = tt // 4
                            tsl = slice(tt * 128, (tt + 1) * 128)
                            bkc = nbank(0, 6)
                            op("pe", lambda: PE.matmul(PS(bkc)[:, 0:128], lhsT=post[:, 2, tsl], rhs=post[:, 3, tsl], start=True, stop=True),
                               reads=[postB[2][tb], postB[3][tb]], writes=[PSB[bkc]])
                            op("act", lambda: ACT.copy(out=cbt[:, pb, :], in_=PS(bkc)[:, 0:128]), reads=[PSB[bkc]], writes=[cbB[pb]])
                            for d, U in ((0, Uf), (1, Ub)):
                                c0 = d * 16 + g * 4
                                op("pool", lambda d=d, U=U, c0=c0: nc.gpsimd.tensor_tensor(
                                    out=Ld[:, 0, d * 4:(d + 1) * 4, :], in0=U.unsqueeze(1).to_broadcast([128, 4, 128]),
                                    in1=dta[:, tt, c0:c0 + 4].unsqueeze(2).to_broadcast([128, 4, 128]), op=ALU.mult),
                                   reads=[cbfB, smB[tt]], writes=[LdB[0]])
                            for d, V, NEG in ((0, Vf, NEGf), (1, Vb, NEGb)):
                                bks = nbank(0, 6)
                                for r in range(4):
                                    op("pe", lambda d=d, r=r, V=V: PE.matmul(PS(bks)[:, r * 128:(r + 1) * 128], lhsT=Ld[:, 0, d * 4 + r, :], rhs=V, start=True, stop=False),
                                       reads=[LdB[0], cbfB], writes=[PSB[bks]], sig=False)
                                    op("pe", lambda r=r, NEG=NEG: PE.matmul(PS(bks)[:, r * 128:(r + 1) * 128], lhsT=ident, rhs=NEG, start=False, stop=True),
                                       reads=[cbfB], writes=[PSB[bks]], sig=(r == 3))
                                op("act", lambda d=d: ACT.activation(out=Et[:, pb, d, :], in_=PS(bks), func=AF.Exp), reads=[PSB[bks]], writes=[EtB[pb][d]])

                        def stageA2(tt):
                            pb = tt % 2
                            for d in range(2):
                                c0 = d * 16 + g * 4
                                weng = "dve" if d == 0 else "pool"
                                wfn = DVE.tensor_tensor if d == 0 else nc.gpsimd.tensor_tensor
                                op(weng, lambda d=d, wfn=wfn: wfn(
                                    out=Wt[:, pb, d * 4:(d + 1) * 4, :], in0=Et[:, pb, d, :].rearrange("p (r i) -> p r i", r=4),
                                    in1=cbt[:, pb, :].unsqueeze(1).to_broadcast([128, 4, 128]), op=ALU.mult),
                                   reads=[EtB[pb][d], cbB[pb]], writes=[WtB[pb][d]])
                                op("dve", lambda d=d, c0=c0: DVE.tensor_tensor(
                                    out=xdt[:, pb, d, :].rearrange("p (r q) -> p r q", r=4), in0=xs_t[:, tt, :].rearrange("p (r q) -> p r q", r=4),
                                    in1=dtv[:, tt, c0:c0 + 4].unsqueeze(2).to_broadcast([128, 4, 64]), op=ALU.mult),
                                   reads=[tkB[tt], smB[tt]], writes=[xdB[pb][d]])

                        def stageB(tt):
                            pb = tt % 2
                            ti = tt - t0
                            tb = tt // 4
                            tsl = slice(tt * 128, (tt + 1) * 128)
                            bky = nbank(0, 6)
                            for r in range(4):
                                xr = xs_t[:, tt, r * 64:(r + 1) * 64]
                                yo_ = PS(bky)[:, r * 64:(r + 1) * 64]
                                op("pe", lambda: PE.matmul(yo_, lhsT=Wt[:, pb, r, :], rhs=xdt[:, pb, 0, r * 64:(r + 1) * 64], start=True, stop=False),
                                   reads=[WtB[pb][0], xdB[pb][0]], writes=[PSB[bky]], sig=False)
                                op("pe", lambda: PE.matmul(yo_, lhsT=Wt[:, pb, 4 + r, :], rhs=xdt[:, pb, 1, r * 64:(r + 1) * 64], start=False, stop=False),
                                   reads=[WtB[pb][1], xdB[pb][1]], writes=[PSB[bky]], sig=False)
                                op("pe", lambda: PE.matmul(yo_, lhsT=dskd[:, g * 4 + r, :], rhs=xr, start=False, stop=True),
                                   reads=[dskB, tkB[tt]], writes=[PSB[bky]], sig=(r == 3))
                            bko = [nbank(0, 6), nbank(0, 6)]
                            for d in range(2):
                                op("pe", lambda d=d: PE.matmul(PS(bko[d])[:, 0:256], lhsT=post[:, 3, tsl], rhs=hbf[:, ti, d, :], start=True, stop=True),
                                   reads=[postB[3][tb], hbB[ti][d]], writes=[PSB[bko[d]]])
                            for d in range(2):
                                c0 = d * 16 + g * 4
                                op("dve", lambda d=d, c0=c0: DVE.tensor_tensor(
                                    out=ytmp[:, d, :].rearrange("p (r q) -> p r q", r=4), in0=PS(bko[d])[:, 0:256].rearrange("p (r q) -> p r q", r=4),
                                    in1=ea[:, tt, c0:c0 + 4].unsqueeze(2).to_broadcast([128, 4, 64]), op=ALU.mult),
                                   reads=[PSB[bko[d]], smB[tt]], writes=[ytB[d]])
                            op("dve", lambda: DVE.tensor_tensor(out=ytmp[:, 0, :], in0=ytmp[:, 0, :], in1=ytmp[:, 1, :], op=ALU.add), reads=ytB, writes=[ytB[0]])
                            op("dve", lambda: DVE.tensor_tensor(out=ytmp[:, 0, :], in0=ytmp[:, 0, :], in1=PS(bky)[:, 0:256], op=ALU.add),
                               reads=[ytB[0], PSB[bky]], writes=[ytB[0]])
                            op("dve", lambda: DVE.tensor_tensor(out=yg[:, pb, :], in0=ytmp[:, 0, :], in1=zs[:, tt, :], op=ALU.mult),
                               reads=[ytB[0], zB[tt]], writes=[ygB[pb]])
                            op("act", lambda: ACT.activation(out=sqj[:], in_=yg[:, pb, :], func=AF.Square, accum_out=ssq[:, tt, g:g + 1]),
                               reads=[ygB[pb]], writes=[sqjB, ssB[tt]])

                        def stageB2(tt):
                            pb = tt % 2
                            tb = tt // 4
                            tsl = slice(tt * 128, (tt + 1) * 128)
                            bkt = nbank(0, 6)
                            for cc in range(2):
                                op("pe", lambda cc=cc: PE.matmul(PS(bkt)[:, cc * 128:(cc + 1) * 128], lhsT=yg[:, pb, cc * 128:(cc + 1) * 128], rhs=ident, start=True, stop=True),
                                   reads=[ygB[pb], cbfB], writes=[PSB[bkt]], sig=(cc == 1))
                            for cc in range(2):
                                ck_ = 2 * g + cc
                                op("act", lambda cc=cc, ck_=ck_: ACT.activation(out=yoT[:, ck_, tsl], in_=PS(bkt)[:, cc * 128:(cc + 1) * 128], func=AF.Identity,
                                                                             scale=pv(f"ssdn{e}")[:, ck_:ck_ + 1]),
                                   reads=[PSB[bkt], parB], writes=[yoB[ck_][tb]])

                        tts = list(range(t0, t0 + ntl))
                        stageA(tts[0])
                        stageA2(tts[0])
                        for i_, tt in enumerate(tts):
                            if i_ + 1 < len(tts):
                                stageA(tts[i_ + 1])
                            if i_ > 0:
                                stageB2(tts[i_ - 1])
                            stageB(tt)
                            if i_ + 1 < len(tts):
                                stageA2(tts[i_ + 1])
                        stageB2(tts[-1])
                    checkpoint('g0scan')
                checkpoint('ssdscan')
                rst = sbt(s2, "rst", [128, NT], F32)
                rstB = Buf()
                rsb = pre[:].rearrange("p a b -> p (a b)").bitcast(F32)
                rsbB = [preB, preB, preB]
                dg = sqj[:].bitcast(F32)
                dgB = sqjB
                op("dve", lambda: DVE.tensor_reduce(out=rst[:], in_=ssq[:], axis=mybir.AxisListType.X, op=ALU.add), reads=ssB, writes=[rstB])
                op("act", lambda: ACT.activation(out=rst[:], in_=rst[:], func=AF.Ln, scale=1.0 / 1024.0, bias=KC(1)), reads=[rstB, kcB_], writes=[rstB])
                op("act", lambda: ACT.activation(out=rst[:], in_=rst[:], func=AF.Exp, scale=-0.5), reads=[rstB], writes=[rstB])
                for tt in range(NT):
                    tb = tt // 4
                    op("dve", lambda tt=tt: DVE.tensor_scalar(out=dg, in0=ident_f, scalar1=rst[:, tt:tt + 1], scalar2=None, op0=ALU.mult),
                       reads=[rstB, cfB], writes=[dgB])
                    op("pe", lambda tt=tt: PE.matmul(PS(6)[:, 0:128], lhsT=ones_f, rhs=dg, start=True, stop=True), reads=[dgB, cfB], writes=[PSB[6]])
                    op("act", lambda tt=tt: ACT.copy(out=rsb[:, tt * 128:(tt + 1) * 128], in_=PS(6)[:, 0:128]), reads=[PSB[6]], writes=[rsbB[tb]])
                outproj(l, [f"O1_{e}_0", f"O1_{e}_1"], yoT, yoB, rowscale=(rsb, rsbB))
                P.fence()

            checkpoint('ssd')
            with ExitStack() as s2:
                rope = sbt(s2, "rope", [128, 2, 1024], F32)
                ropeB = Buf()
                for i in range(2):
                    op("sp", lambda i=i: nc.sync.dma_start(out=rope[:, i, :], in_=rope_in[i]), writes=[ropeB], dkey="rope")
                qT = sbt(s2, "qT", [128, 2, T], BF16)
                qB = BG(2, 3)
                qraw = sbt(s2, "qraw", [128, 2, 1024], BF16)
                qrB = BG(2, 2)
                rt = sbt(s2, "rt", [128, 2, 2, 512], BF16)
                rtB = BG(2, 2)
                kc32 = sbt(s2, "kc32", [128, 2, 512], F32)
                kcB = [Buf(), Buf()]
                vc32 = sbt(s2, "vc32", [128, 2, 4, 128], F32)
                vcB = [Buf(), Buf()]
                kcb = sbt(s2, "kcb", [128, 512], BF16)
                kcbB = Buf()
                va = sbt(s2, "va", [128, 16, 132], BF16)
                vaB = BG(16)
                op("dve", lambda: DVE.memset(va[:], 1.0), writes=vaB)
                kvo = sbt(s2, "kvo", [128, 2, 2, 4, 128], F32)
                kvB = BG(2, 2)
                ET = sbt(s2, "ET", [128, 2, 12, 2, 256], BF16)
                ETB = BG(2, 12)
                osb = sbt(s2, "osb", [128, 2, 128], F32)
                osB = BG(2)
                sm = sbt(s2, "sm", [128, 2, 3, 2], F32)
                smB_ = BG(2, 3)
                onb = sbt(s2, "onb", [128, 4, 128], BF16)
                onB = BG(4)
                oTs = sbt(s2, "oTs", [128, 512], F32)
                oTB = Buf()
                sqb = sbt(s2, "sqb", [128, 512], BF16)
                sqB_ = Buf()
                rsq = sbt(s2, "rsq", [128, 512], F32)
                rsqB = Buf()

                def load_cache(h):
                    pb = h % 2
                    op("sp", lambda: nc.sync.dma_start(out=kc32[:, pb, :], in_=ck_in[e, h]), writes=[kcB[pb]], dkey=f"kc{pb}")
                    src = cv_in[e].rearrange("(kt p) f -> p kt f", p=128)[:, :, h * 128:(h + 1) * 128]
                    op("sp", lambda: nc.sync.dma_start(out=vc32[:, pb], in_=src), writes=[vcB[pb]], dkey=f"vc{pb}")
                load_cache(0)
                tail_prev = [None]
                sk = list(range(0, 4)) + list(range(8, 16))
                blocks = [((0, 256), [4, 5]), ((256, 256), [6, 7]), ((512, 256), sk), ((768, 256), sk), ((1024, 256), sk), ((1280, 256), sk)]
                for h in range(8):
                    if h + 1 < 8:
                        load_cache(h + 1)
                    pb = h % 2
                    wt, wb = wget(f"QKV_{e}_{h}")
                    for qk in range(2):
                        for kc in range(8):
                            for tb in range(3):
                                op("pe", lambda: PE.matmul(PS(3 * qk + tb), lhsT=wt[:, kc, qk * 128:(qk + 1) * 128], rhs=hT[:, kc, tb * 512:(tb + 1) * 512],
                                                           start=(kc == 0), stop=(kc == 7)),
                                   reads=[wb, hB[kc][tb]], writes=[PSB[3 * qk + tb]], sig=(kc == 7))
                        op("dve", lambda: DVE.tensor_copy(out=qT[:, qk, 0:512], in_=PS(3 * qk)), reads=[PSB[3 * qk]], writes=[qB[qk][0]])
                        for sb_ in range(2):
                            op("dve", lambda: DVE.tensor_copy(out=qraw[:, qk, sb_ * 512:(sb_ + 1) * 512], in_=PS(3 * qk + 1 + sb_)), reads=[PSB[3 * qk + 1 + sb_]], writes=[qrB[qk][sb_]])
                    if tail_prev[0] is not None:
                        tail_prev[0]["c"]()
                    for tt in range(NT):
                        tb = tt // 4
                        tsl = slice(tt * 128, (tt + 1) * 128)
                        bk = nbank(3, 6)
                        ncol = 256 if tt < 4 else 128
                        c0 = 128 if tt < 4 else 256
                        for kc in range(8):
                            op("pe", lambda: PE.matmul(PS(bk)[:, 0:ncol], lhsT=hT[:, kc, tsl], rhs=wt[:, kc, c0:c0 + ncol], start=(kc == 0), stop=(kc == 7)),
                               reads=[wb, hB[kc][tb]], writes=[PSB[bk]], sig=(kc == 7))
                        vo = ncol - 128
                        if tt < 4:
                            op("act", lambda: ACT.copy(out=va[:, 4 + tt, 0:128], in_=PS(bk)[:, vo:vo + 128]), reads=[PSB[bk]], writes=[vaB[4 + tt]])
                            op("act", lambda: ACT.copy(out=kvo[:, pb, 0, tt, :], in_=PS(bk)[:, 0:128]), reads=[PSB[bk]], writes=[kvB[pb][0]])
                            op("act", lambda: ACT.copy(out=kvo[:, pb, 1, tt, :], in_=PS(bk)[:, 128:256]), reads=[PSB[bk]], writes=[kvB[pb][1]])
                        else:
                            op("act", lambda: ACT.copy(out=va[:, 4 + tt, 0:128], in_=PS(bk)[:, vo:vo + 128]), reads=[PSB[bk]], writes=[vaB[4 + tt]])
                        if tail_prev[0] is not None and tt == 5:
                            tail_prev[0]["a"]()
                        if tail_prev[0] is not None and tt == 11:
                            tail_prev[0]["b"]()
                    if tail_prev[0] is not None:
                        tail_prev[0]["n"]()
                        tail_prev[0] = None
                    for qk in range(2):
                        for sb_ in range(2):
                            bk = nbank(0, 3)
                            op("pe", lambda: PE.matmul(PS(bk), lhsT=cb("Psw"), rhs=qraw[:, qk, sb_ * 512:(sb_ + 1) * 512], start=True, stop=True),
                               reads=[qrB[qk][sb_], cbfB], writes=[PSB[bk]])
                            ss_ = slice(sb_ * 512, (sb_ + 1) * 512)
                            rp = (qk * 2 + sb_) % 2
                            op("pool", lambda: nc.gpsimd.tensor_tensor(out=rt[:, rp, 0, :], in0=qraw[:, qk, ss_], in1=rope[:, 0, ss_], op=ALU.mult),
                               reads=[qrB[qk][sb_], ropeB], writes=[rtB[rp][0]])
                            op("dve", lambda: DVE.tensor_tensor(out=rt[:, rp, 1, :], in0=PS(bk), in1=rope[:, 1, ss_], op=ALU.mult),
                               reads=[PSB[bk], ropeB], writes=[rtB[rp][1]])
                            op("pool", lambda: nc.gpsimd.tensor_tensor(out=qT[:, qk, 512 + sb_ * 512:1024 + sb_ * 512], in0=rt[:, rp, 0, :], in1=rt[:, rp, 1, :], op=ALU.add),
                               reads=rtB[rp], writes=[qB[qk][1 + sb_]])
                    for i in range(2):
                        dst = (nk_out if i == 0 else nv_out)[e].rearrange("(tt p) f -> p tt f", p=128)[:, :, h * 128:(h + 1) * 128]
                        op("sp", lambda: nc.sync.dma_start(out=dst, in_=kvo[:, pb, i]), reads=[kvB[pb][i]], dkey=f"kvo{pb}")
                    op("act", lambda: ACT.copy(out=kcb[:], in_=kc32[:, pb, :]), reads=[kcB[pb]], writes=[kcbB])
                    op("pool", lambda: nc.gpsimd.tensor_copy(out=va[:, 0:4, 0:128], in_=vc32[:, pb]), reads=[vcB[pb]], writes=vaB[0:4])

                    def pv_ops(bi):
                        (q0, qn), kts = blocks[bi]
                        eb = bi % 2
                        nk_ = len(kts)
                        lst = []
                        for qi in range(2):
                            for mm_ in range(2):
                                for ki, kt in enumerate(kts):
                                    def f(qi=qi, mm_=mm_, ki=ki, kt=kt):
                                        op("pe", lambda: PE.matmul(PS(4 + 2 * eb + qi)[:, mm_ * 132:mm_ * 132 + 129], lhsT=ET[:, eb, ki, mm_, qi * 128:(qi + 1) * 128],
                                                                   rhs=va[:, kt, 0:129], start=(ki == 0), stop=(ki == nk_ - 1)),
                                           reads=[ETB[eb][ki], vaB[kt]], writes=[PSB[4 + 2 * eb + qi]], sig=(mm_ == 1 and ki == nk_ - 1))
                                    lst.append(f)
                        return lst

                    def scores(bi, fill):
                        (q0, qn), kts = blocks[bi]
                        eb = bi % 2
                        qtb = q0 // 512
                        per = -(-len(fill) // len(kts)) if fill else 0
                        for ki, kt in enumerate(kts):
                            bp = nbank(0, 2)
                            for mm_ in range(2):
                                ps_ = slice(mm_ * 64, (mm_ + 1) * 64)
                                if kt < 4:
                                    lhs = kcb[ps_, kt * 128:(kt + 1) * 128]
                                    rd = [kcbB]
                                else:
                                    tk = kt - 4
                                    lhs = qT[ps_, 1, tk * 128:(tk + 1) * 128]
                                    rd = [qB[1][tk // 4]]
                                op("pe", lambda: PE.matmul(PS(2 * bp + mm_)[:, 0:qn], lhsT=lhs, rhs=qT[ps_, 0, q0:q0 + qn], start=True, stop=True),
                                   reads=rd + [qB[0][qtb]], writes=[PSB[2 * bp + mm_]], sig=(mm_ == 1))
                            op("act", lambda: ACT.activation(out=ET[:, eb, ki, :, 0:qn], in_=psum[bp][:, :, 0:qn], func=AF.Exp, scale=0.125, bias=KC(3)),
                               reads=[PSB[2 * bp], PSB[2 * bp + 1], kcB_], writes=[ETB[eb][ki]])
                            for _ in range(per):
                                if fill:
                                    fill.pop(0)()
                        while fill:
                            fill.pop(0)()

                    def chain(bi):
                        (q0, qn), kts = blocks[bi]
                        eb = bi % 2
                        t0_ = q0 // 128
                        sl4 = t0_ % 4
                        pvp = psum[2 + eb]
                        pq = [PSB[4 + 2 * eb], PSB[5 + 2 * eb]]
                        op("dve", lambda: DVE.reciprocal(out=sm[:, eb, 0, :].unsqueeze(2), in_=pvp[:, :, 128:129]), reads=pq, writes=[smB_[eb][0]])
                        op("dve", lambda: DVE.reciprocal(out=sm[:, eb, 1, :].unsqueeze(2), in_=pvp[:, :, 260:261]), reads=pq, writes=[smB_[eb][1]])
                        op("dve", lambda: DVE.tensor_scalar(out=sm[:, eb, 2, :], in0=sm[:, eb, 1, :], scalar1=nlam[:, 3:4], scalar2=None, op0=ALU.mult),
                           reads=[smB_[eb][1], nlB], writes=[smB_[eb][2]])
                        for qi in range(2):
                            bq = 4 + 2 * eb + qi
                            op("dve", lambda: DVE.tensor_scalar(out=osb[:, qi, :], in0=PS(bq)[:, 0:128], scalar1=sm[:, eb, 0, qi:qi + 1], scalar2=None, op0=ALU.mult),
                               reads=[PSB[bq], smB_[eb][0]], writes=[osB[qi]])
                            op("dve", lambda: DVE.scalar_tensor_tensor(out=onb[:, sl4 + qi, :], in0=PS(bq)[:, 132:260], scalar=sm[:, eb, 2, qi:qi + 1], in1=osb[:, qi, :],
                                                                       op0=ALU.mult, op1=ALU.add),
                               reads=[PSB[bq], smB_[eb][2], osB[qi]], writes=[onB[sl4 + qi]])

                    def norm1(tb):
                        bk = nbank(2, 4)
                        for q in range(4):
                            op("pe", lambda: PE.matmul(PS(bk)[:, q * 128:(q + 1) * 128], lhsT=onb[:, q, :], rhs=ident, start=True, stop=True),
                               reads=[onB[q], cbfB], writes=[PSB[bk]], sig=(q == 3))
                        op("dve", lambda: DVE.tensor_scalar(out=oTs[:], in0=PS(bk), scalar1=1.0, scalar2=None, op0=ALU.mult), reads=[PSB[bk]], writes=[oTB])
                        op("pool", lambda: nc.gpsimd.tensor_tensor(out=sqb[:], in0=oTs[:], in1=oTs[:], op=ALU.mult), reads=[oTB], writes=[sqB_])
                        bk2 = nbank(2, 4)
                        op("pe", lambda: PE.matmul(PS(bk2), lhsT=ones_b, rhs=sqb[:], start=True, stop=True), reads=[sqB_, cbfB], writes=[PSB[bk2]])
                        op("dve", lambda: DVE.tensor_scalar(out=rsq[:], in0=PS(bk2), scalar1=1.0 / 128.0, scalar2=None, op0=ALU.mult), reads=[PSB[bk2]], writes=[rsqB])
                        return bk2

                    def norm2(tb, bk2, h=h):
                        op("act", lambda: ACT.activation(out=rsq[:], in_=rsq[:], func=AF.Ln, bias=KC(1)), reads=[rsqB, kcB_], writes=[rsqB])
                        op("act", lambda: ACT.activation(out=rsq[:], in_=rsq[:], func=AF.Exp, scale=-0.5), reads=[rsqB], writes=[rsqB])
                        op("dve", lambda: DVE.scalar_tensor_tensor(out=yoT[:, h, tb * 512:(tb + 1) * 512], in0=oTs[:], scalar=sublnS[:, 0:1], in1=rsq[:],
                                                                   op0=ALU.mult, op1=ALU.mult),
                           reads=[oTB, rsqB, slB], writes=[yoB[h][tb]])

                    def norm1a(tb):
                        bk = nbank(2, 4)
                        for q in range(4):
                            op("pe", lambda: PE.matmul(PS(bk)[:, q * 128:(q + 1) * 128], lhsT=onb[:, q, :], rhs=ident, start=True, stop=True),
                               reads=[onB[q], cbfB], writes=[PSB[bk]], sig=(q == 3))
                        op("dve", lambda: DVE.tensor_scalar(out=oTs[:], in0=PS(bk), scalar1=1.0, scalar2=None, op0=ALU.mult), reads=[PSB[bk]], writes=[oTB])
                        op("pool", lambda: nc.gpsimd.tensor_tensor(out=sqb[:], in0=oTs[:], in1=oTs[:], op=ALU.mult), reads=[oTB], writes=[sqB_])

                    def norm1b(tb):
                        bk2 = nbank(2, 4)
                        op("pe", lambda: PE.matmul(PS(bk2), lhsT=ones_b, rhs=sqb[:], start=True, stop=True), reads=[sqB_, cbfB], writes=[PSB[bk2]])
                        op("dve", lambda: DVE.tensor_scalar(out=rsq[:], in0=PS(bk2), scalar1=1.0 / 128.0, scalar2=None, op0=ALU.mult), reads=[PSB[bk2]], writes=[rsqB])

                    scores(0, [])
                    pend = None
                    pend2 = None
                    nb_ = len(blocks)
                    for bi in range(nb_):
                        fill = pv_ops(bi)
                        if pend is not None and len(fill) >= 40:
                            tb_ = pend
                            fill.insert(8, lambda tb_=tb_: norm1a(tb_))
                            fill.insert(36, lambda tb_=tb_: norm1b(tb_))
                            pend2 = tb_
                            pend = None
                        if bi + 1 < nb_:
                            scores(bi + 1, fill)
                        else:
                            while fill:
                                fill.pop(0)()
                        if pend2 is not None:
                            norm2(pend2, None)
                            pend2 = None
                        if bi + 1 < nb_:
                            chain(bi)
                            if bi % 2 == 1:
                                pend = bi // 2
                    assert pend is None and pend2 is None

                    tail_prev[0] = {"c": (lambda chain=chain, nb_=nb_: chain(nb_ - 1)), "a": (lambda f=norm1a: f(2)),
                                    "b": (lambda f=norm1b: f(2)), "n": (lambda f=norm2: f(2, None))}
                if tail_prev[0] is not None:
                    for k_ in ("c", "a", "b", "n"):
                        tail_prev[0][k_]()
                outproj(l, [f"O2_{e}_0", f"O2_{e}_1"], yoT, yoB)
                P.fence()
            P.fence()

    sublnS = sbt(es, "sublnS", [128, 1], F32)
    slB = Buf()

    try:
        if DEPTH_RUN > 0:
            for t in range(4):
                ada_tile(0, t)
            ada_finish(0, 0)
            checkpoint('ada0')
        for l in range(DEPTH_RUN):
            m = mod[l % 2]
            g = gsc[l % 2]
            with ExitStack() as sl_:
                hT = sbt(sl_, "hT", [128, 8, T], BF16)
                hB = BG(8, 3)
                modulate(sl_, hT, hB, lambda kc, v: g[:, 0, kc, v:v + 1], lambda kc, v: m[:, kc, v:v + 1], [modB[l % 2], gscB[l % 2]])
                checkpoint('mod')
                if l % 2 == 0:
                    li = 0.8 - 0.6 * math.exp(-0.3 * l)
                    op("dve", lambda: DVE.tensor_scalar(out=sublnS[:], in0=pv(f"subln{l // 2}"), scalar1=1.0 - li, scalar2=None, op0=ALU.mult), reads=[parB], writes=[slB])
                    ab_layer(l, hT, hB)
                else:
                    fourier_layer(l, hT, hB)
                P.fence()
            checkpoint('mixer')
            ffn(l)
            checkpoint('ffn')
        with ExitStack() as sl_:
            yo = sbt(sl_, "yo", [128, 8, T], F32)
            yB = BG(8, 3)
            nfs = sbt(sl_, "nfs", [128, 8], F32)
            nfB = Buf()
            op("dve", lambda: DVE.tensor_scalar(out=nfs[:], in0=pv("nfin"), scalar1=32.0, scalar2=None, op0=ALU.mult), reads=[parB], writes=[nfB])
            modulate(sl_, yo, yB, lambda kc, v: nfs[:, kc:kc + 1], lambda kc, v: None, [nfB])
            yout = yT_out.rearrange("(kc p) t -> p kc t", p=128)
            for kc in range(8):
                op("sp", lambda kc=kc: nc.sync.dma_start(out=yout[:, kc, :], in_=yo[:, kc, :]), reads=yB[kc], dkey="st2")

    except _Stop:
        pass
    for k in list(P.isdma):
        if not k.startswith('ring'):
            nc.sync.wait_ge(P.sems[k], P.cnt[k])
    assert STOP_AT is not None or st["used"] == len(plan), (st["used"], len(plan))
    es.close()
    _CACHE['trace'] = P.trace
    return nc, P.nins


_CACHE = {}


def kernel(x_prompt, x_sample, cache_k, cache_v, state_ssd_fwd, state_ssd_bwd, c, c_ctx, w_ada, b_ada, norm_mix,
           norm_ffn, w_in_ab, conv_w, conv_b, dt_bias, a_log, d_skip, ssd_norm, lambda_qk, subln, w_out_ab,
           w_four, b_four, w_ffn_in, w_ffn_out, norm_final):
    f = lambda a: np.ascontiguousarray(np.asarray(a, dtype=np.float32))
    x_prompt, x_sample, cache_k, cache_v = f(x_prompt), f(x_sample), f(cache_k), f(cache_v)
    state_ssd_fwd, state_ssd_bwd, c, c_ctx = f(state_ssd_fwd), f(state_ssd_bwd), f(c), f(c_ctx)
    if "nc" not in _CACHE:
        _CACHE["nc"] = build_program()[0]
        _CACHE["con"] = make_consts()
    nc = _CACHE["nc"]
    con, dft, rope = _CACHE["con"]
    w_inp = f(np.asarray(w_in_ab)[:, :, win_perm()])
    w_ffi = f(np.asarray(w_ffn_in)[:, :, ffi_perm()])
    shared = {"con": con, "dft": dft, "rope": rope, "w_ada": f(w_ada), "w_inp": w_inp, "w_out": f(w_out_ab),
              "w_four": f(w_four), "w_ffi": w_ffi, "w_ffo": f(w_ffn_out)}
    par0 = np.zeros((128, NPAR), np.float32)

    def put(name, a):
        o, w = PL[name]
        par0[:, o:o + w] = np.asarray(a, np.float32).reshape(128, w)
    fm = lambda v, n: np.asarray(v, np.float32).reshape(n, 128).T
    rb = lambda v: np.broadcast_to(np.asarray(v, np.float32).reshape(1, -1), (128, np.asarray(v).size))
    put("nfin", fm(norm_final, 8))
    for l in range(4):
        put(f"nm{l}", fm(norm_mix[l], 8))
        put(f"nf{l}", fm(norm_ffn[l], 8))
        put(f"bada{l}", fm(b_ada[l], 48))
    for e in range(2):
        cw = np.asarray(conv_w[e], np.float32)
        cbv = np.asarray(conv_b[e], np.float32)
        cwm = np.zeros((128, 80), np.float32)
        cbm = np.zeros((128, 16), np.float32)
        for sl in range(16):
            ch = conv_chan(sl)
            cwm[:, sl * 5:(sl + 1) * 5] = cw[:, ch].T
            cbm[:, sl] = cbv[ch]
        put(f"cw{e}", cwm)
        put(f"cb{e}", cbm)
        put(f"ssdn{e}", fm(ssd_norm[e], 8))
        put(f"subln{e}", np.asarray(subln[e], np.float32).reshape(128, 1))
        put(f"dtb{e}", rb(dt_bias[e]))
        put(f"alog{e}", rb(a_log[e]))
        put(f"dsk{e}", rb(d_skip[e]))
        put(f"lqk{e}", rb(lambda_qk[e]))
        put(f"b4{e}", fm(b_four[e], 8))
    in_maps = []
    for i in range(8):
        s = i // 2
        xt = np.concatenate([x_prompt[2 * i], x_prompt[2 * i + 1], x_sample[s]], 0)
        p = par0.copy()
        cvec = np.stack([c_ctx, c[s]], 0).reshape(2, 8, 128).transpose(2, 1, 0).reshape(128, 16)
        o, w = PL["cvec"]
        p[:, o:o + w] = cvec
        mp = dict(shared)
        mp["xT_in"] = np.ascontiguousarray(xt.T)
        mp["par"] = p
        mp["ck"] = np.ascontiguousarray(cache_k[s].reshape(2, 512, 8, 128).transpose(0, 2, 3, 1))
        mp["cv"] = np.ascontiguousarray(cache_v[s].reshape(2, 512, 1024))
        mp["sf"] = np.ascontiguousarray(state_ssd_fwd[s].reshape(2, 1024, 128).transpose(0, 2, 1))
        mp["sb"] = np.ascontiguousarray(state_ssd_bwd[s].reshape(2, 1024, 128).transpose(0, 2, 1))
        in_maps.append(mp)
    res = run_bass_kernel_spmd(nc, in_maps[:DBG_CORES], core_ids=list(range(DBG_CORES)))
    R = list(res.results) + [res.results[0]] * (8 - DBG_CORES)
    y_prompt = np.zeros((16, 256, 1024), np.float32)
    y_sample = np.zeros((4, 1024, 1024), np.float32)
    new_k = np.zeros((16, 2, 256, 8, 2, 64), np.float32)
    new_v = np.zeros((16, 2, 256, 8, 128), np.float32)
    new_f = np.zeros((16, 2, 16, 64, 128), np.float32)
    new_b = np.zeros((16, 2, 16, 64, 128), np.float32)
    for i in range(8):
        yT = np.asarray(R[i]["yT"])
        for q in range(2):
            b = 2 * i + q
            y_prompt[b] = yT[:, q * 256:(q + 1) * 256].T
            new_k[b] = np.asarray(R[i]["nk"])[:, q * 256:(q + 1) * 256].reshape(2, 256, 8, 2, 64)
            new_v[b] = np.asarray(R[i]["nv"])[:, q * 256:(q + 1) * 256].reshape(2, 256, 8, 128)
            new_f[b] = np.asarray(R[i]["nf"])[:, q].transpose(0, 2, 1).reshape(2, 16, 64, 128)
            new_b[b] = np.asarray(R[i]["nb"])[:, q].transpose(0, 2, 1).reshape(2, 16, 64, 128)
        if i % 2 == 0:
            y_sample[i // 2] = yT[:, 512:].T
    return (y_prompt, y_sample, new_k, new_v, new_f, new_b)
```
